# Optimizing a Trainium2 kernel written in Bass

```python
import math
import jax, jax.numpy as jnp
from jax import lax
import numpy as np

D_MODEL = 2048
BATCH = 4
SEQ = 2048
DEPTH = 4

HEAD_DIM = 64
BLOCK = 128
EPS = 1e-6
H_A = D_MODEL // (2 * HEAD_DIM)
Q_LORA = D_MODEL // 4
KV_LORA = D_MODEL // 8
IDX_HEADS = 16
IDX_DIM = 64
TOPK_MAX = 256
TOPK_DIV = 4
H_B = D_MODEL // (2 * HEAD_DIM)
KV_B = H_B // 8
WINDOW = 128
H_C = D_MODEL // HEAD_DIM
N_BUCKETS = 32
MAX_DISTANCE = 1024
D_FF = 128 * (-(-(8 * D_MODEL // 3) // 128))
CONV_WIDTH = 3
PLE_DIM = 256

W_IN_EVEN = Q_LORA + KV_LORA + IDX_DIM + IDX_HEADS + H_B * HEAD_DIM + 2 * KV_B * HEAD_DIM
W_IN_ODD = 3 * H_C * HEAD_DIM + H_C
N_EVEN = (DEPTH + 1) // 2
N_ODD = DEPTH // 2

kernel_name = "hybrid_dsa_swa_fox_convffn_trunk"


def rms_norm(x, g):
    xf = x.astype(jnp.float32)
    y = xf * lax.rsqrt(jnp.mean(xf * xf, axis=-1, keepdims=True) + EPS)
    return y.astype(x.dtype) * g


def rel_bucket(dist):
    n = jnp.maximum(dist, 0)
    exact = N_BUCKETS // 2
    nf = jnp.maximum(n, 1).astype(jnp.float32)
    large = exact + (jnp.log(nf / exact) / math.log(MAX_DISTANCE / exact)
                     * (N_BUCKETS - exact)).astype(jnp.int32)
    large = jnp.minimum(large, N_BUCKETS - 1)
    return jnp.where(n < exact, n, large)


def to_blocks(t):
    b, s = t.shape[:2]
    return jnp.moveaxis(t.reshape((b, s // BLOCK, BLOCK) + t.shape[2:]), 1, 0)


def from_blocks(o):
    nb, b, q, f = o.shape
    return jnp.moveaxis(o, 0, 1).reshape(b, nb * q, f)


def dsa_attention(c_q, c_kv, k_idx, w_idx, cq_g, ckv_g, w_uq, q_g, w_qidx, w_uv, bias_tab):
    B, S, _ = c_q.shape
    nb = S // BLOCK
    topk = min(TOPK_MAX, S // TOPK_DIV)
    cq = rms_norm(c_q, cq_g)
    kv = rms_norm(c_kv, ckv_g)
    q = rms_norm((cq @ w_uq).reshape(B, S, H_A, KV_LORA), q_g)
    q_idx = (cq @ w_qidx).reshape(B, S, IDX_HEADS, IDX_DIM)
    idx_scale = (IDX_DIM * IDX_HEADS) ** -0.5
    att_scale = KV_LORA ** -0.5
    pos = jnp.arange(S)

    def block(args):
        n, q_b, qi_b, wi_b = args
        t = n * BLOCK + jnp.arange(BLOCK)
        dots = jax.nn.relu(jnp.einsum('bqhd,bsd->bqhs', qi_b, k_idx))
        score = jnp.einsum('bqh,bqhs->bqs', wi_b, dots).astype(jnp.float32) * idx_scale
        score = jnp.where(pos[None, None, :] <= t[None, :, None], score, -jnp.inf)
        _, sel = lax.top_k(score, topk)
        kv_sel = jax.vmap(lambda kvb, ib: kvb[ib])(kv, sel)
        dist = t[None, :, None] - sel
        bias = jnp.moveaxis(bias_tab[rel_bucket(dist)], 3, 1)
        logits = (jnp.einsum('bqhc,bqkc->bhqk', q_b, kv_sel).astype(jnp.float32) * att_scale
                  + bias.astype(jnp.float32))
        logits = jnp.where((dist >= 0)[:, None], logits, -jnp.inf)
        probs = jax.nn.softmax(logits, axis=-1).astype(kv.dtype)
        o_lat = jnp.einsum('bhqk,bqkc->bqhc', probs, kv_sel)
        o = jnp.einsum('bqhc,hcd->bqhd', o_lat, w_uv)
        return o.reshape(B, BLOCK, H_A * HEAD_DIM)

    out = lax.map(block, (jnp.arange(nb), to_blocks(q), to_blocks(q_idx), to_blocks(w_idx)))
    return from_blocks(out)


def swa_sink_attention(q, k, v, q_g, k_g, sinks, bias_tab):
    B, S, _ = q.shape
    nb = S // BLOCK
    G = H_B // KV_B
    q = rms_norm(q.reshape(B, S, H_B, HEAD_DIM), q_g).reshape(B, nb, BLOCK, KV_B, G, HEAD_DIM)
    k = rms_norm(k.reshape(B, S, KV_B, HEAD_DIM), k_g)
    v = v.reshape(B, S, KV_B, HEAD_DIM)

    def band(t):
        tb = jnp.pad(t, ((0, 0), (BLOCK, 0), (0, 0), (0, 0))).reshape(B, nb + 1, BLOCK, KV_B, HEAD_DIM)
        return jnp.concatenate([tb[:, :-1], tb[:, 1:]], axis=2)

    kb, vb = band(k), band(v)
    i = jnp.arange(BLOCK)[:, None]
    j = jnp.arange(2 * BLOCK)[None, :]
    dist = i + BLOCK - j
    key_pos = jnp.arange(nb)[:, None, None] * BLOCK - BLOCK + j[None]
    mask = (dist >= 0) & (dist < WINDOW) & (key_pos >= 0)
    bias = jnp.moveaxis(bias_tab[rel_bucket(dist)], -1, 0).reshape(KV_B, G, BLOCK, 2 * BLOCK)
    logits = (jnp.einsum('bnqhgd,bnkhd->bnhgqk', q, kb).astype(jnp.float32) * HEAD_DIM ** -0.5
              + bias.astype(jnp.float32))
    logits = jnp.where(mask[None, :, None, None], logits, -jnp.inf)
    sink = jnp.broadcast_to(sinks.reshape(KV_B, G, 1, 1).astype(jnp.float32), logits.shape[:-1] + (1,))
    probs = jax.nn.softmax(jnp.concatenate([logits, sink], axis=-1), axis=-1)[..., :-1].astype(v.dtype)
    o = jnp.einsum('bnhgqk,bnkhd->bnqhgd', probs, vb)
    return o.reshape(B, S, H_B * HEAD_DIM)


def forgetting_attention(proj, f_bias, q_g, k_g):
    B, S, _ = proj.shape
    nb = S // BLOCK
    HD = H_C * HEAD_DIM
    q, k, v, f = jnp.split(proj, [HD, 2 * HD, 3 * HD], axis=-1)
    q = rms_norm(q.reshape(B, S, H_C, HEAD_DIM), q_g)
    k = rms_norm(k.reshape(B, S, H_C, HEAD_DIM), k_g)
    v = v.reshape(B, S, H_C, HEAD_DIM)
    log_f = jax.nn.log_sigmoid((f + f_bias).astype(jnp.float32))
    cum = jnp.cumsum(log_f, axis=1)
    cum_k = jnp.moveaxis(cum, 2, 1)
    pos = jnp.arange(S)
    scale = HEAD_DIM ** -0.5

    def block(args):
        n, q_b, c_b = args
        t = n * BLOCK + jnp.arange(BLOCK)
        decay = jnp.moveaxis(c_b, 2, 1)[..., None] - cum_k[:, :, None, :]
        logits = jnp.einsum('bqhd,bshd->bhqs', q_b, k).astype(jnp.float32) * scale + decay
        logits = jnp.where(pos[None, :] <= t[:, None], logits, -jnp.inf)
        probs = jax.nn.softmax(logits, axis=-1).astype(v.dtype)
        return jnp.einsum('bhqs,bshd->bqhd', probs, v).reshape(B, BLOCK, HD)

    out = lax.map(block, (jnp.arange(nb), to_blocks(q), to_blocks(cum)))
    return from_blocks(out)


def conv_ffn(h, w_up, conv_w, w_down):
    u = h @ w_up
    C = u.shape[-1]
    u = lax.conv_general_dilated(u, conv_w[:, None, :], window_strides=(1,),
                                 padding=[(CONV_WIDTH - 1, 0)],
                                 dimension_numbers=('NWC', 'WIO', 'NWC'),
                                 feature_group_count=C)
    gate, up = jnp.split(u, 2, axis=-1)
    return (jax.nn.silu(gate) * up) @ w_down


def setup_inputs(seed: int = 0) -> dict:
    key = jax.random.key(seed)
    ks = iter(jax.random.split(key, 40))
    f32 = jnp.float32
    D = D_MODEL

    def w(shape, fan_in):
        return jax.random.normal(next(ks), shape, f32) * fan_in ** -0.5

    def gain(shape):
        return 1.0 + 0.02 * jax.random.normal(next(ks), shape, f32)

    def normal(shape, s):
        return s * jax.random.normal(next(ks), shape, f32)

    return {
        "x": jax.random.normal(next(ks), (BATCH, SEQ, D), f32),
        "p": jax.random.normal(next(ks), (DEPTH, BATCH, SEQ, PLE_DIM), f32),
        "attn_norm": gain((DEPTH, D)),
        "ffn_norm": gain((DEPTH, D)),
        "ple_norm": gain((DEPTH, D)),
        "rel_bias": normal((N_BUCKETS, H_A + H_B), 0.3),
        "w_in_even": w((N_EVEN, D, W_IN_EVEN), D),
        "a_cq_norm": gain((N_EVEN, Q_LORA)),
        "a_ckv_norm": gain((N_EVEN, KV_LORA)),
        "a_w_uq": w((N_EVEN, Q_LORA, H_A * KV_LORA), Q_LORA),
        "a_q_norm": gain((N_EVEN, KV_LORA)),
        "a_w_qidx": w((N_EVEN, Q_LORA, IDX_HEADS * IDX_DIM), Q_LORA),
        "a_w_uv": w((N_EVEN, H_A, KV_LORA, HEAD_DIM), KV_LORA),
        "b_q_norm": gain((N_EVEN, HEAD_DIM)),
        "b_k_norm": gain((N_EVEN, HEAD_DIM)),
        "b_sinks": normal((N_EVEN, H_B), 1.0),
        "w_out_even": w((N_EVEN, D, D), D),
        "w_in_odd": w((N_ODD, D, W_IN_ODD), D),
        "c_forget_bias": 2.5 + normal((N_ODD, H_C), 1.0),
        "c_q_norm": gain((N_ODD, HEAD_DIM)),
        "c_k_norm": gain((N_ODD, HEAD_DIM)),
        "w_out_odd": w((N_ODD, D, D), D),
        "w_up": w((DEPTH, D, 2 * D_FF), D),
        "ffn_conv": w((DEPTH, CONV_WIDTH, 2 * D_FF), CONV_WIDTH),
        "w_down": w((DEPTH, D_FF, D), D_FF),
        "w_ple_gate": w((DEPTH, D, D), D),
        "w_ple_proj": w((DEPTH, PLE_DIM, D), PLE_DIM),
    }


def reference(x, p, attn_norm, ffn_norm, ple_norm, rel_bias, w_in_even, a_cq_norm, a_ckv_norm,
              a_w_uq, a_q_norm, a_w_qidx, a_w_uv, b_q_norm, b_k_norm, b_sinks, w_out_even,
              w_in_odd, c_forget_bias, c_q_norm, c_k_norm, w_out_odd, w_up, ffn_conv, w_down,
              w_ple_gate, w_ple_proj):
    bias_a = rel_bias[:, :H_A]
    bias_b = rel_bias[:, H_A:]
    o1 = Q_LORA
    o2 = o1 + KV_LORA
    o3 = o2 + IDX_DIM
    o4 = o3 + IDX_HEADS
    o5 = o4 + H_B * HEAD_DIM
    o6 = o5 + KV_B * HEAD_DIM
    for i in range(DEPTH):
        h = rms_norm(x, attn_norm[i])
        if i % 2 == 0:
            e = i // 2
            proj = h @ w_in_even[e]
            c_q, c_kv, k_idx, w_idx, qb, kb, vb = jnp.split(proj, [o1, o2, o3, o4, o5, o6], axis=-1)
            y_a = dsa_attention(c_q, c_kv, k_idx, w_idx, a_cq_norm[e], a_ckv_norm[e], a_w_uq[e],
                                a_q_norm[e], a_w_qidx[e], a_w_uv[e], bias_a)
            y_b = swa_sink_attention(qb, kb, vb, b_q_norm[e], b_k_norm[e], b_sinks[e], bias_b)
            y = jnp.concatenate([y_a, y_b], axis=-1) @ w_out_even[e]
        else:
            o = i // 2
            y = forgetting_attention(h @ w_in_odd[o], c_forget_bias[o], c_q_norm[o], c_k_norm[o]) @ w_out_odd[o]
        x = x + y
        x = x + conv_ffn(rms_norm(x, ffn_norm[i]), w_up[i], ffn_conv[i], w_down[i])
        gate = jax.nn.sigmoid(rms_norm(x, ple_norm[i]) @ w_ple_gate[i])
        x = x + gate * (p[i] @ w_ple_proj[i])
    return x
```

```python
import math
from contextlib import ExitStack
import numpy as np
import concourse.bass as bass
import concourse.mybir as mybir
from concourse.bass_utils import run_bass_kernel_spmd

F32 = mybir.dt.float32
BF16 = mybir.dt.bfloat16
ALU = mybir.AluOpType
AF = mybir.ActivationFunctionType

EPOCH = 30000
NDSEM = 40

D = 2048
T = 2048
DEPTH = 4
TT = 1024
TC = 512
DFF = 5504
NFC = 43
EPS = 1e-6
XA = 1280
XB = 384
NEG = -30000.0

SP_ATTN = 0
SP_FFN = 64
SP_PLE = 128
SP_CONV = 192
SP_CQ = 1224
SP_CKV = 1232
SP_AQ = 1236
SP_BQ = 1240
SP_BK = 1242
SP_CQN = 1244
SP_CKN = 1246
SP_FB = 1248
SP_SINK = 1250
SP_RELB = 1282
SP_B31 = 1320
NSP = 1340

WEIGHTS = [
    ("w_in_even", (2, 2048, 2128)), ("a_w_uq", (2, 512, 4096)), ("a_w_qidx", (2, 512, 1024)),
    ("a_w_uv", (2, 16, 256, 64)), ("w_out_even", (2, 2048, 2048)), ("w_in_odd", (2, 2048, 6176)),
    ("w_out_odd", (2, 2048, 2048)), ("w_up", (4, 2048, 11008)), ("w_down", (4, 5504, 2048)),
    ("w_ple_gate", (4, 2048, 2048)), ("w_ple_proj", (4, 256, 2048)),
]


class Buf:
    __slots__ = ("name", "w", "r", "rd")

    def __init__(self, name=""):
        self.name = name
        self.w = None
        self.r = {}
        self.rd = []


class Prog:
    ENGS = ("pe", "act", "dve", "pool", "sp")

    def __init__(self, nc, stack):
        self.nc = nc
        self.stack = stack
        self.eng = {"pe": nc.tensor, "act": nc.scalar, "dve": nc.vector,
                    "pool": nc.gpsimd, "sp": nc.sync}
        self.cnt = {e: 0 for e in self.ENGS}
        self.esems = {e: [] for e in self.ENGS}
        self.seen_e = {e: {p: 0 for p in self.ENGS} for e in self.ENGS}
        self.seen_d = {e: {} for e in self.ENGS}
        self.dsem = {}
        for q in ("sp", "pool"):
            self.dsem[q] = [[self._newsem(f"d{q}{i}"), 0] for i in range(NDSEM)]
        self.dptr = {"sp": 0, "pool": 0}
        self.bar_sem = self._newsem("bar")
        self.bar_cnt = 0
        self.n_inst = 0

    def _newsem(self, name):
        return self.stack.enter_context(self.nc.semaphore(name))

    def _esem(self, e, idx):
        ep = (idx - 1) // EPOCH
        while len(self.esems[e]) <= ep:
            self.esems[e].append(self._newsem(f"e{e}{len(self.esems[e])}"))
        return self.esems[e][ep], (idx - 1) % EPOCH + 1

    def _wait(self, e, ev):
        if ev is None:
            return
        if ev[0] == "e":
            _, p, idx = ev
            if p == e and e == "pe":
                return
            if self.seen_e[e][p] >= idx:
                return
            self.seen_e[e][p] = idx
            s, v = self._esem(p, idx)
            self.eng[e].wait_ge(s, v)
        else:
            _, s, v, key = ev
            if self.seen_d[e].get(key, 0) >= v:
                return
            self.seen_d[e][key] = v
            self.eng[e].wait_ge(s, v)
        self.n_inst += 1

    def _deps(self, e, reads, writes):
        for b in reads:
            self._wait(e, b.w)
        for b in writes:
            self._wait(e, b.w)
            for p, idx in b.r.items():
                if p != e:
                    self._wait(e, ("e", p, idx))
            for ev in b.rd:
                self._wait(e, ev)

    def _mark(self, ev, reads, writes):
        for b in reads:
            if ev[0] == "e":
                b.r[ev[1]] = ev[2]
            else:
                b.rd.append(ev)
        for b in writes:
            b.w = ev
            b.r = {}
            b.rd = []

    def op(self, e, fn, reads=(), writes=()):
        self._deps(e, reads, writes)
        inst = fn(self.eng[e])
        self.cnt[e] += 1
        idx = self.cnt[e]
        s, _ = self._esem(e, idx)
        inst.then_inc(s, 1)
        self.n_inst += 1
        self._mark(("e", e, idx), reads, writes)

    def dma(self, q, out, in_, reads=(), writes=(), **kw):
        self._deps(q, reads, writes)
        slot = self.dsem[q][self.dptr[q]]
        key = (q, self.dptr[q])
        self.dptr[q] = (self.dptr[q] + 1) % NDSEM
        if slot[1] > 0:
            self._wait(q, ("d", slot[0], slot[1], key))
        inst = self.eng[q].dma_start(out=out, in_=in_, **kw)
        slot[1] += 16
        inst.then_inc(slot[0], 16)
        self.n_inst += 1
        self._mark(("d", slot[0], slot[1], key), reads, writes)

    def barrier(self):
        for p in self.ENGS:
            if p != "sp" and self.cnt[p] > 0:
                self._wait("sp", ("e", p, self.cnt[p]))
        for q in ("sp", "pool"):
            for i, slot in enumerate(self.dsem[q]):
                if slot[1] > 0:
                    self._wait("sp", ("d", slot[0], slot[1], (q, i)))
        self.bar_cnt += 1
        self.eng["sp"].sem_inc(self.bar_sem, 1)
        for e in self.ENGS:
            if e != "sp":
                self.eng[e].wait_ge(self.bar_sem, self.bar_cnt)
                for p in self.ENGS:
                    self.seen_e[e][p] = self.cnt[p]
                for q in ("sp", "pool"):
                    for i, slot in enumerate(self.dsem[q]):
                        self.seen_d[e][(q, i)] = slot[1]
        self.n_inst += 6


class Scope:
    def __init__(self, P):
        self.P = P
        self.st = ExitStack()

    def __enter__(self):
        self.st.__enter__()
        return self

    _uid = [0]

    def sb(self, name, shape, dt):
        Scope._uid[0] += 1
        return self.st.enter_context(self.P.nc.sbuf_tensor(f"{name}_{Scope._uid[0]}", list(shape), dt))

    def __exit__(self, *a):
        self.P.barrier()
        return self.st.__exit__(*a)


def rel_bucket_np(n):
    n = np.maximum(n, 0)
    exact = 16
    nf = np.maximum(n, 1).astype(np.float32)
    large = exact + (np.log(nf / np.float32(exact)) / np.float32(math.log(1024 / exact))
                     * np.float32(32 - exact)).astype(np.int32)
    large = np.minimum(large, 31)
    return np.where(n < exact, n, large)


def host_consts():
    cst = np.zeros((128, 512), np.float32)
    cst[:, 0:128] = np.eye(128)
    cst[:, 128:256] = np.eye(128)[::-1]
    i = np.arange(128)
    cst[:, 256:384] = (i[None, :] >= i[:, None]).astype(np.float32)
    cst[:, 384:512] = np.where(i[None, :] <= i[:, None], 0.0, -1e30)
    oh = np.zeros((33, XA + XB), np.float32)
    y = np.arange(XA)
    xx = y - 127
    b = np.where(xx < 0, 32, rel_bucket_np(xx))
    oh[b, y] = 1.0
    y = np.arange(XB)
    xx = y - 127
    b = np.where((xx < 0) | (xx >= 128), 32, rel_bucket_np(xx))
    oh[b, XA + y] = 1.0
    return cst, oh


def pack_small(inp):
    sp = np.zeros((128, NSP), np.float32)

    def fm(v):
        return np.ascontiguousarray(v.reshape(-1, 128).T)
    for l in range(4):
        sp[:, SP_ATTN + l * 16:SP_ATTN + (l + 1) * 16] = fm(inp["attn_norm"][l])
        sp[:, SP_FFN + l * 16:SP_FFN + (l + 1) * 16] = fm(inp["ffn_norm"][l])
        sp[:, SP_PLE + l * 16:SP_PLE + (l + 1) * 16] = fm(inp["ple_norm"][l])
        for k in range(3):
            c0 = SP_CONV + (l * 3 + k) * 86
            sp[:, c0:c0 + 86] = fm(inp["ffn_conv"][l, k])
    for e in range(2):
        sp[:, SP_CQ + e * 4:SP_CQ + e * 4 + 4] = fm(inp["a_cq_norm"][e])
        sp[:, SP_CKV + e * 2:SP_CKV + e * 2 + 2] = fm(inp["a_ckv_norm"][e])
        sp[:, SP_AQ + e * 2:SP_AQ + e * 2 + 2] = fm(inp["a_q_norm"][e])
        sp[:, SP_BQ + e] = np.tile(inp["b_q_norm"][e], 2)
        sp[:, SP_BK + e] = np.tile(inp["b_k_norm"][e], 2)
        sp[:, SP_CQN + e] = np.tile(inp["c_q_norm"][e], 2)
        sp[:, SP_CKN + e] = np.tile(inp["c_k_norm"][e], 2)
        sp[0:32, SP_FB + e] = inp["c_forget_bias"][e]
        sp[:, SP_SINK + e * 16:SP_SINK + (e + 1) * 16] = inp["b_sinks"][e][None, :]
    sp[0:32, SP_RELB:SP_RELB + 32] = inp["rel_bias"]
    sp[:, SP_B31:SP_B31 + 16] = inp["rel_bias"][31, 0:16][None, :]
    return sp


def token_local(E, l, parts):
    P, nc, Wd, db, gemm, simple_blocks, rms_finish, norm_x, resid_epi = (E[k] for k in (
        "P", "nc", "Wd", "db", "gemm", "simple_blocks", "rms_finish", "norm_x", "resid_epi"))
    xT, yT, spk, b_spk, pb, pb_b, halo, b_halo, p_in, ident_f, b_cst, wbuf, wb_b, wptr, wview = (E[k] for k in (
        "xT", "yT", "spk", "b_spk", "pb", "pb_b", "halo", "b_halo", "p_in", "ident_f", "b_cst", "wbuf", "wb_b", "wptr", "wview"))
    NWS = len(wbuf)
    for ps in range(T // TT):
        t0 = ps * TT
        with Scope(P) as so:
            hT = so.sb("hT", [128, 16, TT], BF16)
            hT_b = Buf("hT")

            def norm_scope(gcol):
                with Scope(P) as sn:
                    S = dict(xs=sn.sb("xs", [128, 16, TC], F32), xs_b=Buf(), sq=sn.sb("sq", [128, 16, TC], BF16),
                             sq_b=Buf(), rstd=sn.sb("rstd", [128, TC], F32), rstd_b=Buf())
                    norm_x(S, hT, hT_b, gcol, t0)

            def mk_xr(sc):
                return dict(xr=[sc.sb(f"xr{i}", [128, TC], F32) for i in range(2)], xr_b=[Buf(), Buf()], xr_i=[0])

            if "out" in parts:
                with Scope(P) as s1:
                    S = mk_xr(s1)
                    P.dma("sp", hT[:], yT.ap()[:, t0:t0 + TT].rearrange("(kc p) t -> p kc t", p=128),
                          reads=[db("yT", ps)], writes=[hT_b])
                    wn = "w_out_even" if l % 2 == 0 else "w_out_odd"
                    gemm(hT, hT_b, 16, lambda c0, n: Wd[wn].ap()[l // 2, :, c0:c0 + n],
                         simple_blocks(0, D, 512), resid_epi(S, t0))
            if "ffn" in parts:
                norm_scope(SP_FFN + l * 16)
                with Scope(P) as s2:
                    S = mk_xr(s2)
                    act = s2.sb("act", [128, NFC, TT], BF16)
                    act_b = Buf("act")
                    stg = {"g": s2.sb("sg", [128, TT + 2], F32), "u": s2.sb("su", [128, TT + 2], F32)}
                    stg_b = {"g": Buf("sg"), "u": Buf("su")}
                    cv = {"g": s2.sb("ga", [128, TT], F32), "u": s2.sb("ua", [128, TT], F32)}
                    cv_b = {"g": Buf("ga"), "u": Buf("ua")}
                    if ps == 0:
                        for kk in ("g", "u"):
                            P.op("dve", lambda e: e.memset(stg[kk][:, 0:2], 0.0), writes=[stg_b[kk]])

                    def conv_finish(kind, i):
                        c = i if kind == "g" else NFC + i
                        s_, sb_, a_, ab_ = stg[kind], stg_b[kind], cv[kind], cv_b[kind]
                        wc = [SP_CONV + (l * 3 + k) * 86 + c for k in range(3)]
                        if ps == 0:
                            P.op("dve", lambda e: e.tensor_copy(out=halo[:, c, :], in_=s_[:, TT:TT + 2]),
                                 reads=[sb_], writes=[b_halo])
                        P.op("act", lambda e: e.activation(out=a_[:], in_=s_[:, 2:TT + 2], func=AF.Copy,
                                                           scale=spk[:, wc[2]:wc[2] + 1]),
                             reads=[sb_, b_spk], writes=[ab_])
                        P.op("dve", lambda e: e.scalar_tensor_tensor(out=a_[:], in0=s_[:, 1:TT + 1], scalar=spk[:, wc[1]:wc[1] + 1],
                                                                     in1=a_[:], op0=ALU.mult, op1=ALU.add),
                             reads=[sb_, ab_, b_spk], writes=[ab_])
                        P.op("dve", lambda e: e.scalar_tensor_tensor(out=a_[:], in0=s_[:, 0:TT], scalar=spk[:, wc[0]:wc[0] + 1],
                                                                     in1=a_[:], op0=ALU.mult, op1=ALU.add),
                             reads=[sb_, ab_, b_spk], writes=[ab_])

                    def up_epi(bi, m, tag, tci):
                        kind, i = tag
                        c = i if kind == "g" else NFC + i
                        if tci == 0 and ps == 1:
                            P.op("dve", lambda e: e.tensor_copy(out=stg[kind][:, 0:2], in_=halo[:, c, :]),
                                 reads=[b_halo], writes=[stg_b[kind]])
                        P.op("act", lambda e: e.activation(out=stg[kind][:, 2 + tci * TC:2 + (tci + 1) * TC], in_=pb[bi][:], func=AF.Copy),
                             reads=[], writes=[pb_b[bi], stg_b[kind]])
                        if tci == TT // TC - 1:
                            conv_finish(kind, i)
                            if kind == "u":
                                P.op("act", lambda e: e.activation(out=cv["g"][:], in_=cv["g"][:], func=AF.Silu),
                                     reads=[cv_b["g"]], writes=[cv_b["g"]])
                                P.op("dve", lambda e: e.tensor_tensor(out=act[:, i, :], in0=cv["g"][:], in1=cv["u"][:], op=ALU.mult),
                                     reads=[cv_b["g"], cv_b["u"]], writes=[act_b])
                    blocks = []
                    for i0 in range(0, NFC, 2):
                        npair = min(2, NFC - i0)
                        w = npair * 128
                        segs = [(i0 * 128, w), (DFF + i0 * 128, w)]
                        chunks = []
                        for j in range(npair):
                            chunks.append((j * 128, 128, ("g", i0 + j), 0))
                            chunks.append((w + j * 128, 128, ("u", i0 + j), 0))
                        blocks.append((segs, chunks))
                    gemm(hT, hT_b, 16, lambda c0, n: Wd["w_up"].ap()[l, :, c0:c0 + n], blocks, up_epi)
                    gemm(act, act_b, NFC, lambda c0, n: Wd["w_down"].ap()[l, :, c0:c0 + n],
                         simple_blocks(0, D, 256), resid_epi(S, t0))
            if "ple" in parts:
                norm_scope(SP_PLE + l * 16)
                with Scope(P) as s3:
                    S = mk_xr(s3)
                    pT = s3.sb("pT", [128, 2, TT], BF16)
                    pT_b = Buf("pT")
                    pl = [s3.sb(f"pl{i}", [128, 256], F32) for i in range(2)]
                    pl_b = [Buf(), Buf()]
                    sg = [s3.sb(f"sgt{i}", [128, TC], F32) for i in range(2)]
                    sg_b = [Buf(), Buf()]
                    for tt in range(TT // 128):
                        k = tt % 2
                        P.dma("sp", pl[k][:], p_in.ap()[l, t0 + tt * 128:t0 + (tt + 1) * 128, :], writes=[pl_b[k]])
                        for cc in range(2):
                            P.op("pe", lambda e: e.transpose(pb[5][:, cc * 128:(cc + 1) * 128], pl[k][:, cc * 128:(cc + 1) * 128], ident_f),
                                 reads=[pl_b[k], b_cst], writes=[pb_b[5]])
                        P.op("act", lambda e: e.activation(out=pT[:, :, tt * 128:(tt + 1) * 128],
                                                           in_=pb[5][:, 0:256].rearrange("p (a b) -> p a b", b=128), func=AF.Copy),
                             reads=[], writes=[pb_b[5], pT_b])
                    cnt = 0
                    for nb in range(D // 512):
                        sa = wptr[0]
                        sbb = (wptr[0] + 1) % NWS
                        wa = wview(sa, 16, 512)
                        wp = wview(sbb, 2, 512)
                        P.dma("pool", wa, Wd["w_ple_gate"].ap()[l, :, nb * 512:(nb + 1) * 512].rearrange("(kc p) n -> p kc n", p=128),
                              writes=[wb_b[sa]])
                        P.dma("pool", wp, Wd["w_ple_proj"].ap()[l, :, nb * 512:(nb + 1) * 512].rearrange("(kc p) n -> p kc n", p=128),
                              writes=[wb_b[sbb]])
                        for ci in range(4):
                            nchunk = nb * 4 + ci
                            for tci in range(TT // TC):
                                ba = cnt % 2
                                bb = 2 + cnt % 2
                                kx = cnt % 2
                                cnt += 1
                                ta = t0 + tci * TC
                                for kc in range(16):
                                    P.op("pe", lambda e: e.matmul(pb[ba][:], lhsT=wa[:, kc, ci * 128:(ci + 1) * 128],
                                                                  rhs=hT[:, kc, tci * TC:(tci + 1) * TC], start=(kc == 0), stop=(kc == 15)),
                                         reads=[wb_b[sa], hT_b], writes=[pb_b[ba]])
                                for kc in range(2):
                                    P.op("pe", lambda e: e.matmul(pb[bb][:], lhsT=wp[:, kc, ci * 128:(ci + 1) * 128],
                                                                  rhs=pT[:, kc, tci * TC:(tci + 1) * TC], start=(kc == 0), stop=(kc == 1)),
                                         reads=[wb_b[sbb], pT_b], writes=[pb_b[bb]])
                                P.op("act", lambda e: e.activation(out=sg[kx][:], in_=pb[ba][:], func=AF.Sigmoid),
                                     reads=[], writes=[pb_b[ba], sg_b[kx]])
                                P.op("dve", lambda e: e.tensor_tensor(out=sg[kx][:], in0=sg[kx][:], in1=pb[bb][:], op=ALU.mult),
                                     reads=[sg_b[kx]], writes=[pb_b[bb], sg_b[kx]])
                                xr, xr_b = S["xr"][kx], S["xr_b"][kx]
                                P.dma("sp", xr[:], xT.ap()[nchunk * 128:(nchunk + 1) * 128, ta:ta + TC],
                                      reads=[db("xT", ta // TC)], writes=[xr_b])
                                P.op("dve", lambda e: e.tensor_tensor(out=xr[:], in0=sg[kx][:], in1=xr[:], op=ALU.add),
                                     reads=[sg_b[kx], xr_b], writes=[xr_b])
                                P.dma("sp", xT.ap()[nchunk * 128:(nchunk + 1) * 128, ta:ta + TC], xr[:],
                                      reads=[xr_b], writes=[db("xT", ta // TC)])
                        wptr[0] = (wptr[0] + 2) % NWS


def even_attention(E, l):
    e = l // 2
    P, nc, Wd, db, gemm, simple_blocks, norm_x = (E[k] for k in ("P", "nc", "Wd", "db", "gemm", "simple_blocks", "norm_x"))
    spk, b_spk, pb, pb_b, pbh, pbh_b, ident_f, ident_bf, b_cst, b_const, ones_bf, bd_bf, jflip, cneg, eps_t = (E[k] for k in (
        "spk", "b_spk", "pb", "pb_b", "pbh", "pbh_b", "ident_f", "ident_bf", "b_cst", "b_const", "ones_bf", "bd_bf", "jflip", "cneg", "eps_t"))
    xT, yT, oh_in = E["xT"], E["yT"], E["oh_in"]
    s_kvT, s_kvtok, s_kidxT, s_widx, s_qiT, s_qaT, s_qbT, s_kdupT, s_vbtok, s_vrow = (E[k] for k in (
        "s_kvT", "s_kvtok", "s_kidxT", "s_widx", "s_qiT", "s_qaT", "s_qbT", "s_kdupT", "s_vbtok", "s_vrow"))
    XT = XA + XB

    if not E["state"].get("vrow"):
        E["state"]["vrow"] = True
        with Scope(P) as sv:
            ohs = sv.sb("ohs", [33, XT], F32)
            ohs_b = Buf()
            vr = sv.sb("vr", [16, XT], F32)
            vr_b = Buf()
            P.dma("sp", ohs[:], oh_in.ap(), writes=[ohs_b])
            for (hc, x0, x1) in [(0, 0, 512), (0, 512, 1024), (0, 1024, XA), (16, XA, XT)]:
                P.op("pe", lambda en: en.matmul(pb[5][0:16, 0:x1 - x0], lhsT=spk[0:33, SP_RELB + hc:SP_RELB + hc + 16],
                                                rhs=ohs[:, x0:x1], start=True, stop=True),
                     reads=[ohs_b, b_spk], writes=[pb_b[5]])
                P.op("act", lambda en: en.activation(out=vr[:, x0:x1], in_=pb[5][0:16, 0:x1 - x0], func=AF.Copy),
                     reads=[], writes=[pb_b[5], vr_b])
            P.dma("sp", s_vrow.ap(), vr[:], reads=[vr_b], writes=[db("vrow")])

    def rms_grp(S, srcs, src_b, lhsT_ones, gcol, inv_n, dsts, dst_bufs, ncols=TC):
        C = len(srcs)
        sq, sq_b, rstd, rstd_b = S["sq"], S["sq_b"], S["rstd"], S["rstd_b"]
        for c in range(C):
            P.op("act", lambda en: en.activation(out=sq[:, c, 0:ncols], in_=srcs[c], func=AF.Square),
                 reads=[src_b], writes=[sq_b])
        for c in range(C):
            P.op("pe", lambda en: en.matmul(pb[4][:, 0:ncols], lhsT=lhsT_ones[:, :], rhs=sq[:, c, 0:ncols],
                                            start=(c == 0), stop=(c == C - 1)),
                 reads=[sq_b, b_const], writes=[pb_b[4]])
        P.op("act", lambda en: en.activation(out=rstd[:, 0:ncols], in_=pb[4][:, 0:ncols], func=AF.Sqrt,
                                             scale=inv_n, bias=eps_t[:, 0:1]),
             reads=[b_const], writes=[pb_b[4], rstd_b])
        P.op("dve", lambda en: en.reciprocal(out=rstd[:, 0:ncols], in_=rstd[:, 0:ncols]), reads=[rstd_b], writes=[rstd_b])
        for c in range(C):
            P.op("dve", lambda en: en.scalar_tensor_tensor(out=dsts[c], in0=srcs[c], scalar=spk[:, gcol + c:gcol + c + 1],
                                                           in1=rstd[:, 0:ncols], op0=ALU.mult, op1=ALU.mult),
                 reads=[src_b, rstd_b, b_spk], writes=dst_bufs)
    E["rms_grp"] = rms_grp

    for ps in range(0 if E["cfg"].get("skip_proj") else T // TT):
        t0 = ps * TT
        with Scope(P) as so:
            hT = so.sb("hT", [128, 16, TT], BF16)
            hT_b = Buf("hT")
            with Scope(P) as sn:
                S0 = dict(xs=sn.sb("xs", [128, 16, TC], F32), xs_b=Buf(), sq=sn.sb("sq", [128, 16, TC], BF16),
                          sq_b=Buf(), rstd=sn.sb("rstd", [128, TC], F32), rstd_b=Buf())
                norm_x(S0, hT, hT_b, SP_ATTN + l * 16, t0)
            S = dict(sq=so.sb("sq", [128, 4, TC], BF16), sq_b=Buf(), rstd=so.sb("rstd", [128, TC], F32), rstd_b=Buf())
            stg4 = so.sb("stg4", [128, 4, TT], F32)
            stg4_b = Buf("stg4")
            stg1 = so.sb("stg1", [128, TT], F32)
            stg1_b = Buf("stg1")
            stg2 = so.sb("stg2", [128, 2, TT], F32)
            stg2_b = Buf("stg2")
            cqT = so.sb("cqT", [128, 4, TT], BF16)
            cqT_b = Buf("cqT")
            kvn = so.sb("kvn", [128, 2, TT], BF16)
            kvn_b = Buf("kvn")
            kvtok_st = so.sb("kvtok_st", [128, 8, 256], BF16)
            kvtok_b = Buf()
            kidx_st = so.sb("kidx_st", [64, TT], BF16)
            kidx_b = Buf()
            widx_st = so.sb("widx_st", [16, TT], F32)
            widx_b = Buf()
            widx_tok = so.sb("widx_tok", [128, 8, 16], F32)
            widx_tok_b = Buf()
            ob = so.sb("ob", [128, TT], BF16)
            ob_b = Buf("ob")
            ob2 = so.sb("ob2", [128, 2, TT], BF16)
            ob2_b = Buf("ob2")
            oq = [so.sb(f"oq{i}", [128, TC], BF16) for i in range(2)]
            oq_b = [Buf(), Buf()]
            oq_i = [0]
            vb_st = so.sb("vb_st", [128, TT], BF16)
            vb_b = Buf()
            vtok_st = so.sb("vtok_st", [128, 8, 128], BF16)
            vtok_b = Buf()

            def tsl(tci):
                return slice(tci * TC, (tci + 1) * TC)

            def in_epi(bi, m, tag, tci):
                kind = tag[0]
                last = (tci == TT // TC - 1)
                if kind == "cq":
                    c = tag[1]
                    P.op("act", lambda en: en.activation(out=stg4[:, c, tsl(tci)], in_=pb[bi][:], func=AF.Copy),
                         reads=[], writes=[pb_b[bi], stg4_b])
                    if c == 3 and last:
                        for t2 in range(TT // TC):
                            rms_grp(S, [stg4[:, cc, tsl(t2)] for cc in range(4)], stg4_b, ones_bf, SP_CQ + e * 4, 1.0 / 512,
                                    [cqT[:, cc, tsl(t2)] for cc in range(4)], [cqT_b])
                elif kind == "ckv":
                    c = tag[1]
                    P.op("act", lambda en: en.activation(out=stg2[:, c, tsl(tci)], in_=pb[bi][:], func=AF.Copy),
                         reads=[], writes=[pb_b[bi], stg2_b])
                    if c == 1 and last:
                        for t2 in range(TT // TC):
                            rms_grp(S, [stg2[:, cc, tsl(t2)] for cc in range(2)], stg2_b, ones_bf, SP_CKV + e * 2, 1.0 / 256,
                                    [kvn[:, cc, tsl(t2)] for cc in range(2)], [kvn_b])
                        P.dma("sp", s_kvT.ap()[:, t0:t0 + TT].rearrange("(c p) t -> p c t", p=128), kvn[:],
                              reads=[kvn_b], writes=[db("kvT")])
                        for tt in range(TT // 128):
                            for cc in range(2):
                                P.op("pe", lambda en: en.transpose(pbh[:, cc * 128:(cc + 1) * 128], kvn[:, cc, tt * 128:(tt + 1) * 128], ident_bf[:]),
                                     reads=[kvn_b, b_const], writes=[pbh_b])
                            P.op("act", lambda en: en.activation(out=kvtok_st[:, tt, :], in_=pbh[:, 0:256], func=AF.Copy),
                                 reads=[], writes=[pbh_b, kvtok_b])
                        P.dma("sp", s_kvtok.ap()[t0:t0 + TT, :].rearrange("(tt p) c -> p tt c", p=128), kvtok_st[:],
                              reads=[kvtok_b], writes=[db("kvtok")])
                elif kind == "kidx":
                    P.op("act", lambda en: en.activation(out=kidx_st[:, tsl(tci)], in_=pb[bi][0:64, :], func=AF.Copy),
                         reads=[], writes=[pb_b[bi], kidx_b])
                    if last:
                        P.dma("sp", s_kidxT.ap()[:, t0:t0 + TT], kidx_st[:], reads=[kidx_b], writes=[db("kidxT")])
                elif kind == "widx":
                    P.op("act", lambda en: en.activation(out=widx_st[:, tsl(tci)], in_=pb[bi][0:16, :], func=AF.Copy),
                         reads=[], writes=[pb_b[bi], widx_b])
                    if last:
                        for tt in range(TT // 128):
                            P.op("pe", lambda en: en.transpose(pb[5][:, tt * 16:(tt + 1) * 16], widx_st[0:16, tt * 128:(tt + 1) * 128], ident_f[0:16, 0:16]),
                                 reads=[widx_b, b_cst], writes=[pb_b[5]])
                        P.op("act", lambda en: en.activation(out=widx_tok[:], in_=pb[5][:, 0:128].rearrange("p (a b) -> p a b", b=16), func=AF.Copy),
                             reads=[], writes=[pb_b[5], widx_tok_b])
                        P.dma("sp", s_widx.ap()[t0:t0 + TT, :].rearrange("(tt p) c -> p tt c", p=128), widx_tok[:],
                              reads=[widx_tok_b], writes=[db("widx")])
                elif kind == "qb":
                    c = tag[1]
                    P.op("act", lambda en: en.activation(out=stg1[:, tsl(tci)], in_=pb[bi][:], func=AF.Copy),
                         reads=[], writes=[pb_b[bi], stg1_b])
                    if last:
                        for t2 in range(TT // TC):
                            rms_grp(S, [stg1[:, tsl(t2)]], stg1_b, bd_bf, SP_BQ + e, 1.0 / 64, [ob[:, tsl(t2)]], [ob_b])
                        P.dma("sp", s_qbT.ap()[c * 128:(c + 1) * 128, t0:t0 + TT], ob[:], reads=[ob_b], writes=[db("qbT")])
                elif kind == "kb":
                    g, half = tag[1], tag[2]
                    pbs = half * 64
                    P.op("act", lambda en: en.activation(out=stg1[pbs:pbs + 64, tsl(tci)], in_=pb[bi][pbs:pbs + 64, :], func=AF.Copy),
                         reads=[], writes=[pb_b[bi], stg1_b])
                    if half == 1 and last:
                        for t2 in range(TT // TC):
                            rms_grp(S, [stg1[:, tsl(t2)]], stg1_b, bd_bf, SP_BK + e, 1.0 / 64, [ob[:, tsl(t2)]], [ob_b])
                        P.dma("sp", s_kdupT.ap()[g * 128:(g + 1) * 128, t0:t0 + TT], ob[:], reads=[ob_b], writes=[db("kdupT")])
                elif kind == "vb":
                    P.op("act", lambda en: en.activation(out=vb_st[:, tsl(tci)], in_=pb[bi][:], func=AF.Copy),
                         reads=[], writes=[pb_b[bi], vb_b])
                    if last:
                        for tt in range(TT // 128):
                            P.op("pe", lambda en: en.transpose(pbh[:, (tt % 4) * 128:(tt % 4 + 1) * 128], vb_st[:, tt * 128:(tt + 1) * 128], ident_bf[:]),
                                 reads=[vb_b, b_const], writes=[pbh_b])
                            if tt % 4 == 3:
                                P.op("act", lambda en: en.activation(out=vtok_st[:, tt - 3:tt + 1, :], in_=pbh[:, 0:512].rearrange("p (a b) -> p a b", b=128), func=AF.Copy),
                                     reads=[], writes=[pbh_b, vtok_b])
                        P.dma("sp", s_vbtok.ap()[t0:t0 + TT, :].rearrange("(tt p) c -> p tt c", p=128), vtok_st[:],
                              reads=[vtok_b], writes=[db("vbtok")])
                elif kind == "qa":
                    h, cc = tag[1], tag[2]
                    P.op("act", lambda en: en.activation(out=stg2[:, cc, tsl(tci)], in_=pb[bi][:], func=AF.Copy),
                         reads=[], writes=[pb_b[bi], stg2_b])
                    if cc == 1 and last:
                        for t2 in range(TT // TC):
                            rms_grp(S, [stg2[:, c2, tsl(t2)] for c2 in range(2)], stg2_b, ones_bf, SP_AQ + e * 2, 1.0 / 256,
                                    [ob2[:, c2, tsl(t2)] for c2 in range(2)], [ob2_b])
                        P.dma("sp", s_qaT.ap()[h * 256:(h + 1) * 256, t0:t0 + TT].rearrange("(c p) t -> p c t", p=128), ob2[:],
                              reads=[ob2_b], writes=[db("qaT")])
                elif kind == "qi":
                    c = tag[1]
                    k = oq_i[0]
                    oq_i[0] = (k + 1) % 2
                    P.op("act", lambda en: en.activation(out=oq[k][:], in_=pb[bi][:], func=AF.Copy),
                         reads=[], writes=[pb_b[bi], oq_b[k]])
                    P.dma("sp", s_qiT.ap()[c * 128:(c + 1) * 128, t0 + tci * TC:t0 + (tci + 1) * TC], oq[k][:],
                          reads=[oq_b[k]], writes=[db("qiT")])

            blocks = [
                ([(0, 512)], [(c * 128, 128, ("cq", c), 0) for c in range(4)]),
                ([(512, 336)], [(0, 128, ("ckv", 0), 0), (128, 128, ("ckv", 1), 0), (256, 64, ("kidx",), 0), (320, 16, ("widx",), 0)]),
                ([(848, 512)], [(c * 128, 128, ("qb", c), 0) for c in range(4)]),
                ([(1360, 512)], [(c * 128, 128, ("qb", 4 + c), 0) for c in range(4)]),
                ([(1872, 256)], [(0, 64, ("kb", 0, 0), 0), (0, 64, ("kb", 0, 1), 64), (64, 64, ("kb", 1, 0), 0), (64, 64, ("kb", 1, 1), 64),
                                 (128, 128, ("vb",), 0)]),
            ]
            gemm(hT, hT_b, 16, lambda c0, n: Wd["w_in_even"].ap()[e, :, c0:c0 + n], blocks, in_epi)
            blocks = []
            for b4 in range(2):
                blocks.append(([(b4 * 2048, 2048)], [((hh * 2 + cc) * 128, 128, ("qa", b4 * 8 + hh, cc), 0) for hh in range(8) for cc in range(2)]))
            gemm(cqT, cqT_b, 4, lambda c0, n: Wd["a_w_uq"].ap()[e, :, c0:c0 + n], blocks, in_epi)
            gemm(cqT, cqT_b, 4, lambda c0, n: Wd["a_w_qidx"].ap()[e, :, c0:c0 + n],
                 [([(0, 1024)], [(c * 128, 128, ("qi", c), 0) for c in range(8)])], in_epi)

    if E["cfg"].get("stop_after_proj"):
        return
    att_scale = 1.0 / 16.0
    with Scope(P) as sa:
        kvT = sa.sb("kvT", [128, 2, T], BF16)
        kvtok = sa.sb("kvtok", [128, 16, 256], BF16)
        kidx2 = sa.sb("kidx2", [128, T], BF16)
        wuv = sa.sb("wuv", [128, 16, 2, 64], BF16)
        expA = sa.sb("expA", [128, 16, 9, 128], BF16)
        b_k = Buf("kside")
        b_exp = Buf("expA")
        P.dma("sp", kvT[:], s_kvT.ap().rearrange("(c p) t -> p c t", p=128), reads=[db("kvT")], writes=[b_k])
        P.dma("sp", kvtok[:], s_kvtok.ap().rearrange("(tt p) c -> p tt c", p=128), reads=[db("kvtok")], writes=[b_k])
        P.dma("sp", kidx2[0:64, :], s_kidxT.ap(), reads=[db("kidxT")], writes=[b_k])
        P.dma("sp", kidx2[64:128, :], s_kidxT.ap(), reads=[db("kidxT")], writes=[b_k])
        P.dma("pool", wuv[:], Wd["a_w_uv"].ap()[e].rearrange("h (cc p) d -> p h cc d", p=128), writes=[b_k])
        hk = [sa.sb(f"hk{i}", [128, 128], F32) for i in range(2)]
        hk_b = [Buf(), Buf()]
        n = 0
        for h in range(16):
            for dj in range(9):
                k = n % 2
                n += 1
                P.dma("sp", hk[k][:], bass.AP(s_vrow, h * XT + dj * 128, [[1, 128], [1, 128]]), reads=[db("vrow")], writes=[hk_b[k]])
                P.op("pe", lambda en: en.matmul(pb[5][:, 0:128], lhsT=jflip, rhs=hk[k][:], start=True, stop=True),
                     reads=[hk_b[k], b_cst], writes=[pb_b[5]])
                P.op("act", lambda en: en.activation(out=expA[:, h, 8 - dj, :], in_=pb[5][:, 0:128], func=AF.Exp),
                     reads=[], writes=[pb_b[5], b_exp])
        qi = [sa.sb(f"qi{i}", [128, 8, 128], BF16) for i in range(2)]
        qa = [sa.sb(f"qa{i}", [128, 32, 128], BF16) for i in range(2)]
        wq = [sa.sb(f"wq{i}", [128, 16], F32) for i in range(2)]
        q_b = [Buf(), Buf()]
        score = sa.sb("score", [128, T], F32)
        score_b = Buf("score")
        work = sa.sb("work", [128, T], F32)
        work_b = Buf("work")
        m8 = sa.sb("m8", [128, 8], F32)
        m8_b = Buf("m8")
        mask01 = sa.sb("mask01", [128, T], BF16)
        mask01_b = Buf()
        maskT = sa.sb("maskT", [128, 16, 128], BF16)
        maskT_b = Buf()
        rl = [sa.sb(f"rl{i}", [128, 512], F32) for i in range(2)]
        rl_b = [Buf(), Buf()]
        pf = [sa.sb(f"pf{i}", [128, 512], F32) for i in range(2)]
        pf_b = [Buf(), Buf()]
        pbf = [sa.sb(f"pbf{i}", [128, 512], BF16) for i in range(2)]
        pbf_b = [Buf(), Buf()]
        rc = sa.sb("rc", [128, 128], F32)
        rc_b = Buf()
        on = sa.sb("on", [128, 2, 128], BF16)
        on_b = Buf()
        ya_st = [sa.sb(f"ya_st{i}", [128, 8, 128], BF16) for i in range(2)]
        ya_b = [Buf(), Buf()]
        cnt = [0, 0]
        for i in range(E["cfg"].get("dsa_tiles", 16)):
            qk = i % 2
            N = (i + 1) * 128
            qs = slice(i * 128, (i + 1) * 128)
            P.dma("sp", qi[qk][:], s_qiT.ap()[:, qs].rearrange("(c p) t -> p c t", p=128), reads=[db("qiT")], writes=[q_b[qk]])
            P.dma("sp", qa[qk][:], s_qaT.ap()[:, qs].rearrange("(c p) t -> p c t", p=128), reads=[db("qaT")], writes=[q_b[qk]])
            P.dma("sp", wq[qk][:], s_widx.ap()[qs, :], reads=[db("widx")], writes=[q_b[qk]])
            for h in range(16):
                pbs = (h % 2) * 64
                for n0 in range(0, N, 512):
                    n1 = min(N, n0 + 512)
                    bk = cnt[0] % 2
                    cnt[0] += 1
                    P.op("pe", lambda en: en.matmul(pb[bk][:, 0:n1 - n0], lhsT=qi[qk][pbs:pbs + 64, h // 2, :], rhs=kidx2[pbs:pbs + 64, n0:n1],
                                                    start=True, stop=True),
                         reads=[q_b[qk], b_k], writes=[pb_b[bk]])
                    P.op("act", lambda en: en.activation(out=rl[bk][:, 0:n1 - n0], in_=pb[bk][:, 0:n1 - n0], func=AF.Relu),
                         reads=[], writes=[pb_b[bk], rl_b[bk]])
                    if h == 0:
                        P.op("dve", lambda en: en.tensor_scalar(out=score[:, n0:n1], in0=rl[bk][:, 0:n1 - n0], scalar1=wq[qk][:, 0:1], scalar2=None, op0=ALU.mult),
                             reads=[rl_b[bk], q_b[qk]], writes=[score_b])
                    else:
                        P.op("dve", lambda en: en.scalar_tensor_tensor(out=score[:, n0:n1], in0=rl[bk][:, 0:n1 - n0], scalar=wq[qk][:, h:h + 1],
                                                                       in1=score[:, n0:n1], op0=ALU.mult, op1=ALU.add),
                             reads=[rl_b[bk], q_b[qk], score_b], writes=[score_b])
            P.op("dve", lambda en: en.tensor_tensor(out=score[:, qs], in0=score[:, qs], in1=cneg, op=ALU.add),
                 reads=[score_b, b_cst], writes=[score_b])
            if i >= 2:
                cur, cur_b = score, score_b
                for it in range(32):
                    P.op("dve", lambda en: en.max(out=m8[:], in_=cur[:, 0:N]), reads=[cur_b], writes=[m8_b])
                    if it < 31:
                        P.op("dve", lambda en: en.match_replace(out=work[:, 0:N], in_to_replace=m8[:], in_values=cur[:, 0:N], imm_value=-1e30),
                             reads=[m8_b, cur_b], writes=[work_b])
                        cur, cur_b = work, work_b
                P.op("dve", lambda en: en.tensor_scalar(out=mask01[:, 0:N], in0=score[:, 0:N], scalar1=m8[:, 7:8], scalar2=None, op0=ALU.is_ge),
                     reads=[score_b, m8_b], writes=[mask01_b])
            else:
                P.op("dve", lambda en: en.tensor_scalar(out=mask01[:, 0:N], in0=score[:, 0:N], scalar1=-1e29, scalar2=None, op0=ALU.is_ge),
                     reads=[score_b], writes=[mask01_b])
            for j0 in range(0, i + 1, 8):
                j1 = min(i + 1, j0 + 8)
                for j in range(j0, j1):
                    P.op("pe", lambda en: en.transpose(pbh[:, (j - j0) * 128:(j - j0 + 1) * 128], mask01[:, j * 128:(j + 1) * 128], ident_bf[:]),
                         reads=[mask01_b, b_const], writes=[pbh_b])
                P.op("act", lambda en: en.activation(out=maskT[:, j0:j1, :], in_=pbh[:, 0:(j1 - j0) * 128].rearrange("p (a b) -> p a b", b=128), func=AF.Copy),
                     reads=[], writes=[pbh_b, maskT_b])
            yk = i % 2
            items = [(h, jg) for h in range(16) for jg in range(0, i + 1, 4)]

            def emit_logits(k):
                h, jg = items[k]
                je = min(i + 1, jg + 4)
                L = k % 2
                for j in range(jg, je):
                    sl = j - jg
                    for c in range(2):
                        P.op("pe", lambda en: en.matmul(pb[L][:, sl * 128:(sl + 1) * 128], lhsT=kvT[:, c, j * 128:(j + 1) * 128],
                                                        rhs=qa[qk][:, 2 * h + c, :], start=(c == 0), stop=(c == 1)),
                             reads=[b_k, q_b[qk]], writes=[pb_b[L]])

            def emit_post(k):
                h, jg = items[k]
                je = min(i + 1, jg + 4)
                nj = je - jg
                L = k % 2
                far = (i - (je - 1)) >= 8
                if far:
                    P.op("act", lambda en: en.activation(out=pf[L][:, 0:nj * 128], in_=pb[L][:, 0:nj * 128], func=AF.Exp, scale=att_scale,
                                                         bias=spk[:, SP_B31 + h:SP_B31 + h + 1]),
                         reads=[b_spk], writes=[pb_b[L], pf_b[L]])
                else:
                    P.op("act", lambda en: en.activation(out=pf[L][:, 0:nj * 128], in_=pb[L][:, 0:nj * 128], func=AF.Exp, scale=att_scale),
                         reads=[], writes=[pb_b[L], pf_b[L]])
                    if i - jg <= 8:
                        k0 = 8 - (i - jg)
                        P.op("dve", lambda en: en.tensor_tensor(out=pf[L][:, 0:nj * 128].rearrange("p (a b) -> p a b", b=128),
                                                                in0=pf[L][:, 0:nj * 128].rearrange("p (a b) -> p a b", b=128),
                                                                in1=expA[:, h, k0:k0 + nj, :], op=ALU.mult),
                             reads=[pf_b[L], b_exp], writes=[pf_b[L]])
                    else:
                        for j in range(jg, je):
                            sl = j - jg
                            kk = 8 - min(i - j, 8)
                            P.op("dve", lambda en: en.tensor_tensor(out=pf[L][:, sl * 128:(sl + 1) * 128], in0=pf[L][:, sl * 128:(sl + 1) * 128],
                                                                    in1=expA[:, h, kk, :], op=ALU.mult),
                                 reads=[pf_b[L], b_exp], writes=[pf_b[L]])
                P.op("dve", lambda en: en.tensor_tensor(out=pbf[L][:, 0:nj * 128].rearrange("p (a b) -> p a b", b=128),
                                                        in0=pf[L][:, 0:nj * 128].rearrange("p (a b) -> p a b", b=128),
                                                        in1=maskT[:, jg:je, :], op=ALU.mult),
                     reads=[pf_b[L], maskT_b], writes=[pbf_b[L]])

            def emit_pv(k):
                h, jg = items[k]
                je = min(i + 1, jg + 4)
                L = k % 2
                for j in range(jg, je):
                    sl = j - jg
                    for (bk, lh) in ((2, kvtok[:, j, 0:128]), (3, kvtok[:, j, 128:256]), (6, ones_bf[:, :])):
                        P.op("pe", lambda en: en.matmul(pb[bk][:, 0:128], lhsT=lh, rhs=pbf[L][:, sl * 128:(sl + 1) * 128],
                                                        start=(j == 0), stop=(j == i)),
                             reads=[b_k, pbf_b[L], b_const], writes=[pb_b[bk]])

            def emit_fin_dve(h):
                P.op("dve", lambda en: en.reciprocal(out=rc[:], in_=pb[6][:, 0:128]), reads=[], writes=[pb_b[6], rc_b])
                for c in range(2):
                    P.op("dve", lambda en: en.tensor_tensor(out=on[:, c, :], in0=pb[2 + c][:, 0:128], in1=rc[:], op=ALU.mult),
                         reads=[rc_b], writes=[pb_b[2 + c], on_b])

            def emit_fin_pe(h):
                pbs = (h % 2) * 64
                col = ((h // 2) % 4) * 128
                for c in range(2):
                    P.op("pe", lambda en: en.matmul(pb[5][pbs:pbs + 64, col:col + 128], lhsT=wuv[:, h, c, :], rhs=on[:, c, :],
                                                    start=(c == 0), stop=(c == 1)),
                         reads=[b_k, on_b], writes=[pb_b[5]])
                if h % 2 == 1:
                    P.op("act", lambda en: en.activation(out=ya_st[yk][:, h // 2, :], in_=pb[5][:, col:col + 128], func=AF.Copy),
                         reads=[], writes=[pb_b[5], ya_b[yk]])

            emit_logits(0)
            for k in range(len(items)):
                h, jg = items[k]
                if k + 1 < len(items):
                    emit_logits(k + 1)
                emit_post(k)
                emit_pv(k)
                if jg + 4 > i:
                    emit_fin_dve(h)
                    emit_fin_pe(h)
            P.dma("sp", yT.ap()[0:1024, qs].rearrange("(c p) t -> p c t", p=128), ya_st[yk][:], reads=[ya_b[yk]], writes=[db("yT", i // 8)])

    with Scope(P) as sw:
        kd = sw.sb("kd", [128, 2, T], BF16)
        vtk = sw.sb("vtk", [128, 16, 128], BF16)
        expB = sw.sb("expB", [128, 16, 2, 128], BF16)
        esk = sw.sb("esk", [128, 16], F32)
        b_k = Buf("kside")
        b_exp = Buf("expB")
        P.dma("sp", kd[:], s_kdupT.ap().rearrange("(g p) t -> p g t", p=128), reads=[db("kdupT")], writes=[b_k])
        P.dma("sp", vtk[:], s_vbtok.ap().rearrange("(tt p) c -> p tt c", p=128), reads=[db("vbtok")], writes=[b_k])
        P.op("act", lambda en: en.activation(out=esk[:], in_=spk[:, SP_SINK + e * 16:SP_SINK + (e + 1) * 16], func=AF.Exp),
             reads=[b_spk], writes=[b_exp])
        hk = [sw.sb(f"hk{i}", [128, 128], F32) for i in range(2)]
        hk_b = [Buf(), Buf()]
        n = 0
        for hb in range(16):
            for kx in range(2):
                dj = 1 - kx
                k = n % 2
                n += 1
                P.dma("sp", hk[k][:], bass.AP(s_vrow, hb * XT + XA + dj * 128, [[1, 128], [1, 128]]), reads=[db("vrow")], writes=[hk_b[k]])
                P.op("pe", lambda en: en.matmul(pb[5][:, 0:128], lhsT=jflip, rhs=hk[k][:], start=True, stop=True),
                     reads=[hk_b[k], b_cst], writes=[pb_b[5]])
                P.op("act", lambda en: en.activation(out=expB[:, hb, kx, :], in_=pb[5][:, 0:128], func=AF.Exp),
                     reads=[], writes=[pb_b[5], b_exp])
        qb = [sw.sb(f"qb{i}", [128, 8, 128], BF16) for i in range(2)]
        qb_b = [Buf(), Buf()]
        pf = [sw.sb(f"pf{i}", [128, 512], F32) for i in range(2)]
        pf_b = [Buf(), Buf()]
        pbf = [sw.sb(f"pbf{i}", [128, 512], BF16) for i in range(2)]
        pbf_b = [Buf(), Buf()]
        dn = sw.sb("dn", [128, 128], F32)
        dn_b = Buf()
        yb_st = [sw.sb(f"yb_st{i}", [128, 8, 128], BF16) for i in range(2)]
        yb_b = [Buf(), Buf()]
        cnt = 0
        for nb in range(E["cfg"].get("swa_blocks", 16)):
            qk = nb % 2
            qs = slice(nb * 128, (nb + 1) * 128)
            P.dma("sp", qb[qk][:], s_qbT.ap()[:, qs].rearrange("(c p) t -> p c t", p=128), reads=[db("qbT")], writes=[qb_b[qk]])
            for m in range(8):
                Lb = [(0, 1), (5, 6)][cnt % 2]
                Ls = cnt % 2
                cnt += 1
                units = []
                for hh in range(2):
                    for kx in range(2):
                        dj = 1 - kx
                        if nb - dj >= 0:
                            units.append((hh, kx, nb - dj, hh * 2 + kx))
                for (hh, kx, j, sl) in units:
                    hb = 2 * m + hh
                    g = hb // 8
                    pbs = hh * 64
                    bkx = Lb[hh]
                    P.op("pe", lambda en: en.matmul(pb[bkx][:, kx * 128:(kx + 1) * 128], lhsT=kd[pbs:pbs + 64, g, j * 128:(j + 1) * 128],
                                                    rhs=qb[qk][pbs:pbs + 64, m, :], start=True, stop=True),
                         reads=[b_k, qb_b[qk]], writes=[pb_b[bkx]])
                stg_ = E["cfg"].get("swa_stage", 4)
                if stg_ < 2:
                    continue
                L = Ls
                for hh in range(2):
                    bkx = Lb[hh]
                    a, b = (0, 2) if nb > 0 else (1, 2)
                    P.op("act", lambda en: en.activation(out=pf[L][:, hh * 256 + a * 128:hh * 256 + b * 128], in_=pb[bkx][:, a * 128:b * 128], func=AF.Exp, scale=0.125),
                         reads=[], writes=[pb_b[bkx], pf_b[L]])
                    P.op("dve", lambda en: en.tensor_tensor(out=pbf[L][:, hh * 256 + a * 128:hh * 256 + b * 128], in0=pf[L][:, hh * 256 + a * 128:hh * 256 + b * 128],
                                                            in1=expB[:, 2 * m + hh, a:b, :].rearrange("p k q -> p (k q)"), op=ALU.mult),
                         reads=[pf_b[L], b_exp], writes=[pbf_b[L]])
                if stg_ < 3:
                    continue
                for hh in range(2):
                    us = [u for u in units if u[0] == hh]
                    hb = 2 * m + hh
                    g = hb // 8
                    pbs = hh * 64
                    for ui, (_, kx, j, sl) in enumerate(us):
                        P.op("pe", lambda en: en.matmul(pb[2][pbs:pbs + 64, 0:128], lhsT=vtk[:, j, g * 64:(g + 1) * 64], rhs=pbf[L][:, sl * 128:(sl + 1) * 128],
                                                        start=(ui == 0), stop=(ui == len(us) - 1)),
                             reads=[b_k, pbf_b[L]], writes=[pb_b[2]])
                        P.op("pe", lambda en: en.matmul(pb[3][pbs:pbs + 64, 0:128], lhsT=ones_bf[:, 0:64], rhs=pbf[L][:, sl * 128:(sl + 1) * 128],
                                                        start=(ui == 0), stop=(ui == len(us) - 1)),
                             reads=[b_const, pbf_b[L]], writes=[pb_b[3]])
                if stg_ < 4:
                    continue
                for hh in range(2):
                    hb = 2 * m + hh
                    pbs = hh * 64
                    P.op("dve", lambda en: en.tensor_scalar(out=dn[pbs:pbs + 64, :], in0=pb[3][pbs:pbs + 64, 0:128], scalar1=esk[pbs:pbs + 64, hb:hb + 1],
                                                            scalar2=None, op0=ALU.add),
                         reads=[b_exp], writes=[pb_b[3], dn_b])
                P.op("dve", lambda en: en.reciprocal(out=dn[:], in_=dn[:]), reads=[dn_b], writes=[dn_b])
                P.op("dve", lambda en: en.tensor_tensor(out=yb_st[qk][:, m, :], in0=pb[2][:, 0:128], in1=dn[:], op=ALU.mult),
                     reads=[dn_b], writes=[pb_b[2], yb_b[qk]])
            P.dma("sp", yT.ap()[1024:2048, qs].rearrange("(c p) t -> p c t", p=128), yb_st[qk][:], reads=[yb_b[qk]], writes=[db("yT", nb // 8)])


def odd_attention(E, l):
    o = l // 2
    P, nc, Wd, db, gemm, simple_blocks, norm_x = (E[k] for k in ("P", "nc", "Wd", "db", "gemm", "simple_blocks", "norm_x"))
    spk, b_spk, pb, pb_b, pbh, pbh_b, ident_f, ident_bf, b_cst, b_const, ones_bf, ones_f, bd_bf, triu_f, triu_bf, eps_t = (E[k] for k in (
        "spk", "b_spk", "pb", "pb_b", "pbh", "pbh_b", "ident_f", "ident_bf", "b_cst", "b_const", "ones_bf", "ones_f", "bd_bf", "triu_f", "triu_bf", "eps_t"))
    xT, yT = E["xT"], E["yT"]
    s_qT, s_kT, s_vtok, s_lf = E["s_qT"], E["s_kT"], E["s_vtok"], E["s_lf"]

    def rms_grp(S, srcs, src_b, lhsT_ones, gcol, inv_n, dsts, dst_bufs, ncols=TC):
        C = len(srcs)
        sq, sq_b, rstd, rstd_b = S["sq"], S["sq_b"], S["rstd"], S["rstd_b"]
        for c in range(C):
            P.op("act", lambda en: en.activation(out=sq[:, c, 0:ncols], in_=srcs[c], func=AF.Square),
                 reads=[src_b], writes=[sq_b])
        for c in range(C):
            P.op("pe", lambda en: en.matmul(pb[4][:, 0:ncols], lhsT=lhsT_ones[:, :], rhs=sq[:, c, 0:ncols],
                                            start=(c == 0), stop=(c == C - 1)),
                 reads=[sq_b, b_const], writes=[pb_b[4]])
        P.op("act", lambda en: en.activation(out=rstd[:, 0:ncols], in_=pb[4][:, 0:ncols], func=AF.Sqrt,
                                             scale=inv_n, bias=eps_t[:, 0:1]),
             reads=[b_const], writes=[pb_b[4], rstd_b])
        P.op("dve", lambda en: en.reciprocal(out=rstd[:, 0:ncols], in_=rstd[:, 0:ncols]), reads=[rstd_b], writes=[rstd_b])
        for c in range(C):
            P.op("dve", lambda en: en.scalar_tensor_tensor(out=dsts[c], in0=srcs[c], scalar=spk[:, gcol + c:gcol + c + 1],
                                                           in1=rstd[:, 0:ncols], op0=ALU.mult, op1=ALU.mult),
                 reads=[src_b, rstd_b, b_spk], writes=dst_bufs)

    for ps in range(T // TT):
        t0 = ps * TT
        with Scope(P) as so:
            hT = so.sb("hT", [128, 16, TT], BF16)
            hT_b = Buf("hT")
            with Scope(P) as sn:
                S0 = dict(xs=sn.sb("xs", [128, 16, TC], F32), xs_b=Buf(), sq=sn.sb("sq", [128, 16, TC], BF16),
                          sq_b=Buf(), rstd=sn.sb("rstd", [128, TC], F32), rstd_b=Buf())
                norm_x(S0, hT, hT_b, SP_ATTN + l * 16, t0)
            S = dict(sq=so.sb("sq", [128, 1, TC], BF16), sq_b=Buf(), rstd=so.sb("rstd", [128, TC], F32), rstd_b=Buf())
            stg1 = so.sb("stg1", [128, TT], F32)
            stg1_b = Buf("stg1")
            ob = so.sb("ob", [128, TT], BF16)
            ob_b = Buf("ob")
            vb_st = so.sb("vb_st", [128, TT], BF16)
            vb_b = Buf()
            vtok_st = so.sb("vtok_st", [128, 8, 128], BF16)
            vtok_b = Buf()
            negfb = so.sb("negfb", [32, 1], F32)
            negfb_b = Buf()
            fst = so.sb("fst", [32, TT], F32)
            fst_b = Buf()
            lf_tok = so.sb("lf_tok", [128, 8, 32], F32)
            lf_tok_b = Buf()
            P.op("dve", lambda en: en.tensor_scalar(out=negfb[:], in0=spk[0:32, SP_FB + o:SP_FB + o + 1], scalar1=-1.0, scalar2=None, op0=ALU.mult),
                 reads=[b_spk], writes=[negfb_b])

            def tsl(tci):
                return slice(tci * TC, (tci + 1) * TC)

            def in_epi(bi, m, tag, tci):
                kind = tag[0]
                last = (tci == TT // TC - 1)
                if kind in ("q", "k"):
                    c = tag[1]
                    P.op("act", lambda en: en.activation(out=stg1[:, tsl(tci)], in_=pb[bi][:], func=AF.Copy),
                         reads=[], writes=[pb_b[bi], stg1_b])
                    if last:
                        gcol = (SP_CQN if kind == "q" else SP_CKN) + o
                        dst = s_qT if kind == "q" else s_kT
                        for t2 in range(TT // TC):
                            rms_grp(S, [stg1[:, tsl(t2)]], stg1_b, bd_bf, gcol, 1.0 / 64, [ob[:, tsl(t2)]], [ob_b])
                        P.dma("sp", dst.ap()[c * 128:(c + 1) * 128, t0:t0 + TT], ob[:], reads=[ob_b], writes=[db(kind + "T")])
                elif kind == "v":
                    c = tag[1]
                    P.op("act", lambda en: en.activation(out=vb_st[:, tsl(tci)], in_=pb[bi][:], func=AF.Copy),
                         reads=[], writes=[pb_b[bi], vb_b])
                    if last:
                        for tt in range(TT // 128):
                            P.op("pe", lambda en: en.transpose(pbh[:, (tt % 4) * 128:(tt % 4 + 1) * 128], vb_st[:, tt * 128:(tt + 1) * 128], ident_bf[:]),
                                 reads=[vb_b, b_const], writes=[pbh_b])
                            if tt % 4 == 3:
                                P.op("act", lambda en: en.activation(out=vtok_st[:, tt - 3:tt + 1, :], in_=pbh[:, 0:512].rearrange("p (a b) -> p a b", b=128), func=AF.Copy),
                                     reads=[], writes=[pbh_b, vtok_b])
                        P.dma("sp", s_vtok.ap()[t0:t0 + TT, c * 128:(c + 1) * 128].rearrange("(tt p) c -> p tt c", p=128), vtok_st[:],
                              reads=[vtok_b], writes=[db("vtok")])
                elif kind == "f":
                    P.op("act", lambda en: en.activation(out=fst[:, tsl(tci)], in_=pb[bi][0:32, :], func=AF.Exp, scale=-1.0, bias=negfb[:, 0:1]),
                         reads=[negfb_b], writes=[pb_b[bi], fst_b])
                    if last:
                        P.op("act", lambda en: en.activation(out=fst[:], in_=fst[:], func=AF.Ln, bias=ones_f[0:32, 0:1]),
                             reads=[fst_b, b_const], writes=[fst_b])
                        for tt in range(TT // 128):
                            P.op("pe", lambda en: en.transpose(pb[5][:, tt * 32:(tt + 1) * 32], fst[0:32, tt * 128:(tt + 1) * 128], ident_f[0:32, 0:32]),
                                 reads=[fst_b, b_cst], writes=[pb_b[5]])
                        P.op("act", lambda en: en.activation(out=lf_tok[:], in_=pb[5][:, 0:256].rearrange("p (a b) -> p a b", b=32), func=AF.Copy),
                             reads=[], writes=[pb_b[5], lf_tok_b])
                        P.dma("sp", s_lf.ap()[t0:t0 + TT, :].rearrange("(tt p) c -> p tt c", p=128), lf_tok[:],
                              reads=[lf_tok_b], writes=[db("lf")])
            blocks = []
            for kind, base in (("q", 0), ("k", 2048), ("v", 4096)):
                for b4 in range(4):
                    blocks.append(([(base + b4 * 512, 512)], [(c * 128, 128, (kind, b4 * 4 + c), 0) for c in range(4)]))
            blocks.append(([(6144, 32)], [(0, 32, ("f",), 0)]))
            gemm(hT, hT_b, 16, lambda c0, n: Wd["w_in_odd"].ap()[o, :, c0:c0 + n], blocks, in_epi)

    if E["cfg"].get("stop_after_proj"):
        return
    with Scope(P) as sa:
        lft = sa.sb("lft", [128, 16, 32], F32)
        lft_b = Buf()
        ncum = sa.sb("ncum", [128, 16, 32], F32)
        Cb = sa.sb("Cb", [128, 16, 32], F32)
        cum_b = Buf("cum")
        P.dma("sp", lft[:], s_lf.ap().rearrange("(tt p) c -> p tt c", p=128), reads=[db("lf")], writes=[lft_b])
        for j in range(16):
            for j2 in range(j + 1):
                P.op("pe", lambda en: en.matmul(pb[5][:, j * 32:(j + 1) * 32], lhsT=(triu_f if j2 == j else ones_f[:, :]), rhs=lft[:, j2, :],
                                                start=(j2 == 0), stop=(j2 == j)),
                     reads=[lft_b, b_cst, b_const], writes=[pb_b[5]])
            for j2 in range(j + 1):
                P.op("pe", lambda en: en.matmul(pb[6][:, j * 32:(j + 1) * 32], lhsT=ones_f[:, :], rhs=lft[:, j2, :],
                                                start=(j2 == 0), stop=(j2 == j)),
                     reads=[lft_b, b_const], writes=[pb_b[6]])
        P.op("act", lambda en: en.activation(out=ncum[:], in_=pb[5][:].rearrange("p (a b) -> p a b", b=32), func=AF.Copy),
             reads=[], writes=[pb_b[5], cum_b])
        P.op("act", lambda en: en.activation(out=Cb[:], in_=pb[6][:].rearrange("p (a b) -> p a b", b=32), func=AF.Copy),
             reads=[], writes=[pb_b[6], cum_b])
        s_nq = E["s_nq"]
        dm = sa.sb("dm", [128, 16, 32], F32)
        dhi = sa.sb("dhi", [128, 16, 32], BF16)
        dhf = sa.sb("dhf", [128, 16, 32], F32)
        dlo = sa.sb("dlo", [128, 16, 32], BF16)
        nqT = sa.sb("nqT", [32, 2, T], BF16)
        dm_b = Buf("dm")
        nqT_b = Buf("nqT")
        P.op("dve", lambda en: en.tensor_tensor(out=dm[:], in0=Cb[:], in1=ncum[:], op=ALU.subtract), reads=[cum_b], writes=[dm_b])
        P.op("dve", lambda en: en.tensor_scalar(out=dm[:], in0=dm[:], scalar1=8.0, scalar2=None, op0=ALU.mult), reads=[dm_b], writes=[dm_b])
        P.op("dve", lambda en: en.tensor_copy(out=dhi[:], in_=dm[:]), reads=[dm_b], writes=[dm_b])
        P.op("dve", lambda en: en.tensor_copy(out=dhf[:], in_=dhi[:]), reads=[dm_b], writes=[dm_b])
        P.op("dve", lambda en: en.tensor_tensor(out=dhf[:], in0=dm[:], in1=dhf[:], op=ALU.subtract), reads=[dm_b], writes=[dm_b])
        P.op("dve", lambda en: en.tensor_copy(out=dlo[:], in_=dhf[:]), reads=[dm_b], writes=[dm_b])
        for w, src in enumerate((dhi, dlo)):
            for t8 in range(2):
                for tt in range(8):
                    P.op("pe", lambda en: en.transpose(pbh[0:32, tt * 128:(tt + 1) * 128], src[:, t8 * 8 + tt, :], ident_bf[:]),
                         reads=[dm_b, b_const], writes=[pbh_b])
                P.op("act", lambda en: en.activation(out=nqT[:, w, t8 * 1024:(t8 + 1) * 1024], in_=pbh[0:32, :], func=AF.Copy),
                     reads=[], writes=[pbh_b, nqT_b])
        P.dma("sp", s_nq.ap().rearrange("w h t -> h w t"), nqT[:], reads=[nqT_b], writes=[db("nq")])
        mneg = sa.sb("mneg", [128, 128], BF16)
        mneg_b = Buf("mneg")
        P.op("dve", lambda en: en.tensor_scalar(out=mneg[:], in0=triu_f, scalar1=30000.0, scalar2=-30000.0, op0=ALU.mult, op1=ALU.add),
             reads=[b_cst], writes=[mneg_b])
        kaug = [[sa.sb(f"kaug{i}{hh}", [128, T], BF16) for hh in range(2)] for i in range(2)]
        qaug = [[sa.sb(f"qaug{i}{hh}", [128, T], BF16) for hh in range(2)] for i in range(2)]
        vm = [sa.sb(f"vm{i}", [128, 16, 128], BF16) for i in range(2)]
        m_b = [Buf(), Buf()]
        Bm = [sa.sb(f"Bm{i}", [128, 4], F32) for i in range(2)]
        Bm_b = [Buf(), Buf()]
        pbf = [sa.sb(f"pbf{i}", [128, 512], BF16) for i in range(2)]
        pbf_b = [Buf(), Buf()]
        rcp = sa.sb("rcp", [128, 512], F32)
        rcp_b = Buf()
        yst = [sa.sb(f"yst{i}", [128, 512], BF16) for i in range(2)]
        yst_b = [Buf(), Buf()]
        cnt = 0
        yc = 0
        for m in range(E["cfg"].get("fox_pairs", 16)):
            mk = m % 2
            for hh in range(2):
                h = 2 * m + hh
                own = slice(hh * 64, (hh + 1) * 64)
                oth = slice((1 - hh) * 64, (2 - hh) * 64)
                o0 = (1 - hh) * 64
                P.op("dve", lambda en: en.memset(kaug[mk][hh][oth, :], 0.0), writes=[m_b[mk]])
                P.op("dve", lambda en: en.memset(kaug[mk][hh][o0:o0 + 2, :], 1.0), writes=[m_b[mk]])
                P.op("dve", lambda en: en.memset(qaug[mk][hh][oth, :], 0.0), writes=[m_b[mk]])
                P.dma("sp", kaug[mk][hh][own, :], s_kT.ap()[h * 64:(h + 1) * 64, :], reads=[db("kT")], writes=[m_b[mk]])
                P.dma("sp", qaug[mk][hh][own, :], s_qT.ap()[h * 64:(h + 1) * 64, :], reads=[db("qT")], writes=[m_b[mk]])
                P.dma("sp", qaug[mk][hh][o0:o0 + 2, :], s_nq.ap()[:, h, :], reads=[db("nq")], writes=[m_b[mk]])
            P.dma("sp", vm[mk][:], s_vtok.ap()[:, m * 128:(m + 1) * 128].rearrange("(tt p) c -> p tt c", p=128), reads=[db("vtok")], writes=[m_b[mk]])
            for G in range(4):
                jmax = 4 * G + 3
                items = [(hh, j) for hh in range(2) for j in range(jmax + 1)]

                def f_logits(k):
                    hh, j = items[k]
                    L = k % 2
                    i_lo = max(4 * G, j)
                    col0 = (i_lo - 4 * G) * 128
                    P.op("pe", lambda en: en.matmul(pb[L][:, col0:512], lhsT=kaug[mk][hh][:, j * 128:(j + 1) * 128],
                                                    rhs=qaug[mk][hh][:, 4 * G * 128 + col0:(4 * G + 4) * 128], start=True, stop=(j < 4 * G)),
                         reads=[m_b[mk]], writes=[pb_b[L]])
                    if j >= 4 * G:
                        P.op("pe", lambda en: en.matmul(pb[L][:, col0:col0 + 128], lhsT=ident_bf[:], rhs=mneg[:], start=False, stop=True),
                             reads=[b_const, mneg_b], writes=[pb_b[L]])

                def f_post(k):
                    hh, j = items[k]
                    h = 2 * m + hh
                    L = k % 2
                    i_lo = max(4 * G, j)
                    P.op("dve", lambda en: en.tensor_scalar(out=Bm[L][:], in0=Cb[:, 4 * G:4 * G + 4, h], scalar1=-1.0, scalar2=ncum[:, j, h:h + 1],
                                                            op0=ALU.mult, op1=ALU.add),
                         reads=[cum_b], writes=[Bm_b[L]])
                    for i in range(i_lo, 4 * G + 4):
                        cs = slice((i - 4 * G) * 128, (i - 4 * G + 1) * 128)
                        P.op("act", lambda en: en.activation(out=pbf[L][:, cs], in_=pb[L][:, cs], func=AF.Exp, scale=0.125,
                                                             bias=Bm[L][:, i - 4 * G:i - 4 * G + 1]),
                             reads=[Bm_b[L]], writes=[pb_b[L], pbf_b[L]])

                def f_pv(k):
                    hh, j = items[k]
                    pbs = hh * 64
                    L = k % 2
                    i_lo = max(4 * G, j)
                    col0 = (i_lo - 4 * G) * 128
                    P.op("pe", lambda en: en.matmul(pb[2][pbs:pbs + 64, col0:512], lhsT=vm[mk][:, j, hh * 64:(hh + 1) * 64], rhs=pbf[L][:, col0:512],
                                                    start=(j == 0), stop=(j == jmax)),
                         reads=[m_b[mk], pbf_b[L]], writes=[pb_b[2]])
                    P.op("pe", lambda en: en.matmul(pb[3][pbs:pbs + 64, col0:512], lhsT=ones_bf[:, 0:64], rhs=pbf[L][:, col0:512],
                                                    start=(j == 0), stop=(j == jmax)),
                         reads=[b_const, pbf_b[L]], writes=[pb_b[3]])

                f_logits(0)
                for k in range(len(items)):
                    if k + 1 < len(items):
                        f_logits(k + 1)
                    f_post(k)
                    f_pv(k)
                yk = yc % 2
                yc += 1
                P.op("dve", lambda en: en.reciprocal(out=rcp[:], in_=pb[3][:]), reads=[], writes=[pb_b[3], rcp_b])
                P.op("dve", lambda en: en.tensor_tensor(out=yst[yk][:], in0=pb[2][:], in1=rcp[:], op=ALU.mult),
                     reads=[rcp_b], writes=[pb_b[2], yst_b[yk]])
                P.dma("sp", yT.ap()[m * 128:(m + 1) * 128, G * 512:(G + 1) * 512], yst[yk][:], reads=[yst_b[yk]], writes=[db("yT", G // 2)])


def build(cfg=None):
    cfg = cfg or {}
    layers = cfg.get("layers", list(range(DEPTH)))
    nc = bass.Bass("TRN2", target_bir_lowering=False)

    def din(name, shape):
        return nc.dram_tensor(name, list(shape), F32, kind="ExternalInput")
    x_in = din("x", [T, D])
    p_in = din("p", [DEPTH, T, 256])
    Wd = {n: din(n, s) for n, s in WEIGHTS}
    sp_in = din("sp", [128, NSP])
    cst_in = din("cst", [128, 512])
    oh_in = din("oh", [33, XA + XB])
    out_d = nc.dram_tensor("out", [T, D], F32, kind="ExternalOutput")
    dbg = {}
    for name, shape in cfg.get("dumps", []):
        dbg[name] = nc.dram_tensor("dbg_" + name, list(shape), F32, kind="ExternalOutput")

    def scr(name, shape, dt):
        if name in cfg.get("expose", ()):
            return nc.dram_tensor(name, list(shape), dt, kind="ExternalOutput")
        return nc.dram_tensor(name, list(shape), dt)
    xT = scr("xT", [D, T], F32)
    yT = scr("yT", [D, T], BF16)
    s_kvT = scr("s_kvT", [256, T], BF16)
    s_kvtok = scr("s_kvtok", [T, 256], BF16)
    s_kidxT = scr("s_kidxT", [64, T], BF16)
    s_widx = scr("s_widx", [T, 16], F32)
    s_qiT = scr("s_qiT", [1024, T], BF16)
    s_qaT = scr("s_qaT", [4096, T], BF16)
    s_qbT = scr("s_qbT", [1024, T], BF16)
    s_kdupT = scr("s_kdupT", [256, T], BF16)
    s_vbtok = scr("s_vbtok", [T, 128], BF16)
    s_qT = scr("s_qT", [2048, T], BF16)
    s_kT = scr("s_kT", [2048, T], BF16)
    s_vtok = scr("s_vtok", [T, 2048], BF16)
    s_lf = scr("s_lf", [T, 32], F32)
    s_vrow = scr("s_vrow", [16, XA + XB], F32)
    s_nq = scr("s_nq", [2, 32, T], BF16)
    dbufs = {}

    def db(*key):
        if key not in dbufs:
            dbufs[key] = Buf(str(key))
        return dbufs[key]

    with ExitStack() as st:
        P = Prog(nc, st)

        def gsb(name, shape, dt):
            return st.enter_context(nc.sbuf_tensor(name, list(shape), dt))

        spk = gsb("spk", [128, NSP], F32)
        cst = gsb("cst_sb", [128, 512], F32)
        ident_bf = gsb("ident_bf", [128, 128], BF16)
        ones_bf = gsb("ones_bf", [128, 128], BF16)
        bd_bf = gsb("bd_bf", [128, 128], BF16)
        ones_f = gsb("ones_f", [128, 128], F32)
        eps_t = gsb("eps_t", [128, 1], F32)
        triu_bf = gsb("triu_bf", [128, 128], BF16)
        halo = gsb("halo", [128, 86, 2], F32)
        WSLOT = 11008
        NWS = 2
        wbuf = [gsb(f"wbuf{i}", [128, WSLOT], BF16) for i in range(NWS)]
        wb_b = [Buf(f"wb{i}") for i in range(NWS)]
        wptr = [0]
        b_spk, b_cst, b_const, b_halo = Buf(), Buf(), Buf(), Buf()
        ident_f = cst[:, 0:128]
        jflip = cst[:, 128:256]
        triu_f = cst[:, 256:384]
        cneg = cst[:, 384:512]
        pb = [st.enter_context(nc.psum_tensor(f"pb{i}", [128, 512], F32)) for i in range(7)]
        pbh = st.enter_context(nc.psum_tensor("pbh", [128, 1024], BF16))
        pb_b = [Buf(f"pb{i}") for i in range(7)]
        pbh_b = Buf("pbh")

        P.dma("sp", spk[:], sp_in.ap(), writes=[b_spk])
        P.dma("sp", cst[:], cst_in.ap(), writes=[b_cst])
        P.op("dve", lambda e: e.memset(ones_bf[:], 1.0), writes=[b_const])
        P.op("dve", lambda e: e.memset(ones_f[:], 1.0), writes=[b_const])
        P.op("dve", lambda e: e.memset(eps_t[:], EPS), writes=[b_const])
        P.op("dve", lambda e: e.memset(bd_bf[:], 0.0), writes=[b_const])
        P.op("dve", lambda e: e.memset(bd_bf[0:64, 0:64], 1.0), writes=[b_const])
        P.op("dve", lambda e: e.memset(bd_bf[64:128, 64:128], 1.0), writes=[b_const])
        P.op("dve", lambda e: e.tensor_copy(out=ident_bf[:], in_=ident_f), reads=[b_cst], writes=[b_const])
        P.op("dve", lambda e: e.tensor_copy(out=triu_bf[:], in_=triu_f), reads=[b_cst], writes=[b_const])
        P.op("dve", lambda e: e.memset(spk[32:33, SP_RELB:SP_RELB + 32], NEG), reads=[], writes=[b_spk])
        P.barrier()

        gemm_bank = [0]
        state = {}

        def wview(si, KC, ntot):
            return wbuf[si][:, 0:KC * ntot].rearrange("p (kc n) -> p kc n", n=ntot)

        def gemm(src, src_b, KC, wsrc, blocks, epi, ntc=2, tc_off=0):
            for segs, chunks in blocks:
                si = wptr[0]
                wptr[0] = (wptr[0] + 1) % NWS
                ntot = sum(n for _, n in segs)
                wv = wview(si, KC, ntot)
                off = 0
                for (c0, ncols) in segs:
                    P.dma("pool", wv[:, :, off:off + ncols],
                          wsrc(c0, ncols).rearrange("(kc p) n -> p kc n", p=128), writes=[wb_b[si]])
                    off += ncols
                for (coff, m, tag, pbase) in chunks:
                    for tci in range(ntc):
                        bi = gemm_bank[0]
                        gemm_bank[0] = (gemm_bank[0] + 1) % 4
                        for kc in range(KC):
                            P.op("pe", lambda e: e.matmul(pb[bi][pbase:pbase + m, :], lhsT=wv[:, kc, coff:coff + m],
                                                          rhs=src[:, kc, tc_off + tci * TC:tc_off + (tci + 1) * TC],
                                                          start=(kc == 0), stop=(kc == KC - 1)),
                                 reads=[wb_b[si], src_b], writes=[pb_b[bi]])
                        epi(bi, m, tag, tci)

        def simple_blocks(col0, ncols_total, wcols, tagfn=None, m=128):
            blocks = []
            c = 0
            ci = 0
            while c < ncols_total:
                n = min(wcols, ncols_total - c)
                chunks = []
                o = 0
                while o < n:
                    mm = min(m, n - o)
                    chunks.append((o, mm, ci if tagfn is None else tagfn(ci), 0))
                    o += mm
                    ci += 1
                blocks.append(([(col0 + c, n)], chunks))
                c += n
            return blocks

        def rms_finish(S, src, src_b, C, lhsT_ones, gcol, inv_n, dst_fn, dst_bufs, ncols=TC, nparts=128):
            sq, sq_b, rstd, rstd_b = S["sq"], S["sq_b"], S["rstd"], S["rstd_b"]
            for c in range(C):
                P.op("act", lambda e: e.activation(out=sq[0:nparts, c, 0:ncols], in_=src(c), func=AF.Square),
                     reads=[src_b], writes=[sq_b])
            for c in range(C):
                P.op("pe", lambda e: e.matmul(pb[4][0:nparts, 0:ncols], lhsT=lhsT_ones[0:nparts, 0:nparts],
                                              rhs=sq[0:nparts, c, 0:ncols], start=(c == 0), stop=(c == C - 1)),
                     reads=[sq_b, b_const], writes=[pb_b[4]])
            P.op("act", lambda e: e.activation(out=rstd[0:nparts, 0:ncols], in_=pb[4][0:nparts, 0:ncols], func=AF.Sqrt,
                                               scale=inv_n, bias=eps_t[0:nparts, 0:1]),
                 reads=[b_const], writes=[pb_b[4], rstd_b])
            P.op("dve", lambda e: e.reciprocal(out=rstd[0:nparts, 0:ncols], in_=rstd[0:nparts, 0:ncols]),
                 reads=[rstd_b], writes=[rstd_b])
            for c in range(C):
                P.op("dve", lambda e: e.scalar_tensor_tensor(out=dst_fn(c), in0=src(c),
                                                             scalar=spk[0:nparts, gcol + c:gcol + c + 1],
                                                             in1=rstd[0:nparts, 0:ncols], op0=ALU.mult, op1=ALU.mult),
                     reads=[src_b, rstd_b, b_spk], writes=dst_bufs)

        def norm_x(S, hT, hT_b, gcol, t0):
            xs, xs_b = S["xs"], S["xs_b"]
            for tci in range(TT // TC):
                ta = t0 + tci * TC
                P.dma("sp", xs[:], xT.ap()[:, ta:ta + TC].rearrange("(kc p) t -> p kc t", p=128),
                      reads=[db("xT", ta // TC)], writes=[xs_b])
                rms_finish(S, lambda c: xs[:, c, :], xs_b, 16, ones_bf, gcol, 1.0 / D,
                           lambda c: hT[:, c, tci * TC:(tci + 1) * TC], [hT_b])

        def resid_epi(S, t0):
            def epi(bi, m, tag, tci):
                ta = t0 + tci * TC
                k = S["xr_i"][0]
                S["xr_i"][0] = (k + 1) % 2
                xr, xr_b = S["xr"][k], S["xr_b"][k]
                P.dma("sp", xr[:], xT.ap()[tag * 128:(tag + 1) * 128, ta:ta + TC],
                      reads=[db("xT", ta // TC)], writes=[xr_b])
                P.op("dve", lambda e: e.tensor_tensor(out=xr[:], in0=pb[bi][:], in1=xr[:], op=ALU.add),
                     reads=[xr_b], writes=[pb_b[bi], xr_b])
                P.dma("sp", xT.ap()[tag * 128:(tag + 1) * 128, ta:ta + TC], xr[:],
                      reads=[xr_b], writes=[db("xT", ta // TC)])
            return epi

        def dump(name, src_ap_dram):
            pass

        with Scope(P) as sc:
            xin = [sc.sb(f"xin{i}", [128, D], F32) for i in range(2)]
            xin_b = [Buf(), Buf()]
            stg = [sc.sb(f"xstg{i}", [128, 16, 128], F32) for i in range(2)]
            stg_b = [Buf(), Buf()]
            for tt in range(T // 128):
                k = tt % 2
                P.dma("sp", xin[k][:], x_in.ap()[tt * 128:(tt + 1) * 128, :], writes=[xin_b[k]])
                for g in range(4):
                    bi = 5 + (g % 2)
                    for j in range(4):
                        fc = g * 4 + j
                        P.op("pe", lambda e: e.transpose(pb[bi][:, j * 128:(j + 1) * 128], xin[k][:, fc * 128:(fc + 1) * 128], ident_f),
                             reads=[xin_b[k], b_cst], writes=[pb_b[bi]])
                    P.op("act", lambda e: e.activation(out=stg[k][:, g * 4:(g + 1) * 4, :], in_=pb[bi][:].rearrange("p (a b) -> p a b", b=128), func=AF.Copy),
                         reads=[], writes=[pb_b[bi], stg_b[k]])
                P.dma("sp", xT.ap()[:, tt * 128:(tt + 1) * 128].rearrange("(fc p) t -> p fc t", p=128), stg[k][:],
                      reads=[stg_b[k]], writes=[db("xT", tt // 4)])

        for l in layers:
            E = dict(locals())
            E['state'] = state
            if "attn" in cfg.get("parts", ("attn", "out", "ffn", "ple")):
                if l % 2 == 0:
                    even_attention(E, l)
                else:
                    odd_attention(E, l)
            token_local(E, l, cfg.get("parts", ("attn", "out", "ffn", "ple")))

        with Scope(P) as sc:
            xo = [sc.sb(f"xo{i}", [128, 16, 128], F32) for i in range(2)]
            xo_b = [Buf(), Buf()]
            ostg = [sc.sb(f"ostg{i}", [128, D], F32) for i in range(2)]
            ostg_b = [Buf(), Buf()]
            for tt in range(T // 128):
                k = tt % 2
                P.dma("sp", xo[k][:], xT.ap()[:, tt * 128:(tt + 1) * 128].rearrange("(fc p) t -> p fc t", p=128),
                      reads=[db("xT", tt // 4)], writes=[xo_b[k]])
                for g in range(4):
                    bi = 5 + (g % 2)
                    for j in range(4):
                        fc = g * 4 + j
                        P.op("pe", lambda e: e.transpose(pb[bi][:, j * 128:(j + 1) * 128], xo[k][:, fc, :], ident_f),
                             reads=[xo_b[k], b_cst], writes=[pb_b[bi]])
                    P.op("act", lambda e: e.activation(out=ostg[k][:, g * 512:(g + 1) * 512], in_=pb[bi][:], func=AF.Copy),
                         reads=[], writes=[pb_b[bi], ostg_b[k]])
                P.dma("sp", out_d.ap()[tt * 128:(tt + 1) * 128, :], ostg[k][:], reads=[ostg_b[k]], writes=[db("out")])
        P.barrier()
    return nc


_NC_CACHE = {}


def kernel(**inputs):
    inp = {k: np.asarray(v) for k, v in inputs.items()}
    if "nc" not in _NC_CACHE:
        _NC_CACHE["nc"] = build()
    nc = _NC_CACHE["nc"]
    cst, oh = host_consts()
    sp = pack_small(inp)
    wmap = {n: np.ascontiguousarray(inp[n], dtype=np.float32) for n, _ in WEIGHTS}
    in_maps = []
    for c in range(8):
        b = c % 4
        m = dict(x=np.ascontiguousarray(inp["x"][b], dtype=np.float32),
                 p=np.ascontiguousarray(inp["p"][:, b], dtype=np.float32), sp=sp, cst=cst, oh=oh)
        m.update(wmap)
        in_maps.append(m)
    res = run_bass_kernel_spmd(nc, in_maps, core_ids=list(range(8)))
    out = np.stack([res.results[b]["out"] for b in range(4)], axis=0)
    return out.astype(np.float32)
```

```python
import math
from contextlib import ExitStack
import numpy as np
import concourse.bass as bass
import concourse.mybir as mybir
from concourse.bass_utils import run_bass_kernel_spmd

F32 = mybir.dt.float32
BF16 = mybir.dt.bfloat16
ALU = mybir.AluOpType
AF = mybir.ActivationFunctionType

EPOCH = 30000
NDSEM = 40

D = 2048
T = 2048
DEPTH = 4
TT = 1024
TO = 1024
TC = 512
DFF = 5504
NFC = 43
EPS = 1e-6
XA = 1280
XB = 384
NEG = -30000.0

SP_ATTN = 0
SP_FFN = 64
SP_PLE = 128
SP_CONV = 192
SP_CQ = 1224
SP_CKV = 1232
SP_AQ = 1236
SP_BQ = 1240
SP_BK = 1242
SP_CQN = 1244
SP_CKN = 1246
SP_FB = 1248
SP_SINK = 1250
SP_RELB = 1282
SP_B31 = 1320
NSP = 1340

WEIGHTS = [
    ("w_in_even", (2, 2048, 2128)), ("a_w_uq", (2, 512, 4096)), ("a_w_qidx", (2, 512, 1024)),
    ("a_w_uv", (2, 16, 256, 64)), ("w_out_even", (2, 2048, 2048)), ("w_in_odd", (2, 2048, 6176)),
    ("w_out_odd", (2, 2048, 2048)), ("w_up", (4, 2048, 11008)), ("w_down", (4, 5504, 2048)),
    ("w_ple_gate", (4, 2048, 2048)), ("w_ple_proj", (4, 256, 2048)),
]


class Buf:
    __slots__ = ("name", "w", "r", "rd")

    def __init__(self, name=""):
        self.name = name
        self.w = None
        self.r = {}
        self.rd = []


class Prog:
    ENGS = ("pe", "act", "dve", "pool", "sp")

    def __init__(self, nc, stack):
        self.nc = nc
        self.stack = stack
        self.eng = {"pe": nc.tensor, "act": nc.scalar, "dve": nc.vector,
                    "pool": nc.gpsimd, "sp": nc.sync}
        self.cnt = {e: 0 for e in self.ENGS}
        self.esems = {e: [] for e in self.ENGS}
        self.seen_e = {e: {p: 0 for p in self.ENGS} for e in self.ENGS}
        self.seen_d = {e: {} for e in self.ENGS}
        self.dsem = {}
        for q in ("sp", "pool"):
            self.dsem[q] = [[self._newsem(f"d{q}{i}"), 0] for i in range(NDSEM)]
        self.dptr = {"sp": 0, "pool": 0}
        self.bar_sem = self._newsem("bar")
        self.bar_cnt = 0
        self.n_inst = 0

    def _newsem(self, name):
        return self.stack.enter_context(self.nc.semaphore(name))

    def _esem(self, e, idx):
        ep = (idx - 1) // EPOCH
        while len(self.esems[e]) <= ep:
            self.esems[e].append(self._newsem(f"e{e}{len(self.esems[e])}"))
        return self.esems[e][ep], (idx - 1) % EPOCH + 1

    def _wait(self, e, ev):
        if ev is None:
            return
        if ev[0] == "e":
            _, p, idx = ev
            if p == e and e == "pe":
                return
            if self.seen_e[e][p] >= idx:
                return
            self.seen_e[e][p] = idx
            s, v = self._esem(p, idx)
            self.eng[e].wait_ge(s, v)
        else:
            _, s, v, key = ev
            if self.seen_d[e].get(key, 0) >= v:
                return
            self.seen_d[e][key] = v
            self.eng[e].wait_ge(s, v)
        self.n_inst += 1

    def _deps(self, e, reads, writes):
        for b in reads:
            self._wait(e, b.w)
        for b in writes:
            self._wait(e, b.w)
            for p, idx in b.r.items():
                if p != e:
                    self._wait(e, ("e", p, idx))
            for ev in b.rd:
                self._wait(e, ev)

    def _mark(self, ev, reads, writes):
        for b in reads:
            if ev[0] == "e":
                b.r[ev[1]] = ev[2]
            else:
                b.rd.append(ev)
        for b in writes:
            b.w = ev
            b.r = {}
            b.rd = []

    def op(self, e, fn, reads=(), writes=()):
        self._deps(e, reads, writes)
        inst = fn(self.eng[e])
        self.cnt[e] += 1
        idx = self.cnt[e]
        s, _ = self._esem(e, idx)
        inst.then_inc(s, 1)
        self.n_inst += 1
        self._mark(("e", e, idx), reads, writes)

    def dma(self, q, out, in_, reads=(), writes=(), **kw):
        self._deps(q, reads, writes)
        slot = self.dsem[q][self.dptr[q]]
        key = (q, self.dptr[q])
        self.dptr[q] = (self.dptr[q] + 1) % NDSEM
        if slot[1] > 0:
            self._wait(q, ("d", slot[0], slot[1], key))
        inst = self.eng[q].dma_start(out=out, in_=in_, **kw)
        slot[1] += 16
        inst.then_inc(slot[0], 16)
        self.n_inst += 1
        self._mark(("d", slot[0], slot[1], key), reads, writes)

    def barrier(self):
        for p in self.ENGS:
            if p != "sp" and self.cnt[p] > 0:
                self._wait("sp", ("e", p, self.cnt[p]))
        for q in ("sp", "pool"):
            for i, slot in enumerate(self.dsem[q]):
                if slot[1] > 0:
                    self._wait("sp", ("d", slot[0], slot[1], (q, i)))
        self.bar_cnt += 1
        self.eng["sp"].sem_inc(self.bar_sem, 1)
        for e in self.ENGS:
            if e != "sp":
                self.eng[e].wait_ge(self.bar_sem, self.bar_cnt)
                for p in self.ENGS:
                    self.seen_e[e][p] = self.cnt[p]
                for q in ("sp", "pool"):
                    for i, slot in enumerate(self.dsem[q]):
                        self.seen_d[e][(q, i)] = slot[1]
        self.n_inst += 6


class Scope:
    def __init__(self, P):
        self.P = P
        self.st = ExitStack()

    def __enter__(self):
        self.st.__enter__()
        return self

    _uid = [0]

    def sb(self, name, shape, dt):
        Scope._uid[0] += 1
        return self.st.enter_context(self.P.nc.sbuf_tensor(f"{name}_{Scope._uid[0]}", list(shape), dt))

    def __exit__(self, *a):
        self.P.barrier()
        return self.st.__exit__(*a)


def rel_bucket_np(n):
    n = np.maximum(n, 0)
    exact = 16
    nf = np.maximum(n, 1).astype(np.float32)
    large = exact + (np.log(nf / np.float32(exact)) / np.float32(math.log(1024 / exact))
                     * np.float32(32 - exact)).astype(np.int32)
    large = np.minimum(large, 31)
    return np.where(n < exact, n, large)


def host_consts():
    cst = np.zeros((128, 512), np.float32)
    cst[:, 0:128] = np.eye(128)
    cst[:, 128:256] = np.eye(128)[::-1]
    i = np.arange(128)
    cst[:, 256:384] = (i[None, :] >= i[:, None]).astype(np.float32)
    cst[:, 384:512] = np.where(i[None, :] <= i[:, None], 0.0, -1e30)
    oh = np.zeros((33, XA + XB), np.float32)
    y = np.arange(XA)
    xx = y - 127
    b = np.where(xx < 0, 32, rel_bucket_np(xx))
    oh[b, y] = 1.0
    y = np.arange(XB)
    xx = y - 127
    b = np.where((xx < 0) | (xx >= 128), 32, rel_bucket_np(xx))
    oh[b, XA + y] = 1.0
    return cst, oh


def pack_small(inp):
    sp = np.zeros((128, NSP), np.float32)

    def fm(v):
        return np.ascontiguousarray(v.reshape(-1, 128).T)
    for l in range(4):
        sp[:, SP_ATTN + l * 16:SP_ATTN + (l + 1) * 16] = fm(inp["attn_norm"][l])
        sp[:, SP_FFN + l * 16:SP_FFN + (l + 1) * 16] = fm(inp["ffn_norm"][l])
        sp[:, SP_PLE + l * 16:SP_PLE + (l + 1) * 16] = fm(inp["ple_norm"][l])
        for k in range(3):
            c0 = SP_CONV + (l * 3 + k) * 86
            sp[:, c0:c0 + 86] = fm(inp["ffn_conv"][l, k])
    for e in range(2):
        sp[:, SP_CQ + e * 4:SP_CQ + e * 4 + 4] = fm(inp["a_cq_norm"][e])
        sp[:, SP_CKV + e * 2:SP_CKV + e * 2 + 2] = fm(inp["a_ckv_norm"][e])
        sp[:, SP_AQ + e * 2:SP_AQ + e * 2 + 2] = fm(inp["a_q_norm"][e])
        sp[:, SP_BQ + e] = np.tile(inp["b_q_norm"][e], 2)
        sp[:, SP_BK + e] = np.tile(inp["b_k_norm"][e], 2)
        sp[:, SP_CQN + e] = np.tile(inp["c_q_norm"][e], 2)
        sp[:, SP_CKN + e] = np.tile(inp["c_k_norm"][e], 2)
        sp[0:32, SP_FB + e] = inp["c_forget_bias"][e]
        sp[:, SP_SINK + e * 16:SP_SINK + (e + 1) * 16] = inp["b_sinks"][e][None, :]
    sp[0:32, SP_RELB:SP_RELB + 32] = inp["rel_bias"]
    sp[:, SP_B31:SP_B31 + 16] = inp["rel_bias"][31, 0:16][None, :]
    return sp


def token_local(E, l, parts):
    P, nc, Wd, db, gemm, simple_blocks, rms_finish, norm_x, resid_epi = (E[k] for k in (
        "P", "nc", "Wd", "db", "gemm", "simple_blocks", "rms_finish", "norm_x", "resid_epi"))
    xT, yT, spk, b_spk, pb, pb_b, halo, b_halo, p_in, ident_f, b_cst, wbuf, wb_b, wptr, wview = (E[k] for k in (
        "xT", "yT", "spk", "b_spk", "pb", "pb_b", "halo", "b_halo", "p_in", "ident_f", "b_cst", "wbuf", "wb_b", "wptr", "wview"))
    NWS = len(wbuf)
    flg, b_flg, cc_gather, hx_in, hxg, ones_bf = (E[k] for k in ("flg", "b_flg", "cc_gather", "hx_in", "hxg", "ones_bf"))
    for ps in range(1):
        t0 = 0
        with Scope(P) as so:
            hT = so.sb("hT", [128, 16, TT], BF16)
            hT_b = Buf("hT")
            hh = so.sb("hh", [128, 16, 2], BF16)
            hh_b = Buf("hh")

            def norm_scope(gcol, with_halo=False):
                with Scope(P) as sn:
                    S = dict(xs=sn.sb("xs", [128, 16, TC], F32), xs_b=Buf(), sq=sn.sb("sq", [128, 16, TC], BF16),
                             sq_b=Buf(), rstd=sn.sb("rstd", [128, TC], F32), rstd_b=Buf())
                    norm_x(S, hT, hT_b, gcol, t0)
                    if with_halo:
                        hxo = sn.sb("hxo", [128, 16, 2], F32)
                        hxo_b = Buf("hxo")
                        P.dma("sp", hxo[:], hxg.ap()[0:128, :].rearrange("p (kc t) -> p kc t", t=2), reads=[db("hxg")], writes=[hxo_b])
                        rms_finish(S, lambda c: hxo[:, c, :], hxo_b, 16, ones_bf, gcol, 1.0 / D,
                                   lambda c: hh[:, c, :], [hh_b], ncols=2)

            def mk_xr(sc):
                return dict(xr=[sc.sb(f"xr{i}", [128, TC], F32) for i in range(2)], xr_b=[Buf(), Buf()], xr_i=[0])

            if "out" in parts:
                with Scope(P) as s1:
                    S = mk_xr(s1)
                    P.dma("sp", hT[:], yT.ap()[:, t0:t0 + TT].rearrange("(kc p) t -> p kc t", p=128),
                          reads=[db("yT", 0)], writes=[hT_b])
                    wn = "w_out_even" if l % 2 == 0 else "w_out_odd"
                    gemm(hT, hT_b, 16, lambda c0, n: Wd[wn].ap()[l // 2, :, c0:c0 + n],
                         simple_blocks(0, D, 512), resid_epi(S, t0))
            if "ffn" in parts:
                with Scope(P) as sh:
                    hxs = sh.sb("hxs", [128, 16, 2], F32)
                    hxs_b = Buf("hxs")
                    P.dma("sp", hxs[:], xT.ap()[:, TO - 2:TO].rearrange("(kc p) t -> p kc t", p=128), reads=[db("xT", 1)], writes=[hxs_b])
                    P.dma("sp", hx_in.ap().rearrange("p (kc t) -> p kc t", t=2), hxs[:], reads=[hxs_b], writes=[db("hx_in")])
                cc_gather(hx_in, hxg, [db("hx_in")], [db("hxg")])
                norm_scope(SP_FFN + l * 16, with_halo=True)
                with Scope(P) as s2:
                    S = mk_xr(s2)
                    act = s2.sb("act", [128, NFC, TT], BF16)
                    act_b = Buf("act")
                    stg = {"g": s2.sb("sg", [128, TT + 2], F32), "u": s2.sb("su", [128, TT + 2], F32)}
                    stg_b = {"g": Buf("sg"), "u": Buf("su")}
                    cv = {"g": s2.sb("ga", [128, TT], F32), "u": s2.sb("ua", [128, TT], F32)}
                    cv_b = {"g": Buf("ga"), "u": Buf("ua")}

                    def up_pre(wv, wvb, coff, m, tag):
                        kind = tag[0]
                        for kc in range(16):
                            P.op("pe", lambda e: e.matmul(pb[6][:, 0:2], lhsT=wv[:, kc, coff:coff + m], rhs=hh[:, kc, :],
                                                          start=(kc == 0), stop=(kc == 15)),
                                 reads=[wvb, hh_b], writes=[pb_b[6]])
                        P.op("act", lambda e: e.activation(out=stg[kind][:, 0:2], in_=pb[6][:, 0:2], func=AF.Copy, scale=flg[:, 0:1]),
                             reads=[b_flg], writes=[pb_b[6], stg_b[kind]])

                    def conv_finish(kind, i):
                        c = i if kind == "g" else NFC + i
                        s_, sb_, a_, ab_ = stg[kind], stg_b[kind], cv[kind], cv_b[kind]
                        wc = [SP_CONV + (l * 3 + k) * 86 + c for k in range(3)]
                        P.op("act", lambda e: e.activation(out=a_[:], in_=s_[:, 2:TT + 2], func=AF.Copy,
                                                           scale=spk[:, wc[2]:wc[2] + 1]),
                             reads=[sb_, b_spk], writes=[ab_])
                        P.op("dve", lambda e: e.scalar_tensor_tensor(out=a_[:], in0=s_[:, 1:TT + 1], scalar=spk[:, wc[1]:wc[1] + 1],
                                                                     in1=a_[:], op0=ALU.mult, op1=ALU.add),
                             reads=[sb_, ab_, b_spk], writes=[ab_])
                        P.op("dve", lambda e: e.scalar_tensor_tensor(out=a_[:], in0=s_[:, 0:TT], scalar=spk[:, wc[0]:wc[0] + 1],
                                                                     in1=a_[:], op0=ALU.mult, op1=ALU.add),
                             reads=[sb_, ab_, b_spk], writes=[ab_])

                    def up_epi(bi, m, tag, tci):
                        kind, i = tag
                        c = i if kind == "g" else NFC + i
                        P.op("act", lambda e: e.activation(out=stg[kind][:, 2 + tci * TC:2 + (tci + 1) * TC], in_=pb[bi][:], func=AF.Copy),
                             reads=[], writes=[pb_b[bi], stg_b[kind]])
                        if tci == TT // TC - 1:
                            conv_finish(kind, i)
                            if kind == "u":
                                P.op("act", lambda e: e.activation(out=cv["g"][:], in_=cv["g"][:], func=AF.Silu),
                                     reads=[cv_b["g"]], writes=[cv_b["g"]])
                                P.op("dve", lambda e: e.tensor_tensor(out=act[:, i, :], in0=cv["g"][:], in1=cv["u"][:], op=ALU.mult),
                                     reads=[cv_b["g"], cv_b["u"]], writes=[act_b])
                    blocks = []
                    for i0 in range(0, NFC, 2):
                        npair = min(2, NFC - i0)
                        w = npair * 128
                        segs = [(i0 * 128, w), (DFF + i0 * 128, w)]
                        chunks = []
                        for j in range(npair):
                            chunks.append((j * 128, 128, ("g", i0 + j), 0))
                            chunks.append((w + j * 128, 128, ("u", i0 + j), 0))
                        blocks.append((segs, chunks))
                    gemm(hT, hT_b, 16, lambda c0, n: Wd["w_up"].ap()[l, :, c0:c0 + n], blocks, up_epi, pre=up_pre)
                    gemm(act, act_b, NFC, lambda c0, n: Wd["w_down"].ap()[l, :, c0:c0 + n],
                         simple_blocks(0, D, 256), resid_epi(S, t0))
            if "ple" in parts:
                norm_scope(SP_PLE + l * 16)
                with Scope(P) as s3:
                    S = mk_xr(s3)
                    pT = s3.sb("pT", [128, 2, TT], BF16)
                    pT_b = Buf("pT")
                    pl = [s3.sb(f"pl{i}", [128, 256], F32) for i in range(2)]
                    pl_b = [Buf(), Buf()]
                    sg = [s3.sb(f"sgt{i}", [128, TC], F32) for i in range(2)]
                    sg_b = [Buf(), Buf()]
                    for tt in range(TT // 128):
                        k = tt % 2
                        P.dma("sp", pl[k][:], p_in.ap()[l, t0 + tt * 128:t0 + (tt + 1) * 128, :], writes=[pl_b[k]])
                        for cc in range(2):
                            P.op("pe", lambda e: e.transpose(pb[5][:, cc * 128:(cc + 1) * 128], pl[k][:, cc * 128:(cc + 1) * 128], ident_f),
                                 reads=[pl_b[k], b_cst], writes=[pb_b[5]])
                        P.op("act", lambda e: e.activation(out=pT[:, :, tt * 128:(tt + 1) * 128],
                                                           in_=pb[5][:, 0:256].rearrange("p (a b) -> p a b", b=128), func=AF.Copy),
                             reads=[], writes=[pb_b[5], pT_b])
                    cnt = 0
                    for nb in range(D // 512):
                        sa = wptr[0]
                        sbb = (wptr[0] + 1) % NWS
                        wa = wview(sa, 16, 512)
                        wp = wview(sbb, 2, 512)
                        P.dma("pool", wa, Wd["w_ple_gate"].ap()[l, :, nb * 512:(nb + 1) * 512].rearrange("(kc p) n -> p kc n", p=128),
                              writes=[wb_b[sa]])
                        P.dma("pool", wp, Wd["w_ple_proj"].ap()[l, :, nb * 512:(nb + 1) * 512].rearrange("(kc p) n -> p kc n", p=128),
                              writes=[wb_b[sbb]])
                        for ci in range(4):
                            nchunk = nb * 4 + ci
                            for tci in range(TT // TC):
                                ba = cnt % 2
                                bb = 2 + cnt % 2
                                kx = cnt % 2
                                cnt += 1
                                ta = t0 + tci * TC
                                for kc in range(16):
                                    P.op("pe", lambda e: e.matmul(pb[ba][:], lhsT=wa[:, kc, ci * 128:(ci + 1) * 128],
                                                                  rhs=hT[:, kc, tci * TC:(tci + 1) * TC], start=(kc == 0), stop=(kc == 15)),
                                         reads=[wb_b[sa], hT_b], writes=[pb_b[ba]])
                                for kc in range(2):
                                    P.op("pe", lambda e: e.matmul(pb[bb][:], lhsT=wp[:, kc, ci * 128:(ci + 1) * 128],
                                                                  rhs=pT[:, kc, tci * TC:(tci + 1) * TC], start=(kc == 0), stop=(kc == 1)),
                                         reads=[wb_b[sbb], pT_b], writes=[pb_b[bb]])
                                P.op("act", lambda e: e.activation(out=sg[kx][:], in_=pb[ba][:], func=AF.Sigmoid),
                                     reads=[], writes=[pb_b[ba], sg_b[kx]])
                                P.op("dve", lambda e: e.tensor_tensor(out=sg[kx][:], in0=sg[kx][:], in1=pb[bb][:], op=ALU.mult),
                                     reads=[sg_b[kx]], writes=[pb_b[bb], sg_b[kx]])
                                xr, xr_b = S["xr"][kx], S["xr_b"][kx]
                                P.dma("sp", xr[:], xT.ap()[nchunk * 128:(nchunk + 1) * 128, ta:ta + TC],
                                      reads=[db("xT", ta // TC)], writes=[xr_b])
                                P.op("dve", lambda e: e.tensor_tensor(out=xr[:], in0=sg[kx][:], in1=xr[:], op=ALU.add),
                                     reads=[sg_b[kx], xr_b], writes=[xr_b])
                                P.dma("sp", xT.ap()[nchunk * 128:(nchunk + 1) * 128, ta:ta + TC], xr[:],
                                      reads=[xr_b], writes=[db("xT", ta // TC)])
                        wptr[0] = (wptr[0] + 2) % NWS


def even_attention(E, l):
    e = l // 2
    P, nc, Wd, db, gemm, simple_blocks, norm_x = (E[k] for k in ("P", "nc", "Wd", "db", "gemm", "simple_blocks", "norm_x"))
    spk, b_spk, pb, pb_b, pbh, pbh_b, ident_f, ident_bf, b_cst, b_const, ones_bf, bd_bf, jflip, cneg, eps_t = (E[k] for k in (
        "spk", "b_spk", "pb", "pb_b", "pbh", "pbh_b", "ident_f", "ident_bf", "b_cst", "b_const", "ones_bf", "bd_bf", "jflip", "cneg", "eps_t"))
    xT, yT, oh_in = E["xT"], E["yT"], E["oh_in"]
    s_kvT, s_kvtok, s_kidxT, s_widx, s_qiT, s_qaT, s_qbT, s_kdupT, s_vbtok, s_vrow = (E[k] for k in (
        "s_kvT", "s_kvtok", "s_kidxT", "s_widx", "s_qiT", "s_qaT", "s_qbT", "s_kdupT", "s_vbtok", "s_vrow"))
    XT = XA + XB
    flg, b_flg, cc_gather, kpack_e, gk_e = (E[k] for k in ("flg", "b_flg", "cc_gather", "kpack_e", "gk_e"))

    if not E["state"].get("vrow"):
        E["state"]["vrow"] = True
        with Scope(P) as sv:
            ohs = sv.sb("ohs", [33, XT], F32)
            ohs_b = Buf()
            vr = sv.sb("vr", [16, XT], F32)
            vr_b = Buf()
            P.dma("sp", ohs[:], oh_in.ap(), writes=[ohs_b])
            for (hc, x0, x1) in [(0, 0, 512), (0, 512, 1024), (0, 1024, XA), (16, XA, XT)]:
                P.op("pe", lambda en: en.matmul(pb[5][0:16, 0:x1 - x0], lhsT=spk[0:33, SP_RELB + hc:SP_RELB + hc + 16],
                                                rhs=ohs[:, x0:x1], start=True, stop=True),
                     reads=[ohs_b, b_spk], writes=[pb_b[5]])
                P.op("act", lambda en: en.activation(out=vr[:, x0:x1], in_=pb[5][0:16, 0:x1 - x0], func=AF.Copy),
                     reads=[], writes=[pb_b[5], vr_b])
            P.dma("sp", s_vrow.ap(), vr[:], reads=[vr_b], writes=[db("vrow")])

    def rms_grp(S, srcs, src_b, lhsT_ones, gcol, inv_n, dsts, dst_bufs, ncols=TC):
        C = len(srcs)
        sq, sq_b, rstd, rstd_b = S["sq"], S["sq_b"], S["rstd"], S["rstd_b"]
        for c in range(C):
            P.op("act", lambda en: en.activation(out=sq[:, c, 0:ncols], in_=srcs[c], func=AF.Square),
                 reads=[src_b], writes=[sq_b])
        for c in range(C):
            P.op("pe", lambda en: en.matmul(pb[4][:, 0:ncols], lhsT=lhsT_ones[:, :], rhs=sq[:, c, 0:ncols],
                                            start=(c == 0), stop=(c == C - 1)),
                 reads=[sq_b, b_const], writes=[pb_b[4]])
        P.op("act", lambda en: en.activation(out=rstd[:, 0:ncols], in_=pb[4][:, 0:ncols], func=AF.Sqrt,
                                             scale=inv_n, bias=eps_t[:, 0:1]),
             reads=[b_const], writes=[pb_b[4], rstd_b])
        P.op("dve", lambda en: en.reciprocal(out=rstd[:, 0:ncols], in_=rstd[:, 0:ncols]), reads=[rstd_b], writes=[rstd_b])
        for c in range(C):
            P.op("dve", lambda en: en.scalar_tensor_tensor(out=dsts[c], in0=srcs[c], scalar=spk[:, gcol + c:gcol + c + 1],
                                                           in1=rstd[:, 0:ncols], op0=ALU.mult, op1=ALU.mult),
                 reads=[src_b, rstd_b, b_spk], writes=dst_bufs)
    E["rms_grp"] = rms_grp

    for ps in range(0 if E["cfg"].get("skip_proj") else 1):
        t0 = 0
        with Scope(P) as so:
            hT = so.sb("hT", [128, 16, TT], BF16)
            hT_b = Buf("hT")
            with Scope(P) as sn:
                S0 = dict(xs=sn.sb("xs", [128, 16, TC], F32), xs_b=Buf(), sq=sn.sb("sq", [128, 16, TC], BF16),
                          sq_b=Buf(), rstd=sn.sb("rstd", [128, TC], F32), rstd_b=Buf())
                norm_x(S0, hT, hT_b, SP_ATTN + l * 16, t0)
            S = dict(sq=so.sb("sq", [128, 4, TC], BF16), sq_b=Buf(), rstd=so.sb("rstd", [128, TC], F32), rstd_b=Buf())
            stg4 = so.sb("stg4", [128, 4, TT], F32)
            stg4_b = Buf("stg4")
            stg1 = so.sb("stg1", [128, TT], F32)
            stg1_b = Buf("stg1")
            stg2 = so.sb("stg2", [128, 2, TT], F32)
            stg2_b = Buf("stg2")
            cqT = so.sb("cqT", [128, 4, TT], BF16)
            cqT_b = Buf("cqT")
            kvn = so.sb("kvn", [128, 2, TT], BF16)
            kvn_b = Buf("kvn")
            kvtok_st = so.sb("kvtok_st", [128, 8, 256], BF16)
            kvtok_b = Buf()
            kidx_st = so.sb("kidx_st", [64, TT], BF16)
            kidx_b = Buf()
            widx_st = so.sb("widx_st", [16, TT], F32)
            widx_b = Buf()
            widx_tok = so.sb("widx_tok", [128, 8, 16], F32)
            widx_tok_b = Buf()
            ob = so.sb("ob", [128, TT], BF16)
            ob_b = Buf("ob")
            ob2 = so.sb("ob2", [128, 2, TT], BF16)
            ob2_b = Buf("ob2")
            oq = [so.sb(f"oq{i}", [128, TC], BF16) for i in range(2)]
            oq_b = [Buf(), Buf()]
            oq_i = [0]
            vb_st = so.sb("vb_st", [128, TT], BF16)
            vb_b = Buf()
            vtok_st = so.sb("vtok_st", [128, 8, 128], BF16)
            vtok_b = Buf()

            def tsl(tci):
                return slice(tci * TC, (tci + 1) * TC)

            def in_epi(bi, m, tag, tci):
                kind = tag[0]
                last = (tci == TT // TC - 1)
                if kind == "cq":
                    c = tag[1]
                    P.op("act", lambda en: en.activation(out=stg4[:, c, tsl(tci)], in_=pb[bi][:], func=AF.Copy),
                         reads=[], writes=[pb_b[bi], stg4_b])
                    if c == 3 and last:
                        for t2 in range(TT // TC):
                            rms_grp(S, [stg4[:, cc, tsl(t2)] for cc in range(4)], stg4_b, ones_bf, SP_CQ + e * 4, 1.0 / 512,
                                    [cqT[:, cc, tsl(t2)] for cc in range(4)], [cqT_b])
                elif kind == "ckv":
                    c = tag[1]
                    P.op("act", lambda en: en.activation(out=stg2[:, c, tsl(tci)], in_=pb[bi][:], func=AF.Copy),
                         reads=[], writes=[pb_b[bi], stg2_b])
                    if c == 1 and last:
                        for t2 in range(TT // TC):
                            rms_grp(S, [stg2[:, cc, tsl(t2)] for cc in range(2)], stg2_b, ones_bf, SP_CKV + e * 2, 1.0 / 256,
                                    [kvn[:, cc, tsl(t2)] for cc in range(2)], [kvn_b])
                        P.dma("sp", s_kvT.ap()[:, TO:TO + TT].rearrange("(c p) t -> p c t", p=128), kvn[:],
                              reads=[kvn_b], writes=[db("kvT")])
                        P.dma("sp", kpack_e.ap()[0:256, :].rearrange("(c p) t -> p c t", p=128), kvn[:],
                              reads=[kvn_b], writes=[db("kpack_e")])
                        for tt in range(TT // 128):
                            for cc in range(2):
                                P.op("pe", lambda en: en.transpose(pbh[:, cc * 128:(cc + 1) * 128], kvn[:, cc, tt * 128:(tt + 1) * 128], ident_bf[:]),
                                     reads=[kvn_b, b_const], writes=[pbh_b])
                            P.op("act", lambda en: en.activation(out=kvtok_st[:, tt, :], in_=pbh[:, 0:256], func=AF.Copy),
                                 reads=[], writes=[pbh_b, kvtok_b])
                        P.dma("sp", s_kvtok.ap()[TO:TO + TT, :].rearrange("(tt p) c -> p tt c", p=128), kvtok_st[:],
                              reads=[kvtok_b], writes=[db("kvtok")])
                        P.dma("sp", kpack_e.ap()[576:832, :].rearrange("r (a c) -> (r a) c", c=256).rearrange("(tt p) c -> p tt c", p=128), kvtok_st[:],
                              reads=[kvtok_b], writes=[db("kpack_e")])
                elif kind == "kidx":
                    P.op("act", lambda en: en.activation(out=kidx_st[:, tsl(tci)], in_=pb[bi][0:64, :], func=AF.Copy),
                         reads=[], writes=[pb_b[bi], kidx_b])
                    if last:
                        P.dma("sp", s_kidxT.ap()[:, TO:TO + TT], kidx_st[:], reads=[kidx_b], writes=[db("kidxT")])
                        P.dma("sp", kpack_e.ap()[256:320, :], kidx_st[:], reads=[kidx_b], writes=[db("kpack_e")])
                elif kind == "widx":
                    P.op("act", lambda en: en.activation(out=widx_st[:, tsl(tci)], in_=pb[bi][0:16, :], func=AF.Copy),
                         reads=[], writes=[pb_b[bi], widx_b])
                    if last:
                        for tt in range(TT // 128):
                            P.op("pe", lambda en: en.transpose(pb[5][:, tt * 16:(tt + 1) * 16], widx_st[0:16, tt * 128:(tt + 1) * 128], ident_f[0:16, 0:16]),
                                 reads=[widx_b, b_cst], writes=[pb_b[5]])
                        P.op("act", lambda en: en.activation(out=widx_tok[:], in_=pb[5][:, 0:128].rearrange("p (a b) -> p a b", b=16), func=AF.Copy),
                             reads=[], writes=[pb_b[5], widx_tok_b])
                        P.dma("sp", s_widx.ap()[t0:t0 + TT, :].rearrange("(tt p) c -> p tt c", p=128), widx_tok[:],
                              reads=[widx_tok_b], writes=[db("widx")])
                elif kind == "qb":
                    c = tag[1]
                    P.op("act", lambda en: en.activation(out=stg1[:, tsl(tci)], in_=pb[bi][:], func=AF.Copy),
                         reads=[], writes=[pb_b[bi], stg1_b])
                    if last:
                        for t2 in range(TT // TC):
                            rms_grp(S, [stg1[:, tsl(t2)]], stg1_b, bd_bf, SP_BQ + e, 1.0 / 64, [ob[:, tsl(t2)]], [ob_b])
                        P.dma("sp", s_qbT.ap()[c * 128:(c + 1) * 128, t0:t0 + TT], ob[:], reads=[ob_b], writes=[db("qbT")])
                elif kind == "kb":
                    g, half = tag[1], tag[2]
                    pbs = half * 64
                    P.op("act", lambda en: en.activation(out=stg1[pbs:pbs + 64, tsl(tci)], in_=pb[bi][pbs:pbs + 64, :], func=AF.Copy),
                         reads=[], writes=[pb_b[bi], stg1_b])
                    if half == 1 and last:
                        for t2 in range(TT // TC):
                            rms_grp(S, [stg1[:, tsl(t2)]], stg1_b, bd_bf, SP_BK + e, 1.0 / 64, [ob[:, tsl(t2)]], [ob_b])
                        P.dma("sp", s_kdupT.ap()[g * 128:(g + 1) * 128, TO:TO + TT], ob[:], reads=[ob_b], writes=[db("kdupT")])
                        P.dma("sp", kpack_e.ap()[320 + g * 128:320 + (g + 1) * 128, :], ob[:], reads=[ob_b], writes=[db("kpack_e")])
                elif kind == "vb":
                    P.op("act", lambda en: en.activation(out=vb_st[:, tsl(tci)], in_=pb[bi][:], func=AF.Copy),
                         reads=[], writes=[pb_b[bi], vb_b])
                    if last:
                        for tt in range(TT // 128):
                            P.op("pe", lambda en: en.transpose(pbh[:, (tt % 4) * 128:(tt % 4 + 1) * 128], vb_st[:, tt * 128:(tt + 1) * 128], ident_bf[:]),
                                 reads=[vb_b, b_const], writes=[pbh_b])
                            if tt % 4 == 3:
                                P.op("act", lambda en: en.activation(out=vtok_st[:, tt - 3:tt + 1, :], in_=pbh[:, 0:512].rearrange("p (a b) -> p a b", b=128), func=AF.Copy),
                                     reads=[], writes=[pbh_b, vtok_b])
                        P.dma("sp", s_vbtok.ap()[TO:TO + TT, :].rearrange("(tt p) c -> p tt c", p=128), vtok_st[:],
                              reads=[vtok_b], writes=[db("vbtok")])
                        P.dma("sp", kpack_e.ap()[832:960, :].rearrange("r (a c) -> (r a) c", c=128).rearrange("(tt p) c -> p tt c", p=128), vtok_st[:],
                              reads=[vtok_b], writes=[db("kpack_e")])
                elif kind == "qa":
                    h, cc = tag[1], tag[2]
                    P.op("act", lambda en: en.activation(out=stg2[:, cc, tsl(tci)], in_=pb[bi][:], func=AF.Copy),
                         reads=[], writes=[pb_b[bi], stg2_b])
                    if cc == 1 and last:
                        for t2 in range(TT // TC):
                            rms_grp(S, [stg2[:, c2, tsl(t2)] for c2 in range(2)], stg2_b, ones_bf, SP_AQ + e * 2, 1.0 / 256,
                                    [ob2[:, c2, tsl(t2)] for c2 in range(2)], [ob2_b])
                        P.dma("sp", s_qaT.ap()[h * 256:(h + 1) * 256, t0:t0 + TT].rearrange("(c p) t -> p c t", p=128), ob2[:],
                              reads=[ob2_b], writes=[db("qaT")])
                elif kind == "qi":
                    c = tag[1]
                    k = oq_i[0]
                    oq_i[0] = (k + 1) % 2
                    P.op("act", lambda en: en.activation(out=oq[k][:], in_=pb[bi][:], func=AF.Copy),
                         reads=[], writes=[pb_b[bi], oq_b[k]])
                    P.dma("sp", s_qiT.ap()[c * 128:(c + 1) * 128, t0 + tci * TC:t0 + (tci + 1) * TC], oq[k][:],
                          reads=[oq_b[k]], writes=[db("qiT")])

            blocks = [
                ([(0, 512)], [(c * 128, 128, ("cq", c), 0) for c in range(4)]),
                ([(512, 336)], [(0, 128, ("ckv", 0), 0), (128, 128, ("ckv", 1), 0), (256, 64, ("kidx",), 0), (320, 16, ("widx",), 0)]),
                ([(848, 512)], [(c * 128, 128, ("qb", c), 0) for c in range(4)]),
                ([(1360, 512)], [(c * 128, 128, ("qb", 4 + c), 0) for c in range(4)]),
                ([(1872, 256)], [(0, 64, ("kb", 0, 0), 0), (0, 64, ("kb", 0, 1), 64), (64, 64, ("kb", 1, 0), 0), (64, 64, ("kb", 1, 1), 64),
                                 (128, 128, ("vb",), 0)]),
            ]
            gemm(hT, hT_b, 16, lambda c0, n: Wd["w_in_even"].ap()[e, :, c0:c0 + n], blocks, in_epi)
            blocks = []
            for b4 in range(2):
                blocks.append(([(b4 * 2048, 2048)], [((hh * 2 + cc) * 128, 128, ("qa", b4 * 8 + hh, cc), 0) for hh in range(8) for cc in range(2)]))
            gemm(cqT, cqT_b, 4, lambda c0, n: Wd["a_w_uq"].ap()[e, :, c0:c0 + n], blocks, in_epi)
            gemm(cqT, cqT_b, 4, lambda c0, n: Wd["a_w_qidx"].ap()[e, :, c0:c0 + n],
                 [([(0, 1024)], [(c * 128, 128, ("qi", c), 0) for c in range(8)])], in_epi)

    if not E["cfg"].get("skip_proj"):
        cc_gather(kpack_e, gk_e, [db("kpack_e")], [db("gk_e")])
        P.dma("sp", s_kvT.ap()[:, 0:TO], gk_e.ap()[0:256, :], reads=[db("gk_e")], writes=[db("kvT")])
        P.dma("sp", s_kidxT.ap()[:, 0:TO], gk_e.ap()[256:320, :], reads=[db("gk_e")], writes=[db("kidxT")])
        P.dma("sp", s_kdupT.ap()[:, 0:TO], gk_e.ap()[320:576, :], reads=[db("gk_e")], writes=[db("kdupT")])
        P.dma("sp", s_kvtok.ap()[0:TO, :], gk_e.ap()[576:832, :].rearrange("r (a c) -> (r a) c", c=256), reads=[db("gk_e")], writes=[db("kvtok")])
        P.dma("sp", s_vbtok.ap()[0:TO, :], gk_e.ap()[832:960, :].rearrange("r (a c) -> (r a) c", c=128), reads=[db("gk_e")], writes=[db("vbtok")])
    if E["cfg"].get("stop_after_proj"):
        return
    att_scale = 1.0 / 16.0
    with Scope(P) as sa:
        kvT = sa.sb("kvT", [128, 2, T], BF16)
        kvtok = sa.sb("kvtok", [128, 16, 256], BF16)
        kidx2 = sa.sb("kidx2", [128, T], BF16)
        wuv = sa.sb("wuv", [128, 16, 2, 64], BF16)
        expA = sa.sb("expA", [128, 16, 9, 128], BF16)
        b_k = Buf("kside")
        b_exp = Buf("expA")
        P.dma("sp", kvT[:], s_kvT.ap().rearrange("(c p) t -> p c t", p=128), reads=[db("kvT")], writes=[b_k])
        P.dma("sp", kvtok[:], s_kvtok.ap().rearrange("(tt p) c -> p tt c", p=128), reads=[db("kvtok")], writes=[b_k])
        P.dma("sp", kidx2[0:64, :], s_kidxT.ap(), reads=[db("kidxT")], writes=[b_k])
        P.dma("sp", kidx2[64:128, :], s_kidxT.ap(), reads=[db("kidxT")], writes=[b_k])
        P.dma("pool", wuv[:], Wd["a_w_uv"].ap()[e].rearrange("h (cc p) d -> p h cc d", p=128), writes=[b_k])
        hk = [sa.sb(f"hk{i}", [128, 128], F32) for i in range(2)]
        hk_b = [Buf(), Buf()]
        n = 0
        for h in range(16):
            for dj in range(9):
                k = n % 2
                n += 1
                P.dma("sp", hk[k][:], bass.AP(s_vrow, h * XT + dj * 128, [[1, 128], [1, 128]]), reads=[db("vrow")], writes=[hk_b[k]])
                P.op("pe", lambda en: en.matmul(pb[5][:, 0:128], lhsT=jflip, rhs=hk[k][:], start=True, stop=True),
                     reads=[hk_b[k], b_cst], writes=[pb_b[5]])
                P.op("act", lambda en: en.activation(out=expA[:, h, 8 - dj, :], in_=pb[5][:, 0:128], func=AF.Exp),
                     reads=[], writes=[pb_b[5], b_exp])
        qi = [sa.sb(f"qi{i}", [128, 8, 128], BF16) for i in range(2)]
        qa = [sa.sb(f"qa{i}", [128, 32, 128], BF16) for i in range(2)]
        wq = [sa.sb(f"wq{i}", [128, 16], F32) for i in range(2)]
        q_b = [Buf(), Buf()]
        score = sa.sb("score", [128, T], F32)
        score_b = Buf("score")
        work = sa.sb("work", [128, T], F32)
        work_b = Buf("work")
        m8 = sa.sb("m8", [128, 8], F32)
        m8_b = Buf("m8")
        mask01 = sa.sb("mask01", [128, T], BF16)
        mask01_b = Buf()
        maskT = sa.sb("maskT", [128, 16, 128], BF16)
        maskT_b = Buf()
        rl = [sa.sb(f"rl{i}", [128, 512], F32) for i in range(2)]
        rl_b = [Buf(), Buf()]
        pf = [sa.sb(f"pf{i}", [128, 512], F32) for i in range(2)]
        pf_b = [Buf(), Buf()]
        pbf = [sa.sb(f"pbf{i}", [128, 512], BF16) for i in range(2)]
        pbf_b = [Buf(), Buf()]
        rc = sa.sb("rc", [128, 128], F32)
        rc_b = Buf()
        on = sa.sb("on", [128, 2, 128], BF16)
        on_b = Buf()
        ya_st = [sa.sb(f"ya_st{i}", [128, 8, 128], BF16) for i in range(2)]
        ya_b = [Buf(), Buf()]
        cnt = [0, 0]
        for a_ in range(E["cfg"].get("dsa_tiles", 8)):
            i = 8 + a_
            qk = i % 2
            N = (i + 1) * 128
            qs = slice(a_ * 128, (a_ + 1) * 128)
            ks = slice(i * 128, (i + 1) * 128)
            P.dma("sp", qi[qk][:], s_qiT.ap()[:, qs].rearrange("(c p) t -> p c t", p=128), reads=[db("qiT")], writes=[q_b[qk]])
            P.dma("sp", qa[qk][:], s_qaT.ap()[:, qs].rearrange("(c p) t -> p c t", p=128), reads=[db("qaT")], writes=[q_b[qk]])
            P.dma("sp", wq[qk][:], s_widx.ap()[qs, :], reads=[db("widx")], writes=[q_b[qk]])
            for h in range(16):
                pbs = (h % 2) * 64
                for n0 in range(0, N, 512):
                    n1 = min(N, n0 + 512)
                    bk = cnt[0] % 2
                    cnt[0] += 1
                    P.op("pe", lambda en: en.matmul(pb[bk][:, 0:n1 - n0], lhsT=qi[qk][pbs:pbs + 64, h // 2, :], rhs=kidx2[pbs:pbs + 64, n0:n1],
                                                    start=True, stop=True),
                         reads=[q_b[qk], b_k], writes=[pb_b[bk]])
                    P.op("act", lambda en: en.activation(out=rl[bk][:, 0:n1 - n0], in_=pb[bk][:, 0:n1 - n0], func=AF.Relu),
                         reads=[], writes=[pb_b[bk], rl_b[bk]])
                    if h == 0:
                        P.op("dve", lambda en: en.tensor_scalar(out=score[:, n0:n1], in0=rl[bk][:, 0:n1 - n0], scalar1=wq[qk][:, 0:1], scalar2=None, op0=ALU.mult),
                             reads=[rl_b[bk], q_b[qk]], writes=[score_b])
                    else:
                        P.op("dve", lambda en: en.scalar_tensor_tensor(out=score[:, n0:n1], in0=rl[bk][:, 0:n1 - n0], scalar=wq[qk][:, h:h + 1],
                                                                       in1=score[:, n0:n1], op0=ALU.mult, op1=ALU.add),
                             reads=[rl_b[bk], q_b[qk], score_b], writes=[score_b])
            P.op("dve", lambda en: en.tensor_tensor(out=score[:, ks], in0=score[:, ks], in1=cneg, op=ALU.add),
                 reads=[score_b, b_cst], writes=[score_b])
            P.op("dve", lambda en: en.tensor_scalar(out=score[:, 0:TO], in0=score[:, 0:TO], scalar1=flg[:, 1:2], scalar2=None, op0=ALU.add),
                 reads=[score_b, b_flg], writes=[score_b])
            if i >= 2:
                cur, cur_b = score, score_b
                for it in range(32):
                    P.op("dve", lambda en: en.max(out=m8[:], in_=cur[:, 0:N]), reads=[cur_b], writes=[m8_b])
                    if it < 31:
                        P.op("dve", lambda en: en.match_replace(out=work[:, 0:N], in_to_replace=m8[:], in_values=cur[:, 0:N], imm_value=-1e30),
                             reads=[m8_b, cur_b], writes=[work_b])
                        cur, cur_b = work, work_b
                P.op("dve", lambda en: en.tensor_scalar(out=work[:, 0:N], in0=score[:, 0:N], scalar1=m8[:, 7:8], scalar2=None, op0=ALU.is_ge),
                     reads=[score_b, m8_b], writes=[work_b])
                P.op("dve", lambda en: en.scalar_tensor_tensor(out=mask01[:, 0:N], in0=score[:, 0:N], scalar=-1e29, in1=work[:, 0:N],
                                                               op0=ALU.is_gt, op1=ALU.mult),
                     reads=[score_b, work_b], writes=[mask01_b])
            else:
                P.op("dve", lambda en: en.tensor_scalar(out=mask01[:, 0:N], in0=score[:, 0:N], scalar1=-1e29, scalar2=None, op0=ALU.is_ge),
                     reads=[score_b], writes=[mask01_b])
            for j0 in range(0, i + 1, 8):
                j1 = min(i + 1, j0 + 8)
                for j in range(j0, j1):
                    P.op("pe", lambda en: en.transpose(pbh[:, (j - j0) * 128:(j - j0 + 1) * 128], mask01[:, j * 128:(j + 1) * 128], ident_bf[:]),
                         reads=[mask01_b, b_const], writes=[pbh_b])
                P.op("act", lambda en: en.activation(out=maskT[:, j0:j1, :], in_=pbh[:, 0:(j1 - j0) * 128].rearrange("p (a b) -> p a b", b=128), func=AF.Copy),
                     reads=[], writes=[pbh_b, maskT_b])
            yk = i % 2
            items = [(h, jg) for h in range(16) for jg in range(0, i + 1, 4)]

            def emit_logits(k):
                h, jg = items[k]
                je = min(i + 1, jg + 4)
                L = k % 2
                for j in range(jg, je):
                    sl = j - jg
                    for c in range(2):
                        P.op("pe", lambda en: en.matmul(pb[L][:, sl * 128:(sl + 1) * 128], lhsT=kvT[:, c, j * 128:(j + 1) * 128],
                                                        rhs=qa[qk][:, 2 * h + c, :], start=(c == 0), stop=(c == 1)),
                             reads=[b_k, q_b[qk]], writes=[pb_b[L]])

            def emit_post(k):
                h, jg = items[k]
                je = min(i + 1, jg + 4)
                nj = je - jg
                L = k % 2
                far = (i - (je - 1)) >= 8
                if far:
                    P.op("act", lambda en: en.activation(out=pf[L][:, 0:nj * 128], in_=pb[L][:, 0:nj * 128], func=AF.Exp, scale=att_scale,
                                                         bias=spk[:, SP_B31 + h:SP_B31 + h + 1]),
                         reads=[b_spk], writes=[pb_b[L], pf_b[L]])
                else:
                    P.op("act", lambda en: en.activation(out=pf[L][:, 0:nj * 128], in_=pb[L][:, 0:nj * 128], func=AF.Exp, scale=att_scale),
                         reads=[], writes=[pb_b[L], pf_b[L]])
                    if i - jg <= 8:
                        k0 = 8 - (i - jg)
                        P.op("dve", lambda en: en.tensor_tensor(out=pf[L][:, 0:nj * 128].rearrange("p (a b) -> p a b", b=128),
                                                                in0=pf[L][:, 0:nj * 128].rearrange("p (a b) -> p a b", b=128),
                                                                in1=expA[:, h, k0:k0 + nj, :], op=ALU.mult),
                             reads=[pf_b[L], b_exp], writes=[pf_b[L]])
                    else:
                        for j in range(jg, je):
                            sl = j - jg
                            kk = 8 - min(i - j, 8)
                            P.op("dve", lambda en: en.tensor_tensor(out=pf[L][:, sl * 128:(sl + 1) * 128], in0=pf[L][:, sl * 128:(sl + 1) * 128],
                                                                    in1=expA[:, h, kk, :], op=ALU.mult),
                                 reads=[pf_b[L], b_exp], writes=[pf_b[L]])
                P.op("dve", lambda en: en.tensor_tensor(out=pbf[L][:, 0:nj * 128].rearrange("p (a b) -> p a b", b=128),
                                                        in0=pf[L][:, 0:nj * 128].rearrange("p (a b) -> p a b", b=128),
                                                        in1=maskT[:, jg:je, :], op=ALU.mult),
                     reads=[pf_b[L], maskT_b], writes=[pbf_b[L]])

            def emit_pv(k):
                h, jg = items[k]
                je = min(i + 1, jg + 4)
                L = k % 2
                for j in range(jg, je):
                    sl = j - jg
                    for (bk, lh) in ((2, kvtok[:, j, 0:128]), (3, kvtok[:, j, 128:256]), (6, ones_bf[:, :])):
                        P.op("pe", lambda en: en.matmul(pb[bk][:, 0:128], lhsT=lh, rhs=pbf[L][:, sl * 128:(sl + 1) * 128],
                                                        start=(j == 0), stop=(j == i)),
                             reads=[b_k, pbf_b[L], b_const], writes=[pb_b[bk]])

            def emit_fin_dve(h):
                P.op("dve", lambda en: en.reciprocal(out=rc[:], in_=pb[6][:, 0:128]), reads=[], writes=[pb_b[6], rc_b])
                for c in range(2):
                    P.op("dve", lambda en: en.tensor_tensor(out=on[:, c, :], in0=pb[2 + c][:, 0:128], in1=rc[:], op=ALU.mult),
                         reads=[rc_b], writes=[pb_b[2 + c], on_b])

            def emit_fin_pe(h):
                pbs = (h % 2) * 64
                col = ((h // 2) % 4) * 128
                for c in range(2):
                    P.op("pe", lambda en: en.matmul(pb[5][pbs:pbs + 64, col:col + 128], lhsT=wuv[:, h, c, :], rhs=on[:, c, :],
                                                    start=(c == 0), stop=(c == 1)),
                         reads=[b_k, on_b], writes=[pb_b[5]])
                if h % 2 == 1:
                    P.op("act", lambda en: en.activation(out=ya_st[yk][:, h // 2, :], in_=pb[5][:, col:col + 128], func=AF.Copy),
                         reads=[], writes=[pb_b[5], ya_b[yk]])

            emit_logits(0)
            for k in range(len(items)):
                h, jg = items[k]
                if k + 1 < len(items):
                    emit_logits(k + 1)
                emit_post(k)
                emit_pv(k)
                if jg + 4 > i:
                    emit_fin_dve(h)
                    emit_fin_pe(h)
            P.dma("sp", yT.ap()[0:1024, qs].rearrange("(c p) t -> p c t", p=128), ya_st[yk][:], reads=[ya_b[yk]], writes=[db("yT", 0)])

    with Scope(P) as sw:
        kd = sw.sb("kd", [128, 2, T], BF16)
        vtk = sw.sb("vtk", [128, 16, 128], BF16)
        expB = sw.sb("expB", [128, 16, 2, 128], BF16)
        esk = sw.sb("esk", [128, 16], F32)
        b_k = Buf("kside")
        b_exp = Buf("expB")
        P.dma("sp", kd[:], s_kdupT.ap().rearrange("(g p) t -> p g t", p=128), reads=[db("kdupT")], writes=[b_k])
        P.dma("sp", vtk[:], s_vbtok.ap().rearrange("(tt p) c -> p tt c", p=128), reads=[db("vbtok")], writes=[b_k])
        P.op("act", lambda en: en.activation(out=esk[:], in_=spk[:, SP_SINK + e * 16:SP_SINK + (e + 1) * 16], func=AF.Exp),
             reads=[b_spk], writes=[b_exp])
        hk = [sw.sb(f"hk{i}", [128, 128], F32) for i in range(2)]
        hk_b = [Buf(), Buf()]
        n = 0
        for hb in range(16):
            for kx in range(2):
                dj = 1 - kx
                k = n % 2
                n += 1
                P.dma("sp", hk[k][:], bass.AP(s_vrow, hb * XT + XA + dj * 128, [[1, 128], [1, 128]]), reads=[db("vrow")], writes=[hk_b[k]])
                P.op("pe", lambda en: en.matmul(pb[5][:, 0:128], lhsT=jflip, rhs=hk[k][:], start=True, stop=True),
                     reads=[hk_b[k], b_cst], writes=[pb_b[5]])
                P.op("act", lambda en: en.activation(out=expB[:, hb, kx, :], in_=pb[5][:, 0:128], func=AF.Exp),
                     reads=[], writes=[pb_b[5], b_exp])
        qb = [sw.sb(f"qb{i}", [128, 8, 128], BF16) for i in range(2)]
        qb_b = [Buf(), Buf()]
        pf = [sw.sb(f"pf{i}", [128, 512], F32) for i in range(2)]
        pf_b = [Buf(), Buf()]
        pbf = [sw.sb(f"pbf{i}", [128, 512], BF16) for i in range(2)]
        pbf_b = [Buf(), Buf()]
        dn = sw.sb("dn", [128, 128], F32)
        dn_b = Buf()
        yb_st = [sw.sb(f"yb_st{i}", [128, 8, 128], BF16) for i in range(2)]
        yb_b = [Buf(), Buf()]
        cnt = 0
        for a_ in range(E["cfg"].get("swa_blocks", 8)):
            nb = 8 + a_
            qk = nb % 2
            qs = slice(a_ * 128, (a_ + 1) * 128)
            P.dma("sp", qb[qk][:], s_qbT.ap()[:, qs].rearrange("(c p) t -> p c t", p=128), reads=[db("qbT")], writes=[qb_b[qk]])
            for m in range(8):
                Lb = [(0, 1), (5, 6)][cnt % 2]
                Ls = cnt % 2
                cnt += 1
                units = []
                for hh in range(2):
                    for kx in range(2):
                        dj = 1 - kx
                        if nb - dj >= 0:
                            units.append((hh, kx, nb - dj, hh * 2 + kx))
                for (hh, kx, j, sl) in units:
                    hb = 2 * m + hh
                    g = hb // 8
                    pbs = hh * 64
                    bkx = Lb[hh]
                    P.op("pe", lambda en: en.matmul(pb[bkx][:, kx * 128:(kx + 1) * 128], lhsT=kd[pbs:pbs + 64, g, j * 128:(j + 1) * 128],
                                                    rhs=qb[qk][pbs:pbs + 64, m, :], start=True, stop=True),
                         reads=[b_k, qb_b[qk]], writes=[pb_b[bkx]])
                stg_ = E["cfg"].get("swa_stage", 4)
                if stg_ < 2:
                    continue
                L = Ls
                for hh in range(2):
                    bkx = Lb[hh]
                    a, b = (0, 2)
                    P.op("act", lambda en: en.activation(out=pf[L][:, hh * 256 + a * 128:hh * 256 + b * 128], in_=pb[bkx][:, a * 128:b * 128], func=AF.Exp, scale=0.125),
                         reads=[], writes=[pb_b[bkx], pf_b[L]])
                    P.op("dve", lambda en: en.tensor_tensor(out=pbf[L][:, hh * 256 + a * 128:hh * 256 + b * 128], in0=pf[L][:, hh * 256 + a * 128:hh * 256 + b * 128],
                                                            in1=expB[:, 2 * m + hh, a:b, :].rearrange("p k q -> p (k q)"), op=ALU.mult),
                         reads=[pf_b[L], b_exp], writes=[pbf_b[L]])
                    if a_ == 0:
                        P.op("dve", lambda en: en.tensor_scalar(out=pbf[L][:, hh * 256:hh * 256 + 128], in0=pbf[L][:, hh * 256:hh * 256 + 128],
                                                                scalar1=flg[:, 0:1], scalar2=None, op0=ALU.mult),
                             reads=[pbf_b[L], b_flg], writes=[pbf_b[L]])
                if stg_ < 3:
                    continue
                for hh in range(2):
                    us = [u for u in units if u[0] == hh]
                    hb = 2 * m + hh
                    g = hb // 8
                    pbs = hh * 64
                    for ui, (_, kx, j, sl) in enumerate(us):
                        P.op("pe", lambda en: en.matmul(pb[2][pbs:pbs + 64, 0:128], lhsT=vtk[:, j, g * 64:(g + 1) * 64], rhs=pbf[L][:, sl * 128:(sl + 1) * 128],
                                                        start=(ui == 0), stop=(ui == len(us) - 1)),
                             reads=[b_k, pbf_b[L]], writes=[pb_b[2]])
                        P.op("pe", lambda en: en.matmul(pb[3][pbs:pbs + 64, 0:128], lhsT=ones_bf[:, 0:64], rhs=pbf[L][:, sl * 128:(sl + 1) * 128],
                                                        start=(ui == 0), stop=(ui == len(us) - 1)),
                             reads=[b_const, pbf_b[L]], writes=[pb_b[3]])
                if stg_ < 4:
                    continue
                for hh in range(2):
                    hb = 2 * m + hh
                    pbs = hh * 64
                    P.op("dve", lambda en: en.tensor_scalar(out=dn[pbs:pbs + 64, :], in0=pb[3][pbs:pbs + 64, 0:128], scalar1=esk[pbs:pbs + 64, hb:hb + 1],
                                                            scalar2=None, op0=ALU.add),
                         reads=[b_exp], writes=[pb_b[3], dn_b])
                P.op("dve", lambda en: en.reciprocal(out=dn[:], in_=dn[:]), reads=[dn_b], writes=[dn_b])
                P.op("dve", lambda en: en.tensor_tensor(out=yb_st[qk][:, m, :], in0=pb[2][:, 0:128], in1=dn[:], op=ALU.mult),
                     reads=[dn_b], writes=[pb_b[2], yb_b[qk]])
            P.dma("sp", yT.ap()[1024:2048, qs].rearrange("(c p) t -> p c t", p=128), yb_st[qk][:], reads=[yb_b[qk]], writes=[db("yT", 0)])


def odd_attention(E, l):
    o = l // 2
    P, nc, Wd, db, gemm, simple_blocks, norm_x = (E[k] for k in ("P", "nc", "Wd", "db", "gemm", "simple_blocks", "norm_x"))
    spk, b_spk, pb, pb_b, pbh, pbh_b, ident_f, ident_bf, b_cst, b_const, ones_bf, ones_f, bd_bf, triu_f, triu_bf, eps_t = (E[k] for k in (
        "spk", "b_spk", "pb", "pb_b", "pbh", "pbh_b", "ident_f", "ident_bf", "b_cst", "b_const", "ones_bf", "ones_f", "bd_bf", "triu_f", "triu_bf", "eps_t"))
    xT, yT = E["xT"], E["yT"]
    s_qT, s_kT, s_vtok, s_lf = E["s_qT"], E["s_kT"], E["s_vtok"], E["s_lf"]
    flg, b_flg, cc_gather, kpack_o, gk_o, lfp, glf = (E[k] for k in ("flg", "b_flg", "cc_gather", "kpack_o", "gk_o", "lfp", "glf"))

    def rms_grp(S, srcs, src_b, lhsT_ones, gcol, inv_n, dsts, dst_bufs, ncols=TC):
        C = len(srcs)
        sq, sq_b, rstd, rstd_b = S["sq"], S["sq_b"], S["rstd"], S["rstd_b"]
        for c in range(C):
            P.op("act", lambda en: en.activation(out=sq[:, c, 0:ncols], in_=srcs[c], func=AF.Square),
                 reads=[src_b], writes=[sq_b])
        for c in range(C):
            P.op("pe", lambda en: en.matmul(pb[4][:, 0:ncols], lhsT=lhsT_ones[:, :], rhs=sq[:, c, 0:ncols],
                                            start=(c == 0), stop=(c == C - 1)),
                 reads=[sq_b, b_const], writes=[pb_b[4]])
        P.op("act", lambda en: en.activation(out=rstd[:, 0:ncols], in_=pb[4][:, 0:ncols], func=AF.Sqrt,
                                             scale=inv_n, bias=eps_t[:, 0:1]),
             reads=[b_const], writes=[pb_b[4], rstd_b])
        P.op("dve", lambda en: en.reciprocal(out=rstd[:, 0:ncols], in_=rstd[:, 0:ncols]), reads=[rstd_b], writes=[rstd_b])
        for c in range(C):
            P.op("dve", lambda en: en.scalar_tensor_tensor(out=dsts[c], in0=srcs[c], scalar=spk[:, gcol + c:gcol + c + 1],
                                                           in1=rstd[:, 0:ncols], op0=ALU.mult, op1=ALU.mult),
                 reads=[src_b, rstd_b, b_spk], writes=dst_bufs)

    for ps in range(0 if E["cfg"].get("skip_proj") else 1):
        t0 = 0
        with Scope(P) as so:
            hT = so.sb("hT", [128, 16, TT], BF16)
            hT_b = Buf("hT")
            with Scope(P) as sn:
                S0 = dict(xs=sn.sb("xs", [128, 16, TC], F32), xs_b=Buf(), sq=sn.sb("sq", [128, 16, TC], BF16),
                          sq_b=Buf(), rstd=sn.sb("rstd", [128, TC], F32), rstd_b=Buf())
                norm_x(S0, hT, hT_b, SP_ATTN + l * 16, t0)
            S = dict(sq=so.sb("sq", [128, 1, TC], BF16), sq_b=Buf(), rstd=so.sb("rstd", [128, TC], F32), rstd_b=Buf())
            stg1 = so.sb("stg1", [128, TT], F32)
            stg1_b = Buf("stg1")
            ob = so.sb("ob", [128, TT], BF16)
            ob_b = Buf("ob")
            vb_st = so.sb("vb_st", [128, TT], BF16)
            vb_b = Buf()
            vtok_st = so.sb("vtok_st", [128, 8, 128], BF16)
            vtok_b = Buf()
            negfb = so.sb("negfb", [32, 1], F32)
            negfb_b = Buf()
            fst = so.sb("fst", [32, TT], F32)
            fst_b = Buf()
            lf_tok = so.sb("lf_tok", [128, 8, 32], F32)
            lf_tok_b = Buf()
            P.op("dve", lambda en: en.tensor_scalar(out=negfb[:], in0=spk[0:32, SP_FB + o:SP_FB + o + 1], scalar1=-1.0, scalar2=None, op0=ALU.mult),
                 reads=[b_spk], writes=[negfb_b])

            def tsl(tci):
                return slice(tci * TC, (tci + 1) * TC)

            def in_epi(bi, m, tag, tci):
                kind = tag[0]
                last = (tci == TT // TC - 1)
                if kind in ("q", "k"):
                    c = tag[1]
                    P.op("act", lambda en: en.activation(out=stg1[:, tsl(tci)], in_=pb[bi][:], func=AF.Copy),
                         reads=[], writes=[pb_b[bi], stg1_b])
                    if last:
                        gcol = (SP_CQN if kind == "q" else SP_CKN) + o
                        dst = s_qT if kind == "q" else s_kT
                        for t2 in range(TT // TC):
                            rms_grp(S, [stg1[:, tsl(t2)]], stg1_b, bd_bf, gcol, 1.0 / 64, [ob[:, tsl(t2)]], [ob_b])
                        if kind == "q":
                            P.dma("sp", s_qT.ap()[c * 128:(c + 1) * 128, 0:TT], ob[:], reads=[ob_b], writes=[db("qT")])
                        else:
                            P.dma("sp", s_kT.ap()[c * 128:(c + 1) * 128, TO:TO + TT], ob[:], reads=[ob_b], writes=[db("kT")])
                            P.dma("sp", kpack_o[c // 8].ap()[(c % 8) * 128:(c % 8 + 1) * 128, :], ob[:], reads=[ob_b], writes=[db("kpack_o", c // 8)])
                elif kind == "v":
                    c = tag[1]
                    P.op("act", lambda en: en.activation(out=vb_st[:, tsl(tci)], in_=pb[bi][:], func=AF.Copy),
                         reads=[], writes=[pb_b[bi], vb_b])
                    if last:
                        for tt in range(TT // 128):
                            P.op("pe", lambda en: en.transpose(pbh[:, (tt % 4) * 128:(tt % 4 + 1) * 128], vb_st[:, tt * 128:(tt + 1) * 128], ident_bf[:]),
                                 reads=[vb_b, b_const], writes=[pbh_b])
                            if tt % 4 == 3:
                                P.op("act", lambda en: en.activation(out=vtok_st[:, tt - 3:tt + 1, :], in_=pbh[:, 0:512].rearrange("p (a b) -> p a b", b=128), func=AF.Copy),
                                     reads=[], writes=[pbh_b, vtok_b])
                        P.dma("sp", s_vtok.ap()[TO:TO + TT, c * 128:(c + 1) * 128].rearrange("(tt p) c -> p tt c", p=128), vtok_st[:],
                              reads=[vtok_b], writes=[db("vtok")])
                        for hv in range(2):
                            P.dma("sp", kpack_o[2 + hv].ap().rearrange("(t a) c -> t (a c)", a=2)[:, c * 128:(c + 1) * 128].rearrange("(tt p) c -> p tt c", p=128),
                                  vtok_st[:, hv * 4:(hv + 1) * 4, :], reads=[vtok_b], writes=[db("kpack_o", 2 + hv)])
                elif kind == "f":
                    P.op("act", lambda en: en.activation(out=fst[:, tsl(tci)], in_=pb[bi][0:32, :], func=AF.Exp, scale=-1.0, bias=negfb[:, 0:1]),
                         reads=[negfb_b], writes=[pb_b[bi], fst_b])
                    if last:
                        P.op("act", lambda en: en.activation(out=fst[:], in_=fst[:], func=AF.Ln, bias=ones_f[0:32, 0:1]),
                             reads=[fst_b, b_const], writes=[fst_b])
                        for tt in range(TT // 128):
                            P.op("pe", lambda en: en.transpose(pb[5][:, tt * 32:(tt + 1) * 32], fst[0:32, tt * 128:(tt + 1) * 128], ident_f[0:32, 0:32]),
                                 reads=[fst_b, b_cst], writes=[pb_b[5]])
                        P.op("act", lambda en: en.activation(out=lf_tok[:], in_=pb[5][:, 0:256].rearrange("p (a b) -> p a b", b=32), func=AF.Copy),
                             reads=[], writes=[pb_b[5], lf_tok_b])
                        P.dma("sp", s_lf.ap()[TO:TO + TT, :].rearrange("(tt p) c -> p tt c", p=128), lf_tok[:],
                              reads=[lf_tok_b], writes=[db("lf")])
                        P.dma("sp", lfp.ap().rearrange("(tt p) c -> p tt c", p=128), lf_tok[:],
                              reads=[lf_tok_b], writes=[db("lfp")])
            blocks = []
            for kind, base in (("q", 0), ("k", 2048), ("v", 4096)):
                for b4 in range(4):
                    blocks.append(([(base + b4 * 512, 512)], [(c * 128, 128, (kind, b4 * 4 + c), 0) for c in range(4)]))
            blocks.append(([(6144, 32)], [(0, 32, ("f",), 0)]))
            gemm(hT, hT_b, 16, lambda c0, n: Wd["w_in_odd"].ap()[o, :, c0:c0 + n], blocks, in_epi)

    if not E["cfg"].get("skip_proj"):
        for i4 in range(4):
            cc_gather(kpack_o[i4], gk_o[i4], [db("kpack_o", i4)], [db("gk_o", i4)])
        cc_gather(lfp, glf, [db("lfp")], [db("glf")])
        for i4 in range(2):
            P.dma("sp", s_kT.ap()[i4 * 1024:(i4 + 1) * 1024, 0:TO], gk_o[i4].ap()[0:1024, :], reads=[db("gk_o", i4)], writes=[db("kT")])
            P.dma("sp", s_vtok.ap()[i4 * 512:(i4 + 1) * 512, :], gk_o[2 + i4].ap()[0:1024, :].rearrange("(t a) c -> t (a c)", a=2),
                  reads=[db("gk_o", 2 + i4)], writes=[db("vtok")])
        P.dma("sp", s_lf.ap()[0:TO, :], glf.ap()[0:1024, :], reads=[db("glf")], writes=[db("lf")])
    if E["cfg"].get("stop_after_proj"):
        return
    with Scope(P) as sa:
        lft = sa.sb("lft", [128, 16, 32], F32)
        lft_b = Buf()
        ncum = sa.sb("ncum", [128, 16, 32], F32)
        Cb = sa.sb("Cb", [128, 16, 32], F32)
        cum_b = Buf("cum")
        P.dma("sp", lft[:], s_lf.ap().rearrange("(tt p) c -> p tt c", p=128), reads=[db("lf")], writes=[lft_b])
        P.op("dve", lambda en: en.tensor_scalar(out=lft[:, 0:8, :], in0=lft[:, 0:8, :], scalar1=flg[:, 0:1], scalar2=None, op0=ALU.mult),
             reads=[lft_b, b_flg], writes=[lft_b])
        for j in range(16):
            for j2 in range(j + 1):
                P.op("pe", lambda en: en.matmul(pb[5][:, j * 32:(j + 1) * 32], lhsT=(triu_f if j2 == j else ones_f[:, :]), rhs=lft[:, j2, :],
                                                start=(j2 == 0), stop=(j2 == j)),
                     reads=[lft_b, b_cst, b_const], writes=[pb_b[5]])
            for j2 in range(j + 1):
                P.op("pe", lambda en: en.matmul(pb[6][:, j * 32:(j + 1) * 32], lhsT=ones_f[:, :], rhs=lft[:, j2, :],
                                                start=(j2 == 0), stop=(j2 == j)),
                     reads=[lft_b, b_const], writes=[pb_b[6]])
        P.op("act", lambda en: en.activation(out=ncum[:], in_=pb[5][:].rearrange("p (a b) -> p a b", b=32), func=AF.Copy),
             reads=[], writes=[pb_b[5], cum_b])
        P.op("act", lambda en: en.activation(out=Cb[:], in_=pb[6][:].rearrange("p (a b) -> p a b", b=32), func=AF.Copy),
             reads=[], writes=[pb_b[6], cum_b])
        s_nq = E["s_nq"]
        dm = sa.sb("dm", [128, 16, 32], F32)
        dhi = sa.sb("dhi", [128, 16, 32], BF16)
        dhf = sa.sb("dhf", [128, 16, 32], F32)
        dlo = sa.sb("dlo", [128, 16, 32], BF16)
        nqT = sa.sb("nqT", [32, 2, TO], BF16)
        dm_b = Buf("dm")
        nqT_b = Buf("nqT")
        P.op("dve", lambda en: en.tensor_tensor(out=dm[:], in0=Cb[:], in1=ncum[:], op=ALU.subtract), reads=[cum_b], writes=[dm_b])
        P.op("dve", lambda en: en.tensor_scalar(out=dm[:], in0=dm[:], scalar1=8.0, scalar2=None, op0=ALU.mult), reads=[dm_b], writes=[dm_b])
        P.op("dve", lambda en: en.tensor_copy(out=dhi[:], in_=dm[:]), reads=[dm_b], writes=[dm_b])
        P.op("dve", lambda en: en.tensor_copy(out=dhf[:], in_=dhi[:]), reads=[dm_b], writes=[dm_b])
        P.op("dve", lambda en: en.tensor_tensor(out=dhf[:], in0=dm[:], in1=dhf[:], op=ALU.subtract), reads=[dm_b], writes=[dm_b])
        P.op("dve", lambda en: en.tensor_copy(out=dlo[:], in_=dhf[:]), reads=[dm_b], writes=[dm_b])
        for w, src in enumerate((dhi, dlo)):
            for tt in range(8):
                P.op("pe", lambda en: en.transpose(pbh[0:32, tt * 128:(tt + 1) * 128], src[:, 8 + tt, :], ident_bf[:]),
                     reads=[dm_b, b_const], writes=[pbh_b])
            P.op("act", lambda en: en.activation(out=nqT[:, w, :], in_=pbh[0:32, :], func=AF.Copy),
                 reads=[], writes=[pbh_b, nqT_b])
        P.dma("sp", s_nq.ap().rearrange("w h t -> h w t"), nqT[:], reads=[nqT_b], writes=[db("nq")])
        P.op("dve", lambda en: en.tensor_scalar(out=ncum[:, 0:8, :], in0=ncum[:, 0:8, :], scalar1=flg[:, 2:3], scalar2=None, op0=ALU.add),
             reads=[cum_b, dm_b, b_flg], writes=[cum_b])
        mneg = sa.sb("mneg", [128, 128], BF16)
        mneg_b = Buf("mneg")
        P.op("dve", lambda en: en.tensor_scalar(out=mneg[:], in0=triu_f, scalar1=30000.0, scalar2=-30000.0, op0=ALU.mult, op1=ALU.add),
             reads=[b_cst], writes=[mneg_b])
        kaug = [[sa.sb(f"kaug{i}{hh}", [128, T], BF16) for hh in range(2)] for i in range(2)]
        qaug = [[sa.sb(f"qaug{i}{hh}", [128, TO], BF16) for hh in range(2)] for i in range(2)]
        vm = [sa.sb(f"vm{i}", [128, 16, 128], BF16) for i in range(2)]
        m_b = [Buf(), Buf()]
        Bm = [sa.sb(f"Bm{i}", [128, 4], F32) for i in range(2)]
        Bm_b = [Buf(), Buf()]
        pbf = [sa.sb(f"pbf{i}", [128, 512], BF16) for i in range(2)]
        pbf_b = [Buf(), Buf()]
        rcp = sa.sb("rcp", [128, 512], F32)
        rcp_b = Buf()
        yst = [sa.sb(f"yst{i}", [128, 512], BF16) for i in range(2)]
        yst_b = [Buf(), Buf()]
        cnt = 0
        yc = 0
        for m in range(E["cfg"].get("fox_pairs", 16)):
            mk = m % 2
            for hh in range(2):
                h = 2 * m + hh
                own = slice(hh * 64, (hh + 1) * 64)
                oth = slice((1 - hh) * 64, (2 - hh) * 64)
                o0 = (1 - hh) * 64
                P.op("dve", lambda en: en.memset(kaug[mk][hh][oth, :], 0.0), writes=[m_b[mk]])
                P.op("dve", lambda en: en.memset(kaug[mk][hh][o0:o0 + 2, :], 1.0), writes=[m_b[mk]])
                P.op("dve", lambda en: en.memset(qaug[mk][hh][oth, :], 0.0), writes=[m_b[mk]])
                P.dma("sp", kaug[mk][hh][own, :], s_kT.ap()[h * 64:(h + 1) * 64, :], reads=[db("kT")], writes=[m_b[mk]])
                P.dma("sp", qaug[mk][hh][own, :], s_qT.ap()[h * 64:(h + 1) * 64, :], reads=[db("qT")], writes=[m_b[mk]])
                P.dma("sp", qaug[mk][hh][o0:o0 + 2, :], s_nq.ap()[:, h, :], reads=[db("nq")], writes=[m_b[mk]])
            P.dma("sp", vm[mk][:], s_vtok.ap()[:, m * 128:(m + 1) * 128].rearrange("(tt p) c -> p tt c", p=128), reads=[db("vtok")], writes=[m_b[mk]])
            for Gl in range(2):
                G = 2 + Gl
                jmax = 4 * G + 3
                items = [(hh, j) for hh in range(2) for j in range(jmax + 1)]

                def f_logits(k):
                    hh, j = items[k]
                    L = k % 2
                    i_lo = max(4 * G, j)
                    col0 = (i_lo - 4 * G) * 128
                    P.op("pe", lambda en: en.matmul(pb[L][:, col0:512], lhsT=kaug[mk][hh][:, j * 128:(j + 1) * 128],
                                                    rhs=qaug[mk][hh][:, 4 * Gl * 128 + col0:(4 * Gl + 4) * 128], start=True, stop=(j < 4 * G)),
                         reads=[m_b[mk]], writes=[pb_b[L]])
                    if j >= 4 * G:
                        P.op("pe", lambda en: en.matmul(pb[L][:, col0:col0 + 128], lhsT=ident_bf[:], rhs=mneg[:], start=False, stop=True),
                             reads=[b_const, mneg_b], writes=[pb_b[L]])

                def f_post(k):
                    hh, j = items[k]
                    h = 2 * m + hh
                    L = k % 2
                    i_lo = max(4 * G, j)
                    P.op("dve", lambda en: en.tensor_scalar(out=Bm[L][:], in0=Cb[:, 4 * G:4 * G + 4, h], scalar1=-1.0, scalar2=ncum[:, j, h:h + 1],
                                                            op0=ALU.mult, op1=ALU.add),
                         reads=[cum_b], writes=[Bm_b[L]])
                    for i in range(i_lo, 4 * G + 4):
                        cs = slice((i - 4 * G) * 128, (i - 4 * G + 1) * 128)
                        P.op("act", lambda en: en.activation(out=pbf[L][:, cs], in_=pb[L][:, cs], func=AF.Exp, scale=0.125,
                                                             bias=Bm[L][:, i - 4 * G:i - 4 * G + 1]),
                             reads=[Bm_b[L]], writes=[pb_b[L], pbf_b[L]])

                def f_pv(k):
                    hh, j = items[k]
                    pbs = hh * 64
                    L = k % 2
                    i_lo = max(4 * G, j)
                    col0 = (i_lo - 4 * G) * 128
                    P.op("pe", lambda en: en.matmul(pb[2][pbs:pbs + 64, col0:512], lhsT=vm[mk][:, j, hh * 64:(hh + 1) * 64], rhs=pbf[L][:, col0:512],
                                                    start=(j == 0), stop=(j == jmax)),
                         reads=[m_b[mk], pbf_b[L]], writes=[pb_b[2]])
                    P.op("pe", lambda en: en.matmul(pb[3][pbs:pbs + 64, col0:512], lhsT=ones_bf[:, 0:64], rhs=pbf[L][:, col0:512],
                                                    start=(j == 0), stop=(j == jmax)),
                         reads=[b_const, pbf_b[L]], writes=[pb_b[3]])

                f_logits(0)
                for k in range(len(items)):
                    if k + 1 < len(items):
                        f_logits(k + 1)
                    f_post(k)
                    f_pv(k)
                yk = yc % 2
                yc += 1
                P.op("dve", lambda en: en.reciprocal(out=rcp[:], in_=pb[3][:]), reads=[], writes=[pb_b[3], rcp_b])
                P.op("dve", lambda en: en.tensor_tensor(out=yst[yk][:], in0=pb[2][:], in1=rcp[:], op=ALU.mult),
                     reads=[rcp_b], writes=[pb_b[2], yst_b[yk]])
                P.dma("sp", yT.ap()[m * 128:(m + 1) * 128, Gl * 512:(Gl + 1) * 512], yst[yk][:], reads=[yst_b[yk]], writes=[db("yT", 0)])


def build(cfg=None):
    cfg = cfg or {}
    layers = cfg.get("layers", list(range(DEPTH)))
    nc = bass.Bass("TRN2", target_bir_lowering=False)

    def din(name, shape):
        return nc.dram_tensor(name, list(shape), F32, kind="ExternalInput")
    x_in = din("x", [TO, D])
    p_in = din("p", [DEPTH, TO, 256])
    flg_in = din("flg", [128, 4])
    Wd = {n: din(n, s) for n, s in WEIGHTS}
    sp_in = din("sp", [128, NSP])
    cst_in = din("cst", [128, 512])
    oh_in = din("oh", [33, XA + XB])
    out_d = nc.dram_tensor("out", [TO, D], F32, kind="ExternalOutput")
    dbg = {}
    for name, shape in cfg.get("dumps", []):
        dbg[name] = nc.dram_tensor("dbg_" + name, list(shape), F32, kind="ExternalOutput")

    def scr(name, shape, dt):
        if name in cfg.get("expose", ()):
            return nc.dram_tensor(name, list(shape), dt, kind="ExternalOutput")
        return nc.dram_tensor(name, list(shape), dt)
    xT = scr("xT", [D, TO], F32)
    yT = scr("yT", [D, TO], BF16)
    s_kvT = scr("s_kvT", [256, T], BF16)
    s_kvtok = scr("s_kvtok", [T, 256], BF16)
    s_kidxT = scr("s_kidxT", [64, T], BF16)
    s_widx = scr("s_widx", [TO, 16], F32)
    s_qiT = scr("s_qiT", [1024, TO], BF16)
    s_qaT = scr("s_qaT", [4096, TO], BF16)
    s_qbT = scr("s_qbT", [1024, TO], BF16)
    s_kdupT = scr("s_kdupT", [256, T], BF16)
    s_vbtok = scr("s_vbtok", [T, 128], BF16)
    s_qT = scr("s_qT", [2048, TO], BF16)
    s_kT = scr("s_kT", [2048, T], BF16)
    s_vtok = scr("s_vtok", [T, 2048], BF16)
    s_lf = scr("s_lf", [T, 32], F32)
    s_vrow = scr("s_vrow", [16, XA + XB], F32)
    s_nq = scr("s_nq", [2, 32, TO], BF16)
    kpack_e = scr("kpack_e", [960, 1024], BF16)
    gk_e = scr("gk_e", [1920, 1024], BF16)
    kpack_o = [scr(f"kpack_o{i}", [1024, 1024], BF16) for i in range(4)]
    gk_o = [scr(f"gk_o{i}", [2048, 1024], BF16) for i in range(4)]
    lfp = scr("lfp", [1024, 32], F32)
    glf = scr("glf", [2048, 32], F32)
    hx_in = scr("hx_in", [128, 32], F32)
    hxg = scr("hxg", [256, 32], F32)
    dbufs = {}

    def db(*key):
        if key not in dbufs:
            dbufs[key] = Buf(str(key))
        return dbufs[key]

    with ExitStack() as st:
        P = Prog(nc, st)

        def gsb(name, shape, dt):
            return st.enter_context(nc.sbuf_tensor(name, list(shape), dt))

        spk = gsb("spk", [128, NSP], F32)
        cst = gsb("cst_sb", [128, 512], F32)
        ident_bf = gsb("ident_bf", [128, 128], BF16)
        ones_bf = gsb("ones_bf", [128, 128], BF16)
        bd_bf = gsb("bd_bf", [128, 128], BF16)
        ones_f = gsb("ones_f", [128, 128], F32)
        eps_t = gsb("eps_t", [128, 1], F32)
        triu_bf = gsb("triu_bf", [128, 128], BF16)
        halo = gsb("halo", [128, 2], F32)
        flg = gsb("flg_sb", [128, 4], F32)
        b_flg = Buf("flg")
        ccs = P._newsem("ccs")
        cc_n = [0]
        WSLOT = 11008
        NWS = 2
        wbuf = [gsb(f"wbuf{i}", [128, WSLOT], BF16) for i in range(NWS)]
        wb_b = [Buf(f"wb{i}") for i in range(NWS)]
        wptr = [0]
        b_spk, b_cst, b_const, b_halo = Buf(), Buf(), Buf(), Buf()
        ident_f = cst[:, 0:128]
        jflip = cst[:, 128:256]
        triu_f = cst[:, 256:384]
        cneg = cst[:, 384:512]
        pb = [st.enter_context(nc.psum_tensor(f"pb{i}", [128, 512], F32)) for i in range(7)]
        pbh = st.enter_context(nc.psum_tensor("pbh", [128, 1024], BF16))
        pb_b = [Buf(f"pb{i}") for i in range(7)]
        pbh_b = Buf("pbh")

        P.dma("sp", spk[:], sp_in.ap(), writes=[b_spk])
        P.dma("sp", cst[:], cst_in.ap(), writes=[b_cst])
        P.dma("sp", flg[:], flg_in.ap(), writes=[b_flg])
        P.op("dve", lambda e: e.memset(ones_bf[:], 1.0), writes=[b_const])
        P.op("dve", lambda e: e.memset(ones_f[:], 1.0), writes=[b_const])
        P.op("dve", lambda e: e.memset(eps_t[:], EPS), writes=[b_const])
        P.op("dve", lambda e: e.memset(bd_bf[:], 0.0), writes=[b_const])
        P.op("dve", lambda e: e.memset(bd_bf[0:64, 0:64], 1.0), writes=[b_const])
        P.op("dve", lambda e: e.memset(bd_bf[64:128, 64:128], 1.0), writes=[b_const])
        P.op("dve", lambda e: e.tensor_copy(out=ident_bf[:], in_=ident_f), reads=[b_cst], writes=[b_const])
        P.op("dve", lambda e: e.tensor_copy(out=triu_bf[:], in_=triu_f), reads=[b_cst], writes=[b_const])
        P.op("dve", lambda e: e.memset(spk[32:33, SP_RELB:SP_RELB + 32], NEG), reads=[], writes=[b_spk])
        P.barrier()

        gemm_bank = [0]

        def cc_gather(src_t, dst_t, in_bufs, out_bufs):
            P._deps("pool", list(in_bufs), list(out_bufs))
            cc_n[0] += 1
            nc.gpsimd.collective_compute("AllGather", ALU.bypass, replica_groups=[[0, 4], [1, 5], [2, 6], [3, 7]],
                                         ins=[src_t.ap()], outs=[dst_t.ap()]).then_inc(ccs, 1)
            nc.gpsimd.wait_ge(ccs, cc_n[0])
            P.op("pool", lambda e: e.memset(halo[0:1, 0:1], 0.0), reads=list(in_bufs), writes=list(out_bufs))
        state = {}

        def wview(si, KC, ntot):
            return wbuf[si][:, 0:KC * ntot].rearrange("p (kc n) -> p kc n", n=ntot)

        def gemm(src, src_b, KC, wsrc, blocks, epi, ntc=2, tc_off=0, pre=None):
            for segs, chunks in blocks:
                si = wptr[0]
                wptr[0] = (wptr[0] + 1) % NWS
                ntot = sum(n for _, n in segs)
                wv = wview(si, KC, ntot)
                off = 0
                for (c0, ncols) in segs:
                    P.dma("pool", wv[:, :, off:off + ncols],
                          wsrc(c0, ncols).rearrange("(kc p) n -> p kc n", p=128), writes=[wb_b[si]])
                    off += ncols
                for (coff, m, tag, pbase) in chunks:
                    if pre is not None:
                        pre(wv, wb_b[si], coff, m, tag)
                    for tci in range(ntc):
                        bi = gemm_bank[0]
                        gemm_bank[0] = (gemm_bank[0] + 1) % 4
                        for kc in range(KC):
                            P.op("pe", lambda e: e.matmul(pb[bi][pbase:pbase + m, :], lhsT=wv[:, kc, coff:coff + m],
                                                          rhs=src[:, kc, tc_off + tci * TC:tc_off + (tci + 1) * TC],
                                                          start=(kc == 0), stop=(kc == KC - 1)),
                                 reads=[wb_b[si], src_b], writes=[pb_b[bi]])
                        epi(bi, m, tag, tci)

        def simple_blocks(col0, ncols_total, wcols, tagfn=None, m=128):
            blocks = []
            c = 0
            ci = 0
            while c < ncols_total:
                n = min(wcols, ncols_total - c)
                chunks = []
                o = 0
                while o < n:
                    mm = min(m, n - o)
                    chunks.append((o, mm, ci if tagfn is None else tagfn(ci), 0))
                    o += mm
                    ci += 1
                blocks.append(([(col0 + c, n)], chunks))
                c += n
            return blocks

        def rms_finish(S, src, src_b, C, lhsT_ones, gcol, inv_n, dst_fn, dst_bufs, ncols=TC, nparts=128):
            sq, sq_b, rstd, rstd_b = S["sq"], S["sq_b"], S["rstd"], S["rstd_b"]
            for c in range(C):
                P.op("act", lambda e: e.activation(out=sq[0:nparts, c, 0:ncols], in_=src(c), func=AF.Square),
                     reads=[src_b], writes=[sq_b])
            for c in range(C):
                P.op("pe", lambda e: e.matmul(pb[4][0:nparts, 0:ncols], lhsT=lhsT_ones[0:nparts, 0:nparts],
                                              rhs=sq[0:nparts, c, 0:ncols], start=(c == 0), stop=(c == C - 1)),
                     reads=[sq_b, b_const], writes=[pb_b[4]])
            P.op("act", lambda e: e.activation(out=rstd[0:nparts, 0:ncols], in_=pb[4][0:nparts, 0:ncols], func=AF.Sqrt,
                                               scale=inv_n, bias=eps_t[0:nparts, 0:1]),
                 reads=[b_const], writes=[pb_b[4], rstd_b])
            P.op("dve", lambda e: e.reciprocal(out=rstd[0:nparts, 0:ncols], in_=rstd[0:nparts, 0:ncols]),
                 reads=[rstd_b], writes=[rstd_b])
            for c in range(C):
                P.op("dve", lambda e: e.scalar_tensor_tensor(out=dst_fn(c), in0=src(c),
                                                             scalar=spk[0:nparts, gcol + c:gcol + c + 1],
                                                             in1=rstd[0:nparts, 0:ncols], op0=ALU.mult, op1=ALU.mult),
                     reads=[src_b, rstd_b, b_spk], writes=dst_bufs)

        def norm_x(S, hT, hT_b, gcol, t0):
            xs, xs_b = S["xs"], S["xs_b"]
            for tci in range(TT // TC):
                ta = t0 + tci * TC
                P.dma("sp", xs[:], xT.ap()[:, ta:ta + TC].rearrange("(kc p) t -> p kc t", p=128),
                      reads=[db("xT", ta // TC)], writes=[xs_b])
                rms_finish(S, lambda c: xs[:, c, :], xs_b, 16, ones_bf, gcol, 1.0 / D,
                           lambda c: hT[:, c, tci * TC:(tci + 1) * TC], [hT_b])

        def resid_epi(S, t0):
            def epi(bi, m, tag, tci):
                ta = t0 + tci * TC
                k = S["xr_i"][0]
                S["xr_i"][0] = (k + 1) % 2
                xr, xr_b = S["xr"][k], S["xr_b"][k]
                P.dma("sp", xr[:], xT.ap()[tag * 128:(tag + 1) * 128, ta:ta + TC],
                      reads=[db("xT", ta // TC)], writes=[xr_b])
                P.op("dve", lambda e: e.tensor_tensor(out=xr[:], in0=pb[bi][:], in1=xr[:], op=ALU.add),
                     reads=[xr_b], writes=[pb_b[bi], xr_b])
                P.dma("sp", xT.ap()[tag * 128:(tag + 1) * 128, ta:ta + TC], xr[:],
                      reads=[xr_b], writes=[db("xT", ta // TC)])
            return epi

        def dump(name, src_ap_dram):
            pass

        with Scope(P) as sc:
            xin = [sc.sb(f"xin{i}", [128, D], F32) for i in range(2)]
            xin_b = [Buf(), Buf()]
            stg = [sc.sb(f"xstg{i}", [128, 16, 128], F32) for i in range(2)]
            stg_b = [Buf(), Buf()]
            for tt in range(TO // 128):
                k = tt % 2
                P.dma("sp", xin[k][:], x_in.ap()[tt * 128:(tt + 1) * 128, :], writes=[xin_b[k]])
                for g in range(4):
                    bi = 5 + (g % 2)
                    for j in range(4):
                        fc = g * 4 + j
                        P.op("pe", lambda e: e.transpose(pb[bi][:, j * 128:(j + 1) * 128], xin[k][:, fc * 128:(fc + 1) * 128], ident_f),
                             reads=[xin_b[k], b_cst], writes=[pb_b[bi]])
                    P.op("act", lambda e: e.activation(out=stg[k][:, g * 4:(g + 1) * 4, :], in_=pb[bi][:].rearrange("p (a b) -> p a b", b=128), func=AF.Copy),
                         reads=[], writes=[pb_b[bi], stg_b[k]])
                P.dma("sp", xT.ap()[:, tt * 128:(tt + 1) * 128].rearrange("(fc p) t -> p fc t", p=128), stg[k][:],
                      reads=[stg_b[k]], writes=[db("xT", tt // 4)])

        for l in layers:
            E = dict(locals())
            E['state'] = state
            if "attn" in cfg.get("parts", ("attn", "out", "ffn", "ple")):
                if l % 2 == 0:
                    even_attention(E, l)
                else:
                    odd_attention(E, l)
            token_local(E, l, cfg.get("parts", ("attn", "out", "ffn", "ple")))

        with Scope(P) as sc:
            xo = [sc.sb(f"xo{i}", [128, 16, 128], F32) for i in range(2)]
            xo_b = [Buf(), Buf()]
            ostg = [sc.sb(f"ostg{i}", [128, D], F32) for i in range(2)]
            ostg_b = [Buf(), Buf()]
            for tt in range(TO // 128):
                k = tt % 2
                P.dma("sp", xo[k][:], xT.ap()[:, tt * 128:(tt + 1) * 128].rearrange("(fc p) t -> p fc t", p=128),
                      reads=[db("xT", tt // 4)], writes=[xo_b[k]])
                for g in range(4):
                    bi = 5 + (g % 2)
                    for j in range(4):
                        fc = g * 4 + j
                        P.op("pe", lambda e: e.transpose(pb[bi][:, j * 128:(j + 1) * 128], xo[k][:, fc, :], ident_f),
                             reads=[xo_b[k], b_cst], writes=[pb_b[bi]])
                    P.op("act", lambda e: e.activation(out=ostg[k][:, g * 512:(g + 1) * 512], in_=pb[bi][:], func=AF.Copy),
                         reads=[], writes=[pb_b[bi], ostg_b[k]])
                P.dma("sp", out_d.ap()[tt * 128:(tt + 1) * 128, :], ostg[k][:], reads=[ostg_b[k]], writes=[db("out")])
        P.barrier()
    return nc


_NC_CACHE = {}


def make_in_maps(inp, batches):
    cst, oh = host_consts()
    sp = pack_small(inp)
    wmap = {n: np.ascontiguousarray(inp[n], dtype=np.float32) for n, _ in WEIGHTS}
    in_maps = []
    for c in range(8):
        b = batches[c]
        half = c // 4
        flg = np.zeros((128, 4), np.float32)
        flg[:, 0] = float(half)
        flg[:, 1] = 0.0 if half else -1e30
        flg[:, 2] = 0.0 if half else NEG
        m = dict(x=np.ascontiguousarray(inp["x"][b, half * TO:(half + 1) * TO], dtype=np.float32),
                 p=np.ascontiguousarray(inp["p"][:, b, half * TO:(half + 1) * TO], dtype=np.float32),
                 flg=flg, sp=sp, cst=cst, oh=oh)
        m.update(wmap)
        in_maps.append(m)
    return in_maps


def kernel(**inputs):
    inp = {k: np.asarray(v) for k, v in inputs.items()}
    if "nc" not in _NC_CACHE:
        _NC_CACHE["nc"] = build()
    nc = _NC_CACHE["nc"]
    in_maps = make_in_maps(inp, [0, 1, 2, 3, 0, 1, 2, 3])
    res = run_bass_kernel_spmd(nc, in_maps, core_ids=list(range(8)))
    out = np.stack([np.concatenate([res.results[b]["out"], res.results[b + 4]["out"]], axis=0) for b in range(4)], axis=0)
    return out.astype(np.float32)
```

```python
import math
from contextlib import ExitStack
import numpy as np
import concourse.bass as bass
import concourse.mybir as mybir
from concourse.bass_utils import run_bass_kernel_spmd

F32 = mybir.dt.float32
BF16 = mybir.dt.bfloat16
ALU = mybir.AluOpType
AF = mybir.ActivationFunctionType

EPOCH = 30000
NDSEM = 40

D = 2048
T = 2048
DEPTH = 4
TT = 1024
TO = 1024
TC = 512
DFF = 5504
NFC = 43
EPS = 1e-6
XA = 1280
XB = 384
NEG = -30000.0

SP_ATTN = 0
SP_FFN = 64
SP_PLE = 128
SP_CONV = 192
SP_CQ = 1224
SP_CKV = 1232
SP_AQ = 1236
SP_BQ = 1240
SP_BK = 1242
SP_CQN = 1244
SP_CKN = 1246
SP_FB = 1248
SP_SINK = 1250
SP_RELB = 1282
SP_B31 = 1320
NSP = 1340

WEIGHTS = [
    ("w_in_even", (2, 2048, 2128)), ("a_w_uq", (2, 512, 4096)), ("a_w_qidx", (2, 512, 1024)),
    ("a_w_uv", (2, 16, 256, 64)), ("w_out_even", (2, 2048, 2048)), ("w_in_odd", (2, 2048, 6176)),
    ("w_out_odd", (2, 2048, 2048)), ("w_up", (4, 2048, 11008)), ("w_down", (4, 5504, 2048)),
    ("w_ple_gate", (4, 2048, 2048)), ("w_ple_proj", (4, 256, 2048)),
]


class Buf:
    __slots__ = ("name", "w", "r", "rd")

    def __init__(self, name=""):
        self.name = name
        self.w = None
        self.r = {}
        self.rd = []


class Prog:
    ENGS = ("pe", "act", "dve", "pool", "sp")

    def __init__(self, nc, stack):
        self.nc = nc
        self.stack = stack
        self.eng = {"pe": nc.tensor, "act": nc.scalar, "dve": nc.vector,
                    "pool": nc.gpsimd, "sp": nc.sync}
        self.cnt = {e: 0 for e in self.ENGS}
        self.esems = {e: [] for e in self.ENGS}
        self.seen_e = {e: {p: 0 for p in self.ENGS} for e in self.ENGS}
        self.seen_d = {e: {} for e in self.ENGS}
        self.dsem = {}
        for q in ("sp", "pool"):
            self.dsem[q] = [[self._newsem(f"d{q}{i}"), 0] for i in range(NDSEM)]
        self.dptr = {"sp": 0, "pool": 0}
        self.bar_sem = self._newsem("bar")
        self.bar_cnt = 0
        self.n_inst = 0

    def _newsem(self, name):
        return self.stack.enter_context(self.nc.semaphore(name))

    def _esem(self, e, idx):
        ep = (idx - 1) // EPOCH
        while len(self.esems[e]) <= ep:
            self.esems[e].append(self._newsem(f"e{e}{len(self.esems[e])}"))
        return self.esems[e][ep], (idx - 1) % EPOCH + 1

    def _wait(self, e, ev):
        if ev is None:
            return
        if ev[0] == "e":
            _, p, idx = ev
            if p == e and e == "pe":
                return
            if self.seen_e[e][p] >= idx:
                return
            self.seen_e[e][p] = idx
            s, v = self._esem(p, idx)
            self.eng[e].wait_ge(s, v)
        else:
            _, s, v, key = ev
            if self.seen_d[e].get(key, 0) >= v:
                return
            self.seen_d[e][key] = v
            self.eng[e].wait_ge(s, v)
        self.n_inst += 1

    def _deps(self, e, reads, writes):
        for b in reads:
            self._wait(e, b.w)
        for b in writes:
            self._wait(e, b.w)
            for p, idx in b.r.items():
                if p != e:
                    self._wait(e, ("e", p, idx))
            for ev in b.rd:
                self._wait(e, ev)

    def _mark(self, ev, reads, writes):
        for b in reads:
            if ev[0] == "e":
                b.r[ev[1]] = ev[2]
            else:
                b.rd.append(ev)
        for b in writes:
            b.w = ev
            b.r = {}
            b.rd = []

    def op(self, e, fn, reads=(), writes=()):
        self._deps(e, reads, writes)
        inst = fn(self.eng[e])
        self.cnt[e] += 1
        idx = self.cnt[e]
        s, _ = self._esem(e, idx)
        inst.then_inc(s, 1)
        self.n_inst += 1
        self._mark(("e", e, idx), reads, writes)

    def dma(self, q, out, in_, reads=(), writes=(), **kw):
        self._deps(q, reads, writes)
        slot = self.dsem[q][self.dptr[q]]
        key = (q, self.dptr[q])
        self.dptr[q] = (self.dptr[q] + 1) % NDSEM
        if slot[1] > 0:
            self._wait(q, ("d", slot[0], slot[1], key))
        inst = self.eng[q].dma_start(out=out, in_=in_, **kw)
        slot[1] += 16
        inst.then_inc(slot[0], 16)
        self.n_inst += 1
        self._mark(("d", slot[0], slot[1], key), reads, writes)

    def barrier(self):
        for p in self.ENGS:
            if p != "sp" and self.cnt[p] > 0:
                self._wait("sp", ("e", p, self.cnt[p]))
        for q in ("sp", "pool"):
            for i, slot in enumerate(self.dsem[q]):
                if slot[1] > 0:
                    self._wait("sp", ("d", slot[0], slot[1], (q, i)))
        self.bar_cnt += 1
        self.eng["sp"].sem_inc(self.bar_sem, 1)
        for e in self.ENGS:
            if e != "sp":
                self.eng[e].wait_ge(self.bar_sem, self.bar_cnt)
                for p in self.ENGS:
                    self.seen_e[e][p] = self.cnt[p]
                for q in ("sp", "pool"):
                    for i, slot in enumerate(self.dsem[q]):
                        self.seen_d[e][(q, i)] = slot[1]
        self.n_inst += 6


class Scope:
    def __init__(self, P):
        self.P = P
        self.st = ExitStack()

    def __enter__(self):
        self.st.__enter__()
        return self

    _uid = [0]

    def sb(self, name, shape, dt):
        Scope._uid[0] += 1
        return self.st.enter_context(self.P.nc.sbuf_tensor(f"{name}_{Scope._uid[0]}", list(shape), dt))

    def __exit__(self, *a):
        self.P.barrier()
        return self.st.__exit__(*a)


def rel_bucket_np(n):
    n = np.maximum(n, 0)
    exact = 16
    nf = np.maximum(n, 1).astype(np.float32)
    large = exact + (np.log(nf / np.float32(exact)) / np.float32(math.log(1024 / exact))
                     * np.float32(32 - exact)).astype(np.int32)
    large = np.minimum(large, 31)
    return np.where(n < exact, n, large)


def host_consts():
    cst = np.zeros((128, 512), np.float32)
    cst[:, 0:128] = np.eye(128)
    cst[:, 128:256] = np.eye(128)[::-1]
    i = np.arange(128)
    cst[:, 256:384] = (i[None, :] >= i[:, None]).astype(np.float32)
    cst[:, 384:512] = np.where(i[None, :] <= i[:, None], 0.0, -1e30)
    oh = np.zeros((33, XA + XB), np.float32)
    y = np.arange(XA)
    xx = y - 127
    b = np.where(xx < 0, 32, rel_bucket_np(xx))
    oh[b, y] = 1.0
    y = np.arange(XB)
    xx = y - 127
    b = np.where((xx < 0) | (xx >= 128), 32, rel_bucket_np(xx))
    oh[b, XA + y] = 1.0
    return cst, oh


def pack_small(inp):
    sp = np.zeros((128, NSP), np.float32)

    def fm(v):
        return np.ascontiguousarray(v.reshape(-1, 128).T)
    for l in range(4):
        sp[:, SP_ATTN + l * 16:SP_ATTN + (l + 1) * 16] = fm(inp["attn_norm"][l])
        sp[:, SP_FFN + l * 16:SP_FFN + (l + 1) * 16] = fm(inp["ffn_norm"][l])
        sp[:, SP_PLE + l * 16:SP_PLE + (l + 1) * 16] = fm(inp["ple_norm"][l])
        for k in range(3):
            c0 = SP_CONV + (l * 3 + k) * 86
            sp[:, c0:c0 + 86] = fm(inp["ffn_conv"][l, k])
    for e in range(2):
        sp[:, SP_CQ + e * 4:SP_CQ + e * 4 + 4] = fm(inp["a_cq_norm"][e])
        sp[:, SP_CKV + e * 2:SP_CKV + e * 2 + 2] = fm(inp["a_ckv_norm"][e])
        sp[:, SP_AQ + e * 2:SP_AQ + e * 2 + 2] = fm(inp["a_q_norm"][e])
        sp[:, SP_BQ + e] = np.tile(inp["b_q_norm"][e], 2)
        sp[:, SP_BK + e] = np.tile(inp["b_k_norm"][e], 2)
        sp[:, SP_CQN + e] = np.tile(inp["c_q_norm"][e], 2)
        sp[:, SP_CKN + e] = np.tile(inp["c_k_norm"][e], 2)
        sp[0:32, SP_FB + e] = inp["c_forget_bias"][e]
        sp[:, SP_SINK + e * 16:SP_SINK + (e + 1) * 16] = inp["b_sinks"][e][None, :]
    sp[0:32, SP_RELB:SP_RELB + 32] = inp["rel_bias"]
    sp[:, SP_B31:SP_B31 + 16] = inp["rel_bias"][31, 0:16][None, :]
    return sp


def token_local(E, l, parts):
    P, nc, Wd, db, gemm, simple_blocks, rms_finish, norm_x, resid_epi = (E[k] for k in (
        "P", "nc", "Wd", "db", "gemm", "simple_blocks", "rms_finish", "norm_x", "resid_epi"))
    xT, yT, spk, b_spk, pb, pb_b, halo, b_halo, p_in, ident_f, b_cst, wbuf, wb_b, wptr, wview = (E[k] for k in (
        "xT", "yT", "spk", "b_spk", "pb", "pb_b", "halo", "b_halo", "p_in", "ident_f", "b_cst", "wbuf", "wb_b", "wptr", "wview"))
    NWS = len(wbuf)
    flg, b_flg, cc_gather, hx_in, hxg, ones_bf = (E[k] for k in ("flg", "b_flg", "cc_gather", "hx_in", "hxg", "ones_bf"))
    for ps in range(1):
        t0 = 0
        with Scope(P) as so:
            hT = so.sb("hT", [128, 16, TT], BF16)
            hT_b = Buf("hT")
            hh = so.sb("hh", [128, 16, 2], BF16)
            hh_b = Buf("hh")

            def norm_scope(gcol, with_halo=False):
                with Scope(P) as sn:
                    S = dict(xs=sn.sb("xs", [128, 16, TC], F32), xs_b=Buf(), sq=sn.sb("sq", [128, 16, TC], BF16),
                             sq_b=Buf(), rstd=sn.sb("rstd", [128, TC], F32), rstd_b=Buf())
                    norm_x(S, hT, hT_b, gcol, t0)
                    if with_halo:
                        hxo = sn.sb("hxo", [128, 16, 2], F32)
                        hxo_b = Buf("hxo")
                        P.dma("sp", hxo[:], hxg.ap()[0:128, :].rearrange("p (kc t) -> p kc t", t=2), reads=[db("hxg")], writes=[hxo_b])
                        rms_finish(S, lambda c: hxo[:, c, :], hxo_b, 16, ones_bf, gcol, 1.0 / D,
                                   lambda c: hh[:, c, :], [hh_b], ncols=2)

            def mk_xr(sc):
                return dict(xr=[sc.sb(f"xr{i}", [128, TC], F32) for i in range(2)], xr_b=[Buf(), Buf()], xr_i=[0])

            if "out" in parts:
                with Scope(P) as s1:
                    S = mk_xr(s1)
                    P.dma("sp", hT[:], yT.ap()[:, t0:t0 + TT].rearrange("(kc p) t -> p kc t", p=128),
                          reads=[db("yT", 0)], writes=[hT_b])
                    wn = "w_out_even" if l % 2 == 0 else "w_out_odd"
                    gemm(hT, hT_b, 16, lambda c0, n: Wd[wn].ap()[l // 2, :, c0:c0 + n],
                         simple_blocks(0, D, 512), resid_epi(S, t0))
            if "ffn" in parts:
                with Scope(P) as sh:
                    hxs = sh.sb("hxs", [128, 16, 2], F32)
                    hxs_b = Buf("hxs")
                    P.dma("sp", hxs[:], xT.ap()[:, TO - 2:TO].rearrange("(kc p) t -> p kc t", p=128), reads=[db("xT", 1)], writes=[hxs_b])
                    P.dma("sp", hx_in.ap().rearrange("p (kc t) -> p kc t", t=2), hxs[:], reads=[hxs_b], writes=[db("hx_in")])
                cc_gather(hx_in, hxg, [db("hx_in")], [db("hxg")])
                norm_scope(SP_FFN + l * 16, with_halo=True)
                with Scope(P) as s2:
                    S = mk_xr(s2)
                    act = s2.sb("act", [128, NFC, TT], BF16)
                    act_b = Buf("act")
                    stg = {"g": s2.sb("sg", [128, TT + 2], F32), "u": s2.sb("su", [128, TT + 2], F32)}
                    stg_b = {"g": Buf("sg"), "u": Buf("su")}
                    cv = {"g": s2.sb("ga", [128, TT], F32), "u": s2.sb("ua", [128, TT], F32)}
                    cv_b = {"g": Buf("ga"), "u": Buf("ua")}

                    def up_pre(wv, wvb, coff, m, tag):
                        kind = tag[0]
                        for kc in range(16):
                            P.op("pe", lambda e: e.matmul(pb[6][:, 0:2], lhsT=wv[:, kc, coff:coff + m], rhs=hh[:, kc, :],
                                                          start=(kc == 0), stop=(kc == 15)),
                                 reads=[wvb, hh_b], writes=[pb_b[6]])
                        P.op("act", lambda e: e.activation(out=stg[kind][:, 0:2], in_=pb[6][:, 0:2], func=AF.Copy, scale=flg[:, 0:1]),
                             reads=[b_flg], writes=[pb_b[6], stg_b[kind]])

                    def conv_finish(kind, i):
                        c = i if kind == "g" else NFC + i
                        s_, sb_, a_, ab_ = stg[kind], stg_b[kind], cv[kind], cv_b[kind]
                        wc = [SP_CONV + (l * 3 + k) * 86 + c for k in range(3)]
                        P.op("act", lambda e: e.activation(out=a_[:], in_=s_[:, 2:TT + 2], func=AF.Copy,
                                                           scale=spk[:, wc[2]:wc[2] + 1]),
                             reads=[sb_, b_spk], writes=[ab_])
                        P.op("dve", lambda e: e.scalar_tensor_tensor(out=a_[:], in0=s_[:, 1:TT + 1], scalar=spk[:, wc[1]:wc[1] + 1],
                                                                     in1=a_[:], op0=ALU.mult, op1=ALU.add),
                             reads=[sb_, ab_, b_spk], writes=[ab_])
                        P.op("dve", lambda e: e.scalar_tensor_tensor(out=a_[:], in0=s_[:, 0:TT], scalar=spk[:, wc[0]:wc[0] + 1],
                                                                     in1=a_[:], op0=ALU.mult, op1=ALU.add),
                             reads=[sb_, ab_, b_spk], writes=[ab_])

                    def up_epi(bi, m, tag, tci):
                        kind, i = tag
                        c = i if kind == "g" else NFC + i
                        P.op("act", lambda e: e.activation(out=stg[kind][:, 2 + tci * TC:2 + (tci + 1) * TC], in_=pb[bi][:], func=AF.Copy),
                             reads=[], writes=[pb_b[bi], stg_b[kind]])
                        if tci == TT // TC - 1:
                            conv_finish(kind, i)
                            if kind == "u":
                                P.op("act", lambda e: e.activation(out=cv["g"][:], in_=cv["g"][:], func=AF.Silu),
                                     reads=[cv_b["g"]], writes=[cv_b["g"]])
                                P.op("dve", lambda e: e.tensor_tensor(out=act[:, i, :], in0=cv["g"][:], in1=cv["u"][:], op=ALU.mult),
                                     reads=[cv_b["g"], cv_b["u"]], writes=[act_b])
                    blocks = []
                    for i0 in range(0, NFC, 2):
                        npair = min(2, NFC - i0)
                        w = npair * 128
                        segs = [(i0 * 128, w), (DFF + i0 * 128, w)]
                        chunks = []
                        for j in range(npair):
                            chunks.append((j * 128, 128, ("g", i0 + j), 0))
                            chunks.append((w + j * 128, 128, ("u", i0 + j), 0))
                        blocks.append((segs, chunks))
                    gemm(hT, hT_b, 16, lambda c0, n: Wd["w_up"].ap()[l, :, c0:c0 + n], blocks, up_epi, pre=up_pre)
                    gemm(act, act_b, NFC, lambda c0, n: Wd["w_down"].ap()[l, :, c0:c0 + n],
                         simple_blocks(0, D, 256), resid_epi(S, t0))
            if "ple" in parts:
                norm_scope(SP_PLE + l * 16)
                with Scope(P) as s3:
                    S = mk_xr(s3)
                    pT = s3.sb("pT", [128, 2, TT], BF16)
                    pT_b = Buf("pT")
                    pl = [s3.sb(f"pl{i}", [128, 256], F32) for i in range(2)]
                    pl_b = [Buf(), Buf()]
                    sg = [s3.sb(f"sgt{i}", [128, TC], F32) for i in range(2)]
                    sg_b = [Buf(), Buf()]
                    for tt in range(TT // 128):
                        k = tt % 2
                        P.dma("sp", pl[k][:], p_in.ap()[l, t0 + tt * 128:t0 + (tt + 1) * 128, :], writes=[pl_b[k]])
                        for cc in range(2):
                            P.op("pe", lambda e: e.transpose(pb[5][:, cc * 128:(cc + 1) * 128], pl[k][:, cc * 128:(cc + 1) * 128], ident_f),
                                 reads=[pl_b[k], b_cst], writes=[pb_b[5]])
                        P.op("act", lambda e: e.activation(out=pT[:, :, tt * 128:(tt + 1) * 128],
                                                           in_=pb[5][:, 0:256].rearrange("p (a b) -> p a b", b=128), func=AF.Copy),
                             reads=[], writes=[pb_b[5], pT_b])
                    cnt = 0
                    for nb in range(D // 512):
                        sa = wptr[0]
                        sbb = (wptr[0] + 1) % NWS
                        wa = wview(sa, 16, 512)
                        wp = wview(sbb, 2, 512)
                        P.dma("pool", wa, Wd["w_ple_gate"].ap()[l, :, nb * 512:(nb + 1) * 512].rearrange("(kc p) n -> p kc n", p=128),
                              writes=[wb_b[sa]])
                        P.dma("pool", wp, Wd["w_ple_proj"].ap()[l, :, nb * 512:(nb + 1) * 512].rearrange("(kc p) n -> p kc n", p=128),
                              writes=[wb_b[sbb]])
                        for ci in range(4):
                            nchunk = nb * 4 + ci
                            for tci in range(TT // TC):
                                ba = cnt % 2
                                bb = 2 + cnt % 2
                                kx = cnt % 2
                                cnt += 1
                                ta = t0 + tci * TC
                                for kc in range(16):
                                    P.op("pe", lambda e: e.matmul(pb[ba][:], lhsT=wa[:, kc, ci * 128:(ci + 1) * 128],
                                                                  rhs=hT[:, kc, tci * TC:(tci + 1) * TC], start=(kc == 0), stop=(kc == 15)),
                                         reads=[wb_b[sa], hT_b], writes=[pb_b[ba]])
                                for kc in range(2):
                                    P.op("pe", lambda e: e.matmul(pb[bb][:], lhsT=wp[:, kc, ci * 128:(ci + 1) * 128],
                                                                  rhs=pT[:, kc, tci * TC:(tci + 1) * TC], start=(kc == 0), stop=(kc == 1)),
                                         reads=[wb_b[sbb], pT_b], writes=[pb_b[bb]])
                                P.op("act", lambda e: e.activation(out=sg[kx][:], in_=pb[ba][:], func=AF.Sigmoid),
                                     reads=[], writes=[pb_b[ba], sg_b[kx]])
                                P.op("dve", lambda e: e.tensor_tensor(out=sg[kx][:], in0=sg[kx][:], in1=pb[bb][:], op=ALU.mult),
                                     reads=[sg_b[kx]], writes=[pb_b[bb], sg_b[kx]])
                                xr, xr_b = S["xr"][kx], S["xr_b"][kx]
                                P.dma("sp", xr[:], xT.ap()[nchunk * 128:(nchunk + 1) * 128, ta:ta + TC],
                                      reads=[db("xT", ta // TC)], writes=[xr_b])
                                P.op("dve", lambda e: e.tensor_tensor(out=xr[:], in0=sg[kx][:], in1=xr[:], op=ALU.add),
                                     reads=[sg_b[kx], xr_b], writes=[xr_b])
                                P.dma("sp", xT.ap()[nchunk * 128:(nchunk + 1) * 128, ta:ta + TC], xr[:],
                                      reads=[xr_b], writes=[db("xT", ta // TC)])
                        wptr[0] = (wptr[0] + 2) % NWS


def even_attention(E, l):
    e = l // 2
    P, nc, Wd, db, gemm, simple_blocks, norm_x = (E[k] for k in ("P", "nc", "Wd", "db", "gemm", "simple_blocks", "norm_x"))
    spk, b_spk, pb, pb_b, pbh, pbh_b, ident_f, ident_bf, b_cst, b_const, ones_bf, bd_bf, jflip, cneg, eps_t = (E[k] for k in (
        "spk", "b_spk", "pb", "pb_b", "pbh", "pbh_b", "ident_f", "ident_bf", "b_cst", "b_const", "ones_bf", "bd_bf", "jflip", "cneg", "eps_t"))
    xT, yT, oh_in = E["xT"], E["yT"], E["oh_in"]
    s_kvT, s_kvtok, s_kidxT, s_widx, s_qiT, s_qaT, s_qbT, s_kdupT, s_vbtok, s_vrow = (E[k] for k in (
        "s_kvT", "s_kvtok", "s_kidxT", "s_widx", "s_qiT", "s_qaT", "s_qbT", "s_kdupT", "s_vbtok", "s_vrow"))
    XT = XA + XB
    flg, b_flg, cc_gather, kpack_e, gk_e = (E[k] for k in ("flg", "b_flg", "cc_gather", "kpack_e", "gk_e"))

    if not E["state"].get("vrow"):
        E["state"]["vrow"] = True
        with Scope(P) as sv:
            ohs = sv.sb("ohs", [33, XT], F32)
            ohs_b = Buf()
            vr = sv.sb("vr", [16, XT], F32)
            vr_b = Buf()
            P.dma("sp", ohs[:], oh_in.ap(), writes=[ohs_b])
            for (hc, x0, x1) in [(0, 0, 512), (0, 512, 1024), (0, 1024, XA), (16, XA, XT)]:
                P.op("pe", lambda en: en.matmul(pb[5][0:16, 0:x1 - x0], lhsT=spk[0:33, SP_RELB + hc:SP_RELB + hc + 16],
                                                rhs=ohs[:, x0:x1], start=True, stop=True),
                     reads=[ohs_b, b_spk], writes=[pb_b[5]])
                P.op("act", lambda en: en.activation(out=vr[:, x0:x1], in_=pb[5][0:16, 0:x1 - x0], func=AF.Copy),
                     reads=[], writes=[pb_b[5], vr_b])
            P.dma("sp", s_vrow.ap(), vr[:], reads=[vr_b], writes=[db("vrow")])

    def rms_grp(S, srcs, src_b, lhsT_ones, gcol, inv_n, dsts, dst_bufs, ncols=TC):
        C = len(srcs)
        sq, sq_b, rstd, rstd_b = S["sq"], S["sq_b"], S["rstd"], S["rstd_b"]
        for c in range(C):
            P.op("act", lambda en: en.activation(out=sq[:, c, 0:ncols], in_=srcs[c], func=AF.Square),
                 reads=[src_b], writes=[sq_b])
        for c in range(C):
            P.op("pe", lambda en: en.matmul(pb[4][:, 0:ncols], lhsT=lhsT_ones[:, :], rhs=sq[:, c, 0:ncols],
                                            start=(c == 0), stop=(c == C - 1)),
                 reads=[sq_b, b_const], writes=[pb_b[4]])
        P.op("act", lambda en: en.activation(out=rstd[:, 0:ncols], in_=pb[4][:, 0:ncols], func=AF.Sqrt,
                                             scale=inv_n, bias=eps_t[:, 0:1]),
             reads=[b_const], writes=[pb_b[4], rstd_b])
        P.op("dve", lambda en: en.reciprocal(out=rstd[:, 0:ncols], in_=rstd[:, 0:ncols]), reads=[rstd_b], writes=[rstd_b])
        for c in range(C):
            P.op("dve", lambda en: en.scalar_tensor_tensor(out=dsts[c], in0=srcs[c], scalar=spk[:, gcol + c:gcol + c + 1],
                                                           in1=rstd[:, 0:ncols], op0=ALU.mult, op1=ALU.mult),
                 reads=[src_b, rstd_b, b_spk], writes=dst_bufs)
    E["rms_grp"] = rms_grp

    for ps in range(0 if E["cfg"].get("skip_proj") else 1):
        t0 = 0
        with Scope(P) as so:
            hT = so.sb("hT", [128, 16, TT], BF16)
            hT_b = Buf("hT")
            with Scope(P) as sn:
                S0 = dict(xs=sn.sb("xs", [128, 16, TC], F32), xs_b=Buf(), sq=sn.sb("sq", [128, 16, TC], BF16),
                          sq_b=Buf(), rstd=sn.sb("rstd", [128, TC], F32), rstd_b=Buf())
                norm_x(S0, hT, hT_b, SP_ATTN + l * 16, t0)
            S = dict(sq=so.sb("sq", [128, 4, TC], BF16), sq_b=Buf(), rstd=so.sb("rstd", [128, TC], F32), rstd_b=Buf())
            stg4 = so.sb("stg4", [128, 4, TT], F32)
            stg4_b = Buf("stg4")
            stg1 = so.sb("stg1", [128, TT], F32)
            stg1_b = Buf("stg1")
            stg2 = so.sb("stg2", [128, 2, TT], F32)
            stg2_b = Buf("stg2")
            cqT = so.sb("cqT", [128, 4, TT], BF16)
            cqT_b = Buf("cqT")
            kvn = so.sb("kvn", [128, 2, TT], BF16)
            kvn_b = Buf("kvn")
            kvtok_st = so.sb("kvtok_st", [128, 8, 256], BF16)
            kvtok_b = Buf()
            kidx_st = so.sb("kidx_st", [64, TT], BF16)
            kidx_b = Buf()
            widx_st = so.sb("widx_st", [16, TT], F32)
            widx_b = Buf()
            widx_tok = so.sb("widx_tok", [128, 8, 16], F32)
            widx_tok_b = Buf()
            ob = so.sb("ob", [128, TT], BF16)
            ob_b = Buf("ob")
            ob2 = so.sb("ob2", [128, 2, TT], BF16)
            ob2_b = Buf("ob2")
            oq = [so.sb(f"oq{i}", [128, TC], BF16) for i in range(2)]
            oq_b = [Buf(), Buf()]
            oq_i = [0]
            vb_st = so.sb("vb_st", [128, TT], BF16)
            vb_b = Buf()
            vtok_st = so.sb("vtok_st", [128, 8, 128], BF16)
            vtok_b = Buf()

            def tsl(tci):
                return slice(tci * TC, (tci + 1) * TC)

            def in_epi(bi, m, tag, tci):
                kind = tag[0]
                last = (tci == TT // TC - 1)
                if kind == "cq":
                    c = tag[1]
                    P.op("act", lambda en: en.activation(out=stg4[:, c, tsl(tci)], in_=pb[bi][:], func=AF.Copy),
                         reads=[], writes=[pb_b[bi], stg4_b])
                    if c == 3 and last:
                        for t2 in range(TT // TC):
                            rms_grp(S, [stg4[:, cc, tsl(t2)] for cc in range(4)], stg4_b, ones_bf, SP_CQ + e * 4, 1.0 / 512,
                                    [cqT[:, cc, tsl(t2)] for cc in range(4)], [cqT_b])
                elif kind == "ckv":
                    c = tag[1]
                    P.op("act", lambda en: en.activation(out=stg2[:, c, tsl(tci)], in_=pb[bi][:], func=AF.Copy),
                         reads=[], writes=[pb_b[bi], stg2_b])
                    if c == 1 and last:
                        for t2 in range(TT // TC):
                            rms_grp(S, [stg2[:, cc, tsl(t2)] for cc in range(2)], stg2_b, ones_bf, SP_CKV + e * 2, 1.0 / 256,
                                    [kvn[:, cc, tsl(t2)] for cc in range(2)], [kvn_b])
                        P.dma("sp", s_kvT.ap()[:, TO:TO + TT].rearrange("(c p) t -> p c t", p=128), kvn[:],
                              reads=[kvn_b], writes=[db("kvT")])
                        P.dma("sp", kpack_e.ap()[0:256, :].rearrange("(c p) t -> p c t", p=128), kvn[:],
                              reads=[kvn_b], writes=[db("kpack_e")])
                        for tt in range(TT // 128):
                            for cc in range(2):
                                P.op("pe", lambda en: en.transpose(pbh[:, cc * 128:(cc + 1) * 128], kvn[:, cc, tt * 128:(tt + 1) * 128], ident_bf[:]),
                                     reads=[kvn_b, b_const], writes=[pbh_b])
                            P.op("act", lambda en: en.activation(out=kvtok_st[:, tt, :], in_=pbh[:, 0:256], func=AF.Copy),
                                 reads=[], writes=[pbh_b, kvtok_b])
                        P.dma("sp", s_kvtok.ap()[TO:TO + TT, :].rearrange("(tt p) c -> p tt c", p=128), kvtok_st[:],
                              reads=[kvtok_b], writes=[db("kvtok")])
                        P.dma("sp", kpack_e.ap()[576:832, :].rearrange("r (a c) -> (r a) c", c=256).rearrange("(tt p) c -> p tt c", p=128), kvtok_st[:],
                              reads=[kvtok_b], writes=[db("kpack_e")])
                elif kind == "kidx":
                    P.op("act", lambda en: en.activation(out=kidx_st[:, tsl(tci)], in_=pb[bi][0:64, :], func=AF.Copy),
                         reads=[], writes=[pb_b[bi], kidx_b])
                    if last:
                        P.dma("sp", s_kidxT.ap()[:, TO:TO + TT], kidx_st[:], reads=[kidx_b], writes=[db("kidxT")])
                        P.dma("sp", kpack_e.ap()[256:320, :], kidx_st[:], reads=[kidx_b], writes=[db("kpack_e")])
                elif kind == "widx":
                    P.op("act", lambda en: en.activation(out=widx_st[:, tsl(tci)], in_=pb[bi][0:16, :], func=AF.Copy),
                         reads=[], writes=[pb_b[bi], widx_b])
                    if last:
                        for tt in range(TT // 128):
                            P.op("pe", lambda en: en.transpose(pb[5][:, tt * 16:(tt + 1) * 16], widx_st[0:16, tt * 128:(tt + 1) * 128], ident_f[0:16, 0:16]),
                                 reads=[widx_b, b_cst], writes=[pb_b[5]])
                        P.op("act", lambda en: en.activation(out=widx_tok[:], in_=pb[5][:, 0:128].rearrange("p (a b) -> p a b", b=16), func=AF.Copy),
                             reads=[], writes=[pb_b[5], widx_tok_b])
                        P.dma("sp", s_widx.ap()[t0:t0 + TT, :].rearrange("(tt p) c -> p tt c", p=128), widx_tok[:],
                              reads=[widx_tok_b], writes=[db("widx")])
                elif kind == "qb":
                    c = tag[1]
                    P.op("act", lambda en: en.activation(out=stg1[:, tsl(tci)], in_=pb[bi][:], func=AF.Copy),
                         reads=[], writes=[pb_b[bi], stg1_b])
                    if last:
                        for t2 in range(TT // TC):
                            rms_grp(S, [stg1[:, tsl(t2)]], stg1_b, bd_bf, SP_BQ + e, 1.0 / 64, [ob[:, tsl(t2)]], [ob_b])
                        P.dma("sp", s_qbT.ap()[c * 128:(c + 1) * 128, t0:t0 + TT], ob[:], reads=[ob_b], writes=[db("qbT")])
                elif kind == "kb":
                    g, half = tag[1], tag[2]
                    pbs = half * 64
                    P.op("act", lambda en: en.activation(out=stg1[pbs:pbs + 64, tsl(tci)], in_=pb[bi][pbs:pbs + 64, :], func=AF.Copy),
                         reads=[], writes=[pb_b[bi], stg1_b])
                    if half == 1 and last:
                        for t2 in range(TT // TC):
                            rms_grp(S, [stg1[:, tsl(t2)]], stg1_b, bd_bf, SP_BK + e, 1.0 / 64, [ob[:, tsl(t2)]], [ob_b])
                        P.dma("sp", s_kdupT.ap()[g * 128:(g + 1) * 128, TO:TO + TT], ob[:], reads=[ob_b], writes=[db("kdupT")])
                        P.dma("sp", kpack_e.ap()[320 + g * 128:320 + (g + 1) * 128, :], ob[:], reads=[ob_b], writes=[db("kpack_e")])
                elif kind == "vb":
                    P.op("act", lambda en: en.activation(out=vb_st[:, tsl(tci)], in_=pb[bi][:], func=AF.Copy),
                         reads=[], writes=[pb_b[bi], vb_b])
                    if last:
                        for tt in range(TT // 128):
                            P.op("pe", lambda en: en.transpose(pbh[:, (tt % 4) * 128:(tt % 4 + 1) * 128], vb_st[:, tt * 128:(tt + 1) * 128], ident_bf[:]),
                                 reads=[vb_b, b_const], writes=[pbh_b])
                            if tt % 4 == 3:
                                P.op("act", lambda en: en.activation(out=vtok_st[:, tt - 3:tt + 1, :], in_=pbh[:, 0:512].rearrange("p (a b) -> p a b", b=128), func=AF.Copy),
                                     reads=[], writes=[pbh_b, vtok_b])
                        P.dma("sp", s_vbtok.ap()[TO:TO + TT, :].rearrange("(tt p) c -> p tt c", p=128), vtok_st[:],
                              reads=[vtok_b], writes=[db("vbtok")])
                        P.dma("sp", kpack_e.ap()[832:960, :].rearrange("r (a c) -> (r a) c", c=128).rearrange("(tt p) c -> p tt c", p=128), vtok_st[:],
                              reads=[vtok_b], writes=[db("kpack_e")])
                elif kind == "qa":
                    h, cc = tag[1], tag[2]
                    P.op("act", lambda en: en.activation(out=stg2[:, cc, tsl(tci)], in_=pb[bi][:], func=AF.Copy),
                         reads=[], writes=[pb_b[bi], stg2_b])
                    if cc == 1 and last:
                        for t2 in range(TT // TC):
                            rms_grp(S, [stg2[:, c2, tsl(t2)] for c2 in range(2)], stg2_b, ones_bf, SP_AQ + e * 2, 1.0 / 256,
                                    [ob2[:, c2, tsl(t2)] for c2 in range(2)], [ob2_b])
                        P.dma("sp", s_qaT.ap()[h * 256:(h + 1) * 256, t0:t0 + TT].rearrange("(c p) t -> p c t", p=128), ob2[:],
                              reads=[ob2_b], writes=[db("qaT")])
                elif kind == "qi":
                    c = tag[1]
                    k = oq_i[0]
                    oq_i[0] = (k + 1) % 2
                    P.op("act", lambda en: en.activation(out=oq[k][:], in_=pb[bi][:], func=AF.Copy),
                         reads=[], writes=[pb_b[bi], oq_b[k]])
                    P.dma("sp", s_qiT.ap()[c * 128:(c + 1) * 128, t0 + tci * TC:t0 + (tci + 1) * TC], oq[k][:],
                          reads=[oq_b[k]], writes=[db("qiT")])

            blocks = [
                ([(0, 512)], [(c * 128, 128, ("cq", c), 0) for c in range(4)]),
                ([(512, 336)], [(0, 128, ("ckv", 0), 0), (128, 128, ("ckv", 1), 0), (256, 64, ("kidx",), 0), (320, 16, ("widx",), 0)]),
                ([(848, 512)], [(c * 128, 128, ("qb", c), 0) for c in range(4)]),
                ([(1360, 512)], [(c * 128, 128, ("qb", 4 + c), 0) for c in range(4)]),
                ([(1872, 256)], [(0, 64, ("kb", 0, 0), 0), (0, 64, ("kb", 0, 1), 64), (64, 64, ("kb", 1, 0), 0), (64, 64, ("kb", 1, 1), 64),
                                 (128, 128, ("vb",), 0)]),
            ]
            gemm(hT, hT_b, 16, lambda c0, n: Wd["w_in_even"].ap()[e, :, c0:c0 + n], blocks, in_epi)
            blocks = []
            for b4 in range(2):
                blocks.append(([(b4 * 2048, 2048)], [((hh * 2 + cc) * 128, 128, ("qa", b4 * 8 + hh, cc), 0) for hh in range(8) for cc in range(2)]))
            gemm(cqT, cqT_b, 4, lambda c0, n: Wd["a_w_uq"].ap()[e, :, c0:c0 + n], blocks, in_epi)
            gemm(cqT, cqT_b, 4, lambda c0, n: Wd["a_w_qidx"].ap()[e, :, c0:c0 + n],
                 [([(0, 1024)], [(c * 128, 128, ("qi", c), 0) for c in range(8)])], in_epi)

    if not E["cfg"].get("skip_proj"):
        cc_gather(kpack_e, gk_e, [db("kpack_e")], [db("gk_e")])
        P.dma("sp", s_kvT.ap()[:, 0:TO], gk_e.ap()[0:256, :], reads=[db("gk_e")], writes=[db("kvT")])
        P.dma("sp", s_kidxT.ap()[:, 0:TO], gk_e.ap()[256:320, :], reads=[db("gk_e")], writes=[db("kidxT")])
        P.dma("sp", s_kdupT.ap()[:, 0:TO], gk_e.ap()[320:576, :], reads=[db("gk_e")], writes=[db("kdupT")])
        P.dma("sp", s_kvtok.ap()[0:TO, :], gk_e.ap()[576:832, :].rearrange("r (a c) -> (r a) c", c=256), reads=[db("gk_e")], writes=[db("kvtok")])
        P.dma("sp", s_vbtok.ap()[0:TO, :], gk_e.ap()[832:960, :].rearrange("r (a c) -> (r a) c", c=128), reads=[db("gk_e")], writes=[db("vbtok")])
    if E["cfg"].get("stop_after_proj"):
        return
    att_scale = 1.0 / 16.0
    with Scope(P) as sa:
        kvT = sa.sb("kvT", [128, 2, T], BF16)
        kvtok = sa.sb("kvtok", [128, 16, 256], BF16)
        kidx2 = sa.sb("kidx2", [128, T], BF16)
        wuv = sa.sb("wuv", [128, 16, 2, 64], BF16)
        expA = sa.sb("expA", [128, 16, 9, 128], BF16)
        b_k = Buf("kside")
        b_exp = Buf("expA")
        P.dma("sp", kvT[:], s_kvT.ap().rearrange("(c p) t -> p c t", p=128), reads=[db("kvT")], writes=[b_k])
        P.dma("sp", kvtok[:], s_kvtok.ap().rearrange("(tt p) c -> p tt c", p=128), reads=[db("kvtok")], writes=[b_k])
        P.dma("sp", kidx2[0:64, :], s_kidxT.ap(), reads=[db("kidxT")], writes=[b_k])
        P.dma("sp", kidx2[64:128, :], s_kidxT.ap(), reads=[db("kidxT")], writes=[b_k])
        P.dma("pool", wuv[:], Wd["a_w_uv"].ap()[e].rearrange("h (cc p) d -> p h cc d", p=128), writes=[b_k])
        hk = [sa.sb(f"hk{i}", [128, 128], F32) for i in range(2)]
        hk_b = [Buf(), Buf()]
        n = 0
        for h in range(16):
            for dj in range(9):
                k = n % 2
                n += 1
                P.dma("sp", hk[k][:], bass.AP(s_vrow, h * XT + dj * 128, [[1, 128], [1, 128]]), reads=[db("vrow")], writes=[hk_b[k]])
                P.op("pe", lambda en: en.matmul(pb[5][:, 0:128], lhsT=jflip, rhs=hk[k][:], start=True, stop=True),
                     reads=[hk_b[k], b_cst], writes=[pb_b[5]])
                P.op("act", lambda en: en.activation(out=expA[:, h, 8 - dj, :], in_=pb[5][:, 0:128], func=AF.Exp),
                     reads=[], writes=[pb_b[5], b_exp])
        qi = [sa.sb(f"qi{i}", [128, 8, 128], BF16) for i in range(2)]
        qa = [sa.sb(f"qa{i}", [128, 32, 128], BF16) for i in range(2)]
        wq = [sa.sb(f"wq{i}", [128, 16], F32) for i in range(2)]
        q_b = [Buf(), Buf()]
        score = sa.sb("score", [128, T], F32)
        score_b = Buf("score")
        work = sa.sb("work", [128, T], F32)
        work_b = Buf("work")
        m8 = sa.sb("m8", [128, 8], F32)
        m8_b = Buf("m8")
        mask01 = sa.sb("mask01", [128, T], BF16)
        mask01_b = Buf()
        maskT = sa.sb("maskT", [128, 16, 128], BF16)
        maskT_b = Buf()
        rl = [sa.sb(f"rl{i}", [128, 512], F32) for i in range(2)]
        rl_b = [Buf(), Buf()]
        NBD = 3
        LBD = [0, 1, 4]
        pf = [sa.sb(f"pf{i}", [128, 512], F32) for i in range(NBD)]
        pf_b = [Buf() for _ in range(NBD)]
        pbf = [sa.sb(f"pbf{i}", [128, 512], BF16) for i in range(NBD)]
        pbf_b = [Buf() for _ in range(NBD)]
        rc = sa.sb("rc", [128, 128], F32)
        rc_b = Buf()
        on = sa.sb("on", [128, 2, 128], BF16)
        on_b = Buf()
        ya_st = [sa.sb(f"ya_st{i}", [128, 8, 128], BF16) for i in range(2)]
        ya_b = [Buf(), Buf()]
        cnt = [0, 0]
        for a_ in range(E["cfg"].get("dsa_tiles", 8)):
            i = 8 + a_
            qk = i % 2
            N = (i + 1) * 128
            qs = slice(a_ * 128, (a_ + 1) * 128)
            ks = slice(i * 128, (i + 1) * 128)
            P.dma("sp", qi[qk][:], s_qiT.ap()[:, qs].rearrange("(c p) t -> p c t", p=128), reads=[db("qiT")], writes=[q_b[qk]])
            P.dma("sp", qa[qk][:], s_qaT.ap()[:, qs].rearrange("(c p) t -> p c t", p=128), reads=[db("qaT")], writes=[q_b[qk]])
            P.dma("sp", wq[qk][:], s_widx.ap()[qs, :], reads=[db("widx")], writes=[q_b[qk]])
            for h in range(16):
                pbs = (h % 2) * 64
                for n0 in range(0, N, 512):
                    n1 = min(N, n0 + 512)
                    bk = cnt[0] % 2
                    cnt[0] += 1
                    P.op("pe", lambda en: en.matmul(pb[bk][:, 0:n1 - n0], lhsT=qi[qk][pbs:pbs + 64, h // 2, :], rhs=kidx2[pbs:pbs + 64, n0:n1],
                                                    start=True, stop=True),
                         reads=[q_b[qk], b_k], writes=[pb_b[bk]])
                    P.op("act", lambda en: en.activation(out=rl[bk][:, 0:n1 - n0], in_=pb[bk][:, 0:n1 - n0], func=AF.Relu),
                         reads=[], writes=[pb_b[bk], rl_b[bk]])
                    if h == 0:
                        P.op("dve", lambda en: en.tensor_scalar(out=score[:, n0:n1], in0=rl[bk][:, 0:n1 - n0], scalar1=wq[qk][:, 0:1], scalar2=None, op0=ALU.mult),
                             reads=[rl_b[bk], q_b[qk]], writes=[score_b])
                    else:
                        P.op("dve", lambda en: en.scalar_tensor_tensor(out=score[:, n0:n1], in0=rl[bk][:, 0:n1 - n0], scalar=wq[qk][:, h:h + 1],
                                                                       in1=score[:, n0:n1], op0=ALU.mult, op1=ALU.add),
                             reads=[rl_b[bk], q_b[qk], score_b], writes=[score_b])
            P.op("dve", lambda en: en.tensor_tensor(out=score[:, ks], in0=score[:, ks], in1=cneg, op=ALU.add),
                 reads=[score_b, b_cst], writes=[score_b])
            P.op("dve", lambda en: en.tensor_scalar(out=score[:, 0:TO], in0=score[:, 0:TO], scalar1=flg[:, 1:2], scalar2=None, op0=ALU.add),
                 reads=[score_b, b_flg], writes=[score_b])
            if i >= 2:
                cur, cur_b = score, score_b
                for it in range(32):
                    P.op("dve", lambda en: en.max(out=m8[:], in_=cur[:, 0:N]), reads=[cur_b], writes=[m8_b])
                    if it < 31:
                        P.op("dve", lambda en: en.match_replace(out=work[:, 0:N], in_to_replace=m8[:], in_values=cur[:, 0:N], imm_value=-1e30),
                             reads=[m8_b, cur_b], writes=[work_b])
                        cur, cur_b = work, work_b
                P.op("dve", lambda en: en.tensor_scalar(out=work[:, 0:N], in0=score[:, 0:N], scalar1=m8[:, 7:8], scalar2=None, op0=ALU.is_ge),
                     reads=[score_b, m8_b], writes=[work_b])
                P.op("dve", lambda en: en.scalar_tensor_tensor(out=mask01[:, 0:N], in0=score[:, 0:N], scalar=-1e29, in1=work[:, 0:N],
                                                               op0=ALU.is_gt, op1=ALU.mult),
                     reads=[score_b, work_b], writes=[mask01_b])
            else:
                P.op("dve", lambda en: en.tensor_scalar(out=mask01[:, 0:N], in0=score[:, 0:N], scalar1=-1e29, scalar2=None, op0=ALU.is_ge),
                     reads=[score_b], writes=[mask01_b])
            for j0 in range(0, i + 1, 8):
                j1 = min(i + 1, j0 + 8)
                for j in range(j0, j1):
                    P.op("pe", lambda en: en.transpose(pbh[:, (j - j0) * 128:(j - j0 + 1) * 128], mask01[:, j * 128:(j + 1) * 128], ident_bf[:]),
                         reads=[mask01_b, b_const], writes=[pbh_b])
                P.op("act", lambda en: en.activation(out=maskT[:, j0:j1, :], in_=pbh[:, 0:(j1 - j0) * 128].rearrange("p (a b) -> p a b", b=128), func=AF.Copy),
                     reads=[], writes=[pbh_b, maskT_b])
            yk = i % 2
            items = [(h, jg) for h in range(16) for jg in range(0, i + 1, 4)]

            def emit_logits(k):
                h, jg = items[k]
                je = min(i + 1, jg + 4)
                L = k % NBD
                BK = LBD[L]
                for j in range(jg, je):
                    sl = j - jg
                    for c in range(2):
                        P.op("pe", lambda en: en.matmul(pb[BK][:, sl * 128:(sl + 1) * 128], lhsT=kvT[:, c, j * 128:(j + 1) * 128],
                                                        rhs=qa[qk][:, 2 * h + c, :], start=(c == 0), stop=(c == 1)),
                             reads=[b_k, q_b[qk]], writes=[pb_b[BK]])

            def emit_post(k):
                h, jg = items[k]
                je = min(i + 1, jg + 4)
                nj = je - jg
                L = k % NBD
                BK = LBD[L]
                far = (i - (je - 1)) >= 8
                if far:
                    P.op("act", lambda en: en.activation(out=pf[L][:, 0:nj * 128], in_=pb[BK][:, 0:nj * 128], func=AF.Exp, scale=att_scale,
                                                         bias=spk[:, SP_B31 + h:SP_B31 + h + 1]),
                         reads=[b_spk], writes=[pb_b[BK], pf_b[L]])
                else:
                    P.op("act", lambda en: en.activation(out=pf[L][:, 0:nj * 128], in_=pb[BK][:, 0:nj * 128], func=AF.Exp, scale=att_scale),
                         reads=[], writes=[pb_b[BK], pf_b[L]])
                    if i - jg <= 8:
                        k0 = 8 - (i - jg)
                        P.op("dve", lambda en: en.tensor_tensor(out=pf[L][:, 0:nj * 128].rearrange("p (a b) -> p a b", b=128),
                                                                in0=pf[L][:, 0:nj * 128].rearrange("p (a b) -> p a b", b=128),
                                                                in1=expA[:, h, k0:k0 + nj, :], op=ALU.mult),
                             reads=[pf_b[L], b_exp], writes=[pf_b[L]])
                    else:
                        for j in range(jg, je):
                            sl = j - jg
                            kk = 8 - min(i - j, 8)
                            P.op("dve", lambda en: en.tensor_tensor(out=pf[L][:, sl * 128:(sl + 1) * 128], in0=pf[L][:, sl * 128:(sl + 1) * 128],
                                                                    in1=expA[:, h, kk, :], op=ALU.mult),
                                 reads=[pf_b[L], b_exp], writes=[pf_b[L]])
                P.op("pool", lambda en: en.tensor_tensor(out=pbf[L][:, 0:nj * 128].rearrange("p (a b) -> p a b", b=128),
                                                         in0=pf[L][:, 0:nj * 128].rearrange("p (a b) -> p a b", b=128),
                                                         in1=maskT[:, jg:je, :], op=ALU.mult),
                     reads=[pf_b[L], maskT_b], writes=[pbf_b[L]])

            def emit_pv(k):
                h, jg = items[k]
                je = min(i + 1, jg + 4)
                L = k % NBD
                for j in range(jg, je):
                    sl = j - jg
                    for (bk, lh) in ((2, kvtok[:, j, 0:128]), (3, kvtok[:, j, 128:256]), (6, ones_bf[:, :])):
                        P.op("pe", lambda en: en.matmul(pb[bk][:, 0:128], lhsT=lh, rhs=pbf[L][:, sl * 128:(sl + 1) * 128],
                                                        start=(j == 0), stop=(j == i)),
                             reads=[b_k, pbf_b[L], b_const], writes=[pb_b[bk]])

            def emit_fin_dve(h):
                P.op("dve", lambda en: en.reciprocal(out=rc[:], in_=pb[6][:, 0:128]), reads=[], writes=[pb_b[6], rc_b])
                for c in range(2):
                    P.op("dve", lambda en: en.tensor_tensor(out=on[:, c, :], in0=pb[2 + c][:, 0:128], in1=rc[:], op=ALU.mult),
                         reads=[rc_b], writes=[pb_b[2 + c], on_b])

            def emit_fin_pe(h):
                pbs = (h % 2) * 64
                col = ((h // 2) % 4) * 128
                for c in range(2):
                    P.op("pe", lambda en: en.matmul(pb[5][pbs:pbs + 64, col:col + 128], lhsT=wuv[:, h, c, :], rhs=on[:, c, :],
                                                    start=(c == 0), stop=(c == 1)),
                         reads=[b_k, on_b], writes=[pb_b[5]])
                if h % 2 == 1:
                    P.op("act", lambda en: en.activation(out=ya_st[yk][:, h // 2, :], in_=pb[5][:, col:col + 128], func=AF.Copy),
                         reads=[], writes=[pb_b[5], ya_b[yk]])

            for k0 in range(min(NBD - 1, len(items))):
                emit_logits(k0)
            for k in range(len(items)):
                h, jg = items[k]
                if k + NBD - 1 < len(items):
                    emit_logits(k + NBD - 1)
                emit_post(k)
                emit_pv(k)
                if jg + 4 > i:
                    emit_fin_dve(h)
                    emit_fin_pe(h)
            P.dma("sp", yT.ap()[0:1024, qs].rearrange("(c p) t -> p c t", p=128), ya_st[yk][:], reads=[ya_b[yk]], writes=[db("yT", 0)])

    with Scope(P) as sw:
        kd = sw.sb("kd", [128, 2, T], BF16)
        vtk = sw.sb("vtk", [128, 16, 128], BF16)
        expB = sw.sb("expB", [128, 16, 2, 128], BF16)
        esk = sw.sb("esk", [128, 16], F32)
        b_k = Buf("kside")
        b_exp = Buf("expB")
        P.dma("sp", kd[:], s_kdupT.ap().rearrange("(g p) t -> p g t", p=128), reads=[db("kdupT")], writes=[b_k])
        P.dma("sp", vtk[:], s_vbtok.ap().rearrange("(tt p) c -> p tt c", p=128), reads=[db("vbtok")], writes=[b_k])
        P.op("act", lambda en: en.activation(out=esk[:], in_=spk[:, SP_SINK + e * 16:SP_SINK + (e + 1) * 16], func=AF.Exp),
             reads=[b_spk], writes=[b_exp])
        hk = [sw.sb(f"hk{i}", [128, 128], F32) for i in range(2)]
        hk_b = [Buf(), Buf()]
        n = 0
        for hb in range(16):
            for kx in range(2):
                dj = 1 - kx
                k = n % 2
                n += 1
                P.dma("sp", hk[k][:], bass.AP(s_vrow, hb * XT + XA + dj * 128, [[1, 128], [1, 128]]), reads=[db("vrow")], writes=[hk_b[k]])
                P.op("pe", lambda en: en.matmul(pb[5][:, 0:128], lhsT=jflip, rhs=hk[k][:], start=True, stop=True),
                     reads=[hk_b[k], b_cst], writes=[pb_b[5]])
                P.op("act", lambda en: en.activation(out=expB[:, hb, kx, :], in_=pb[5][:, 0:128], func=AF.Exp),
                     reads=[], writes=[pb_b[5], b_exp])
        qb = [sw.sb(f"qb{i}", [128, 8, 128], BF16) for i in range(2)]
        qb_b = [Buf(), Buf()]
        pf = [sw.sb(f"pf{i}", [128, 512], F32) for i in range(2)]
        pf_b = [Buf(), Buf()]
        pbf = [sw.sb(f"pbf{i}", [128, 512], BF16) for i in range(2)]
        pbf_b = [Buf(), Buf()]
        dn = sw.sb("dn", [128, 128], F32)
        dn_b = Buf()
        yb_st = [sw.sb(f"yb_st{i}", [128, 8, 128], BF16) for i in range(2)]
        yb_b = [Buf(), Buf()]
        cnt = 0
        for a_ in range(E["cfg"].get("swa_blocks", 8)):
            nb = 8 + a_
            qk = nb % 2
            qs = slice(a_ * 128, (a_ + 1) * 128)
            P.dma("sp", qb[qk][:], s_qbT.ap()[:, qs].rearrange("(c p) t -> p c t", p=128), reads=[db("qbT")], writes=[qb_b[qk]])
            for m in range(8):
                Lb = [(0, 1), (5, 6)][cnt % 2]
                Ls = cnt % 2
                cnt += 1
                units = []
                for hh in range(2):
                    for kx in range(2):
                        dj = 1 - kx
                        if nb - dj >= 0:
                            units.append((hh, kx, nb - dj, hh * 2 + kx))
                for (hh, kx, j, sl) in units:
                    hb = 2 * m + hh
                    g = hb // 8
                    pbs = hh * 64
                    bkx = Lb[hh]
                    P.op("pe", lambda en: en.matmul(pb[bkx][:, kx * 128:(kx + 1) * 128], lhsT=kd[pbs:pbs + 64, g, j * 128:(j + 1) * 128],
                                                    rhs=qb[qk][pbs:pbs + 64, m, :], start=True, stop=True),
                         reads=[b_k, qb_b[qk]], writes=[pb_b[bkx]])
                stg_ = E["cfg"].get("swa_stage", 4)
                if stg_ < 2:
                    continue
                L = Ls
                for hh in range(2):
                    bkx = Lb[hh]
                    a, b = (0, 2)
                    P.op("act", lambda en: en.activation(out=pf[L][:, hh * 256 + a * 128:hh * 256 + b * 128], in_=pb[bkx][:, a * 128:b * 128], func=AF.Exp, scale=0.125),
                         reads=[], writes=[pb_b[bkx], pf_b[L]])
                    P.op("dve", lambda en: en.tensor_tensor(out=pbf[L][:, hh * 256 + a * 128:hh * 256 + b * 128], in0=pf[L][:, hh * 256 + a * 128:hh * 256 + b * 128],
                                                            in1=expB[:, 2 * m + hh, a:b, :].rearrange("p k q -> p (k q)"), op=ALU.mult),
                         reads=[pf_b[L], b_exp], writes=[pbf_b[L]])
                    if a_ == 0:
                        P.op("dve", lambda en: en.tensor_scalar(out=pbf[L][:, hh * 256:hh * 256 + 128], in0=pbf[L][:, hh * 256:hh * 256 + 128],
                                                                scalar1=flg[:, 0:1], scalar2=None, op0=ALU.mult),
                             reads=[pbf_b[L], b_flg], writes=[pbf_b[L]])
                if stg_ < 3:
                    continue
                for hh in range(2):
                    us = [u for u in units if u[0] == hh]
                    hb = 2 * m + hh
                    g = hb // 8
                    pbs = hh * 64
                    for ui, (_, kx, j, sl) in enumerate(us):
                        P.op("pe", lambda en: en.matmul(pb[2][pbs:pbs + 64, 0:128], lhsT=vtk[:, j, g * 64:(g + 1) * 64], rhs=pbf[L][:, sl * 128:(sl + 1) * 128],
                                                        start=(ui == 0), stop=(ui == len(us) - 1)),
                             reads=[b_k, pbf_b[L]], writes=[pb_b[2]])
                        P.op("pe", lambda en: en.matmul(pb[3][pbs:pbs + 64, 0:128], lhsT=ones_bf[:, 0:64], rhs=pbf[L][:, sl * 128:(sl + 1) * 128],
                                                        start=(ui == 0), stop=(ui == len(us) - 1)),
                             reads=[b_const, pbf_b[L]], writes=[pb_b[3]])
                if stg_ < 4:
                    continue
                for hh in range(2):
                    hb = 2 * m + hh
                    pbs = hh * 64
                    P.op("dve", lambda en: en.tensor_scalar(out=dn[pbs:pbs + 64, :], in0=pb[3][pbs:pbs + 64, 0:128], scalar1=esk[pbs:pbs + 64, hb:hb + 1],
                                                            scalar2=None, op0=ALU.add),
                         reads=[b_exp], writes=[pb_b[3], dn_b])
                P.op("dve", lambda en: en.reciprocal(out=dn[:], in_=dn[:]), reads=[dn_b], writes=[dn_b])
                P.op("dve", lambda en: en.tensor_tensor(out=yb_st[qk][:, m, :], in0=pb[2][:, 0:128], in1=dn[:], op=ALU.mult),
                     reads=[dn_b], writes=[pb_b[2], yb_b[qk]])
            P.dma("sp", yT.ap()[1024:2048, qs].rearrange("(c p) t -> p c t", p=128), yb_st[qk][:], reads=[yb_b[qk]], writes=[db("yT", 0)])


def odd_attention(E, l):
    o = l // 2
    P, nc, Wd, db, gemm, simple_blocks, norm_x = (E[k] for k in ("P", "nc", "Wd", "db", "gemm", "simple_blocks", "norm_x"))
    spk, b_spk, pb, pb_b, pbh, pbh_b, ident_f, ident_bf, b_cst, b_const, ones_bf, ones_f, bd_bf, triu_f, triu_bf, eps_t = (E[k] for k in (
        "spk", "b_spk", "pb", "pb_b", "pbh", "pbh_b", "ident_f", "ident_bf", "b_cst", "b_const", "ones_bf", "ones_f", "bd_bf", "triu_f", "triu_bf", "eps_t"))
    xT, yT = E["xT"], E["yT"]
    s_qT, s_kT, s_vtok, s_lf = E["s_qT"], E["s_kT"], E["s_vtok"], E["s_lf"]
    flg, b_flg, cc_gather, kpack_o, gk_o, lfp, glf = (E[k] for k in ("flg", "b_flg", "cc_gather", "kpack_o", "gk_o", "lfp", "glf"))

    def rms_grp(S, srcs, src_b, lhsT_ones, gcol, inv_n, dsts, dst_bufs, ncols=TC):
        C = len(srcs)
        sq, sq_b, rstd, rstd_b = S["sq"], S["sq_b"], S["rstd"], S["rstd_b"]
        for c in range(C):
            P.op("act", lambda en: en.activation(out=sq[:, c, 0:ncols], in_=srcs[c], func=AF.Square),
                 reads=[src_b], writes=[sq_b])
        for c in range(C):
            P.op("pe", lambda en: en.matmul(pb[4][:, 0:ncols], lhsT=lhsT_ones[:, :], rhs=sq[:, c, 0:ncols],
                                            start=(c == 0), stop=(c == C - 1)),
                 reads=[sq_b, b_const], writes=[pb_b[4]])
        P.op("act", lambda en: en.activation(out=rstd[:, 0:ncols], in_=pb[4][:, 0:ncols], func=AF.Sqrt,
                                             scale=inv_n, bias=eps_t[:, 0:1]),
             reads=[b_const], writes=[pb_b[4], rstd_b])
        P.op("dve", lambda en: en.reciprocal(out=rstd[:, 0:ncols], in_=rstd[:, 0:ncols]), reads=[rstd_b], writes=[rstd_b])
        for c in range(C):
            P.op("dve", lambda en: en.scalar_tensor_tensor(out=dsts[c], in0=srcs[c], scalar=spk[:, gcol + c:gcol + c + 1],
                                                           in1=rstd[:, 0:ncols], op0=ALU.mult, op1=ALU.mult),
                 reads=[src_b, rstd_b, b_spk], writes=dst_bufs)

    for ps in range(0 if E["cfg"].get("skip_proj") else 1):
        t0 = 0
        with Scope(P) as so:
            hT = so.sb("hT", [128, 16, TT], BF16)
            hT_b = Buf("hT")
            with Scope(P) as sn:
                S0 = dict(xs=sn.sb("xs", [128, 16, TC], F32), xs_b=Buf(), sq=sn.sb("sq", [128, 16, TC], BF16),
                          sq_b=Buf(), rstd=sn.sb("rstd", [128, TC], F32), rstd_b=Buf())
                norm_x(S0, hT, hT_b, SP_ATTN + l * 16, t0)
            S = dict(sq=so.sb("sq", [128, 1, TC], BF16), sq_b=Buf(), rstd=so.sb("rstd", [128, TC], F32), rstd_b=Buf())
            stg1 = so.sb("stg1", [128, TT], F32)
            stg1_b = Buf("stg1")
            ob = so.sb("ob", [128, TT], BF16)
            ob_b = Buf("ob")
            vb_st = so.sb("vb_st", [128, TT], BF16)
            vb_b = Buf()
            vtok_st = so.sb("vtok_st", [128, 8, 128], BF16)
            vtok_b = Buf()
            negfb = so.sb("negfb", [32, 1], F32)
            negfb_b = Buf()
            fst = so.sb("fst", [32, TT], F32)
            fst_b = Buf()
            lf_tok = so.sb("lf_tok", [128, 8, 32], F32)
            lf_tok_b = Buf()
            P.op("dve", lambda en: en.tensor_scalar(out=negfb[:], in0=spk[0:32, SP_FB + o:SP_FB + o + 1], scalar1=-1.0, scalar2=None, op0=ALU.mult),
                 reads=[b_spk], writes=[negfb_b])

            def tsl(tci):
                return slice(tci * TC, (tci + 1) * TC)

            def in_epi(bi, m, tag, tci):
                kind = tag[0]
                last = (tci == TT // TC - 1)
                if kind in ("q", "k"):
                    c = tag[1]
                    P.op("act", lambda en: en.activation(out=stg1[:, tsl(tci)], in_=pb[bi][:], func=AF.Copy),
                         reads=[], writes=[pb_b[bi], stg1_b])
                    if last:
                        gcol = (SP_CQN if kind == "q" else SP_CKN) + o
                        dst = s_qT if kind == "q" else s_kT
                        for t2 in range(TT // TC):
                            rms_grp(S, [stg1[:, tsl(t2)]], stg1_b, bd_bf, gcol, 1.0 / 64, [ob[:, tsl(t2)]], [ob_b])
                        if kind == "q":
                            P.dma("sp", s_qT.ap()[c * 128:(c + 1) * 128, 0:TT], ob[:], reads=[ob_b], writes=[db("qT")])
                        else:
                            P.dma("sp", s_kT.ap()[c * 128:(c + 1) * 128, TO:TO + TT], ob[:], reads=[ob_b], writes=[db("kT")])
                            P.dma("sp", kpack_o[c // 8].ap()[(c % 8) * 128:(c % 8 + 1) * 128, :], ob[:], reads=[ob_b], writes=[db("kpack_o", c // 8)])
                elif kind == "v":
                    c = tag[1]
                    P.op("act", lambda en: en.activation(out=vb_st[:, tsl(tci)], in_=pb[bi][:], func=AF.Copy),
                         reads=[], writes=[pb_b[bi], vb_b])
                    if last:
                        for tt in range(TT // 128):
                            P.op("pe", lambda en: en.transpose(pbh[:, (tt % 4) * 128:(tt % 4 + 1) * 128], vb_st[:, tt * 128:(tt + 1) * 128], ident_bf[:]),
                                 reads=[vb_b, b_const], writes=[pbh_b])
                            if tt % 4 == 3:
                                P.op("act", lambda en: en.activation(out=vtok_st[:, tt - 3:tt + 1, :], in_=pbh[:, 0:512].rearrange("p (a b) -> p a b", b=128), func=AF.Copy),
                                     reads=[], writes=[pbh_b, vtok_b])
                        P.dma("sp", s_vtok.ap()[TO:TO + TT, c * 128:(c + 1) * 128].rearrange("(tt p) c -> p tt c", p=128), vtok_st[:],
                              reads=[vtok_b], writes=[db("vtok")])
                        for hv in range(2):
                            P.dma("sp", kpack_o[2 + hv].ap().rearrange("(t a) c -> t (a c)", a=2)[:, c * 128:(c + 1) * 128].rearrange("(tt p) c -> p tt c", p=128),
                                  vtok_st[:, hv * 4:(hv + 1) * 4, :], reads=[vtok_b], writes=[db("kpack_o", 2 + hv)])
                elif kind == "f":
                    P.op("act", lambda en: en.activation(out=fst[:, tsl(tci)], in_=pb[bi][0:32, :], func=AF.Exp, scale=-1.0, bias=negfb[:, 0:1]),
                         reads=[negfb_b], writes=[pb_b[bi], fst_b])
                    if last:
                        P.op("act", lambda en: en.activation(out=fst[:], in_=fst[:], func=AF.Ln, bias=ones_f[0:32, 0:1]),
                             reads=[fst_b, b_const], writes=[fst_b])
                        for tt in range(TT // 128):
                            P.op("pe", lambda en: en.transpose(pb[5][:, tt * 32:(tt + 1) * 32], fst[0:32, tt * 128:(tt + 1) * 128], ident_f[0:32, 0:32]),
                                 reads=[fst_b, b_cst], writes=[pb_b[5]])
                        P.op("act", lambda en: en.activation(out=lf_tok[:], in_=pb[5][:, 0:256].rearrange("p (a b) -> p a b", b=32), func=AF.Copy),
                             reads=[], writes=[pb_b[5], lf_tok_b])
                        P.dma("sp", s_lf.ap()[TO:TO + TT, :].rearrange("(tt p) c -> p tt c", p=128), lf_tok[:],
                              reads=[lf_tok_b], writes=[db("lf")])
                        P.dma("sp", lfp.ap().rearrange("(tt p) c -> p tt c", p=128), lf_tok[:],
                              reads=[lf_tok_b], writes=[db("lfp")])
            blocks = []
            for kind, base in (("q", 0), ("k", 2048), ("v", 4096)):
                for b4 in range(4):
                    blocks.append(([(base + b4 * 512, 512)], [(c * 128, 128, (kind, b4 * 4 + c), 0) for c in range(4)]))
            blocks.append(([(6144, 32)], [(0, 32, ("f",), 0)]))
            gemm(hT, hT_b, 16, lambda c0, n: Wd["w_in_odd"].ap()[o, :, c0:c0 + n], blocks, in_epi)

    if not E["cfg"].get("skip_proj"):
        for i4 in range(4):
            cc_gather(kpack_o[i4], gk_o[i4], [db("kpack_o", i4)], [db("gk_o", i4)])
        cc_gather(lfp, glf, [db("lfp")], [db("glf")])
        for i4 in range(2):
            P.dma("sp", s_kT.ap()[i4 * 1024:(i4 + 1) * 1024, 0:TO], gk_o[i4].ap()[0:1024, :], reads=[db("gk_o", i4)], writes=[db("kT")])
            P.dma("sp", s_vtok.ap()[i4 * 512:(i4 + 1) * 512, :], gk_o[2 + i4].ap()[0:1024, :].rearrange("(t a) c -> t (a c)", a=2),
                  reads=[db("gk_o", 2 + i4)], writes=[db("vtok")])
        P.dma("sp", s_lf.ap()[0:TO, :], glf.ap()[0:1024, :], reads=[db("glf")], writes=[db("lf")])
    if E["cfg"].get("stop_after_proj"):
        return
    with Scope(P) as sa:
        lft = sa.sb("lft", [128, 16, 32], F32)
        lft_b = Buf()
        ncum = sa.sb("ncum", [128, 16, 32], F32)
        Cb = sa.sb("Cb", [128, 16, 32], F32)
        cum_b = Buf("cum")
        P.dma("sp", lft[:], s_lf.ap().rearrange("(tt p) c -> p tt c", p=128), reads=[db("lf")], writes=[lft_b])
        P.op("dve", lambda en: en.tensor_scalar(out=lft[:, 0:8, :], in0=lft[:, 0:8, :], scalar1=flg[:, 0:1], scalar2=None, op0=ALU.mult),
             reads=[lft_b, b_flg], writes=[lft_b])
        for j in range(16):
            for j2 in range(j + 1):
                P.op("pe", lambda en: en.matmul(pb[5][:, j * 32:(j + 1) * 32], lhsT=(triu_f if j2 == j else ones_f[:, :]), rhs=lft[:, j2, :],
                                                start=(j2 == 0), stop=(j2 == j)),
                     reads=[lft_b, b_cst, b_const], writes=[pb_b[5]])
            for j2 in range(j + 1):
                P.op("pe", lambda en: en.matmul(pb[6][:, j * 32:(j + 1) * 32], lhsT=ones_f[:, :], rhs=lft[:, j2, :],
                                                start=(j2 == 0), stop=(j2 == j)),
                     reads=[lft_b, b_const], writes=[pb_b[6]])
        P.op("act", lambda en: en.activation(out=ncum[:], in_=pb[5][:].rearrange("p (a b) -> p a b", b=32), func=AF.Copy),
             reads=[], writes=[pb_b[5], cum_b])
        P.op("act", lambda en: en.activation(out=Cb[:], in_=pb[6][:].rearrange("p (a b) -> p a b", b=32), func=AF.Copy),
             reads=[], writes=[pb_b[6], cum_b])
        s_nq = E["s_nq"]
        dm = sa.sb("dm", [128, 16, 32], F32)
        dhi = sa.sb("dhi", [128, 16, 32], BF16)
        dhf = sa.sb("dhf", [128, 16, 32], F32)
        dlo = sa.sb("dlo", [128, 16, 32], BF16)
        nqT = sa.sb("nqT", [32, 2, TO], BF16)
        dm_b = Buf("dm")
        nqT_b = Buf("nqT")
        P.op("dve", lambda en: en.tensor_tensor(out=dm[:], in0=Cb[:], in1=ncum[:], op=ALU.subtract), reads=[cum_b], writes=[dm_b])
        P.op("dve", lambda en: en.tensor_scalar(out=dm[:], in0=dm[:], scalar1=8.0, scalar2=None, op0=ALU.mult), reads=[dm_b], writes=[dm_b])
        P.op("dve", lambda en: en.tensor_copy(out=dhi[:], in_=dm[:]), reads=[dm_b], writes=[dm_b])
        P.op("dve", lambda en: en.tensor_copy(out=dhf[:], in_=dhi[:]), reads=[dm_b], writes=[dm_b])
        P.op("dve", lambda en: en.tensor_tensor(out=dhf[:], in0=dm[:], in1=dhf[:], op=ALU.subtract), reads=[dm_b], writes=[dm_b])
        P.op("dve", lambda en: en.tensor_copy(out=dlo[:], in_=dhf[:]), reads=[dm_b], writes=[dm_b])
        for w, src in enumerate((dhi, dlo)):
            for tt in range(8):
                P.op("pe", lambda en: en.transpose(pbh[0:32, tt * 128:(tt + 1) * 128], src[:, 8 + tt, :], ident_bf[:]),
                     reads=[dm_b, b_const], writes=[pbh_b])
            P.op("act", lambda en: en.activation(out=nqT[:, w, :], in_=pbh[0:32, :], func=AF.Copy),
                 reads=[], writes=[pbh_b, nqT_b])
        P.dma("sp", s_nq.ap().rearrange("w h t -> h w t"), nqT[:], reads=[nqT_b], writes=[db("nq")])
        P.op("dve", lambda en: en.tensor_scalar(out=ncum[:, 0:8, :], in0=ncum[:, 0:8, :], scalar1=flg[:, 2:3], scalar2=None, op0=ALU.add),
             reads=[cum_b, dm_b, b_flg], writes=[cum_b])
        mneg = sa.sb("mneg", [128, 128], BF16)
        mneg_b = Buf("mneg")
        P.op("dve", lambda en: en.tensor_scalar(out=mneg[:], in0=triu_f, scalar1=30000.0, scalar2=-30000.0, op0=ALU.mult, op1=ALU.add),
             reads=[b_cst], writes=[mneg_b])
        kaug = [[sa.sb(f"kaug{i}{hh}", [128, T], BF16) for hh in range(2)] for i in range(2)]
        qaug = [[sa.sb(f"qaug{i}{hh}", [128, TO], BF16) for hh in range(2)] for i in range(2)]
        vm = [sa.sb(f"vm{i}", [128, 16, 128], BF16) for i in range(2)]
        m_b = [Buf(), Buf()]
        NBF = 4
        LBF = [0, 1, 4, 5]
        Bm = [sa.sb(f"Bm{i}", [128, 4], F32) for i in range(NBF)]
        Bm_b = [Buf() for _ in range(NBF)]
        pbf = [sa.sb(f"pbf{i}", [128, 512], BF16) for i in range(NBF)]
        pbf_b = [Buf() for _ in range(NBF)]
        rcp = sa.sb("rcp", [128, 512], F32)
        rcp_b = Buf()
        yst = [sa.sb(f"yst{i}", [128, 512], BF16) for i in range(2)]
        yst_b = [Buf(), Buf()]
        cnt = 0
        yc = 0
        for m in range(E["cfg"].get("fox_pairs", 16)):
            mk = m % 2
            for hh in range(2):
                h = 2 * m + hh
                own = slice(hh * 64, (hh + 1) * 64)
                oth = slice((1 - hh) * 64, (2 - hh) * 64)
                o0 = (1 - hh) * 64
                P.op("dve", lambda en: en.memset(kaug[mk][hh][oth, :], 0.0), writes=[m_b[mk]])
                P.op("dve", lambda en: en.memset(kaug[mk][hh][o0:o0 + 2, :], 1.0), writes=[m_b[mk]])
                P.op("dve", lambda en: en.memset(qaug[mk][hh][oth, :], 0.0), writes=[m_b[mk]])
                P.dma("sp", kaug[mk][hh][own, :], s_kT.ap()[h * 64:(h + 1) * 64, :], reads=[db("kT")], writes=[m_b[mk]])
                P.dma("sp", qaug[mk][hh][own, :], s_qT.ap()[h * 64:(h + 1) * 64, :], reads=[db("qT")], writes=[m_b[mk]])
                P.dma("sp", qaug[mk][hh][o0:o0 + 2, :], s_nq.ap()[:, h, :], reads=[db("nq")], writes=[m_b[mk]])
            P.dma("sp", vm[mk][:], s_vtok.ap()[:, m * 128:(m + 1) * 128].rearrange("(tt p) c -> p tt c", p=128), reads=[db("vtok")], writes=[m_b[mk]])
            for Gl in range(2):
                G = 2 + Gl
                jmax = 4 * G + 3
                items = [(hh, j) for hh in range(2) for j in range(jmax + 1)]

                def f_logits(k):
                    hh, j = items[k]
                    L = LBF[k % NBF]
                    i_lo = max(4 * G, j)
                    col0 = (i_lo - 4 * G) * 128
                    P.op("pe", lambda en: en.matmul(pb[L][:, col0:512], lhsT=kaug[mk][hh][:, j * 128:(j + 1) * 128],
                                                    rhs=qaug[mk][hh][:, 4 * Gl * 128 + col0:(4 * Gl + 4) * 128], start=True, stop=(j < 4 * G)),
                         reads=[m_b[mk]], writes=[pb_b[L]])
                    if j >= 4 * G:
                        P.op("pe", lambda en: en.matmul(pb[L][:, col0:col0 + 128], lhsT=ident_bf[:], rhs=mneg[:], start=False, stop=True),
                             reads=[b_const, mneg_b], writes=[pb_b[L]])

                def f_post(k):
                    hh, j = items[k]
                    h = 2 * m + hh
                    L = k % NBF
                    BK = LBF[L]
                    i_lo = max(4 * G, j)
                    P.op("dve", lambda en: en.tensor_scalar(out=Bm[L][:], in0=Cb[:, 4 * G:4 * G + 4, h], scalar1=-1.0, scalar2=ncum[:, j, h:h + 1],
                                                            op0=ALU.mult, op1=ALU.add),
                         reads=[cum_b], writes=[Bm_b[L]])
                    for i in range(i_lo, 4 * G + 4):
                        cs = slice((i - 4 * G) * 128, (i - 4 * G + 1) * 128)
                        P.op("act", lambda en: en.activation(out=pbf[L][:, cs], in_=pb[BK][:, cs], func=AF.Exp, scale=0.125,
                                                             bias=Bm[L][:, i - 4 * G:i - 4 * G + 1]),
                             reads=[Bm_b[L]], writes=[pb_b[BK], pbf_b[L]])

                def f_pv(k):
                    hh, j = items[k]
                    pbs = hh * 64
                    L = k % NBF
                    i_lo = max(4 * G, j)
                    col0 = (i_lo - 4 * G) * 128
                    P.op("pe", lambda en: en.matmul(pb[2][pbs:pbs + 64, col0:512], lhsT=vm[mk][:, j, hh * 64:(hh + 1) * 64], rhs=pbf[L][:, col0:512],
                                                    start=(j == 0), stop=(j == jmax)),
                         reads=[m_b[mk], pbf_b[L]], writes=[pb_b[2]])
                    P.op("pe", lambda en: en.matmul(pb[3][pbs:pbs + 64, col0:512], lhsT=ones_bf[:, 0:64], rhs=pbf[L][:, col0:512],
                                                    start=(j == 0), stop=(j == jmax)),
                         reads=[b_const, pbf_b[L]], writes=[pb_b[3]])

                for k0 in range(NBF - 1):
                    f_logits(k0)
                for k in range(len(items)):
                    if k + NBF - 1 < len(items):
                        f_logits(k + NBF - 1)
                    f_post(k)
                    f_pv(k)
                yk = yc % 2
                yc += 1
                P.op("dve", lambda en: en.reciprocal(out=rcp[:], in_=pb[3][:]), reads=[], writes=[pb_b[3], rcp_b])
                P.op("dve", lambda en: en.tensor_tensor(out=yst[yk][:], in0=pb[2][:], in1=rcp[:], op=ALU.mult),
                     reads=[rcp_b], writes=[pb_b[2], yst_b[yk]])
                P.dma("sp", yT.ap()[m * 128:(m + 1) * 128, Gl * 512:(Gl + 1) * 512], yst[yk][:], reads=[yst_b[yk]], writes=[db("yT", 0)])


def build(cfg=None):
    cfg = cfg or {}
    layers = cfg.get("layers", list(range(DEPTH)))
    nc = bass.Bass("TRN2", target_bir_lowering=False)

    def din(name, shape):
        return nc.dram_tensor(name, list(shape), F32, kind="ExternalInput")
    x_in = din("x", [TO, D])
    p_in = din("p", [DEPTH, TO, 256])
    flg_in = din("flg", [128, 4])
    Wd = {n: din(n, s) for n, s in WEIGHTS}
    sp_in = din("sp", [128, NSP])
    cst_in = din("cst", [128, 512])
    oh_in = din("oh", [33, XA + XB])
    out_d = nc.dram_tensor("out", [TO, D], F32, kind="ExternalOutput")
    dbg = {}
    for name, shape in cfg.get("dumps", []):
        dbg[name] = nc.dram_tensor("dbg_" + name, list(shape), F32, kind="ExternalOutput")

    def scr(name, shape, dt):
        if name in cfg.get("expose", ()):
            return nc.dram_tensor(name, list(shape), dt, kind="ExternalOutput")
        return nc.dram_tensor(name, list(shape), dt)
    xT = scr("xT", [D, TO], F32)
    yT = scr("yT", [D, TO], BF16)
    s_kvT = scr("s_kvT", [256, T], BF16)
    s_kvtok = scr("s_kvtok", [T, 256], BF16)
    s_kidxT = scr("s_kidxT", [64, T], BF16)
    s_widx = scr("s_widx", [TO, 16], F32)
    s_qiT = scr("s_qiT", [1024, TO], BF16)
    s_qaT = scr("s_qaT", [4096, TO], BF16)
    s_qbT = scr("s_qbT", [1024, TO], BF16)
    s_kdupT = scr("s_kdupT", [256, T], BF16)
    s_vbtok = scr("s_vbtok", [T, 128], BF16)
    s_qT = scr("s_qT", [2048, TO], BF16)
    s_kT = scr("s_kT", [2048, T], BF16)
    s_vtok = scr("s_vtok", [T, 2048], BF16)
    s_lf = scr("s_lf", [T, 32], F32)
    s_vrow = scr("s_vrow", [16, XA + XB], F32)
    s_nq = scr("s_nq", [2, 32, TO], BF16)
    kpack_e = scr("kpack_e", [960, 1024], BF16)
    gk_e = scr("gk_e", [1920, 1024], BF16)
    kpack_o = [scr(f"kpack_o{i}", [1024, 1024], BF16) for i in range(4)]
    gk_o = [scr(f"gk_o{i}", [2048, 1024], BF16) for i in range(4)]
    lfp = scr("lfp", [1024, 32], F32)
    glf = scr("glf", [2048, 32], F32)
    hx_in = scr("hx_in", [128, 32], F32)
    hxg = scr("hxg", [256, 32], F32)
    dbufs = {}

    def db(*key):
        if key not in dbufs:
            dbufs[key] = Buf(str(key))
        return dbufs[key]

    with ExitStack() as st:
        P = Prog(nc, st)

        def gsb(name, shape, dt):
            return st.enter_context(nc.sbuf_tensor(name, list(shape), dt))

        spk = gsb("spk", [128, NSP], F32)
        cst = gsb("cst_sb", [128, 512], F32)
        ident_bf = gsb("ident_bf", [128, 128], BF16)
        ones_bf = gsb("ones_bf", [128, 128], BF16)
        bd_bf = gsb("bd_bf", [128, 128], BF16)
        ones_f = gsb("ones_f", [128, 128], F32)
        eps_t = gsb("eps_t", [128, 1], F32)
        triu_bf = gsb("triu_bf", [128, 128], BF16)
        halo = gsb("halo", [128, 2], F32)
        flg = gsb("flg_sb", [128, 4], F32)
        b_flg = Buf("flg")
        ccs = P._newsem("ccs")
        cc_n = [0]
        WSLOT = 11008
        NWS = 2
        wbuf = [gsb(f"wbuf{i}", [128, WSLOT], BF16) for i in range(NWS)]
        wb_b = [Buf(f"wb{i}") for i in range(NWS)]
        wptr = [0]
        b_spk, b_cst, b_const, b_halo = Buf(), Buf(), Buf(), Buf()
        ident_f = cst[:, 0:128]
        jflip = cst[:, 128:256]
        triu_f = cst[:, 256:384]
        cneg = cst[:, 384:512]
        pb = [st.enter_context(nc.psum_tensor(f"pb{i}", [128, 512], F32)) for i in range(7)]
        pbh = st.enter_context(nc.psum_tensor("pbh", [128, 1024], BF16))
        pb_b = [Buf(f"pb{i}") for i in range(7)]
        pbh_b = Buf("pbh")

        P.dma("sp", spk[:], sp_in.ap(), writes=[b_spk])
        P.dma("sp", cst[:], cst_in.ap(), writes=[b_cst])
        P.dma("sp", flg[:], flg_in.ap(), writes=[b_flg])
        P.op("dve", lambda e: e.memset(ones_bf[:], 1.0), writes=[b_const])
        P.op("dve", lambda e: e.memset(ones_f[:], 1.0), writes=[b_const])
        P.op("dve", lambda e: e.memset(eps_t[:], EPS), writes=[b_const])
        P.op("dve", lambda e: e.memset(bd_bf[:], 0.0), writes=[b_const])
        P.op("dve", lambda e: e.memset(bd_bf[0:64, 0:64], 1.0), writes=[b_const])
        P.op("dve", lambda e: e.memset(bd_bf[64:128, 64:128], 1.0), writes=[b_const])
        P.op("dve", lambda e: e.tensor_copy(out=ident_bf[:], in_=ident_f), reads=[b_cst], writes=[b_const])
        P.op("dve", lambda e: e.tensor_copy(out=triu_bf[:], in_=triu_f), reads=[b_cst], writes=[b_const])
        P.op("dve", lambda e: e.memset(spk[32:33, SP_RELB:SP_RELB + 32], NEG), reads=[], writes=[b_spk])
        P.barrier()

        gemm_bank = [0]

        def cc_gather(src_t, dst_t, in_bufs, out_bufs):
            P._deps("pool", list(in_bufs), list(out_bufs))
            cc_n[0] += 1
            nc.gpsimd.collective_compute("AllGather", ALU.bypass, replica_groups=[[0, 4], [1, 5], [2, 6], [3, 7]],
                                         ins=[src_t.ap()], outs=[dst_t.ap()]).then_inc(ccs, 1)
            nc.gpsimd.wait_ge(ccs, cc_n[0])
            P.op("pool", lambda e: e.memset(halo[0:1, 0:1], 0.0), reads=list(in_bufs), writes=list(out_bufs))
        state = {}

        def wview(si, KC, ntot):
            return wbuf[si][:, 0:KC * ntot].rearrange("p (kc n) -> p kc n", n=ntot)

        def gemm(src, src_b, KC, wsrc, blocks, epi, ntc=2, tc_off=0, pre=None):
            for segs, chunks in blocks:
                si = wptr[0]
                wptr[0] = (wptr[0] + 1) % NWS
                ntot = sum(n for _, n in segs)
                wv = wview(si, KC, ntot)
                off = 0
                for (c0, ncols) in segs:
                    P.dma("pool", wv[:, :, off:off + ncols],
                          wsrc(c0, ncols).rearrange("(kc p) n -> p kc n", p=128), writes=[wb_b[si]])
                    off += ncols
                for (coff, m, tag, pbase) in chunks:
                    if pre is not None:
                        pre(wv, wb_b[si], coff, m, tag)
                    for tci in range(ntc):
                        bi = gemm_bank[0]
                        gemm_bank[0] = (gemm_bank[0] + 1) % 4
                        for kc in range(KC):
                            P.op("pe", lambda e: e.matmul(pb[bi][pbase:pbase + m, :], lhsT=wv[:, kc, coff:coff + m],
                                                          rhs=src[:, kc, tc_off + tci * TC:tc_off + (tci + 1) * TC],
                                                          start=(kc == 0), stop=(kc == KC - 1)),
                                 reads=[wb_b[si], src_b], writes=[pb_b[bi]])
                        epi(bi, m, tag, tci)

        def simple_blocks(col0, ncols_total, wcols, tagfn=None, m=128):
            blocks = []
            c = 0
            ci = 0
            while c < ncols_total:
                n = min(wcols, ncols_total - c)
                chunks = []
                o = 0
                while o < n:
                    mm = min(m, n - o)
                    chunks.append((o, mm, ci if tagfn is None else tagfn(ci), 0))
                    o += mm
                    ci += 1
                blocks.append(([(col0 + c, n)], chunks))
                c += n
            return blocks

        def rms_finish(S, src, src_b, C, lhsT_ones, gcol, inv_n, dst_fn, dst_bufs, ncols=TC, nparts=128):
            sq, sq_b, rstd, rstd_b = S["sq"], S["sq_b"], S["rstd"], S["rstd_b"]
            for c in range(C):
                P.op("act", lambda e: e.activation(out=sq[0:nparts, c, 0:ncols], in_=src(c), func=AF.Square),
                     reads=[src_b], writes=[sq_b])
            for c in range(C):
                P.op("pe", lambda e: e.matmul(pb[4][0:nparts, 0:ncols], lhsT=lhsT_ones[0:nparts, 0:nparts],
                                              rhs=sq[0:nparts, c, 0:ncols], start=(c == 0), stop=(c == C - 1)),
                     reads=[sq_b, b_const], writes=[pb_b[4]])
            P.op("act", lambda e: e.activation(out=rstd[0:nparts, 0:ncols], in_=pb[4][0:nparts, 0:ncols], func=AF.Sqrt,
                                               scale=inv_n, bias=eps_t[0:nparts, 0:1]),
                 reads=[b_const], writes=[pb_b[4], rstd_b])
            P.op("dve", lambda e: e.reciprocal(out=rstd[0:nparts, 0:ncols], in_=rstd[0:nparts, 0:ncols]),
                 reads=[rstd_b], writes=[rstd_b])
            for c in range(C):
                P.op("dve", lambda e: e.scalar_tensor_tensor(out=dst_fn(c), in0=src(c),
                                                             scalar=spk[0:nparts, gcol + c:gcol + c + 1],
                                                             in1=rstd[0:nparts, 0:ncols], op0=ALU.mult, op1=ALU.mult),
                     reads=[src_b, rstd_b, b_spk], writes=dst_bufs)

        def norm_x(S, hT, hT_b, gcol, t0):
            xs, xs_b = S["xs"], S["xs_b"]
            for tci in range(TT // TC):
                ta = t0 + tci * TC
                P.dma("sp", xs[:], xT.ap()[:, ta:ta + TC].rearrange("(kc p) t -> p kc t", p=128),
                      reads=[db("xT", ta // TC)], writes=[xs_b])
                rms_finish(S, lambda c: xs[:, c, :], xs_b, 16, ones_bf, gcol, 1.0 / D,
                           lambda c: hT[:, c, tci * TC:(tci + 1) * TC], [hT_b])

        def resid_epi(S, t0):
            def epi(bi, m, tag, tci):
                ta = t0 + tci * TC
                k = S["xr_i"][0]
                S["xr_i"][0] = (k + 1) % 2
                xr, xr_b = S["xr"][k], S["xr_b"][k]
                P.dma("sp", xr[:], xT.ap()[tag * 128:(tag + 1) * 128, ta:ta + TC],
                      reads=[db("xT", ta // TC)], writes=[xr_b])
                P.op("dve", lambda e: e.tensor_tensor(out=xr[:], in0=pb[bi][:], in1=xr[:], op=ALU.add),
                     reads=[xr_b], writes=[pb_b[bi], xr_b])
                P.dma("sp", xT.ap()[tag * 128:(tag + 1) * 128, ta:ta + TC], xr[:],
                      reads=[xr_b], writes=[db("xT", ta // TC)])
            return epi

        def dump(name, src_ap_dram):
            pass

        with Scope(P) as sc:
            xin = [sc.sb(f"xin{i}", [128, D], F32) for i in range(2)]
            xin_b = [Buf(), Buf()]
            stg = [sc.sb(f"xstg{i}", [128, 16, 128], F32) for i in range(2)]
            stg_b = [Buf(), Buf()]
            for tt in range(TO // 128):
                k = tt % 2
                P.dma("sp", xin[k][:], x_in.ap()[tt * 128:(tt + 1) * 128, :], writes=[xin_b[k]])
                for g in range(4):
                    bi = 5 + (g % 2)
                    for j in range(4):
                        fc = g * 4 + j
                        P.op("pe", lambda e: e.transpose(pb[bi][:, j * 128:(j + 1) * 128], xin[k][:, fc * 128:(fc + 1) * 128], ident_f),
                             reads=[xin_b[k], b_cst], writes=[pb_b[bi]])
                    P.op("act", lambda e: e.activation(out=stg[k][:, g * 4:(g + 1) * 4, :], in_=pb[bi][:].rearrange("p (a b) -> p a b", b=128), func=AF.Copy),
                         reads=[], writes=[pb_b[bi], stg_b[k]])
                P.dma("sp", xT.ap()[:, tt * 128:(tt + 1) * 128].rearrange("(fc p) t -> p fc t", p=128), stg[k][:],
                      reads=[stg_b[k]], writes=[db("xT", tt // 4)])

        for l in layers:
            E = dict(locals())
            E['state'] = state
            if "attn" in cfg.get("parts", ("attn", "out", "ffn", "ple")):
                if l % 2 == 0:
                    even_attention(E, l)
                else:
                    odd_attention(E, l)
            token_local(E, l, cfg.get("parts", ("attn", "out", "ffn", "ple")))

        with Scope(P) as sc:
            xo = [sc.sb(f"xo{i}", [128, 16, 128], F32) for i in range(2)]
            xo_b = [Buf(), Buf()]
            ostg = [sc.sb(f"ostg{i}", [128, D], F32) for i in range(2)]
            ostg_b = [Buf(), Buf()]
            for tt in range(TO // 128):
                k = tt % 2
                P.dma("sp", xo[k][:], xT.ap()[:, tt * 128:(tt + 1) * 128].rearrange("(fc p) t -> p fc t", p=128),
                      reads=[db("xT", tt // 4)], writes=[xo_b[k]])
                for g in range(4):
                    bi = 5 + (g % 2)
                    for j in range(4):
                        fc = g * 4 + j
                        P.op("pe", lambda e: e.transpose(pb[bi][:, j * 128:(j + 1) * 128], xo[k][:, fc, :], ident_f),
                             reads=[xo_b[k], b_cst], writes=[pb_b[bi]])
                    P.op("act", lambda e: e.activation(out=ostg[k][:, g * 512:(g + 1) * 512], in_=pb[bi][:], func=AF.Copy),
                         reads=[], writes=[pb_b[bi], ostg_b[k]])
                P.dma("sp", out_d.ap()[tt * 128:(tt + 1) * 128, :], ostg[k][:], reads=[ostg_b[k]], writes=[db("out")])
        P.barrier()
    return nc


_NC_CACHE = {}


def make_in_maps(inp, batches):
    cst, oh = host_consts()
    sp = pack_small(inp)
    wmap = {n: np.ascontiguousarray(inp[n], dtype=np.float32) for n, _ in WEIGHTS}
    in_maps = []
    for c in range(8):
        b = batches[c]
        half = c // 4
        flg = np.zeros((128, 4), np.float32)
        flg[:, 0] = float(half)
        flg[:, 1] = 0.0 if half else -1e30
        flg[:, 2] = 0.0 if half else NEG
        m = dict(x=np.ascontiguousarray(inp["x"][b, half * TO:(half + 1) * TO], dtype=np.float32),
                 p=np.ascontiguousarray(inp["p"][:, b, half * TO:(half + 1) * TO], dtype=np.float32),
                 flg=flg, sp=sp, cst=cst, oh=oh)
        m.update(wmap)
        in_maps.append(m)
    return in_maps


def kernel(**inputs):
    inp = {k: np.asarray(v) for k, v in inputs.items()}
    if "nc" not in _NC_CACHE:
        _NC_CACHE["nc"] = build()
    nc = _NC_CACHE["nc"]
    in_maps = make_in_maps(inp, [0, 1, 2, 3, 0, 1, 2, 3])
    res = run_bass_kernel_spmd(nc, in_maps, core_ids=list(range(8)))
    out = np.stack([np.concatenate([res.results[b]["out"], res.results[b + 4]["out"]], axis=0) for b in range(4)], axis=0)
    return out.astype(np.float32)
```

```python
import math
from contextlib import ExitStack
import numpy as np
import concourse.bass as bass
import concourse.mybir as mybir
from concourse.bass_utils import run_bass_kernel_spmd

F32 = mybir.dt.float32
BF16 = mybir.dt.bfloat16
ALU = mybir.AluOpType
AF = mybir.ActivationFunctionType

EPOCH = 30000
NDSEM = 40

D = 2048
T = 2048
DEPTH = 4
TT = 1024
TO = 1024
TC = 512
DFF = 5504
NFC = 43
EPS = 1e-6
XA = 1280
XB = 384
NEG = -30000.0

SP_ATTN = 0
SP_FFN = 64
SP_PLE = 128
SP_CONV = 192
SP_CQ = 1224
SP_CKV = 1232
SP_AQ = 1236
SP_BQ = 1240
SP_BK = 1242
SP_CQN = 1244
SP_CKN = 1246
SP_FB = 1248
SP_SINK = 1250
SP_RELB = 1282
SP_B31 = 1320
NSP = 1340

WEIGHTS = [
    ("w_in_even", (2, 2048, 2128)), ("a_w_uq", (2, 512, 4096)), ("a_w_qidx", (2, 512, 1024)),
    ("a_w_uv", (2, 16, 256, 64)), ("w_out_even", (2, 2048, 2048)), ("w_in_odd", (2, 2048, 6176)),
    ("w_out_odd", (2, 2048, 2048)), ("w_up", (4, 2048, 11008)), ("w_down", (4, 5504, 2048)),
    ("w_ple_gate", (4, 2048, 2048)), ("w_ple_proj", (4, 256, 2048)),
]


class Buf:
    __slots__ = ("name", "w", "r", "rd")

    def __init__(self, name=""):
        self.name = name
        self.w = None
        self.r = {}
        self.rd = []


class Prog:
    ENGS = ("pe", "act", "dve", "pool", "sp")

    def __init__(self, nc, stack):
        self.nc = nc
        self.stack = stack
        self.eng = {"pe": nc.tensor, "act": nc.scalar, "dve": nc.vector,
                    "pool": nc.gpsimd, "sp": nc.sync}
        self.cnt = {e: 0 for e in self.ENGS}
        self.esems = {e: [] for e in self.ENGS}
        self.seen_e = {e: {p: 0 for p in self.ENGS} for e in self.ENGS}
        self.seen_d = {e: {} for e in self.ENGS}
        self.dsem = {}
        for q in ("sp", "pool"):
            self.dsem[q] = [[self._newsem(f"d{q}{i}"), 0] for i in range(NDSEM)]
        self.dptr = {"sp": 0, "pool": 0}
        self.bar_sem = self._newsem("bar")
        self.bar_cnt = 0
        self.n_inst = 0

    def _newsem(self, name):
        return self.stack.enter_context(self.nc.semaphore(name))

    def _esem(self, e, idx):
        ep = (idx - 1) // EPOCH
        while len(self.esems[e]) <= ep:
            self.esems[e].append(self._newsem(f"e{e}{len(self.esems[e])}"))
        return self.esems[e][ep], (idx - 1) % EPOCH + 1

    def _wait(self, e, ev):
        if ev is None:
            return
        if ev[0] == "e":
            _, p, idx = ev
            if p == e and e == "pe":
                return
            if self.seen_e[e][p] >= idx:
                return
            self.seen_e[e][p] = idx
            s, v = self._esem(p, idx)
            self.eng[e].wait_ge(s, v)
        else:
            _, s, v, key = ev
            if self.seen_d[e].get(key, 0) >= v:
                return
            self.seen_d[e][key] = v
            self.eng[e].wait_ge(s, v)
        self.n_inst += 1

    def _deps(self, e, reads, writes):
        for b in reads:
            self._wait(e, b.w)
        for b in writes:
            self._wait(e, b.w)
            for p, idx in b.r.items():
                if p != e:
                    self._wait(e, ("e", p, idx))
            for ev in b.rd:
                self._wait(e, ev)

    def _mark(self, ev, reads, writes):
        for b in reads:
            if ev[0] == "e":
                b.r[ev[1]] = ev[2]
            else:
                b.rd.append(ev)
        for b in writes:
            b.w = ev
            b.r = {}
            b.rd = []

    def op(self, e, fn, reads=(), writes=()):
        self._deps(e, reads, writes)
        inst = fn(self.eng[e])
        self.cnt[e] += 1
        idx = self.cnt[e]
        s, _ = self._esem(e, idx)
        inst.then_inc(s, 1)
        self.n_inst += 1
        self._mark(("e", e, idx), reads, writes)

    def dma(self, q, out, in_, reads=(), writes=(), **kw):
        self._deps(q, reads, writes)
        slot = self.dsem[q][self.dptr[q]]
        key = (q, self.dptr[q])
        self.dptr[q] = (self.dptr[q] + 1) % NDSEM
        if slot[1] > 0:
            self._wait(q, ("d", slot[0], slot[1], key))
        inst = self.eng[q].dma_start(out=out, in_=in_, **kw)
        slot[1] += 16
        inst.then_inc(slot[0], 16)
        self.n_inst += 1
        self._mark(("d", slot[0], slot[1], key), reads, writes)

    def barrier(self):
        for p in self.ENGS:
            if p != "sp" and self.cnt[p] > 0:
                self._wait("sp", ("e", p, self.cnt[p]))
        for q in ("sp", "pool"):
            for i, slot in enumerate(self.dsem[q]):
                if slot[1] > 0:
                    self._wait("sp", ("d", slot[0], slot[1], (q, i)))
        self.bar_cnt += 1
        self.eng["sp"].sem_inc(self.bar_sem, 1)
        for e in self.ENGS:
            if e != "sp":
                self.eng[e].wait_ge(self.bar_sem, self.bar_cnt)
                for p in self.ENGS:
                    self.seen_e[e][p] = self.cnt[p]
                for q in ("sp", "pool"):
                    for i, slot in enumerate(self.dsem[q]):
                        self.seen_d[e][(q, i)] = slot[1]
        self.n_inst += 6


class Scope:
    def __init__(self, P):
        self.P = P
        self.st = ExitStack()

    def __enter__(self):
        self.st.__enter__()
        return self

    _uid = [0]

    def sb(self, name, shape, dt):
        Scope._uid[0] += 1
        return self.st.enter_context(self.P.nc.sbuf_tensor(f"{name}_{Scope._uid[0]}", list(shape), dt))

    def __exit__(self, *a):
        self.P.barrier()
        return self.st.__exit__(*a)


def rel_bucket_np(n):
    n = np.maximum(n, 0)
    exact = 16
    nf = np.maximum(n, 1).astype(np.float32)
    large = exact + (np.log(nf / np.float32(exact)) / np.float32(math.log(1024 / exact))
                     * np.float32(32 - exact)).astype(np.int32)
    large = np.minimum(large, 31)
    return np.where(n < exact, n, large)


def host_consts():
    cst = np.zeros((128, 512), np.float32)
    cst[:, 0:128] = np.eye(128)
    cst[:, 128:256] = np.eye(128)[::-1]
    i = np.arange(128)
    cst[:, 256:384] = (i[None, :] >= i[:, None]).astype(np.float32)
    cst[:, 384:512] = np.where(i[None, :] <= i[:, None], 0.0, -1e30)
    oh = np.zeros((33, XA + XB), np.float32)
    y = np.arange(XA)
    xx = y - 127
    b = np.where(xx < 0, 32, rel_bucket_np(xx))
    oh[b, y] = 1.0
    y = np.arange(XB)
    xx = y - 127
    b = np.where((xx < 0) | (xx >= 128), 32, rel_bucket_np(xx))
    oh[b, XA + y] = 1.0
    return cst, oh


def pack_small(inp):
    sp = np.zeros((128, NSP), np.float32)

    def fm(v):
        return np.ascontiguousarray(v.reshape(-1, 128).T)
    for l in range(4):
        sp[:, SP_ATTN + l * 16:SP_ATTN + (l + 1) * 16] = fm(inp["attn_norm"][l])
        sp[:, SP_FFN + l * 16:SP_FFN + (l + 1) * 16] = fm(inp["ffn_norm"][l])
        sp[:, SP_PLE + l * 16:SP_PLE + (l + 1) * 16] = fm(inp["ple_norm"][l])
        for k in range(3):
            c0 = SP_CONV + (l * 3 + k) * 86
            sp[:, c0:c0 + 86] = fm(inp["ffn_conv"][l, k])
    for e in range(2):
        sp[:, SP_CQ + e * 4:SP_CQ + e * 4 + 4] = fm(inp["a_cq_norm"][e])
        sp[:, SP_CKV + e * 2:SP_CKV + e * 2 + 2] = fm(inp["a_ckv_norm"][e])
        sp[:, SP_AQ + e * 2:SP_AQ + e * 2 + 2] = fm(inp["a_q_norm"][e])
        sp[:, SP_BQ + e] = np.tile(inp["b_q_norm"][e], 2)
        sp[:, SP_BK + e] = np.tile(inp["b_k_norm"][e], 2)
        sp[:, SP_CQN + e] = np.tile(inp["c_q_norm"][e], 2)
        sp[:, SP_CKN + e] = np.tile(inp["c_k_norm"][e], 2)
        sp[0:32, SP_FB + e] = inp["c_forget_bias"][e]
        sp[:, SP_SINK + e * 16:SP_SINK + (e + 1) * 16] = inp["b_sinks"][e][None, :]
    sp[0:32, SP_RELB:SP_RELB + 32] = inp["rel_bias"]
    sp[:, SP_B31:SP_B31 + 16] = inp["rel_bias"][31, 0:16][None, :]
    return sp


def token_local(E, l, parts):
    P, nc, Wd, db, gemm, simple_blocks, rms_finish, norm_x, resid_epi = (E[k] for k in (
        "P", "nc", "Wd", "db", "gemm", "simple_blocks", "rms_finish", "norm_x", "resid_epi"))
    xT, yT, spk, b_spk, pb, pb_b, halo, b_halo, p_in, ident_f, b_cst, wbuf, wb_b, wptr, wview = (E[k] for k in (
        "xT", "yT", "spk", "b_spk", "pb", "pb_b", "halo", "b_halo", "p_in", "ident_f", "b_cst", "wbuf", "wb_b", "wptr", "wview"))
    NWS = len(wbuf)
    flg, b_flg, cc_gather, hx_in, hxg, ones_bf = (E[k] for k in ("flg", "b_flg", "cc_gather", "hx_in", "hxg", "ones_bf"))
    for ps in range(1):
        t0 = 0
        with Scope(P) as so:
            hT = so.sb("hT", [128, 16, TT], BF16)
            hT_b = Buf("hT")
            hh = so.sb("hh", [128, 16, 2], BF16)
            hh_b = Buf("hh")

            def norm_scope(gcol, with_halo=False):
                with Scope(P) as sn:
                    S = dict(xs=sn.sb("xs", [128, 16, TC], F32), xs_b=Buf(), sq=sn.sb("sq", [128, 16, TC], BF16),
                             sq_b=Buf(), rstd=sn.sb("rstd", [128, TC], F32), rstd_b=Buf())
                    norm_x(S, hT, hT_b, gcol, t0)
                    if with_halo:
                        hxo = sn.sb("hxo", [128, 16, 2], F32)
                        hxo_b = Buf("hxo")
                        P.dma("sp", hxo[:], hxg.ap()[0:128, :].rearrange("p (kc t) -> p kc t", t=2), reads=[db("hxg")], writes=[hxo_b])
                        rms_finish(S, lambda c: hxo[:, c, :], hxo_b, 16, ones_bf, gcol, 1.0 / D,
                                   lambda c: hh[:, c, :], [hh_b], ncols=2)

            def mk_xr(sc):
                return dict(xr=[sc.sb(f"xr{i}", [128, TC], F32) for i in range(2)], xr_b=[Buf(), Buf()], xr_i=[0])

            if "out" in parts:
                with Scope(P) as s1:
                    S = mk_xr(s1)
                    P.dma("sp", hT[:], yT.ap()[:, t0:t0 + TT].rearrange("(kc p) t -> p kc t", p=128),
                          reads=[db("yT", 0)], writes=[hT_b])
                    wn = "w_out_even" if l % 2 == 0 else "w_out_odd"
                    gemm(hT, hT_b, 16, lambda c0, n: Wd[wn].ap()[l // 2, :, c0:c0 + n],
                         simple_blocks(0, D, 512), resid_epi(S, t0))
            if "ffn" in parts:
                with Scope(P) as sh:
                    hxs = sh.sb("hxs", [128, 16, 2], F32)
                    hxs_b = Buf("hxs")
                    P.dma("sp", hxs[:], xT.ap()[:, TO - 2:TO].rearrange("(kc p) t -> p kc t", p=128), reads=[db("xT", 1)], writes=[hxs_b])
                    P.dma("sp", hx_in.ap().rearrange("p (kc t) -> p kc t", t=2), hxs[:], reads=[hxs_b], writes=[db("hx_in")])
                cc_gather(hx_in, hxg, [db("hx_in")], [db("hxg")])
                norm_scope(SP_FFN + l * 16, with_halo=True)
                with Scope(P) as s2:
                    S = mk_xr(s2)
                    act = s2.sb("act", [128, NFC, TT], BF16)
                    act_b = Buf("act")
                    stg = {"g": s2.sb("sg", [128, TT + 2], F32), "u": s2.sb("su", [128, TT + 2], F32)}
                    stg_b = {"g": Buf("sg"), "u": Buf("su")}
                    cv = {"g": s2.sb("ga", [128, TT], F32), "u": s2.sb("ua", [128, TT], F32)}
                    cv_b = {"g": Buf("ga"), "u": Buf("ua")}

                    def up_pre(wv, wvb, coff, m, tag):
                        kind = tag[0]
                        for kc in range(16):
                            P.op("pe", lambda e: e.matmul(pb[6][:, 0:2], lhsT=wv[:, kc, coff:coff + m], rhs=hh[:, kc, :],
                                                          start=(kc == 0), stop=(kc == 15)),
                                 reads=[wvb, hh_b], writes=[pb_b[6]])
                        P.op("act", lambda e: e.activation(out=stg[kind][:, 0:2], in_=pb[6][:, 0:2], func=AF.Copy, scale=flg[:, 0:1]),
                             reads=[b_flg], writes=[pb_b[6], stg_b[kind]])

                    def conv_finish(kind, i):
                        c = i if kind == "g" else NFC + i
                        s_, sb_, a_, ab_ = stg[kind], stg_b[kind], cv[kind], cv_b[kind]
                        wc = [SP_CONV + (l * 3 + k) * 86 + c for k in range(3)]
                        P.op("act", lambda e: e.activation(out=a_[:], in_=s_[:, 2:TT + 2], func=AF.Copy,
                                                           scale=spk[:, wc[2]:wc[2] + 1]),
                             reads=[sb_, b_spk], writes=[ab_])
                        P.op("dve", lambda e: e.scalar_tensor_tensor(out=a_[:], in0=s_[:, 1:TT + 1], scalar=spk[:, wc[1]:wc[1] + 1],
                                                                     in1=a_[:], op0=ALU.mult, op1=ALU.add),
                             reads=[sb_, ab_, b_spk], writes=[ab_])
                        P.op("dve", lambda e: e.scalar_tensor_tensor(out=a_[:], in0=s_[:, 0:TT], scalar=spk[:, wc[0]:wc[0] + 1],
                                                                     in1=a_[:], op0=ALU.mult, op1=ALU.add),
                             reads=[sb_, ab_, b_spk], writes=[ab_])

                    def up_epi(bi, m, tag, tci):
                        kind, i = tag
                        c = i if kind == "g" else NFC + i
                        P.op("act", lambda e: e.activation(out=stg[kind][:, 2 + tci * TC:2 + (tci + 1) * TC], in_=pb[bi][:], func=AF.Copy),
                             reads=[], writes=[pb_b[bi], stg_b[kind]])
                        if tci == TT // TC - 1:
                            conv_finish(kind, i)
                            if kind == "u":
                                P.op("act", lambda e: e.activation(out=cv["g"][:], in_=cv["g"][:], func=AF.Silu),
                                     reads=[cv_b["g"]], writes=[cv_b["g"]])
                                P.op("dve", lambda e: e.tensor_tensor(out=act[:, i, :], in0=cv["g"][:], in1=cv["u"][:], op=ALU.mult),
                                     reads=[cv_b["g"], cv_b["u"]], writes=[act_b])
                    blocks = []
                    for i0 in range(0, NFC, 2):
                        npair = min(2, NFC - i0)
                        w = npair * 128
                        segs = [(i0 * 128, w), (DFF + i0 * 128, w)]
                        chunks = []
                        for j in range(npair):
                            chunks.append((j * 128, 128, ("g", i0 + j), 0))
                            chunks.append((w + j * 128, 128, ("u", i0 + j), 0))
                        blocks.append((segs, chunks))
                    gemm(hT, hT_b, 16, lambda c0, n: Wd["w_up"].ap()[l, :, c0:c0 + n], blocks, up_epi, pre=up_pre)
                    gemm(act, act_b, NFC, lambda c0, n: Wd["w_down"].ap()[l, :, c0:c0 + n],
                         simple_blocks(0, D, 256), resid_epi(S, t0))
            if "ple" in parts:
                norm_scope(SP_PLE + l * 16)
                with Scope(P) as s3:
                    S = mk_xr(s3)
                    pT = s3.sb("pT", [128, 2, TT], BF16)
                    pT_b = Buf("pT")
                    pl = [s3.sb(f"pl{i}", [128, 256], F32) for i in range(2)]
                    pl_b = [Buf(), Buf()]
                    sg = [s3.sb(f"sgt{i}", [128, TC], F32) for i in range(2)]
                    sg_b = [Buf(), Buf()]
                    for tt in range(TT // 128):
                        k = tt % 2
                        P.dma("sp", pl[k][:], p_in.ap()[l, t0 + tt * 128:t0 + (tt + 1) * 128, :], writes=[pl_b[k]])
                        for cc in range(2):
                            P.op("pe", lambda e: e.transpose(pb[5][:, cc * 128:(cc + 1) * 128], pl[k][:, cc * 128:(cc + 1) * 128], ident_f),
                                 reads=[pl_b[k], b_cst], writes=[pb_b[5]])
                        P.op("act", lambda e: e.activation(out=pT[:, :, tt * 128:(tt + 1) * 128],
                                                           in_=pb[5][:, 0:256].rearrange("p (a b) -> p a b", b=128), func=AF.Copy),
                             reads=[], writes=[pb_b[5], pT_b])
                    cnt = 0
                    for nb in range(D // 512):
                        sa = wptr[0]
                        sbb = (wptr[0] + 1) % NWS
                        wa = wview(sa, 16, 512)
                        wp = wview(sbb, 2, 512)
                        P.dma("pool", wa, Wd["w_ple_gate"].ap()[l, :, nb * 512:(nb + 1) * 512].rearrange("(kc p) n -> p kc n", p=128),
                              writes=[wb_b[sa]])
                        P.dma("pool", wp, Wd["w_ple_proj"].ap()[l, :, nb * 512:(nb + 1) * 512].rearrange("(kc p) n -> p kc n", p=128),
                              writes=[wb_b[sbb]])
                        for ci in range(4):
                            nchunk = nb * 4 + ci
                            for tci in range(TT // TC):
                                ba = cnt % 2
                                bb = 2 + cnt % 2
                                kx = cnt % 2
                                cnt += 1
                                ta = t0 + tci * TC
                                for kc in range(16):
                                    P.op("pe", lambda e: e.matmul(pb[ba][:], lhsT=wa[:, kc, ci * 128:(ci + 1) * 128],
                                                                  rhs=hT[:, kc, tci * TC:(tci + 1) * TC], start=(kc == 0), stop=(kc == 15)),
                                         reads=[wb_b[sa], hT_b], writes=[pb_b[ba]])
                                for kc in range(2):
                                    P.op("pe", lambda e: e.matmul(pb[bb][:], lhsT=wp[:, kc, ci * 128:(ci + 1) * 128],
                                                                  rhs=pT[:, kc, tci * TC:(tci + 1) * TC], start=(kc == 0), stop=(kc == 1)),
                                         reads=[wb_b[sbb], pT_b], writes=[pb_b[bb]])
                                P.op("act", lambda e: e.activation(out=sg[kx][:], in_=pb[ba][:], func=AF.Sigmoid),
                                     reads=[], writes=[pb_b[ba], sg_b[kx]])
                                P.op("dve", lambda e: e.tensor_tensor(out=sg[kx][:], in0=sg[kx][:], in1=pb[bb][:], op=ALU.mult),
                                     reads=[sg_b[kx]], writes=[pb_b[bb], sg_b[kx]])
                                xr, xr_b = S["xr"][kx], S["xr_b"][kx]
                                P.dma("sp", xr[:], xT.ap()[nchunk * 128:(nchunk + 1) * 128, ta:ta + TC],
                                      reads=[db("xT", ta // TC)], writes=[xr_b])
                                P.op("dve", lambda e: e.tensor_tensor(out=xr[:], in0=sg[kx][:], in1=xr[:], op=ALU.add),
                                     reads=[sg_b[kx], xr_b], writes=[xr_b])
                                P.dma("sp", xT.ap()[nchunk * 128:(nchunk + 1) * 128, ta:ta + TC], xr[:],
                                      reads=[xr_b], writes=[db("xT", ta // TC)])
                        wptr[0] = (wptr[0] + 2) % NWS


def even_attention(E, l):
    e = l // 2
    P, nc, Wd, db, gemm, simple_blocks, norm_x = (E[k] for k in ("P", "nc", "Wd", "db", "gemm", "simple_blocks", "norm_x"))
    spk, b_spk, pb, pb_b, pbh, pbh_b, ident_f, ident_bf, b_cst, b_const, ones_bf, bd_bf, jflip, cneg, eps_t = (E[k] for k in (
        "spk", "b_spk", "pb", "pb_b", "pbh", "pbh_b", "ident_f", "ident_bf", "b_cst", "b_const", "ones_bf", "bd_bf", "jflip", "cneg", "eps_t"))
    xT, yT, oh_in = E["xT"], E["yT"], E["oh_in"]
    s_kvT, s_kvtok, s_kidxT, s_widx, s_qiT, s_qaT, s_qbT, s_kdupT, s_vbtok, s_vrow = (E[k] for k in (
        "s_kvT", "s_kvtok", "s_kidxT", "s_widx", "s_qiT", "s_qaT", "s_qbT", "s_kdupT", "s_vbtok", "s_vrow"))
    XT = XA + XB
    flg, b_flg, cc_gather, kpack_e, gk_e = (E[k] for k in ("flg", "b_flg", "cc_gather", "kpack_e", "gk_e"))

    if not E["state"].get("vrow"):
        E["state"]["vrow"] = True
        with Scope(P) as sv:
            ohs = sv.sb("ohs", [33, XT], F32)
            ohs_b = Buf()
            vr = sv.sb("vr", [16, XT], F32)
            vr_b = Buf()
            P.dma("sp", ohs[:], oh_in.ap(), writes=[ohs_b])
            for (hc, x0, x1) in [(0, 0, 512), (0, 512, 1024), (0, 1024, XA), (16, XA, XT)]:
                P.op("pe", lambda en: en.matmul(pb[5][0:16, 0:x1 - x0], lhsT=spk[0:33, SP_RELB + hc:SP_RELB + hc + 16],
                                                rhs=ohs[:, x0:x1], start=True, stop=True),
                     reads=[ohs_b, b_spk], writes=[pb_b[5]])
                P.op("act", lambda en: en.activation(out=vr[:, x0:x1], in_=pb[5][0:16, 0:x1 - x0], func=AF.Copy),
                     reads=[], writes=[pb_b[5], vr_b])
            P.dma("sp", s_vrow.ap(), vr[:], reads=[vr_b], writes=[db("vrow")])

    def rms_grp(S, srcs, src_b, lhsT_ones, gcol, inv_n, dsts, dst_bufs, ncols=TC):
        C = len(srcs)
        sq, sq_b, rstd, rstd_b = S["sq"], S["sq_b"], S["rstd"], S["rstd_b"]
        for c in range(C):
            P.op("act", lambda en: en.activation(out=sq[:, c, 0:ncols], in_=srcs[c], func=AF.Square),
                 reads=[src_b], writes=[sq_b])
        for c in range(C):
            P.op("pe", lambda en: en.matmul(pb[4][:, 0:ncols], lhsT=lhsT_ones[:, :], rhs=sq[:, c, 0:ncols],
                                            start=(c == 0), stop=(c == C - 1)),
                 reads=[sq_b, b_const], writes=[pb_b[4]])
        P.op("act", lambda en: en.activation(out=rstd[:, 0:ncols], in_=pb[4][:, 0:ncols], func=AF.Sqrt,
                                             scale=inv_n, bias=eps_t[:, 0:1]),
             reads=[b_const], writes=[pb_b[4], rstd_b])
        P.op("dve", lambda en: en.reciprocal(out=rstd[:, 0:ncols], in_=rstd[:, 0:ncols]), reads=[rstd_b], writes=[rstd_b])
        for c in range(C):
            P.op("dve", lambda en: en.scalar_tensor_tensor(out=dsts[c], in0=srcs[c], scalar=spk[:, gcol + c:gcol + c + 1],
                                                           in1=rstd[:, 0:ncols], op0=ALU.mult, op1=ALU.mult),
                 reads=[src_b, rstd_b, b_spk], writes=dst_bufs)
    E["rms_grp"] = rms_grp

    for ps in range(0 if E["cfg"].get("skip_proj") else 1):
        t0 = 0
        with Scope(P) as so:
            hT = so.sb("hT", [128, 16, TT], BF16)
            hT_b = Buf("hT")
            with Scope(P) as sn:
                S0 = dict(xs=sn.sb("xs", [128, 16, TC], F32), xs_b=Buf(), sq=sn.sb("sq", [128, 16, TC], BF16),
                          sq_b=Buf(), rstd=sn.sb("rstd", [128, TC], F32), rstd_b=Buf())
                norm_x(S0, hT, hT_b, SP_ATTN + l * 16, t0)
            S = dict(sq=so.sb("sq", [128, 4, TC], BF16), sq_b=Buf(), rstd=so.sb("rstd", [128, TC], F32), rstd_b=Buf())
            stg4 = so.sb("stg4", [128, 4, TT], F32)
            stg4_b = Buf("stg4")
            stg1 = so.sb("stg1", [128, TT], F32)
            stg1_b = Buf("stg1")
            stg2 = so.sb("stg2", [128, 2, TT], F32)
            stg2_b = Buf("stg2")
            cqT = so.sb("cqT", [128, 4, TT], BF16)
            cqT_b = Buf("cqT")
            kvn = so.sb("kvn", [128, 2, TT], BF16)
            kvn_b = Buf("kvn")
            kvtok_st = so.sb("kvtok_st", [128, 8, 256], BF16)
            kvtok_b = Buf()
            kidx_st = so.sb("kidx_st", [64, TT], BF16)
            kidx_b = Buf()
            widx_st = so.sb("widx_st", [16, TT], F32)
            widx_b = Buf()
            widx_tok = so.sb("widx_tok", [128, 8, 16], F32)
            widx_tok_b = Buf()
            ob = so.sb("ob", [128, TT], BF16)
            ob_b = Buf("ob")
            ob2 = so.sb("ob2", [128, 2, TT], BF16)
            ob2_b = Buf("ob2")
            oq = [so.sb(f"oq{i}", [128, TC], BF16) for i in range(2)]
            oq_b = [Buf(), Buf()]
            oq_i = [0]
            vb_st = so.sb("vb_st", [128, TT], BF16)
            vb_b = Buf()
            vtok_st = so.sb("vtok_st", [128, 8, 128], BF16)
            vtok_b = Buf()

            def tsl(tci):
                return slice(tci * TC, (tci + 1) * TC)

            def in_epi(bi, m, tag, tci):
                kind = tag[0]
                last = (tci == TT // TC - 1)
                if kind == "cq":
                    c = tag[1]
                    P.op("act", lambda en: en.activation(out=stg4[:, c, tsl(tci)], in_=pb[bi][:], func=AF.Copy),
                         reads=[], writes=[pb_b[bi], stg4_b])
                    if c == 3 and last:
                        for t2 in range(TT // TC):
                            rms_grp(S, [stg4[:, cc, tsl(t2)] for cc in range(4)], stg4_b, ones_bf, SP_CQ + e * 4, 1.0 / 512,
                                    [cqT[:, cc, tsl(t2)] for cc in range(4)], [cqT_b])
                elif kind == "ckv":
                    c = tag[1]
                    P.op("act", lambda en: en.activation(out=stg2[:, c, tsl(tci)], in_=pb[bi][:], func=AF.Copy),
                         reads=[], writes=[pb_b[bi], stg2_b])
                    if c == 1 and last:
                        for t2 in range(TT // TC):
                            rms_grp(S, [stg2[:, cc, tsl(t2)] for cc in range(2)], stg2_b, ones_bf, SP_CKV + e * 2, 1.0 / 256,
                                    [kvn[:, cc, tsl(t2)] for cc in range(2)], [kvn_b])
                        P.dma("sp", s_kvT.ap()[:, TO:TO + TT].rearrange("(c p) t -> p c t", p=128), kvn[:],
                              reads=[kvn_b], writes=[db("kvT")])
                        P.dma("sp", kpack_e.ap()[0:256, :].rearrange("(c p) t -> p c t", p=128), kvn[:],
                              reads=[kvn_b], writes=[db("kpack_e")])
                        for tt in range(TT // 128):
                            for cc in range(2):
                                P.op("pe", lambda en: en.transpose(pbh[:, cc * 128:(cc + 1) * 128], kvn[:, cc, tt * 128:(tt + 1) * 128], ident_bf[:]),
                                     reads=[kvn_b, b_const], writes=[pbh_b])
                            P.op("act", lambda en: en.activation(out=kvtok_st[:, tt, :], in_=pbh[:, 0:256], func=AF.Copy),
                                 reads=[], writes=[pbh_b, kvtok_b])
                        P.dma("sp", s_kvtok.ap()[TO:TO + TT, :].rearrange("(tt p) c -> p tt c", p=128), kvtok_st[:],
                              reads=[kvtok_b], writes=[db("kvtok")])
                        P.dma("sp", kpack_e.ap()[576:832, :].rearrange("r (a c) -> (r a) c", c=256).rearrange("(tt p) c -> p tt c", p=128), kvtok_st[:],
                              reads=[kvtok_b], writes=[db("kpack_e")])
                elif kind == "kidx":
                    P.op("act", lambda en: en.activation(out=kidx_st[:, tsl(tci)], in_=pb[bi][0:64, :], func=AF.Copy),
                         reads=[], writes=[pb_b[bi], kidx_b])
                    if last:
                        P.dma("sp", s_kidxT.ap()[:, TO:TO + TT], kidx_st[:], reads=[kidx_b], writes=[db("kidxT")])
                        P.dma("sp", kpack_e.ap()[256:320, :], kidx_st[:], reads=[kidx_b], writes=[db("kpack_e")])
                elif kind == "widx":
                    P.op("act", lambda en: en.activation(out=widx_st[:, tsl(tci)], in_=pb[bi][0:16, :], func=AF.Copy),
                         reads=[], writes=[pb_b[bi], widx_b])
                    if last:
                        for tt in range(TT // 128):
                            P.op("pe", lambda en: en.transpose(pb[5][:, tt * 16:(tt + 1) * 16], widx_st[0:16, tt * 128:(tt + 1) * 128], ident_f[0:16, 0:16]),
                                 reads=[widx_b, b_cst], writes=[pb_b[5]])
                        P.op("act", lambda en: en.activation(out=widx_tok[:], in_=pb[5][:, 0:128].rearrange("p (a b) -> p a b", b=16), func=AF.Copy),
                             reads=[], writes=[pb_b[5], widx_tok_b])
                        P.dma("sp", s_widx.ap()[t0:t0 + TT, :].rearrange("(tt p) c -> p tt c", p=128), widx_tok[:],
                              reads=[widx_tok_b], writes=[db("widx")])
                elif kind == "qb":
                    c = tag[1]
                    P.op("act", lambda en: en.activation(out=stg1[:, tsl(tci)], in_=pb[bi][:], func=AF.Copy),
                         reads=[], writes=[pb_b[bi], stg1_b])
                    if last:
                        for t2 in range(TT // TC):
                            rms_grp(S, [stg1[:, tsl(t2)]], stg1_b, bd_bf, SP_BQ + e, 1.0 / 64, [ob[:, tsl(t2)]], [ob_b])
                        P.dma("sp", s_qbT.ap()[c * 128:(c + 1) * 128, t0:t0 + TT], ob[:], reads=[ob_b], writes=[db("qbT")])
                elif kind == "kb":
                    g, half = tag[1], tag[2]
                    pbs = half * 64
                    P.op("act", lambda en: en.activation(out=stg1[pbs:pbs + 64, tsl(tci)], in_=pb[bi][pbs:pbs + 64, :], func=AF.Copy),
                         reads=[], writes=[pb_b[bi], stg1_b])
                    if half == 1 and last:
                        for t2 in range(TT // TC):
                            rms_grp(S, [stg1[:, tsl(t2)]], stg1_b, bd_bf, SP_BK + e, 1.0 / 64, [ob[:, tsl(t2)]], [ob_b])
                        P.dma("sp", s_kdupT.ap()[g * 128:(g + 1) * 128, TO:TO + TT], ob[:], reads=[ob_b], writes=[db("kdupT")])
                        P.dma("sp", kpack_e.ap()[320 + g * 128:320 + (g + 1) * 128, :], ob[:], reads=[ob_b], writes=[db("kpack_e")])
                elif kind == "vb":
                    P.op("act", lambda en: en.activation(out=vb_st[:, tsl(tci)], in_=pb[bi][:], func=AF.Copy),
                         reads=[], writes=[pb_b[bi], vb_b])
                    if last:
                        for tt in range(TT // 128):
                            P.op("pe", lambda en: en.transpose(pbh[:, (tt % 4) * 128:(tt % 4 + 1) * 128], vb_st[:, tt * 128:(tt + 1) * 128], ident_bf[:]),
                                 reads=[vb_b, b_const], writes=[pbh_b])
                            if tt % 4 == 3:
                                P.op("act", lambda en: en.activation(out=vtok_st[:, tt - 3:tt + 1, :], in_=pbh[:, 0:512].rearrange("p (a b) -> p a b", b=128), func=AF.Copy),
                                     reads=[], writes=[pbh_b, vtok_b])
                        P.dma("sp", s_vbtok.ap()[TO:TO + TT, :].rearrange("(tt p) c -> p tt c", p=128), vtok_st[:],
                              reads=[vtok_b], writes=[db("vbtok")])
                        P.dma("sp", kpack_e.ap()[832:960, :].rearrange("r (a c) -> (r a) c", c=128).rearrange("(tt p) c -> p tt c", p=128), vtok_st[:],
                              reads=[vtok_b], writes=[db("kpack_e")])
                elif kind == "qa":
                    h, cc = tag[1], tag[2]
                    P.op("act", lambda en: en.activation(out=stg2[:, cc, tsl(tci)], in_=pb[bi][:], func=AF.Copy),
                         reads=[], writes=[pb_b[bi], stg2_b])
                    if cc == 1 and last:
                        for t2 in range(TT // TC):
                            rms_grp(S, [stg2[:, c2, tsl(t2)] for c2 in range(2)], stg2_b, ones_bf, SP_AQ + e * 2, 1.0 / 256,
                                    [ob2[:, c2, tsl(t2)] for c2 in range(2)], [ob2_b])
                        P.dma("sp", s_qaT.ap()[h * 256:(h + 1) * 256, t0:t0 + TT].rearrange("(c p) t -> p c t", p=128), ob2[:],
                              reads=[ob2_b], writes=[db("qaT")])
                elif kind == "qi":
                    c = tag[1]
                    k = oq_i[0]
                    oq_i[0] = (k + 1) % 2
                    P.op("act", lambda en: en.activation(out=oq[k][:], in_=pb[bi][:], func=AF.Copy),
                         reads=[], writes=[pb_b[bi], oq_b[k]])
                    P.dma("sp", s_qiT.ap()[c * 128:(c + 1) * 128, t0 + tci * TC:t0 + (tci + 1) * TC], oq[k][:],
                          reads=[oq_b[k]], writes=[db("qiT")])

            blocks = [
                ([(0, 512)], [(c * 128, 128, ("cq", c), 0) for c in range(4)]),
                ([(512, 336)], [(0, 128, ("ckv", 0), 0), (128, 128, ("ckv", 1), 0), (256, 64, ("kidx",), 0), (320, 16, ("widx",), 0)]),
                ([(848, 512)], [(c * 128, 128, ("qb", c), 0) for c in range(4)]),
                ([(1360, 512)], [(c * 128, 128, ("qb", 4 + c), 0) for c in range(4)]),
                ([(1872, 256)], [(0, 64, ("kb", 0, 0), 0), (0, 64, ("kb", 0, 1), 64), (64, 64, ("kb", 1, 0), 0), (64, 64, ("kb", 1, 1), 64),
                                 (128, 128, ("vb",), 0)]),
            ]
            gemm(hT, hT_b, 16, lambda c0, n: Wd["w_in_even"].ap()[e, :, c0:c0 + n], blocks, in_epi)
            blocks = []
            for b4 in range(2):
                blocks.append(([(b4 * 2048, 2048)], [((hh * 2 + cc) * 128, 128, ("qa", b4 * 8 + hh, cc), 0) for hh in range(8) for cc in range(2)]))
            gemm(cqT, cqT_b, 4, lambda c0, n: Wd["a_w_uq"].ap()[e, :, c0:c0 + n], blocks, in_epi)
            gemm(cqT, cqT_b, 4, lambda c0, n: Wd["a_w_qidx"].ap()[e, :, c0:c0 + n],
                 [([(0, 1024)], [(c * 128, 128, ("qi", c), 0) for c in range(8)])], in_epi)

    if not E["cfg"].get("skip_proj"):
        cc_gather(kpack_e, gk_e, [db("kpack_e")], [db("gk_e")])
        P.dma("sp", s_kvT.ap()[:, 0:TO], gk_e.ap()[0:256, :], reads=[db("gk_e")], writes=[db("kvT")])
        P.dma("sp", s_kidxT.ap()[:, 0:TO], gk_e.ap()[256:320, :], reads=[db("gk_e")], writes=[db("kidxT")])
        P.dma("sp", s_kdupT.ap()[:, 0:TO], gk_e.ap()[320:576, :], reads=[db("gk_e")], writes=[db("kdupT")])
        P.dma("sp", s_kvtok.ap()[0:TO, :], gk_e.ap()[576:832, :].rearrange("r (a c) -> (r a) c", c=256), reads=[db("gk_e")], writes=[db("kvtok")])
        P.dma("sp", s_vbtok.ap()[0:TO, :], gk_e.ap()[832:960, :].rearrange("r (a c) -> (r a) c", c=128), reads=[db("gk_e")], writes=[db("vbtok")])
    if E["cfg"].get("stop_after_proj"):
        return
    att_scale = 1.0 / 16.0
    with Scope(P) as sa:
        kvT = sa.sb("kvT", [128, 2, T], BF16)
        kvtok = sa.sb("kvtok", [128, 16, 256], BF16)
        kidx2 = sa.sb("kidx2", [128, T], BF16)
        wuv = sa.sb("wuv", [128, 16, 2, 64], BF16)
        expA = sa.sb("expA", [128, 16, 9, 128], BF16)
        b_k = Buf("kside")
        b_exp = Buf("expA")
        P.dma("sp", kvT[:], s_kvT.ap().rearrange("(c p) t -> p c t", p=128), reads=[db("kvT")], writes=[b_k])
        P.dma("sp", kvtok[:], s_kvtok.ap().rearrange("(tt p) c -> p tt c", p=128), reads=[db("kvtok")], writes=[b_k])
        P.dma("sp", kidx2[0:64, :], s_kidxT.ap(), reads=[db("kidxT")], writes=[b_k])
        P.dma("sp", kidx2[64:128, :], s_kidxT.ap(), reads=[db("kidxT")], writes=[b_k])
        P.dma("pool", wuv[:], Wd["a_w_uv"].ap()[e].rearrange("h (cc p) d -> p h cc d", p=128), writes=[b_k])
        hk = [sa.sb(f"hk{i}", [128, 128], F32) for i in range(2)]
        hk_b = [Buf(), Buf()]
        n = 0
        for h in range(16):
            for dj in range(9):
                k = n % 2
                n += 1
                P.dma("sp", hk[k][:], bass.AP(s_vrow, h * XT + dj * 128, [[1, 128], [1, 128]]), reads=[db("vrow")], writes=[hk_b[k]])
                P.op("pe", lambda en: en.matmul(pb[5][:, 0:128], lhsT=jflip, rhs=hk[k][:], start=True, stop=True),
                     reads=[hk_b[k], b_cst], writes=[pb_b[5]])
                P.op("act", lambda en: en.activation(out=expA[:, h, 8 - dj, :], in_=pb[5][:, 0:128], func=AF.Exp),
                     reads=[], writes=[pb_b[5], b_exp])
        qi = [sa.sb(f"qi{i}", [128, 8, 128], BF16) for i in range(2)]
        qa = [sa.sb(f"qa{i}", [128, 32, 128], BF16) for i in range(2)]
        wq = [sa.sb(f"wq{i}", [128, 16], F32) for i in range(2)]
        q_b = [Buf(), Buf()]
        score = sa.sb("score", [128, T], F32)
        score_b = Buf("score")
        work = sa.sb("work", [128, T], F32)
        work_b = Buf("work")
        m8 = sa.sb("m8", [128, 8], F32)
        m8_b = Buf("m8")
        mask01 = sa.sb("mask01", [128, T], BF16)
        mask01_b = Buf()
        maskT = sa.sb("maskT", [128, 16, 128], BF16)
        maskT_b = Buf()
        rl = [sa.sb(f"rl{i}", [128, 512], F32) for i in range(2)]
        rl_b = [Buf(), Buf()]
        NBD = 3
        LBD = [0, 1, 4]
        pf = [sa.sb(f"pf{i}", [128, 512], F32) for i in range(NBD)]
        pf_b = [Buf() for _ in range(NBD)]
        pbf = [sa.sb(f"pbf{i}", [128, 512], BF16) for i in range(NBD)]
        pbf_b = [Buf() for _ in range(NBD)]
        rc = sa.sb("rc", [128, 128], F32)
        rc_b = Buf()
        on = sa.sb("on", [128, 2, 128], BF16)
        on_b = Buf()
        ya_st = [sa.sb(f"ya_st{i}", [128, 8, 128], BF16) for i in range(2)]
        ya_b = [Buf(), Buf()]
        cnt = [0, 0]
        for a_ in range(E["cfg"].get("dsa_tiles", 8)):
            i = 8 + a_
            qk = i % 2
            N = (i + 1) * 128
            qs = slice(a_ * 128, (a_ + 1) * 128)
            ks = slice(i * 128, (i + 1) * 128)
            P.dma("sp", qi[qk][:], s_qiT.ap()[:, qs].rearrange("(c p) t -> p c t", p=128), reads=[db("qiT")], writes=[q_b[qk]])
            P.dma("sp", qa[qk][:], s_qaT.ap()[:, qs].rearrange("(c p) t -> p c t", p=128), reads=[db("qaT")], writes=[q_b[qk]])
            P.dma("sp", wq[qk][:], s_widx.ap()[qs, :], reads=[db("widx")], writes=[q_b[qk]])
            for h in range(16):
                pbs = (h % 2) * 64
                for n0 in range(0, N, 512):
                    n1 = min(N, n0 + 512)
                    bk = cnt[0] % 2
                    cnt[0] += 1
                    P.op("pe", lambda en: en.matmul(pb[bk][:, 0:n1 - n0], lhsT=qi[qk][pbs:pbs + 64, h // 2, :], rhs=kidx2[pbs:pbs + 64, n0:n1],
                                                    start=True, stop=True),
                         reads=[q_b[qk], b_k], writes=[pb_b[bk]])
                    P.op("act", lambda en: en.activation(out=rl[bk][:, 0:n1 - n0], in_=pb[bk][:, 0:n1 - n0], func=AF.Relu),
                         reads=[], writes=[pb_b[bk], rl_b[bk]])
                    if h == 0:
                        P.op("dve", lambda en: en.tensor_scalar(out=score[:, n0:n1], in0=rl[bk][:, 0:n1 - n0], scalar1=wq[qk][:, 0:1], scalar2=None, op0=ALU.mult),
                             reads=[rl_b[bk], q_b[qk]], writes=[score_b])
                    else:
                        P.op("dve", lambda en: en.scalar_tensor_tensor(out=score[:, n0:n1], in0=rl[bk][:, 0:n1 - n0], scalar=wq[qk][:, h:h + 1],
                                                                       in1=score[:, n0:n1], op0=ALU.mult, op1=ALU.add),
                             reads=[rl_b[bk], q_b[qk], score_b], writes=[score_b])
            P.op("dve", lambda en: en.tensor_tensor(out=score[:, ks], in0=score[:, ks], in1=cneg, op=ALU.add),
                 reads=[score_b, b_cst], writes=[score_b])
            P.op("dve", lambda en: en.tensor_scalar(out=score[:, 0:TO], in0=score[:, 0:TO], scalar1=flg[:, 1:2], scalar2=None, op0=ALU.add),
                 reads=[score_b, b_flg], writes=[score_b])
            if i >= 2:
                cur, cur_b = score, score_b
                for it in range(32):
                    P.op("dve", lambda en: en.max(out=m8[:], in_=cur[:, 0:N]), reads=[cur_b], writes=[m8_b])
                    if it < 31:
                        P.op("dve", lambda en: en.match_replace(out=work[:, 0:N], in_to_replace=m8[:], in_values=cur[:, 0:N], imm_value=-1e30),
                             reads=[m8_b, cur_b], writes=[work_b])
                        cur, cur_b = work, work_b
                P.op("dve", lambda en: en.tensor_scalar(out=work[:, 0:N], in0=score[:, 0:N], scalar1=m8[:, 7:8], scalar2=None, op0=ALU.is_ge),
                     reads=[score_b, m8_b], writes=[work_b])
                P.op("dve", lambda en: en.scalar_tensor_tensor(out=mask01[:, 0:N], in0=score[:, 0:N], scalar=-1e29, in1=work[:, 0:N],
                                                               op0=ALU.is_gt, op1=ALU.mult),
                     reads=[score_b, work_b], writes=[mask01_b])
            else:
                P.op("dve", lambda en: en.tensor_scalar(out=mask01[:, 0:N], in0=score[:, 0:N], scalar1=-1e29, scalar2=None, op0=ALU.is_ge),
                     reads=[score_b], writes=[mask01_b])
            for j0 in range(0, i + 1, 8):
                j1 = min(i + 1, j0 + 8)
                for j in range(j0, j1):
                    P.op("pe", lambda en: en.transpose(pbh[:, (j - j0) * 128:(j - j0 + 1) * 128], mask01[:, j * 128:(j + 1) * 128], ident_bf[:]),
                         reads=[mask01_b, b_const], writes=[pbh_b])
                P.op("act", lambda en: en.activation(out=maskT[:, j0:j1, :], in_=pbh[:, 0:(j1 - j0) * 128].rearrange("p (a b) -> p a b", b=128), func=AF.Copy),
                     reads=[], writes=[pbh_b, maskT_b])
            yk = i % 2
            items = [(h, jg) for h in range(16) for jg in range(0, i + 1, 4)]

            def emit_logits(k):
                h, jg = items[k]
                je = min(i + 1, jg + 4)
                L = k % NBD
                BK = LBD[L]
                for j in range(jg, je):
                    sl = j - jg
                    for c in range(2):
                        P.op("pe", lambda en: en.matmul(pb[BK][:, sl * 128:(sl + 1) * 128], lhsT=kvT[:, c, j * 128:(j + 1) * 128],
                                                        rhs=qa[qk][:, 2 * h + c, :], start=(c == 0), stop=(c == 1)),
                             reads=[b_k, q_b[qk]], writes=[pb_b[BK]])

            def emit_post(k):
                h, jg = items[k]
                je = min(i + 1, jg + 4)
                nj = je - jg
                L = k % NBD
                BK = LBD[L]
                far = (i - (je - 1)) >= 8
                if far:
                    P.op("act", lambda en: en.activation(out=pf[L][:, 0:nj * 128], in_=pb[BK][:, 0:nj * 128], func=AF.Exp, scale=att_scale,
                                                         bias=spk[:, SP_B31 + h:SP_B31 + h + 1]),
                         reads=[b_spk], writes=[pb_b[BK], pf_b[L]])
                else:
                    P.op("act", lambda en: en.activation(out=pf[L][:, 0:nj * 128], in_=pb[BK][:, 0:nj * 128], func=AF.Exp, scale=att_scale),
                         reads=[], writes=[pb_b[BK], pf_b[L]])
                    if i - jg <= 8:
                        k0 = 8 - (i - jg)
                        P.op("dve", lambda en: en.tensor_tensor(out=pf[L][:, 0:nj * 128].rearrange("p (a b) -> p a b", b=128),
                                                                in0=pf[L][:, 0:nj * 128].rearrange("p (a b) -> p a b", b=128),
                                                                in1=expA[:, h, k0:k0 + nj, :], op=ALU.mult),
                             reads=[pf_b[L], b_exp], writes=[pf_b[L]])
                    else:
                        for j in range(jg, je):
                            sl = j - jg
                            kk = 8 - min(i - j, 8)
                            P.op("dve", lambda en: en.tensor_tensor(out=pf[L][:, sl * 128:(sl + 1) * 128], in0=pf[L][:, sl * 128:(sl + 1) * 128],
                                                                    in1=expA[:, h, kk, :], op=ALU.mult),
                                 reads=[pf_b[L], b_exp], writes=[pf_b[L]])
                P.op("pool", lambda en: en.tensor_tensor(out=pbf[L][:, 0:nj * 128].rearrange("p (a b) -> p a b", b=128),
                                                         in0=pf[L][:, 0:nj * 128].rearrange("p (a b) -> p a b", b=128),
                                                         in1=maskT[:, jg:je, :], op=ALU.mult),
                     reads=[pf_b[L], maskT_b], writes=[pbf_b[L]])

            def emit_pv(k):
                h, jg = items[k]
                je = min(i + 1, jg + 4)
                L = k % NBD
                for j in range(jg, je):
                    sl = j - jg
                    for (bk, lh) in ((2, kvtok[:, j, 0:128]), (3, kvtok[:, j, 128:256]), (6, ones_bf[:, :])):
                        P.op("pe", lambda en: en.matmul(pb[bk][:, 0:128], lhsT=lh, rhs=pbf[L][:, sl * 128:(sl + 1) * 128],
                                                        start=(j == 0), stop=(j == i)),
                             reads=[b_k, pbf_b[L], b_const], writes=[pb_b[bk]])

            def emit_fin_dve(h):
                P.op("dve", lambda en: en.reciprocal(out=rc[:], in_=pb[6][:, 0:128]), reads=[], writes=[pb_b[6], rc_b])
                for c in range(2):
                    P.op("dve", lambda en: en.tensor_tensor(out=on[:, c, :], in0=pb[2 + c][:, 0:128], in1=rc[:], op=ALU.mult),
                         reads=[rc_b], writes=[pb_b[2 + c], on_b])

            def emit_fin_pe(h):
                pbs = (h % 2) * 64
                col = ((h // 2) % 4) * 128
                for c in range(2):
                    P.op("pe", lambda en: en.matmul(pb[5][pbs:pbs + 64, col:col + 128], lhsT=wuv[:, h, c, :], rhs=on[:, c, :],
                                                    start=(c == 0), stop=(c == 1)),
                         reads=[b_k, on_b], writes=[pb_b[5]])
                if h % 2 == 1:
                    P.op("act", lambda en: en.activation(out=ya_st[yk][:, h // 2, :], in_=pb[5][:, col:col + 128], func=AF.Copy),
                         reads=[], writes=[pb_b[5], ya_b[yk]])

            for k0 in range(min(NBD - 1, len(items))):
                emit_logits(k0)
            for k in range(len(items)):
                h, jg = items[k]
                if k + NBD - 1 < len(items):
                    emit_logits(k + NBD - 1)
                emit_post(k)
                emit_pv(k)
                if jg + 4 > i:
                    emit_fin_dve(h)
                    emit_fin_pe(h)
            P.dma("sp", yT.ap()[0:1024, qs].rearrange("(c p) t -> p c t", p=128), ya_st[yk][:], reads=[ya_b[yk]], writes=[db("yT", 0)])

    with Scope(P) as sw:
        kd = sw.sb("kd", [128, 2, T], BF16)
        vtk = sw.sb("vtk", [128, 16, 128], BF16)
        expB = sw.sb("expB", [128, 16, 2, 128], BF16)
        esk = sw.sb("esk", [128, 16], F32)
        b_k = Buf("kside")
        b_exp = Buf("expB")
        P.dma("sp", kd[:], s_kdupT.ap().rearrange("(g p) t -> p g t", p=128), reads=[db("kdupT")], writes=[b_k])
        P.dma("sp", vtk[:], s_vbtok.ap().rearrange("(tt p) c -> p tt c", p=128), reads=[db("vbtok")], writes=[b_k])
        P.op("act", lambda en: en.activation(out=esk[:], in_=spk[:, SP_SINK + e * 16:SP_SINK + (e + 1) * 16], func=AF.Exp),
             reads=[b_spk], writes=[b_exp])
        hk = [sw.sb(f"hk{i}", [128, 128], F32) for i in range(2)]
        hk_b = [Buf(), Buf()]
        n = 0
        for hb in range(16):
            for kx in range(2):
                dj = 1 - kx
                k = n % 2
                n += 1
                P.dma("sp", hk[k][:], bass.AP(s_vrow, hb * XT + XA + dj * 128, [[1, 128], [1, 128]]), reads=[db("vrow")], writes=[hk_b[k]])
                P.op("pe", lambda en: en.matmul(pb[5][:, 0:128], lhsT=jflip, rhs=hk[k][:], start=True, stop=True),
                     reads=[hk_b[k], b_cst], writes=[pb_b[5]])
                P.op("act", lambda en: en.activation(out=expB[:, hb, kx, :], in_=pb[5][:, 0:128], func=AF.Exp),
                     reads=[], writes=[pb_b[5], b_exp])
        qb = [sw.sb(f"qb{i}", [128, 8, 128], BF16) for i in range(2)]
        qb_b = [Buf(), Buf()]
        pf = [sw.sb(f"pf{i}", [128, 512], F32) for i in range(2)]
        pf_b = [Buf(), Buf()]
        pbf = [sw.sb(f"pbf{i}", [128, 512], BF16) for i in range(2)]
        pbf_b = [Buf(), Buf()]
        dn = sw.sb("dn", [128, 128], F32)
        dn_b = Buf()
        yb_st = [sw.sb(f"yb_st{i}", [128, 8, 128], BF16) for i in range(2)]
        yb_b = [Buf(), Buf()]
        cnt = 0
        for a_ in range(E["cfg"].get("swa_blocks", 8)):
            nb = 8 + a_
            qk = nb % 2
            qs = slice(a_ * 128, (a_ + 1) * 128)
            P.dma("sp", qb[qk][:], s_qbT.ap()[:, qs].rearrange("(c p) t -> p c t", p=128), reads=[db("qbT")], writes=[qb_b[qk]])
            for m in range(8):
                Lb = [(0, 1), (5, 6)][cnt % 2]
                Ls = cnt % 2
                cnt += 1
                units = []
                for hh in range(2):
                    for kx in range(2):
                        dj = 1 - kx
                        if nb - dj >= 0:
                            units.append((hh, kx, nb - dj, hh * 2 + kx))
                for (hh, kx, j, sl) in units:
                    hb = 2 * m + hh
                    g = hb // 8
                    pbs = hh * 64
                    bkx = Lb[hh]
                    P.op("pe", lambda en: en.matmul(pb[bkx][:, kx * 128:(kx + 1) * 128], lhsT=kd[pbs:pbs + 64, g, j * 128:(j + 1) * 128],
                                                    rhs=qb[qk][pbs:pbs + 64, m, :], start=True, stop=True),
                         reads=[b_k, qb_b[qk]], writes=[pb_b[bkx]])
                stg_ = E["cfg"].get("swa_stage", 4)
                if stg_ < 2:
                    continue
                L = Ls
                for hh in range(2):
                    bkx = Lb[hh]
                    a, b = (0, 2)
                    P.op("act", lambda en: en.activation(out=pf[L][:, hh * 256 + a * 128:hh * 256 + b * 128], in_=pb[bkx][:, a * 128:b * 128], func=AF.Exp, scale=0.125),
                         reads=[], writes=[pb_b[bkx], pf_b[L]])
                    P.op("dve", lambda en: en.tensor_tensor(out=pbf[L][:, hh * 256 + a * 128:hh * 256 + b * 128], in0=pf[L][:, hh * 256 + a * 128:hh * 256 + b * 128],
                                                            in1=expB[:, 2 * m + hh, a:b, :].rearrange("p k q -> p (k q)"), op=ALU.mult),
                         reads=[pf_b[L], b_exp], writes=[pbf_b[L]])
                    if a_ == 0:
                        P.op("dve", lambda en: en.tensor_scalar(out=pbf[L][:, hh * 256:hh * 256 + 128], in0=pbf[L][:, hh * 256:hh * 256 + 128],
                                                                scalar1=flg[:, 0:1], scalar2=None, op0=ALU.mult),
                             reads=[pbf_b[L], b_flg], writes=[pbf_b[L]])
                if stg_ < 3:
                    continue
                for hh in range(2):
                    us = [u for u in units if u[0] == hh]
                    hb = 2 * m + hh
                    g = hb // 8
                    pbs = hh * 64
                    for ui, (_, kx, j, sl) in enumerate(us):
                        P.op("pe", lambda en: en.matmul(pb[2][pbs:pbs + 64, 0:128], lhsT=vtk[:, j, g * 64:(g + 1) * 64], rhs=pbf[L][:, sl * 128:(sl + 1) * 128],
                                                        start=(ui == 0), stop=(ui == len(us) - 1)),
                             reads=[b_k, pbf_b[L]], writes=[pb_b[2]])
                        P.op("pe", lambda en: en.matmul(pb[3][pbs:pbs + 64, 0:128], lhsT=ones_bf[:, 0:64], rhs=pbf[L][:, sl * 128:(sl + 1) * 128],
                                                        start=(ui == 0), stop=(ui == len(us) - 1)),
                             reads=[b_const, pbf_b[L]], writes=[pb_b[3]])
                if stg_ < 4:
                    continue
                for hh in range(2):
                    hb = 2 * m + hh
                    pbs = hh * 64
                    P.op("dve", lambda en: en.tensor_scalar(out=dn[pbs:pbs + 64, :], in0=pb[3][pbs:pbs + 64, 0:128], scalar1=esk[pbs:pbs + 64, hb:hb + 1],
                                                            scalar2=None, op0=ALU.add),
                         reads=[b_exp], writes=[pb_b[3], dn_b])
                P.op("dve", lambda en: en.reciprocal(out=dn[:], in_=dn[:]), reads=[dn_b], writes=[dn_b])
                P.op("dve", lambda en: en.tensor_tensor(out=yb_st[qk][:, m, :], in0=pb[2][:, 0:128], in1=dn[:], op=ALU.mult),
                     reads=[dn_b], writes=[pb_b[2], yb_b[qk]])
            P.dma("sp", yT.ap()[1024:2048, qs].rearrange("(c p) t -> p c t", p=128), yb_st[qk][:], reads=[yb_b[qk]], writes=[db("yT", 0)])


def odd_attention(E, l):
    o = l // 2
    P, nc, Wd, db, gemm, simple_blocks, norm_x = (E[k] for k in ("P", "nc", "Wd", "db", "gemm", "simple_blocks", "norm_x"))
    spk, b_spk, pb, pb_b, pbh, pbh_b, ident_f, ident_bf, b_cst, b_const, ones_bf, ones_f, bd_bf, triu_f, triu_bf, eps_t = (E[k] for k in (
        "spk", "b_spk", "pb", "pb_b", "pbh", "pbh_b", "ident_f", "ident_bf", "b_cst", "b_const", "ones_bf", "ones_f", "bd_bf", "triu_f", "triu_bf", "eps_t"))
    xT, yT = E["xT"], E["yT"]
    s_qT, s_kT, s_vtok, s_lf = E["s_qT"], E["s_kT"], E["s_vtok"], E["s_lf"]
    flg, b_flg, cc_gather, kpack_o, gk_o, lfp, glf = (E[k] for k in ("flg", "b_flg", "cc_gather", "kpack_o", "gk_o", "lfp", "glf"))

    def rms_grp(S, srcs, src_b, lhsT_ones, gcol, inv_n, dsts, dst_bufs, ncols=TC):
        C = len(srcs)
        sq, sq_b, rstd, rstd_b = S["sq"], S["sq_b"], S["rstd"], S["rstd_b"]
        for c in range(C):
            P.op("act", lambda en: en.activation(out=sq[:, c, 0:ncols], in_=srcs[c], func=AF.Square),
                 reads=[src_b], writes=[sq_b])
        for c in range(C):
            P.op("pe", lambda en: en.matmul(pb[4][:, 0:ncols], lhsT=lhsT_ones[:, :], rhs=sq[:, c, 0:ncols],
                                            start=(c == 0), stop=(c == C - 1)),
                 reads=[sq_b, b_const], writes=[pb_b[4]])
        P.op("act", lambda en: en.activation(out=rstd[:, 0:ncols], in_=pb[4][:, 0:ncols], func=AF.Sqrt,
                                             scale=inv_n, bias=eps_t[:, 0:1]),
             reads=[b_const], writes=[pb_b[4], rstd_b])
        P.op("dve", lambda en: en.reciprocal(out=rstd[:, 0:ncols], in_=rstd[:, 0:ncols]), reads=[rstd_b], writes=[rstd_b])
        for c in range(C):
            P.op("dve", lambda en: en.scalar_tensor_tensor(out=dsts[c], in0=srcs[c], scalar=spk[:, gcol + c:gcol + c + 1],
                                                           in1=rstd[:, 0:ncols], op0=ALU.mult, op1=ALU.mult),
                 reads=[src_b, rstd_b, b_spk], writes=dst_bufs)

    for ps in range(0 if E["cfg"].get("skip_proj") else 1):
        t0 = 0
        with Scope(P) as so:
            hT = so.sb("hT", [128, 16, TT], BF16)
            hT_b = Buf("hT")
            with Scope(P) as sn:
                S0 = dict(xs=sn.sb("xs", [128, 16, TC], F32), xs_b=Buf(), sq=sn.sb("sq", [128, 16, TC], BF16),
                          sq_b=Buf(), rstd=sn.sb("rstd", [128, TC], F32), rstd_b=Buf())
                norm_x(S0, hT, hT_b, SP_ATTN + l * 16, t0)
            S = dict(sq=so.sb("sq", [128, 1, TC], BF16), sq_b=Buf(), rstd=so.sb("rstd", [128, TC], F32), rstd_b=Buf())
            stg1 = so.sb("stg1", [128, TT], F32)
            stg1_b = Buf("stg1")
            ob = so.sb("ob", [128, TT], BF16)
            ob_b = Buf("ob")
            vb_st = so.sb("vb_st", [128, TT], BF16)
            vb_b = Buf()
            vtok_st = so.sb("vtok_st", [128, 8, 128], BF16)
            vtok_b = Buf()
            negfb = so.sb("negfb", [32, 1], F32)
            negfb_b = Buf()
            fst = so.sb("fst", [32, TT], F32)
            fst_b = Buf()
            lf_tok = so.sb("lf_tok", [128, 8, 32], F32)
            lf_tok_b = Buf()
            P.op("dve", lambda en: en.tensor_scalar(out=negfb[:], in0=spk[0:32, SP_FB + o:SP_FB + o + 1], scalar1=-1.0, scalar2=None, op0=ALU.mult),
                 reads=[b_spk], writes=[negfb_b])

            def tsl(tci):
                return slice(tci * TC, (tci + 1) * TC)

            def in_epi(bi, m, tag, tci):
                kind = tag[0]
                last = (tci == TT // TC - 1)
                if kind in ("q", "k"):
                    c = tag[1]
                    P.op("act", lambda en: en.activation(out=stg1[:, tsl(tci)], in_=pb[bi][:], func=AF.Copy),
                         reads=[], writes=[pb_b[bi], stg1_b])
                    if last:
                        gcol = (SP_CQN if kind == "q" else SP_CKN) + o
                        dst = s_qT if kind == "q" else s_kT
                        for t2 in range(TT // TC):
                            rms_grp(S, [stg1[:, tsl(t2)]], stg1_b, bd_bf, gcol, 1.0 / 64, [ob[:, tsl(t2)]], [ob_b])
                        if kind == "q":
                            P.dma("sp", s_qT.ap()[c * 128:(c + 1) * 128, 0:TT], ob[:], reads=[ob_b], writes=[db("qT")])
                        else:
                            P.dma("sp", s_kT.ap()[c * 128:(c + 1) * 128, TO:TO + TT], ob[:], reads=[ob_b], writes=[db("kT")])
                            P.dma("sp", kpack_o[c // 8].ap()[(c % 8) * 128:(c % 8 + 1) * 128, :], ob[:], reads=[ob_b], writes=[db("kpack_o", c // 8)])
                elif kind == "v":
                    c = tag[1]
                    P.op("act", lambda en: en.activation(out=vb_st[:, tsl(tci)], in_=pb[bi][:], func=AF.Copy),
                         reads=[], writes=[pb_b[bi], vb_b])
                    if last:
                        for tt in range(TT // 128):
                            P.op("pe", lambda en: en.transpose(pbh[:, (tt % 4) * 128:(tt % 4 + 1) * 128], vb_st[:, tt * 128:(tt + 1) * 128], ident_bf[:]),
                                 reads=[vb_b, b_const], writes=[pbh_b])
                            if tt % 4 == 3:
                                P.op("act", lambda en: en.activation(out=vtok_st[:, tt - 3:tt + 1, :], in_=pbh[:, 0:512].rearrange("p (a b) -> p a b", b=128), func=AF.Copy),
                                     reads=[], writes=[pbh_b, vtok_b])
                        P.dma("sp", s_vtok.ap()[TO:TO + TT, c * 128:(c + 1) * 128].rearrange("(tt p) c -> p tt c", p=128), vtok_st[:],
                              reads=[vtok_b], writes=[db("vtok")])
                        for hv in range(2):
                            P.dma("sp", kpack_o[2 + hv].ap().rearrange("(t a) c -> t (a c)", a=2)[:, c * 128:(c + 1) * 128].rearrange("(tt p) c -> p tt c", p=128),
                                  vtok_st[:, hv * 4:(hv + 1) * 4, :], reads=[vtok_b], writes=[db("kpack_o", 2 + hv)])
                elif kind == "f":
                    P.op("act", lambda en: en.activation(out=fst[:, tsl(tci)], in_=pb[bi][0:32, :], func=AF.Exp, scale=-1.0, bias=negfb[:, 0:1]),
                         reads=[negfb_b], writes=[pb_b[bi], fst_b])
                    if last:
                        P.op("act", lambda en: en.activation(out=fst[:], in_=fst[:], func=AF.Ln, bias=ones_f[0:32, 0:1]),
                             reads=[fst_b, b_const], writes=[fst_b])
                        for tt in range(TT // 128):
                            P.op("pe", lambda en: en.transpose(pb[5][:, tt * 32:(tt + 1) * 32], fst[0:32, tt * 128:(tt + 1) * 128], ident_f[0:32, 0:32]),
                                 reads=[fst_b, b_cst], writes=[pb_b[5]])
                        P.op("act", lambda en: en.activation(out=lf_tok[:], in_=pb[5][:, 0:256].rearrange("p (a b) -> p a b", b=32), func=AF.Copy),
                             reads=[], writes=[pb_b[5], lf_tok_b])
                        P.dma("sp", s_lf.ap()[TO:TO + TT, :].rearrange("(tt p) c -> p tt c", p=128), lf_tok[:],
                              reads=[lf_tok_b], writes=[db("lf")])
                        P.dma("sp", lfp.ap().rearrange("(tt p) c -> p tt c", p=128), lf_tok[:],
                              reads=[lf_tok_b], writes=[db("lfp")])
            blocks = []
            for kind, base in (("q", 0), ("k", 2048), ("v", 4096)):
                for b4 in range(4):
                    blocks.append(([(base + b4 * 512, 512)], [(c * 128, 128, (kind, b4 * 4 + c), 0) for c in range(4)]))
            blocks.append(([(6144, 32)], [(0, 32, ("f",), 0)]))
            gemm(hT, hT_b, 16, lambda c0, n: Wd["w_in_odd"].ap()[o, :, c0:c0 + n], blocks, in_epi)

    if not E["cfg"].get("skip_proj"):
        for i4 in range(4):
            cc_gather(kpack_o[i4], gk_o[i4], [db("kpack_o", i4)], [db("gk_o", i4)])
        cc_gather(lfp, glf, [db("lfp")], [db("glf")])
        for i4 in range(2):
            P.dma("sp", s_kT.ap()[i4 * 1024:(i4 + 1) * 1024, 0:TO], gk_o[i4].ap()[0:1024, :], reads=[db("gk_o", i4)], writes=[db("kT")])
            P.dma("sp", s_vtok.ap()[i4 * 512:(i4 + 1) * 512, :], gk_o[2 + i4].ap()[0:1024, :].rearrange("(t a) c -> t (a c)", a=2),
                  reads=[db("gk_o", 2 + i4)], writes=[db("vtok")])
        P.dma("sp", s_lf.ap()[0:TO, :], glf.ap()[0:1024, :], reads=[db("glf")], writes=[db("lf")])
    if E["cfg"].get("stop_after_proj"):
        return
    with Scope(P) as sa:
        lft = sa.sb("lft", [128, 16, 32], F32)
        lft_b = Buf()
        ncum = sa.sb("ncum", [128, 16, 32], F32)
        Cb = sa.sb("Cb", [128, 16, 32], F32)
        cum_b = Buf("cum")
        P.dma("sp", lft[:], s_lf.ap().rearrange("(tt p) c -> p tt c", p=128), reads=[db("lf")], writes=[lft_b])
        P.op("dve", lambda en: en.tensor_scalar(out=lft[:, 0:8, :], in0=lft[:, 0:8, :], scalar1=flg[:, 0:1], scalar2=None, op0=ALU.mult),
             reads=[lft_b, b_flg], writes=[lft_b])
        for j in range(16):
            for j2 in range(j + 1):
                P.op("pe", lambda en: en.matmul(pb[5][:, j * 32:(j + 1) * 32], lhsT=(triu_f if j2 == j else ones_f[:, :]), rhs=lft[:, j2, :],
                                                start=(j2 == 0), stop=(j2 == j)),
                     reads=[lft_b, b_cst, b_const], writes=[pb_b[5]])
            for j2 in range(j + 1):
                P.op("pe", lambda en: en.matmul(pb[6][:, j * 32:(j + 1) * 32], lhsT=ones_f[:, :], rhs=lft[:, j2, :],
                                                start=(j2 == 0), stop=(j2 == j)),
                     reads=[lft_b, b_const], writes=[pb_b[6]])
        P.op("act", lambda en: en.activation(out=ncum[:], in_=pb[5][:].rearrange("p (a b) -> p a b", b=32), func=AF.Copy),
             reads=[], writes=[pb_b[5], cum_b])
        P.op("act", lambda en: en.activation(out=Cb[:], in_=pb[6][:].rearrange("p (a b) -> p a b", b=32), func=AF.Copy),
             reads=[], writes=[pb_b[6], cum_b])
        s_nq = E["s_nq"]
        dm = sa.sb("dm", [128, 16, 32], F32)
        dhi = sa.sb("dhi", [128, 16, 32], BF16)
        dhf = sa.sb("dhf", [128, 16, 32], F32)
        dlo = sa.sb("dlo", [128, 16, 32], BF16)
        dl2 = sa.sb("dl2", [128, 16, 32], BF16)
        nqT = sa.sb("nqT", [32, 3, TO], BF16)
        dm_b = Buf("dm")
        nqT_b = Buf("nqT")
        P.op("dve", lambda en: en.memset(dm[:, 0:8, :], 0.0), writes=[dm_b])
        for t_ in range(8, 16):
            ge = 4 * (t_ // 4) + 3
            P.op("dve", lambda en: en.tensor_tensor(out=dm[:, t_, :], in0=Cb[:, ge, :], in1=ncum[:, t_, :], op=ALU.subtract),
                 reads=[cum_b], writes=[dm_b])
        P.op("dve", lambda en: en.tensor_scalar(out=dm[:], in0=dm[:], scalar1=8.0, scalar2=None, op0=ALU.mult), reads=[dm_b], writes=[dm_b])
        P.op("dve", lambda en: en.tensor_copy(out=dhi[:], in_=dm[:]), reads=[dm_b], writes=[dm_b])
        P.op("dve", lambda en: en.tensor_copy(out=dhf[:], in_=dhi[:]), reads=[dm_b], writes=[dm_b])
        P.op("dve", lambda en: en.tensor_tensor(out=dhf[:], in0=dm[:], in1=dhf[:], op=ALU.subtract), reads=[dm_b], writes=[dm_b])
        P.op("dve", lambda en: en.tensor_copy(out=dlo[:], in_=dhf[:]), reads=[dm_b], writes=[dm_b])
        P.op("dve", lambda en: en.tensor_copy(out=dm[:], in_=dlo[:]), reads=[dm_b], writes=[dm_b])
        P.op("dve", lambda en: en.tensor_tensor(out=dhf[:], in0=dhf[:], in1=dm[:], op=ALU.subtract), reads=[dm_b], writes=[dm_b])
        P.op("dve", lambda en: en.tensor_copy(out=dl2[:], in_=dhf[:]), reads=[dm_b], writes=[dm_b])
        for w, src in enumerate((dhi, dlo, dl2)):
            for tt in range(8):
                P.op("pe", lambda en: en.transpose(pbh[0:32, tt * 128:(tt + 1) * 128], src[:, 8 + tt, :], ident_bf[:]),
                     reads=[dm_b, b_const], writes=[pbh_b])
            P.op("act", lambda en: en.activation(out=nqT[:, w, :], in_=pbh[0:32, :], func=AF.Copy),
                 reads=[], writes=[pbh_b, nqT_b])
        P.dma("sp", s_nq.ap().rearrange("w h t -> h w t"), nqT[:], reads=[nqT_b], writes=[db("nq")])
        P.op("dve", lambda en: en.tensor_scalar(out=ncum[:, 0:8, :], in0=ncum[:, 0:8, :], scalar1=flg[:, 2:3], scalar2=None, op0=ALU.add),
             reads=[cum_b, dm_b, b_flg], writes=[cum_b])
        mneg = sa.sb("mneg", [128, 128], BF16)
        mneg_b = Buf("mneg")
        P.op("dve", lambda en: en.tensor_scalar(out=mneg[:], in0=triu_f, scalar1=30000.0, scalar2=-30000.0, op0=ALU.mult, op1=ALU.add),
             reads=[b_cst], writes=[mneg_b])
        kaug = [[sa.sb(f"kaug{i}{hh}", [128, T], BF16) for hh in range(2)] for i in range(2)]
        qaug = [[sa.sb(f"qaug{i}{hh}", [128, TO], BF16) for hh in range(2)] for i in range(2)]
        vm = [sa.sb(f"vm{i}", [128, 16, 128], BF16) for i in range(2)]
        m_b = [Buf(), Buf()]
        NBF = 4
        LBF = [0, 1, 4, 5]
        Bm = [sa.sb(f"Bm{i}", [128, 4], F32) for i in range(NBF)]
        Bm_b = [Buf() for _ in range(NBF)]
        pbf = [sa.sb(f"pbf{i}", [128, 512], BF16) for i in range(NBF)]
        pbf_b = [Buf() for _ in range(NBF)]
        rcp = sa.sb("rcp", [128, 512], F32)
        rcp_b = Buf()
        yst = [sa.sb(f"yst{i}", [128, 512], BF16) for i in range(2)]
        yst_b = [Buf(), Buf()]
        cnt = 0
        yc = 0
        for m in range(E["cfg"].get("fox_pairs", 16)):
            mk = m % 2
            for hh in range(2):
                h = 2 * m + hh
                own = slice(hh * 64, (hh + 1) * 64)
                oth = slice((1 - hh) * 64, (2 - hh) * 64)
                o0 = (1 - hh) * 64
                P.op("dve", lambda en: en.memset(kaug[mk][hh][oth, :], 0.0), writes=[m_b[mk]])
                P.op("dve", lambda en: en.memset(kaug[mk][hh][o0:o0 + 3, :], 1.0), writes=[m_b[mk]])
                P.op("dve", lambda en: en.memset(qaug[mk][hh][oth, :], 0.0), writes=[m_b[mk]])
                P.dma("sp", kaug[mk][hh][own, :], s_kT.ap()[h * 64:(h + 1) * 64, :], reads=[db("kT")], writes=[m_b[mk]])
                P.dma("sp", qaug[mk][hh][own, :], s_qT.ap()[h * 64:(h + 1) * 64, :], reads=[db("qT")], writes=[m_b[mk]])
                P.dma("sp", qaug[mk][hh][o0:o0 + 3, :], s_nq.ap()[:, h, :], reads=[db("nq")], writes=[m_b[mk]])
            P.dma("sp", vm[mk][:], s_vtok.ap()[:, m * 128:(m + 1) * 128].rearrange("(tt p) c -> p tt c", p=128), reads=[db("vtok")], writes=[m_b[mk]])
            for Gl in range(2):
                G = 2 + Gl
                jmax = 4 * G + 3
                items = [(hh, j) for hh in range(2) for j in range(jmax + 1)]

                def f_logits(k):
                    hh, j = items[k]
                    L = LBF[k % NBF]
                    i_lo = max(4 * G, j)
                    col0 = (i_lo - 4 * G) * 128
                    P.op("pe", lambda en: en.matmul(pb[L][:, col0:512], lhsT=kaug[mk][hh][:, j * 128:(j + 1) * 128],
                                                    rhs=qaug[mk][hh][:, 4 * Gl * 128 + col0:(4 * Gl + 4) * 128], start=True, stop=(j < 4 * G)),
                         reads=[m_b[mk]], writes=[pb_b[L]])
                    if j >= 4 * G:
                        P.op("pe", lambda en: en.matmul(pb[L][:, col0:col0 + 128], lhsT=ident_bf[:], rhs=mneg[:], start=False, stop=True),
                             reads=[b_const, mneg_b], writes=[pb_b[L]])

                def f_post(k):
                    hh, j = items[k]
                    h = 2 * m + hh
                    L = k % NBF
                    BK = LBF[L]
                    i_lo = max(4 * G, j)
                    P.op("dve", lambda en: en.tensor_scalar(out=Bm[L][:, 0:1], in0=Cb[:, 4 * G + 3, h:h + 1], scalar1=-1.0, scalar2=ncum[:, j, h:h + 1],
                                                            op0=ALU.mult, op1=ALU.add),
                         reads=[cum_b], writes=[Bm_b[L]])
                    cs = slice((i_lo - 4 * G) * 128, 512)
                    P.op("act", lambda en: en.activation(out=pbf[L][:, cs], in_=pb[BK][:, cs], func=AF.Exp, scale=0.125,
                                                         bias=Bm[L][:, 0:1]),
                         reads=[Bm_b[L]], writes=[pb_b[BK], pbf_b[L]])

                def f_pv(k):
                    hh, j = items[k]
                    pbs = hh * 64
                    L = k % NBF
                    i_lo = max(4 * G, j)
                    col0 = (i_lo - 4 * G) * 128
                    P.op("pe", lambda en: en.matmul(pb[2][pbs:pbs + 64, col0:512], lhsT=vm[mk][:, j, hh * 64:(hh + 1) * 64], rhs=pbf[L][:, col0:512],
                                                    start=(j == 0), stop=(j == jmax)),
                         reads=[m_b[mk], pbf_b[L]], writes=[pb_b[2]])
                    P.op("pe", lambda en: en.matmul(pb[3][pbs:pbs + 64, col0:512], lhsT=ones_bf[:, 0:64], rhs=pbf[L][:, col0:512],
                                                    start=(j == 0), stop=(j == jmax)),
                         reads=[b_const, pbf_b[L]], writes=[pb_b[3]])

                for k0 in range(NBF - 1):
                    f_logits(k0)
                for k in range(len(items)):
                    if k + NBF - 1 < len(items):
                        f_logits(k + NBF - 1)
                    f_post(k)
                    f_pv(k)
                yk = yc % 2
                yc += 1
                P.op("dve", lambda en: en.reciprocal(out=rcp[:], in_=pb[3][:]), reads=[], writes=[pb_b[3], rcp_b])
                P.op("dve", lambda en: en.tensor_tensor(out=yst[yk][:], in0=pb[2][:], in1=rcp[:], op=ALU.mult),
                     reads=[rcp_b], writes=[pb_b[2], yst_b[yk]])
                P.dma("sp", yT.ap()[m * 128:(m + 1) * 128, Gl * 512:(Gl + 1) * 512], yst[yk][:], reads=[yst_b[yk]], writes=[db("yT", 0)])


def build(cfg=None):
    cfg = cfg or {}
    layers = cfg.get("layers", list(range(DEPTH)))
    nc = bass.Bass("TRN2", target_bir_lowering=False)

    def din(name, shape):
        return nc.dram_tensor(name, list(shape), F32, kind="ExternalInput")
    x_in = din("x", [TO, D])
    p_in = din("p", [DEPTH, TO, 256])
    flg_in = din("flg", [128, 4])
    Wd = {n: din(n, s) for n, s in WEIGHTS}
    sp_in = din("sp", [128, NSP])
    cst_in = din("cst", [128, 512])
    oh_in = din("oh", [33, XA + XB])
    out_d = nc.dram_tensor("out", [TO, D], F32, kind="ExternalOutput")
    dbg = {}
    for name, shape in cfg.get("dumps", []):
        dbg[name] = nc.dram_tensor("dbg_" + name, list(shape), F32, kind="ExternalOutput")

    def scr(name, shape, dt):
        if name in cfg.get("expose", ()):
            return nc.dram_tensor(name, list(shape), dt, kind="ExternalOutput")
        return nc.dram_tensor(name, list(shape), dt)
    xT = scr("xT", [D, TO], F32)
    yT = scr("yT", [D, TO], BF16)
    s_kvT = scr("s_kvT", [256, T], BF16)
    s_kvtok = scr("s_kvtok", [T, 256], BF16)
    s_kidxT = scr("s_kidxT", [64, T], BF16)
    s_widx = scr("s_widx", [TO, 16], F32)
    s_qiT = scr("s_qiT", [1024, TO], BF16)
    s_qaT = scr("s_qaT", [4096, TO], BF16)
    s_qbT = scr("s_qbT", [1024, TO], BF16)
    s_kdupT = scr("s_kdupT", [256, T], BF16)
    s_vbtok = scr("s_vbtok", [T, 128], BF16)
    s_qT = scr("s_qT", [2048, TO], BF16)
    s_kT = scr("s_kT", [2048, T], BF16)
    s_vtok = scr("s_vtok", [T, 2048], BF16)
    s_lf = scr("s_lf", [T, 32], F32)
    s_vrow = scr("s_vrow", [16, XA + XB], F32)
    s_nq = scr("s_nq", [3, 32, TO], BF16)
    kpack_e = scr("kpack_e", [960, 1024], BF16)
    gk_e = scr("gk_e", [1920, 1024], BF16)
    kpack_o = [scr(f"kpack_o{i}", [1024, 1024], BF16) for i in range(4)]
    gk_o = [scr(f"gk_o{i}", [2048, 1024], BF16) for i in range(4)]
    lfp = scr("lfp", [1024, 32], F32)
    glf = scr("glf", [2048, 32], F32)
    hx_in = scr("hx_in", [128, 32], F32)
    hxg = scr("hxg", [256, 32], F32)
    dbufs = {}

    def db(*key):
        if key not in dbufs:
            dbufs[key] = Buf(str(key))
        return dbufs[key]

    with ExitStack() as st:
        P = Prog(nc, st)

        def gsb(name, shape, dt):
            return st.enter_context(nc.sbuf_tensor(name, list(shape), dt))

        spk = gsb("spk", [128, NSP], F32)
        cst = gsb("cst_sb", [128, 512], F32)
        ident_bf = gsb("ident_bf", [128, 128], BF16)
        ones_bf = gsb("ones_bf", [128, 128], BF16)
        bd_bf = gsb("bd_bf", [128, 128], BF16)
        ones_f = gsb("ones_f", [128, 128], F32)
        eps_t = gsb("eps_t", [128, 1], F32)
        triu_bf = gsb("triu_bf", [128, 128], BF16)
        halo = gsb("halo", [128, 2], F32)
        flg = gsb("flg_sb", [128, 4], F32)
        b_flg = Buf("flg")
        ccs = P._newsem("ccs")
        cc_n = [0]
        WSLOT = 11008
        NWS = 2
        wbuf = [gsb(f"wbuf{i}", [128, WSLOT], BF16) for i in range(NWS)]
        wb_b = [Buf(f"wb{i}") for i in range(NWS)]
        wptr = [0]
        b_spk, b_cst, b_const, b_halo = Buf(), Buf(), Buf(), Buf()
        ident_f = cst[:, 0:128]
        jflip = cst[:, 128:256]
        triu_f = cst[:, 256:384]
        cneg = cst[:, 384:512]
        pb = [st.enter_context(nc.psum_tensor(f"pb{i}", [128, 512], F32)) for i in range(7)]
        pbh = st.enter_context(nc.psum_tensor("pbh", [128, 1024], BF16))
        pb_b = [Buf(f"pb{i}") for i in range(7)]
        pbh_b = Buf("pbh")

        P.dma("sp", spk[:], sp_in.ap(), writes=[b_spk])
        P.dma("sp", cst[:], cst_in.ap(), writes=[b_cst])
        P.dma("sp", flg[:], flg_in.ap(), writes=[b_flg])
        P.op("dve", lambda e: e.memset(ones_bf[:], 1.0), writes=[b_const])
        P.op("dve", lambda e: e.memset(ones_f[:], 1.0), writes=[b_const])
        P.op("dve", lambda e: e.memset(eps_t[:], EPS), writes=[b_const])
        P.op("dve", lambda e: e.memset(bd_bf[:], 0.0), writes=[b_const])
        P.op("dve", lambda e: e.memset(bd_bf[0:64, 0:64], 1.0), writes=[b_const])
        P.op("dve", lambda e: e.memset(bd_bf[64:128, 64:128], 1.0), writes=[b_const])
        P.op("dve", lambda e: e.tensor_copy(out=ident_bf[:], in_=ident_f), reads=[b_cst], writes=[b_const])
        P.op("dve", lambda e: e.tensor_copy(out=triu_bf[:], in_=triu_f), reads=[b_cst], writes=[b_const])
        P.op("dve", lambda e: e.memset(spk[32:33, SP_RELB:SP_RELB + 32], NEG), reads=[], writes=[b_spk])
        P.barrier()

        gemm_bank = [0]

        def cc_gather(src_t, dst_t, in_bufs, out_bufs):
            P._deps("pool", list(in_bufs), list(out_bufs))
            cc_n[0] += 1
            nc.gpsimd.collective_compute("AllGather", ALU.bypass, replica_groups=[[0, 4], [1, 5], [2, 6], [3, 7]],
                                         ins=[src_t.ap()], outs=[dst_t.ap()]).then_inc(ccs, 1)
            nc.gpsimd.wait_ge(ccs, cc_n[0])
            P.op("pool", lambda e: e.memset(halo[0:1, 0:1], 0.0), reads=list(in_bufs), writes=list(out_bufs))
        state = {}

        def wview(si, KC, ntot):
            return wbuf[si][:, 0:KC * ntot].rearrange("p (kc n) -> p kc n", n=ntot)

        def gemm(src, src_b, KC, wsrc, blocks, epi, ntc=2, tc_off=0, pre=None):
            for segs, chunks in blocks:
                si = wptr[0]
                wptr[0] = (wptr[0] + 1) % NWS
                ntot = sum(n for _, n in segs)
                wv = wview(si, KC, ntot)
                off = 0
                for (c0, ncols) in segs:
                    P.dma("pool", wv[:, :, off:off + ncols],
                          wsrc(c0, ncols).rearrange("(kc p) n -> p kc n", p=128), writes=[wb_b[si]])
                    off += ncols
                for (coff, m, tag, pbase) in chunks:
                    if pre is not None:
                        pre(wv, wb_b[si], coff, m, tag)
                    for tci in range(ntc):
                        bi = gemm_bank[0]
                        gemm_bank[0] = (gemm_bank[0] + 1) % 4
                        for kc in range(KC):
                            P.op("pe", lambda e: e.matmul(pb[bi][pbase:pbase + m, :], lhsT=wv[:, kc, coff:coff + m],
                                                          rhs=src[:, kc, tc_off + tci * TC:tc_off + (tci + 1) * TC],
                                                          start=(kc == 0), stop=(kc == KC - 1)),
                                 reads=[wb_b[si], src_b], writes=[pb_b[bi]])
                        epi(bi, m, tag, tci)

        def simple_blocks(col0, ncols_total, wcols, tagfn=None, m=128):
            blocks = []
            c = 0
            ci = 0
            while c < ncols_total:
                n = min(wcols, ncols_total - c)
                chunks = []
                o = 0
                while o < n:
                    mm = min(m, n - o)
                    chunks.append((o, mm, ci if tagfn is None else tagfn(ci), 0))
                    o += mm
                    ci += 1
                blocks.append(([(col0 + c, n)], chunks))
                c += n
            return blocks

        def rms_finish(S, src, src_b, C, lhsT_ones, gcol, inv_n, dst_fn, dst_bufs, ncols=TC, nparts=128):
            sq, sq_b, rstd, rstd_b = S["sq"], S["sq_b"], S["rstd"], S["rstd_b"]
            for c in range(C):
                P.op("act", lambda e: e.activation(out=sq[0:nparts, c, 0:ncols], in_=src(c), func=AF.Square),
                     reads=[src_b], writes=[sq_b])
            for c in range(C):
                P.op("pe", lambda e: e.matmul(pb[4][0:nparts, 0:ncols], lhsT=lhsT_ones[0:nparts, 0:nparts],
                                              rhs=sq[0:nparts, c, 0:ncols], start=(c == 0), stop=(c == C - 1)),
                     reads=[sq_b, b_const], writes=[pb_b[4]])
            P.op("act", lambda e: e.activation(out=rstd[0:nparts, 0:ncols], in_=pb[4][0:nparts, 0:ncols], func=AF.Sqrt,
                                               scale=inv_n, bias=eps_t[0:nparts, 0:1]),
                 reads=[b_const], writes=[pb_b[4], rstd_b])
            P.op("dve", lambda e: e.reciprocal(out=rstd[0:nparts, 0:ncols], in_=rstd[0:nparts, 0:ncols]),
                 reads=[rstd_b], writes=[rstd_b])
            for c in range(C):
                P.op("dve", lambda e: e.scalar_tensor_tensor(out=dst_fn(c), in0=src(c),
                                                             scalar=spk[0:nparts, gcol + c:gcol + c + 1],
                                                             in1=rstd[0:nparts, 0:ncols], op0=ALU.mult, op1=ALU.mult),
                     reads=[src_b, rstd_b, b_spk], writes=dst_bufs)

        def norm_x(S, hT, hT_b, gcol, t0):
            xs, xs_b = S["xs"], S["xs_b"]
            for tci in range(TT // TC):
                ta = t0 + tci * TC
                P.dma("sp", xs[:], xT.ap()[:, ta:ta + TC].rearrange("(kc p) t -> p kc t", p=128),
                      reads=[db("xT", ta // TC)], writes=[xs_b])
                rms_finish(S, lambda c: xs[:, c, :], xs_b, 16, ones_bf, gcol, 1.0 / D,
                           lambda c: hT[:, c, tci * TC:(tci + 1) * TC], [hT_b])

        def resid_epi(S, t0):
            def epi(bi, m, tag, tci):
                ta = t0 + tci * TC
                k = S["xr_i"][0]
                S["xr_i"][0] = (k + 1) % 2
                xr, xr_b = S["xr"][k], S["xr_b"][k]
                P.dma("sp", xr[:], xT.ap()[tag * 128:(tag + 1) * 128, ta:ta + TC],
                      reads=[db("xT", ta // TC)], writes=[xr_b])
                P.op("dve", lambda e: e.tensor_tensor(out=xr[:], in0=pb[bi][:], in1=xr[:], op=ALU.add),
                     reads=[xr_b], writes=[pb_b[bi], xr_b])
                P.dma("sp", xT.ap()[tag * 128:(tag + 1) * 128, ta:ta + TC], xr[:],
                      reads=[xr_b], writes=[db("xT", ta // TC)])
            return epi

        def dump(name, src_ap_dram):
            pass

        with Scope(P) as sc:
            xin = [sc.sb(f"xin{i}", [128, D], F32) for i in range(2)]
            xin_b = [Buf(), Buf()]
            stg = [sc.sb(f"xstg{i}", [128, 16, 128], F32) for i in range(2)]
            stg_b = [Buf(), Buf()]
            for tt in range(TO // 128):
                k = tt % 2
                P.dma("sp", xin[k][:], x_in.ap()[tt * 128:(tt + 1) * 128, :], writes=[xin_b[k]])
                for g in range(4):
                    bi = 5 + (g % 2)
                    for j in range(4):
                        fc = g * 4 + j
                        P.op("pe", lambda e: e.transpose(pb[bi][:, j * 128:(j + 1) * 128], xin[k][:, fc * 128:(fc + 1) * 128], ident_f),
                             reads=[xin_b[k], b_cst], writes=[pb_b[bi]])
                    P.op("act", lambda e: e.activation(out=stg[k][:, g * 4:(g + 1) * 4, :], in_=pb[bi][:].rearrange("p (a b) -> p a b", b=128), func=AF.Copy),
                         reads=[], writes=[pb_b[bi], stg_b[k]])
                P.dma("sp", xT.ap()[:, tt * 128:(tt + 1) * 128].rearrange("(fc p) t -> p fc t", p=128), stg[k][:],
                      reads=[stg_b[k]], writes=[db("xT", tt // 4)])

        for l in layers:
            E = dict(locals())
            E['state'] = state
            if "attn" in cfg.get("parts", ("attn", "out", "ffn", "ple")):
                if l % 2 == 0:
                    even_attention(E, l)
                else:
                    odd_attention(E, l)
            token_local(E, l, cfg.get("parts", ("attn", "out", "ffn", "ple")))

        with Scope(P) as sc:
            xo = [sc.sb(f"xo{i}", [128, 16, 128], F32) for i in range(2)]
            xo_b = [Buf(), Buf()]
            ostg = [sc.sb(f"ostg{i}", [128, D], F32) for i in range(2)]
            ostg_b = [Buf(), Buf()]
            for tt in range(TO // 128):
                k = tt % 2
                P.dma("sp", xo[k][:], xT.ap()[:, tt * 128:(tt + 1) * 128].rearrange("(fc p) t -> p fc t", p=128),
                      reads=[db("xT", tt // 4)], writes=[xo_b[k]])
                for g in range(4):
                    bi = 5 + (g % 2)
                    for j in range(4):
                        fc = g * 4 + j
                        P.op("pe", lambda e: e.transpose(pb[bi][:, j * 128:(j + 1) * 128], xo[k][:, fc, :], ident_f),
                             reads=[xo_b[k], b_cst], writes=[pb_b[bi]])
                    P.op("act", lambda e: e.activation(out=ostg[k][:, g * 512:(g + 1) * 512], in_=pb[bi][:], func=AF.Copy),
                         reads=[], writes=[pb_b[bi], ostg_b[k]])
                P.dma("sp", out_d.ap()[tt * 128:(tt + 1) * 128, :], ostg[k][:], reads=[ostg_b[k]], writes=[db("out")])
        P.barrier()
    return nc


_NC_CACHE = {}


def make_in_maps(inp, batches):
    cst, oh = host_consts()
    sp = pack_small(inp)
    wmap = {n: np.ascontiguousarray(inp[n], dtype=np.float32) for n, _ in WEIGHTS}
    in_maps = []
    for c in range(8):
        b = batches[c]
        half = c // 4
        flg = np.zeros((128, 4), np.float32)
        flg[:, 0] = float(half)
        flg[:, 1] = 0.0 if half else -1e30
        flg[:, 2] = 0.0 if half else NEG
        m = dict(x=np.ascontiguousarray(inp["x"][b, half * TO:(half + 1) * TO], dtype=np.float32),
                 p=np.ascontiguousarray(inp["p"][:, b, half * TO:(half + 1) * TO], dtype=np.float32),
                 flg=flg, sp=sp, cst=cst, oh=oh)
        m.update(wmap)
        in_maps.append(m)
    return in_maps


def kernel(**inputs):
    inp = {k: np.asarray(v) for k, v in inputs.items()}
    if "nc" not in _NC_CACHE:
        _NC_CACHE["nc"] = build()
    nc = _NC_CACHE["nc"]
    in_maps = make_in_maps(inp, [0, 1, 2, 3, 0, 1, 2, 3])
    res = run_bass_kernel_spmd(nc, in_maps, core_ids=list(range(8)))
    out = np.stack([np.concatenate([res.results[b]["out"], res.results[b + 4]["out"]], axis=0) for b in range(4)], axis=0)
    return out.astype(np.float32)
```

```python
import math
from contextlib import ExitStack
import numpy as np
import concourse.bass as bass
import concourse.mybir as mybir
from concourse.bass_utils import run_bass_kernel_spmd

F32 = mybir.dt.float32
BF16 = mybir.dt.bfloat16
ALU = mybir.AluOpType
AF = mybir.ActivationFunctionType

EPOCH = 30000
NDSEM = 40

D = 2048
T = 2048
DEPTH = 4
TT = 1024
TO = 1024
TC = 512
DFF = 5504
NFC = 43
EPS = 1e-6
XA = 1280
XB = 384
NEG = -30000.0

SP_ATTN = 0
SP_FFN = 64
SP_PLE = 128
SP_CONV = 192
SP_CQ = 1224
SP_CKV = 1232
SP_AQ = 1236
SP_BQ = 1240
SP_BK = 1242
SP_CQN = 1244
SP_CKN = 1246
SP_FB = 1248
SP_SINK = 1250
SP_RELB = 1282
SP_B31 = 1320
NSP = 1340

WEIGHTS = [
    ("w_in_even", (2, 2048, 2128)), ("a_w_uq", (2, 512, 4096)), ("a_w_qidx", (2, 512, 1024)),
    ("a_w_uv", (2, 16, 256, 64)), ("w_out_even", (2, 2048, 2048)), ("w_in_odd", (2, 2048, 6176)),
    ("w_out_odd", (2, 2048, 2048)), ("w_up", (4, 2048, 11008)), ("w_down", (4, 5504, 2048)),
    ("w_ple_gate", (4, 2048, 2048)), ("w_ple_proj", (4, 256, 2048)),
]


class Buf:
    __slots__ = ("name", "w", "r", "rd")

    def __init__(self, name=""):
        self.name = name
        self.w = None
        self.r = {}
        self.rd = []


class Prog:
    ENGS = ("pe", "act", "dve", "pool", "sp")

    def __init__(self, nc, stack):
        self.nc = nc
        self.stack = stack
        self.eng = {"pe": nc.tensor, "act": nc.scalar, "dve": nc.vector,
                    "pool": nc.gpsimd, "sp": nc.sync}
        self.cnt = {e: 0 for e in self.ENGS}
        self.esems = {e: [] for e in self.ENGS}
        self.seen_e = {e: {p: 0 for p in self.ENGS} for e in self.ENGS}
        self.seen_d = {e: {} for e in self.ENGS}
        self.dsem = {}
        for q in ("sp", "pool"):
            self.dsem[q] = [[self._newsem(f"d{q}{i}"), 0] for i in range(NDSEM)]
        self.dptr = {"sp": 0, "pool": 0}
        self.bar_sem = self._newsem("bar")
        self.bar_cnt = 0
        self.n_inst = 0

    def _newsem(self, name):
        return self.stack.enter_context(self.nc.semaphore(name))

    def _esem(self, e, idx):
        ep = (idx - 1) // EPOCH
        while len(self.esems[e]) <= ep:
            self.esems[e].append(self._newsem(f"e{e}{len(self.esems[e])}"))
        return self.esems[e][ep], (idx - 1) % EPOCH + 1

    def _wait(self, e, ev):
        if ev is None:
            return
        if ev[0] == "e":
            _, p, idx = ev
            if p == e and e == "pe":
                return
            if self.seen_e[e][p] >= idx:
                return
            self.seen_e[e][p] = idx
            s, v = self._esem(p, idx)
            self.eng[e].wait_ge(s, v)
        else:
            _, s, v, key = ev
            if self.seen_d[e].get(key, 0) >= v:
                return
            self.seen_d[e][key] = v
            self.eng[e].wait_ge(s, v)
        self.n_inst += 1

    def _deps(self, e, reads, writes):
        for b in reads:
            self._wait(e, b.w)
        for b in writes:
            self._wait(e, b.w)
            for p, idx in b.r.items():
                if p != e:
                    self._wait(e, ("e", p, idx))
            for ev in b.rd:
                self._wait(e, ev)

    def _mark(self, ev, reads, writes):
        for b in reads:
            if ev[0] == "e":
                b.r[ev[1]] = ev[2]
            else:
                b.rd.append(ev)
        for b in writes:
            b.w = ev
            b.r = {}
            b.rd = []

    def op(self, e, fn, reads=(), writes=()):
        self._deps(e, reads, writes)
        inst = fn(self.eng[e])
        self.cnt[e] += 1
        idx = self.cnt[e]
        s, _ = self._esem(e, idx)
        inst.then_inc(s, 1)
        self.n_inst += 1
        self._mark(("e", e, idx), reads, writes)

    def dma(self, q, out, in_, reads=(), writes=(), **kw):
        self._deps(q, reads, writes)
        slot = self.dsem[q][self.dptr[q]]
        key = (q, self.dptr[q])
        self.dptr[q] = (self.dptr[q] + 1) % NDSEM
        if slot[1] > 0:
            self._wait(q, ("d", slot[0], slot[1], key))
        inst = self.eng[q].dma_start(out=out, in_=in_, **kw)
        slot[1] += 16
        inst.then_inc(slot[0], 16)
        self.n_inst += 1
        self._mark(("d", slot[0], slot[1], key), reads, writes)

    def barrier(self):
        for p in self.ENGS:
            if p != "sp" and self.cnt[p] > 0:
                self._wait("sp", ("e", p, self.cnt[p]))
        for q in ("sp", "pool"):
            for i, slot in enumerate(self.dsem[q]):
                if slot[1] > 0:
                    self._wait("sp", ("d", slot[0], slot[1], (q, i)))
        self.bar_cnt += 1
        self.eng["sp"].sem_inc(self.bar_sem, 1)
        for e in self.ENGS:
            if e != "sp":
                self.eng[e].wait_ge(self.bar_sem, self.bar_cnt)
                for p in self.ENGS:
                    self.seen_e[e][p] = self.cnt[p]
                for q in ("sp", "pool"):
                    for i, slot in enumerate(self.dsem[q]):
                        self.seen_d[e][(q, i)] = slot[1]
        self.n_inst += 6


class Scope:
    def __init__(self, P):
        self.P = P
        self.st = ExitStack()

    def __enter__(self):
        self.st.__enter__()
        return self

    _uid = [0]

    def sb(self, name, shape, dt):
        Scope._uid[0] += 1
        return self.st.enter_context(self.P.nc.sbuf_tensor(f"{name}_{Scope._uid[0]}", list(shape), dt))

    def __exit__(self, *a):
        self.P.barrier()
        return self.st.__exit__(*a)


def rel_bucket_np(n):
    n = np.maximum(n, 0)
    exact = 16
    nf = np.maximum(n, 1).astype(np.float32)
    large = exact + (np.log(nf / np.float32(exact)) / np.float32(math.log(1024 / exact))
                     * np.float32(32 - exact)).astype(np.int32)
    large = np.minimum(large, 31)
    return np.where(n < exact, n, large)


def host_consts():
    cst = np.zeros((128, 512), np.float32)
    cst[:, 0:128] = np.eye(128)
    cst[:, 128:256] = np.eye(128)[::-1]
    i = np.arange(128)
    cst[:, 256:384] = (i[None, :] >= i[:, None]).astype(np.float32)
    cst[:, 384:512] = np.where(i[None, :] <= i[:, None], 0.0, -1e30)
    oh = np.zeros((33, XA + XB), np.float32)
    y = np.arange(XA)
    xx = y - 127
    b = np.where(xx < 0, 32, rel_bucket_np(xx))
    oh[b, y] = 1.0
    y = np.arange(XB)
    xx = y - 127
    b = np.where((xx < 0) | (xx >= 128), 32, rel_bucket_np(xx))
    oh[b, XA + y] = 1.0
    return cst, oh


def pack_small(inp):
    sp = np.zeros((128, NSP), np.float32)

    def fm(v):
        return np.ascontiguousarray(v.reshape(-1, 128).T)
    for l in range(4):
        sp[:, SP_ATTN + l * 16:SP_ATTN + (l + 1) * 16] = fm(inp["attn_norm"][l])
        sp[:, SP_FFN + l * 16:SP_FFN + (l + 1) * 16] = fm(inp["ffn_norm"][l])
        sp[:, SP_PLE + l * 16:SP_PLE + (l + 1) * 16] = fm(inp["ple_norm"][l])
        for k in range(3):
            c0 = SP_CONV + (l * 3 + k) * 86
            sp[:, c0:c0 + 86] = fm(inp["ffn_conv"][l, k])
    for e in range(2):
        sp[:, SP_CQ + e * 4:SP_CQ + e * 4 + 4] = fm(inp["a_cq_norm"][e])
        sp[:, SP_CKV + e * 2:SP_CKV + e * 2 + 2] = fm(inp["a_ckv_norm"][e])
        sp[:, SP_AQ + e * 2:SP_AQ + e * 2 + 2] = fm(inp["a_q_norm"][e])
        sp[:, SP_BQ + e] = np.tile(inp["b_q_norm"][e], 2)
        sp[:, SP_BK + e] = np.tile(inp["b_k_norm"][e], 2)
        sp[:, SP_CQN + e] = np.tile(inp["c_q_norm"][e], 2)
        sp[:, SP_CKN + e] = np.tile(inp["c_k_norm"][e], 2)
        sp[0:32, SP_FB + e] = inp["c_forget_bias"][e]
        sp[:, SP_SINK + e * 16:SP_SINK + (e + 1) * 16] = inp["b_sinks"][e][None, :]
    sp[0:32, SP_RELB:SP_RELB + 32] = inp["rel_bias"]
    sp[:, SP_B31:SP_B31 + 16] = inp["rel_bias"][31, 0:16][None, :]
    return sp


def token_local(E, l, parts):
    P, nc, Wd, db, gemm, simple_blocks, rms_finish, norm_x, resid_epi = (E[k] for k in (
        "P", "nc", "Wd", "db", "gemm", "simple_blocks", "rms_finish", "norm_x", "resid_epi"))
    xT, yT, spk, b_spk, pb, pb_b, halo, b_halo, p_in, ident_f, b_cst, wbuf, wb_b, wptr, wview = (E[k] for k in (
        "xT", "yT", "spk", "b_spk", "pb", "pb_b", "halo", "b_halo", "p_in", "ident_f", "b_cst", "wbuf", "wb_b", "wptr", "wview"))
    NWS = len(wbuf)
    flg, b_flg, cc_gather, hx_in, hxg, ones_bf = (E[k] for k in ("flg", "b_flg", "cc_gather", "hx_in", "hxg", "ones_bf"))
    for ps in range(1):
        t0 = 0
        with Scope(P) as so:
            hT = so.sb("hT", [128, 16, TT], BF16)
            hT_b = Buf("hT")
            hh = so.sb("hh", [128, 16, 2], BF16)
            hh_b = Buf("hh")

            def norm_scope(gcol, with_halo=False):
                with Scope(P) as sn:
                    S = dict(xs=sn.sb("xs", [128, 16, TC], F32), xs_b=Buf(), sq=sn.sb("sq", [128, 16, TC], BF16),
                             sq_b=Buf(), rstd=sn.sb("rstd", [128, TC], F32), rstd_b=Buf())
                    norm_x(S, hT, hT_b, gcol, t0)
                    if with_halo:
                        hxo = sn.sb("hxo", [128, 16, 2], F32)
                        hxo_b = Buf("hxo")
                        P.dma("sp", hxo[:], hxg.ap()[0:128, :].rearrange("p (kc t) -> p kc t", t=2), reads=[db("hxg")], writes=[hxo_b])
                        rms_finish(S, lambda c: hxo[:, c, :], hxo_b, 16, ones_bf, gcol, 1.0 / D,
                                   lambda c: hh[:, c, :], [hh_b], ncols=2)

            def mk_xr(sc):
                return dict(xr=[sc.sb(f"xr{i}", [128, TC], F32) for i in range(2)], xr_b=[Buf(), Buf()], xr_i=[0])

            if "out" in parts:
                with Scope(P) as s1:
                    S = mk_xr(s1)
                    P.dma("sp", hT[:], yT.ap()[:, t0:t0 + TT].rearrange("(kc p) t -> p kc t", p=128),
                          reads=[db("yT", 0)], writes=[hT_b])
                    wn = "w_out_even" if l % 2 == 0 else "w_out_odd"
                    gemm(hT, hT_b, 16, lambda c0, n: Wd[wn].ap()[l // 2, :, c0:c0 + n],
                         simple_blocks(0, D, 512), resid_epi(S, t0))
            if "ffn" in parts:
                with Scope(P) as sh:
                    hxs = sh.sb("hxs", [128, 16, 2], F32)
                    hxs_b = Buf("hxs")
                    P.dma("sp", hxs[:], xT.ap()[:, TO - 2:TO].rearrange("(kc p) t -> p kc t", p=128), reads=[db("xT", 1)], writes=[hxs_b])
                    P.dma("sp", hx_in.ap().rearrange("p (kc t) -> p kc t", t=2), hxs[:], reads=[hxs_b], writes=[db("hx_in")])
                cc_gather(hx_in, hxg, [db("hx_in")], [db("hxg")])
                norm_scope(SP_FFN + l * 16, with_halo=True)
                with Scope(P) as s2:
                    S = mk_xr(s2)
                    act = s2.sb("act", [128, NFC, TT], BF16)
                    act_b = Buf("act")
                    stg = {"g": s2.sb("sg", [128, TT + 2], F32), "u": s2.sb("su", [128, TT + 2], F32)}
                    stg_b = {"g": Buf("sg"), "u": Buf("su")}
                    cv = {"g": s2.sb("ga", [128, TT], F32), "u": s2.sb("ua", [128, TT], F32)}
                    cv_b = {"g": Buf("ga"), "u": Buf("ua")}

                    def up_pre(wv, wvb, coff, m, tag):
                        kind = tag[0]
                        for kc in range(16):
                            P.op("pe", lambda e: e.matmul(pb[6][:, 0:2], lhsT=wv[:, kc, coff:coff + m], rhs=hh[:, kc, :],
                                                          start=(kc == 0), stop=(kc == 15)),
                                 reads=[wvb, hh_b], writes=[pb_b[6]])
                        P.op("act", lambda e: e.activation(out=stg[kind][:, 0:2], in_=pb[6][:, 0:2], func=AF.Copy, scale=flg[:, 0:1]),
                             reads=[b_flg], writes=[pb_b[6], stg_b[kind]])

                    def conv_finish(kind, i):
                        c = i if kind == "g" else NFC + i
                        s_, sb_, a_, ab_ = stg[kind], stg_b[kind], cv[kind], cv_b[kind]
                        wc = [SP_CONV + (l * 3 + k) * 86 + c for k in range(3)]
                        P.op("act", lambda e: e.activation(out=a_[:], in_=s_[:, 2:TT + 2], func=AF.Copy,
                                                           scale=spk[:, wc[2]:wc[2] + 1]),
                             reads=[sb_, b_spk], writes=[ab_])
                        P.op("dve", lambda e: e.scalar_tensor_tensor(out=a_[:], in0=s_[:, 1:TT + 1], scalar=spk[:, wc[1]:wc[1] + 1],
                                                                     in1=a_[:], op0=ALU.mult, op1=ALU.add),
                             reads=[sb_, ab_, b_spk], writes=[ab_])
                        P.op("dve", lambda e: e.scalar_tensor_tensor(out=a_[:], in0=s_[:, 0:TT], scalar=spk[:, wc[0]:wc[0] + 1],
                                                                     in1=a_[:], op0=ALU.mult, op1=ALU.add),
                             reads=[sb_, ab_, b_spk], writes=[ab_])

                    def up_epi(bi, m, tag, tci):
                        kind, i = tag
                        c = i if kind == "g" else NFC + i
                        P.op("act", lambda e: e.activation(out=stg[kind][:, 2 + tci * TC:2 + (tci + 1) * TC], in_=pb[bi][:], func=AF.Copy),
                             reads=[], writes=[pb_b[bi], stg_b[kind]])
                        if tci == TT // TC - 1:
                            conv_finish(kind, i)
                            if kind == "u":
                                P.op("act", lambda e: e.activation(out=cv["g"][:], in_=cv["g"][:], func=AF.Silu),
                                     reads=[cv_b["g"]], writes=[cv_b["g"]])
                                P.op("dve", lambda e: e.tensor_tensor(out=act[:, i, :], in0=cv["g"][:], in1=cv["u"][:], op=ALU.mult),
                                     reads=[cv_b["g"], cv_b["u"]], writes=[act_b])
                    blocks = []
                    for i0 in range(0, NFC, 2):
                        npair = min(2, NFC - i0)
                        w = npair * 128
                        segs = [(i0 * 128, w), (DFF + i0 * 128, w)]
                        chunks = []
                        for j in range(npair):
                            chunks.append((j * 128, 128, ("g", i0 + j), 0))
                            chunks.append((w + j * 128, 128, ("u", i0 + j), 0))
                        blocks.append((segs, chunks))
                    gemm(hT, hT_b, 16, lambda c0, n: Wd["w_up"].ap()[l, :, c0:c0 + n], blocks, up_epi, pre=up_pre)
                    gemm(act, act_b, NFC, lambda c0, n: Wd["w_down"].ap()[l, :, c0:c0 + n],
                         simple_blocks(0, D, 256), resid_epi(S, t0))
            if "ple" in parts:
                norm_scope(SP_PLE + l * 16)
                with Scope(P) as s3:
                    S = mk_xr(s3)
                    pT = s3.sb("pT", [128, 2, TT], BF16)
                    pT_b = Buf("pT")
                    pl = [s3.sb(f"pl{i}", [128, 256], F32) for i in range(2)]
                    pl_b = [Buf(), Buf()]
                    sg = [s3.sb(f"sgt{i}", [128, TC], F32) for i in range(2)]
                    sg_b = [Buf(), Buf()]
                    for tt in range(TT // 128):
                        k = tt % 2
                        P.dma("sp", pl[k][:], p_in.ap()[l, t0 + tt * 128:t0 + (tt + 1) * 128, :], writes=[pl_b[k]])
                        for cc in range(2):
                            P.op("pe", lambda e: e.transpose(pb[5][:, cc * 128:(cc + 1) * 128], pl[k][:, cc * 128:(cc + 1) * 128], ident_f),
                                 reads=[pl_b[k], b_cst], writes=[pb_b[5]])
                        P.op("act", lambda e: e.activation(out=pT[:, :, tt * 128:(tt + 1) * 128],
                                                           in_=pb[5][:, 0:256].rearrange("p (a b) -> p a b", b=128), func=AF.Copy),
                             reads=[], writes=[pb_b[5], pT_b])
                    cnt = 0
                    for nb in range(D // 512):
                        sa = wptr[0]
                        sbb = (wptr[0] + 1) % NWS
                        wa = wview(sa, 16, 512)
                        wp = wview(sbb, 2, 512)
                        P.dma("pool", wa, Wd["w_ple_gate"].ap()[l, :, nb * 512:(nb + 1) * 512].rearrange("(kc p) n -> p kc n", p=128),
                              writes=[wb_b[sa]])
                        P.dma("pool", wp, Wd["w_ple_proj"].ap()[l, :, nb * 512:(nb + 1) * 512].rearrange("(kc p) n -> p kc n", p=128),
                              writes=[wb_b[sbb]])
                        for ci in range(4):
                            nchunk = nb * 4 + ci
                            for tci in range(TT // TC):
                                ba = cnt % 2
                                bb = 2 + cnt % 2
                                kx = cnt % 2
                                cnt += 1
                                ta = t0 + tci * TC
                                for kc in range(16):
                                    P.op("pe", lambda e: e.matmul(pb[ba][:], lhsT=wa[:, kc, ci * 128:(ci + 1) * 128],
                                                                  rhs=hT[:, kc, tci * TC:(tci + 1) * TC], start=(kc == 0), stop=(kc == 15)),
                                         reads=[wb_b[sa], hT_b], writes=[pb_b[ba]])
                                for kc in range(2):
                                    P.op("pe", lambda e: e.matmul(pb[bb][:], lhsT=wp[:, kc, ci * 128:(ci + 1) * 128],
                                                                  rhs=pT[:, kc, tci * TC:(tci + 1) * TC], start=(kc == 0), stop=(kc == 1)),
                                         reads=[wb_b[sbb], pT_b], writes=[pb_b[bb]])
                                P.op("act", lambda e: e.activation(out=sg[kx][:], in_=pb[ba][:], func=AF.Sigmoid),
                                     reads=[], writes=[pb_b[ba], sg_b[kx]])
                                P.op("dve", lambda e: e.tensor_tensor(out=sg[kx][:], in0=sg[kx][:], in1=pb[bb][:], op=ALU.mult),
                                     reads=[sg_b[kx]], writes=[pb_b[bb], sg_b[kx]])
                                xr, xr_b = S["xr"][kx], S["xr_b"][kx]
                                P.dma("sp", xr[:], xT.ap()[nchunk * 128:(nchunk + 1) * 128, ta:ta + TC],
                                      reads=[db("xT", ta // TC)], writes=[xr_b])
                                P.op("dve", lambda e: e.tensor_tensor(out=xr[:], in0=sg[kx][:], in1=xr[:], op=ALU.add),
                                     reads=[sg_b[kx], xr_b], writes=[xr_b])
                                P.dma("sp", xT.ap()[nchunk * 128:(nchunk + 1) * 128, ta:ta + TC], xr[:],
                                      reads=[xr_b], writes=[db("xT", ta // TC)])
                        wptr[0] = (wptr[0] + 2) % NWS


def even_attention(E, l):
    e = l // 2
    P, nc, Wd, db, gemm, simple_blocks, norm_x = (E[k] for k in ("P", "nc", "Wd", "db", "gemm", "simple_blocks", "norm_x"))
    spk, b_spk, pb, pb_b, pbh, pbh_b, ident_f, ident_bf, b_cst, b_const, ones_bf, bd_bf, jflip, cneg, eps_t = (E[k] for k in (
        "spk", "b_spk", "pb", "pb_b", "pbh", "pbh_b", "ident_f", "ident_bf", "b_cst", "b_const", "ones_bf", "bd_bf", "jflip", "cneg", "eps_t"))
    xT, yT, oh_in = E["xT"], E["yT"], E["oh_in"]
    s_kvT, s_kvtok, s_kidxT, s_widx, s_qiT, s_qaT, s_qbT, s_kdupT, s_vbtok, s_vrow = (E[k] for k in (
        "s_kvT", "s_kvtok", "s_kidxT", "s_widx", "s_qiT", "s_qaT", "s_qbT", "s_kdupT", "s_vbtok", "s_vrow"))
    XT = XA + XB
    flg, b_flg, cc_gather, kpack_e, gk_e = (E[k] for k in ("flg", "b_flg", "cc_gather", "kpack_e", "gk_e"))

    if not E["state"].get("vrow"):
        E["state"]["vrow"] = True
        with Scope(P) as sv:
            ohs = sv.sb("ohs", [33, XT], F32)
            ohs_b = Buf()
            vr = sv.sb("vr", [16, XT], F32)
            vr_b = Buf()
            P.dma("sp", ohs[:], oh_in.ap(), writes=[ohs_b])
            for (hc, x0, x1) in [(0, 0, 512), (0, 512, 1024), (0, 1024, XA), (16, XA, XT)]:
                P.op("pe", lambda en: en.matmul(pb[5][0:16, 0:x1 - x0], lhsT=spk[0:33, SP_RELB + hc:SP_RELB + hc + 16],
                                                rhs=ohs[:, x0:x1], start=True, stop=True),
                     reads=[ohs_b, b_spk], writes=[pb_b[5]])
                P.op("act", lambda en: en.activation(out=vr[:, x0:x1], in_=pb[5][0:16, 0:x1 - x0], func=AF.Copy),
                     reads=[], writes=[pb_b[5], vr_b])
            P.dma("sp", s_vrow.ap(), vr[:], reads=[vr_b], writes=[db("vrow")])

    def rms_grp(S, srcs, src_b, lhsT_ones, gcol, inv_n, dsts, dst_bufs, ncols=TC):
        C = len(srcs)
        sq, sq_b, rstd, rstd_b = S["sq"], S["sq_b"], S["rstd"], S["rstd_b"]
        for c in range(C):
            P.op("act", lambda en: en.activation(out=sq[:, c, 0:ncols], in_=srcs[c], func=AF.Square),
                 reads=[src_b], writes=[sq_b])
        for c in range(C):
            P.op("pe", lambda en: en.matmul(pb[4][:, 0:ncols], lhsT=lhsT_ones[:, :], rhs=sq[:, c, 0:ncols],
                                            start=(c == 0), stop=(c == C - 1)),
                 reads=[sq_b, b_const], writes=[pb_b[4]])
        P.op("act", lambda en: en.activation(out=rstd[:, 0:ncols], in_=pb[4][:, 0:ncols], func=AF.Sqrt,
                                             scale=inv_n, bias=eps_t[:, 0:1]),
             reads=[b_const], writes=[pb_b[4], rstd_b])
        P.op("dve", lambda en: en.reciprocal(out=rstd[:, 0:ncols], in_=rstd[:, 0:ncols]), reads=[rstd_b], writes=[rstd_b])
        for c in range(C):
            P.op("dve", lambda en: en.scalar_tensor_tensor(out=dsts[c], in0=srcs[c], scalar=spk[:, gcol + c:gcol + c + 1],
                                                           in1=rstd[:, 0:ncols], op0=ALU.mult, op1=ALU.mult),
                 reads=[src_b, rstd_b, b_spk], writes=dst_bufs)
    E["rms_grp"] = rms_grp

    for ps in range(0 if E["cfg"].get("skip_proj") else 1):
        t0 = 0
        with Scope(P) as so:
            hT = so.sb("hT", [128, 16, TT], BF16)
            hT_b = Buf("hT")
            with Scope(P) as sn:
                S0 = dict(xs=sn.sb("xs", [128, 16, TC], F32), xs_b=Buf(), sq=sn.sb("sq", [128, 16, TC], BF16),
                          sq_b=Buf(), rstd=sn.sb("rstd", [128, TC], F32), rstd_b=Buf())
                norm_x(S0, hT, hT_b, SP_ATTN + l * 16, t0)
            S = dict(sq=so.sb("sq", [128, 4, TC], BF16), sq_b=Buf(), rstd=so.sb("rstd", [128, TC], F32), rstd_b=Buf())
            stg4 = so.sb("stg4", [128, 4, TT], F32)
            stg4_b = Buf("stg4")
            stg1 = so.sb("stg1", [128, TT], F32)
            stg1_b = Buf("stg1")
            stg2 = so.sb("stg2", [128, 2, TT], F32)
            stg2_b = Buf("stg2")
            cqT = so.sb("cqT", [128, 4, TT], BF16)
            cqT_b = Buf("cqT")
            kvn = so.sb("kvn", [128, 2, TT], BF16)
            kvn_b = Buf("kvn")
            kvtok_st = so.sb("kvtok_st", [128, 8, 256], BF16)
            kvtok_b = Buf()
            kidx_st = so.sb("kidx_st", [64, TT], BF16)
            kidx_b = Buf()
            widx_st = so.sb("widx_st", [16, TT], F32)
            widx_b = Buf()
            widx_tok = so.sb("widx_tok", [128, 8, 16], F32)
            widx_tok_b = Buf()
            ob = so.sb("ob", [128, TT], BF16)
            ob_b = Buf("ob")
            ob2 = so.sb("ob2", [128, 2, TT], BF16)
            ob2_b = Buf("ob2")
            oq = [so.sb(f"oq{i}", [128, TC], BF16) for i in range(2)]
            oq_b = [Buf(), Buf()]
            oq_i = [0]
            vb_st = so.sb("vb_st", [128, TT], BF16)
            vb_b = Buf()
            vtok_st = so.sb("vtok_st", [128, 8, 128], BF16)
            vtok_b = Buf()

            def tsl(tci):
                return slice(tci * TC, (tci + 1) * TC)

            def in_epi(bi, m, tag, tci):
                kind = tag[0]
                last = (tci == TT // TC - 1)
                if kind == "cq":
                    c = tag[1]
                    P.op("act", lambda en: en.activation(out=stg4[:, c, tsl(tci)], in_=pb[bi][:], func=AF.Copy),
                         reads=[], writes=[pb_b[bi], stg4_b])
                    if c == 3 and last:
                        for t2 in range(TT // TC):
                            rms_grp(S, [stg4[:, cc, tsl(t2)] for cc in range(4)], stg4_b, ones_bf, SP_CQ + e * 4, 1.0 / 512,
                                    [cqT[:, cc, tsl(t2)] for cc in range(4)], [cqT_b])
                elif kind == "ckv":
                    c = tag[1]
                    P.op("act", lambda en: en.activation(out=stg2[:, c, tsl(tci)], in_=pb[bi][:], func=AF.Copy),
                         reads=[], writes=[pb_b[bi], stg2_b])
                    if c == 1 and last:
                        for t2 in range(TT // TC):
                            rms_grp(S, [stg2[:, cc, tsl(t2)] for cc in range(2)], stg2_b, ones_bf, SP_CKV + e * 2, 1.0 / 256,
                                    [kvn[:, cc, tsl(t2)] for cc in range(2)], [kvn_b])
                        P.dma("sp", s_kvT.ap()[:, TO:TO + TT].rearrange("(c p) t -> p c t", p=128), kvn[:],
                              reads=[kvn_b], writes=[db("kvT")])
                        P.dma("sp", kpack_e.ap()[0:256, :].rearrange("(c p) t -> p c t", p=128), kvn[:],
                              reads=[kvn_b], writes=[db("kpack_e")])
                        for tt in range(TT // 128):
                            for cc in range(2):
                                P.op("pe", lambda en: en.transpose(pbh[:, cc * 128:(cc + 1) * 128], kvn[:, cc, tt * 128:(tt + 1) * 128], ident_bf[:]),
                                     reads=[kvn_b, b_const], writes=[pbh_b])
                            P.op("act", lambda en: en.activation(out=kvtok_st[:, tt, :], in_=pbh[:, 0:256], func=AF.Copy),
                                 reads=[], writes=[pbh_b, kvtok_b])
                        P.dma("sp", s_kvtok.ap()[TO:TO + TT, :].rearrange("(tt p) c -> p tt c", p=128), kvtok_st[:],
                              reads=[kvtok_b], writes=[db("kvtok")])
                        P.dma("sp", kpack_e.ap()[576:832, :].rearrange("r (a c) -> (r a) c", c=256).rearrange("(tt p) c -> p tt c", p=128), kvtok_st[:],
                              reads=[kvtok_b], writes=[db("kpack_e")])
                elif kind == "kidx":
                    P.op("act", lambda en: en.activation(out=kidx_st[:, tsl(tci)], in_=pb[bi][0:64, :], func=AF.Copy),
                         reads=[], writes=[pb_b[bi], kidx_b])
                    if last:
                        P.dma("sp", s_kidxT.ap()[:, TO:TO + TT], kidx_st[:], reads=[kidx_b], writes=[db("kidxT")])
                        P.dma("sp", kpack_e.ap()[256:320, :], kidx_st[:], reads=[kidx_b], writes=[db("kpack_e")])
                elif kind == "widx":
                    P.op("act", lambda en: en.activation(out=widx_st[:, tsl(tci)], in_=pb[bi][0:16, :], func=AF.Copy),
                         reads=[], writes=[pb_b[bi], widx_b])
                    if last:
                        for tt in range(TT // 128):
                            P.op("pe", lambda en: en.transpose(pb[5][:, tt * 16:(tt + 1) * 16], widx_st[0:16, tt * 128:(tt + 1) * 128], ident_f[0:16, 0:16]),
                                 reads=[widx_b, b_cst], writes=[pb_b[5]])
                        P.op("act", lambda en: en.activation(out=widx_tok[:], in_=pb[5][:, 0:128].rearrange("p (a b) -> p a b", b=16), func=AF.Copy),
                             reads=[], writes=[pb_b[5], widx_tok_b])
                        P.dma("sp", s_widx.ap()[t0:t0 + TT, :].rearrange("(tt p) c -> p tt c", p=128), widx_tok[:],
                              reads=[widx_tok_b], writes=[db("widx")])
                elif kind == "qb":
                    c = tag[1]
                    P.op("act", lambda en: en.activation(out=stg1[:, tsl(tci)], in_=pb[bi][:], func=AF.Copy),
                         reads=[], writes=[pb_b[bi], stg1_b])
                    if last:
                        for t2 in range(TT // TC):
                            rms_grp(S, [stg1[:, tsl(t2)]], stg1_b, bd_bf, SP_BQ + e, 1.0 / 64, [ob[:, tsl(t2)]], [ob_b])
                        P.dma("sp", s_qbT.ap()[c * 128:(c + 1) * 128, t0:t0 + TT], ob[:], reads=[ob_b], writes=[db("qbT")])
                elif kind == "kb":
                    g, half = tag[1], tag[2]
                    pbs = half * 64
                    P.op("act", lambda en: en.activation(out=stg1[pbs:pbs + 64, tsl(tci)], in_=pb[bi][pbs:pbs + 64, :], func=AF.Copy),
                         reads=[], writes=[pb_b[bi], stg1_b])
                    if half == 1 and last:
                        for t2 in range(TT // TC):
                            rms_grp(S, [stg1[:, tsl(t2)]], stg1_b, bd_bf, SP_BK + e, 1.0 / 64, [ob[:, tsl(t2)]], [ob_b])
                        P.dma("sp", s_kdupT.ap()[g * 128:(g + 1) * 128, TO:TO + TT], ob[:], reads=[ob_b], writes=[db("kdupT")])
                        P.dma("sp", kpack_e.ap()[320 + g * 128:320 + (g + 1) * 128, :], ob[:], reads=[ob_b], writes=[db("kpack_e")])
                elif kind == "vb":
                    P.op("act", lambda en: en.activation(out=vb_st[:, tsl(tci)], in_=pb[bi][:], func=AF.Copy),
                         reads=[], writes=[pb_b[bi], vb_b])
                    if last:
                        for tt in range(TT // 128):
                            P.op("pe", lambda en: en.transpose(pbh[:, (tt % 4) * 128:(tt % 4 + 1) * 128], vb_st[:, tt * 128:(tt + 1) * 128], ident_bf[:]),
                                 reads=[vb_b, b_const], writes=[pbh_b])
                            if tt % 4 == 3:
                                P.op("act", lambda en: en.activation(out=vtok_st[:, tt - 3:tt + 1, :], in_=pbh[:, 0:512].rearrange("p (a b) -> p a b", b=128), func=AF.Copy),
                                     reads=[], writes=[pbh_b, vtok_b])
                        P.dma("sp", s_vbtok.ap()[TO:TO + TT, :].rearrange("(tt p) c -> p tt c", p=128), vtok_st[:],
                              reads=[vtok_b], writes=[db("vbtok")])
                        P.dma("sp", kpack_e.ap()[832:960, :].rearrange("r (a c) -> (r a) c", c=128).rearrange("(tt p) c -> p tt c", p=128), vtok_st[:],
                              reads=[vtok_b], writes=[db("kpack_e")])
                elif kind == "qa":
                    h, cc = tag[1], tag[2]
                    P.op("act", lambda en: en.activation(out=stg2[:, cc, tsl(tci)], in_=pb[bi][:], func=AF.Copy),
                         reads=[], writes=[pb_b[bi], stg2_b])
                    if cc == 1 and last:
                        for t2 in range(TT // TC):
                            rms_grp(S, [stg2[:, c2, tsl(t2)] for c2 in range(2)], stg2_b, ones_bf, SP_AQ + e * 2, 1.0 / 256,
                                    [ob2[:, c2, tsl(t2)] for c2 in range(2)], [ob2_b])
                        P.dma("sp", s_qaT.ap()[h * 256:(h + 1) * 256, t0:t0 + TT].rearrange("(c p) t -> p c t", p=128), ob2[:],
                              reads=[ob2_b], writes=[db("qaT")])
                elif kind == "qi":
                    c = tag[1]
                    k = oq_i[0]
                    oq_i[0] = (k + 1) % 2
                    P.op("act", lambda en: en.activation(out=oq[k][:], in_=pb[bi][:], func=AF.Copy),
                         reads=[], writes=[pb_b[bi], oq_b[k]])
                    P.dma("sp", s_qiT.ap()[c * 128:(c + 1) * 128, t0 + tci * TC:t0 + (tci + 1) * TC], oq[k][:],
                          reads=[oq_b[k]], writes=[db("qiT")])

            blocks = [
                ([(0, 512)], [(c * 128, 128, ("cq", c), 0) for c in range(4)]),
                ([(512, 336)], [(0, 128, ("ckv", 0), 0), (128, 128, ("ckv", 1), 0), (256, 64, ("kidx",), 0), (320, 16, ("widx",), 0)]),
                ([(848, 512)], [(c * 128, 128, ("qb", c), 0) for c in range(4)]),
                ([(1360, 512)], [(c * 128, 128, ("qb", 4 + c), 0) for c in range(4)]),
                ([(1872, 256)], [(0, 64, ("kb", 0, 0), 0), (0, 64, ("kb", 0, 1), 64), (64, 64, ("kb", 1, 0), 0), (64, 64, ("kb", 1, 1), 64),
                                 (128, 128, ("vb",), 0)]),
            ]
            gemm(hT, hT_b, 16, lambda c0, n: Wd["w_in_even"].ap()[e, :, c0:c0 + n], blocks, in_epi)
            blocks = []
            for b4 in range(2):
                blocks.append(([(b4 * 2048, 2048)], [((hh * 2 + cc) * 128, 128, ("qa", b4 * 8 + hh, cc), 0) for hh in range(8) for cc in range(2)]))
            gemm(cqT, cqT_b, 4, lambda c0, n: Wd["a_w_uq"].ap()[e, :, c0:c0 + n], blocks, in_epi)
            gemm(cqT, cqT_b, 4, lambda c0, n: Wd["a_w_qidx"].ap()[e, :, c0:c0 + n],
                 [([(0, 1024)], [(c * 128, 128, ("qi", c), 0) for c in range(8)])], in_epi)

    if not E["cfg"].get("skip_proj"):
        cc_gather(kpack_e, gk_e, [db("kpack_e")], [db("gk_e")])
        P.dma("sp", s_kvT.ap()[:, 0:TO], gk_e.ap()[0:256, :], reads=[db("gk_e")], writes=[db("kvT")])
        P.dma("sp", s_kidxT.ap()[:, 0:TO], gk_e.ap()[256:320, :], reads=[db("gk_e")], writes=[db("kidxT")])
        P.dma("sp", s_kdupT.ap()[:, 0:TO], gk_e.ap()[320:576, :], reads=[db("gk_e")], writes=[db("kdupT")])
        P.dma("sp", s_kvtok.ap()[0:TO, :], gk_e.ap()[576:832, :].rearrange("r (a c) -> (r a) c", c=256), reads=[db("gk_e")], writes=[db("kvtok")])
        P.dma("sp", s_vbtok.ap()[0:TO, :], gk_e.ap()[832:960, :].rearrange("r (a c) -> (r a) c", c=128), reads=[db("gk_e")], writes=[db("vbtok")])
    if E["cfg"].get("stop_after_proj"):
        return
    att_scale = 1.0 / 16.0
    with Scope(P) as sa:
        kvT = sa.sb("kvT", [128, 2, T], BF16)
        kvtok = sa.sb("kvtok", [128, 16, 256], BF16)
        kidx2 = sa.sb("kidx2", [128, T], BF16)
        wuv = sa.sb("wuv", [128, 16, 2, 64], BF16)
        expA = sa.sb("expA", [128, 16, 9, 128], BF16)
        b_k = Buf("kside")
        b_exp = Buf("expA")
        P.dma("sp", kvT[:], s_kvT.ap().rearrange("(c p) t -> p c t", p=128), reads=[db("kvT")], writes=[b_k])
        P.dma("sp", kvtok[:], s_kvtok.ap().rearrange("(tt p) c -> p tt c", p=128), reads=[db("kvtok")], writes=[b_k])
        P.dma("sp", kidx2[0:64, :], s_kidxT.ap(), reads=[db("kidxT")], writes=[b_k])
        P.dma("sp", kidx2[64:128, :], s_kidxT.ap(), reads=[db("kidxT")], writes=[b_k])
        P.dma("pool", wuv[:], Wd["a_w_uv"].ap()[e].rearrange("h (cc p) d -> p h cc d", p=128), writes=[b_k])
        s_expA = E["s_expA"]
        if not E["state"].get("expA"):
            E["state"]["expA"] = True
            hk = [sa.sb(f"hk{i}", [128, 128], F32) for i in range(4)]
            hk_b = [Buf() for _ in range(4)]
            n = 0
            for h in range(16):
                for dj in range(9):
                    k = n % 4
                    bk5 = 5 + (n % 2)
                    n += 1
                    P.dma("sp", hk[k][:], bass.AP(s_vrow, h * XT + dj * 128, [[1, 128], [1, 128]]), reads=[db("vrow")], writes=[hk_b[k]])
                    P.op("pe", lambda en: en.matmul(pb[bk5][:, 0:128], lhsT=jflip, rhs=hk[k][:], start=True, stop=True),
                         reads=[hk_b[k], b_cst], writes=[pb_b[bk5]])
                    P.op("act", lambda en: en.activation(out=expA[:, h, 8 - dj, :], in_=pb[bk5][:, 0:128], func=AF.Exp),
                         reads=[], writes=[pb_b[bk5], b_exp])
            P.dma("sp", s_expA.ap(), expA[:].rearrange("p h k q -> p (h k q)"), reads=[b_exp], writes=[db("expA")])
        else:
            P.dma("sp", expA[:].rearrange("p h k q -> p (h k q)"), s_expA.ap(), reads=[db("expA")], writes=[b_exp])
        qi = [sa.sb(f"qi{i}", [128, 8, 128], BF16) for i in range(2)]
        qa = [sa.sb(f"qa{i}", [128, 32, 128], BF16) for i in range(2)]
        wq = [sa.sb(f"wq{i}", [128, 16], F32) for i in range(2)]
        q_b = [Buf(), Buf()]
        score = sa.sb("score", [128, T], F32)
        score_b = Buf("score")
        work = sa.sb("work", [128, T], F32)
        work_b = Buf("work")
        m8 = sa.sb("m8", [128, 8], F32)
        m8_b = Buf("m8")
        mask01 = sa.sb("mask01", [128, T], BF16)
        mask01_b = Buf()
        maskT = sa.sb("maskT", [128, 16, 128], BF16)
        maskT_b = Buf()
        rl = [sa.sb(f"rl{i}", [128, 512], F32) for i in range(2)]
        rl_b = [Buf(), Buf()]
        NBD = 3
        LBD = [0, 1, 4]
        pf = [sa.sb(f"pf{i}", [128, 512], F32) for i in range(NBD)]
        pf_b = [Buf() for _ in range(NBD)]
        pbf = [sa.sb(f"pbf{i}", [128, 512], BF16) for i in range(NBD)]
        pbf_b = [Buf() for _ in range(NBD)]
        rc = sa.sb("rc", [128, 128], F32)
        rc_b = Buf()
        on = sa.sb("on", [128, 2, 128], BF16)
        on_b = Buf()
        ya_st = [sa.sb(f"ya_st{i}", [128, 8, 128], BF16) for i in range(2)]
        ya_b = [Buf(), Buf()]
        cnt = [0, 0]
        for a_ in range(E["cfg"].get("dsa_tiles", 8)):
            i = 8 + a_
            qk = i % 2
            N = (i + 1) * 128
            qs = slice(a_ * 128, (a_ + 1) * 128)
            ks = slice(i * 128, (i + 1) * 128)
            P.dma("sp", qi[qk][:], s_qiT.ap()[:, qs].rearrange("(c p) t -> p c t", p=128), reads=[db("qiT")], writes=[q_b[qk]])
            P.dma("sp", qa[qk][:], s_qaT.ap()[:, qs].rearrange("(c p) t -> p c t", p=128), reads=[db("qaT")], writes=[q_b[qk]])
            P.dma("sp", wq[qk][:], s_widx.ap()[qs, :], reads=[db("widx")], writes=[q_b[qk]])
            for h in range(16):
                pbs = (h % 2) * 64
                for n0 in range(0, N, 512):
                    n1 = min(N, n0 + 512)
                    bk = cnt[0] % 2
                    cnt[0] += 1
                    P.op("pe", lambda en: en.matmul(pb[bk][:, 0:n1 - n0], lhsT=qi[qk][pbs:pbs + 64, h // 2, :], rhs=kidx2[pbs:pbs + 64, n0:n1],
                                                    start=True, stop=True),
                         reads=[q_b[qk], b_k], writes=[pb_b[bk]])
                    P.op("act", lambda en: en.activation(out=rl[bk][:, 0:n1 - n0], in_=pb[bk][:, 0:n1 - n0], func=AF.Relu),
                         reads=[], writes=[pb_b[bk], rl_b[bk]])
                    if h == 0:
                        P.op("dve", lambda en: en.tensor_scalar(out=score[:, n0:n1], in0=rl[bk][:, 0:n1 - n0], scalar1=wq[qk][:, 0:1], scalar2=None, op0=ALU.mult),
                             reads=[rl_b[bk], q_b[qk]], writes=[score_b])
                    else:
                        P.op("dve", lambda en: en.scalar_tensor_tensor(out=score[:, n0:n1], in0=rl[bk][:, 0:n1 - n0], scalar=wq[qk][:, h:h + 1],
                                                                       in1=score[:, n0:n1], op0=ALU.mult, op1=ALU.add),
                             reads=[rl_b[bk], q_b[qk], score_b], writes=[score_b])
            P.op("dve", lambda en: en.tensor_tensor(out=score[:, ks], in0=score[:, ks], in1=cneg, op=ALU.add),
                 reads=[score_b, b_cst], writes=[score_b])
            P.op("dve", lambda en: en.tensor_scalar(out=score[:, 0:TO], in0=score[:, 0:TO], scalar1=flg[:, 1:2], scalar2=None, op0=ALU.add),
                 reads=[score_b, b_flg], writes=[score_b])
            if i >= 2:
                cur, cur_b = score, score_b
                for it in range(32):
                    P.op("dve", lambda en: en.max(out=m8[:], in_=cur[:, 0:N]), reads=[cur_b], writes=[m8_b])
                    if it < 31:
                        P.op("dve", lambda en: en.match_replace(out=work[:, 0:N], in_to_replace=m8[:], in_values=cur[:, 0:N], imm_value=-1e30),
                             reads=[m8_b, cur_b], writes=[work_b])
                        cur, cur_b = work, work_b
                P.op("dve", lambda en: en.tensor_scalar(out=work[:, 0:N], in0=score[:, 0:N], scalar1=m8[:, 7:8], scalar2=None, op0=ALU.is_ge),
                     reads=[score_b, m8_b], writes=[work_b])
                P.op("dve", lambda en: en.scalar_tensor_tensor(out=mask01[:, 0:N], in0=score[:, 0:N], scalar=-1e29, in1=work[:, 0:N],
                                                               op0=ALU.is_gt, op1=ALU.mult),
                     reads=[score_b, work_b], writes=[mask01_b])
            else:
                P.op("dve", lambda en: en.tensor_scalar(out=mask01[:, 0:N], in0=score[:, 0:N], scalar1=-1e29, scalar2=None, op0=ALU.is_ge),
                     reads=[score_b], writes=[mask01_b])
            for j0 in range(0, i + 1, 8):
                j1 = min(i + 1, j0 + 8)
                for j in range(j0, j1):
                    P.op("pe", lambda en: en.transpose(pbh[:, (j - j0) * 128:(j - j0 + 1) * 128], mask01[:, j * 128:(j + 1) * 128], ident_bf[:]),
                         reads=[mask01_b, b_const], writes=[pbh_b])
                P.op("act", lambda en: en.activation(out=maskT[:, j0:j1, :], in_=pbh[:, 0:(j1 - j0) * 128].rearrange("p (a b) -> p a b", b=128), func=AF.Copy),
                     reads=[], writes=[pbh_b, maskT_b])
            yk = i % 2
            items = [(h, jg) for h in range(16) for jg in range(0, i + 1, 4)]

            def emit_logits(k):
                h, jg = items[k]
                je = min(i + 1, jg + 4)
                L = k % NBD
                BK = LBD[L]
                for j in range(jg, je):
                    sl = j - jg
                    for c in range(2):
                        P.op("pe", lambda en: en.matmul(pb[BK][:, sl * 128:(sl + 1) * 128], lhsT=kvT[:, c, j * 128:(j + 1) * 128],
                                                        rhs=qa[qk][:, 2 * h + c, :], start=(c == 0), stop=(c == 1)),
                             reads=[b_k, q_b[qk]], writes=[pb_b[BK]])

            def emit_post(k):
                h, jg = items[k]
                je = min(i + 1, jg + 4)
                nj = je - jg
                L = k % NBD
                BK = LBD[L]
                far = (i - (je - 1)) >= 8
                if far:
                    P.op("act", lambda en: en.activation(out=pf[L][:, 0:nj * 128], in_=pb[BK][:, 0:nj * 128], func=AF.Exp, scale=att_scale,
                                                         bias=spk[:, SP_B31 + h:SP_B31 + h + 1]),
                         reads=[b_spk], writes=[pb_b[BK], pf_b[L]])
                else:
                    P.op("act", lambda en: en.activation(out=pf[L][:, 0:nj * 128], in_=pb[BK][:, 0:nj * 128], func=AF.Exp, scale=att_scale),
                         reads=[], writes=[pb_b[BK], pf_b[L]])
                    if i - jg <= 8:
                        k0 = 8 - (i - jg)
                        P.op("dve", lambda en: en.tensor_tensor(out=pf[L][:, 0:nj * 128].rearrange("p (a b) -> p a b", b=128),
                                                                in0=pf[L][:, 0:nj * 128].rearrange("p (a b) -> p a b", b=128),
                                                                in1=expA[:, h, k0:k0 + nj, :], op=ALU.mult),
                             reads=[pf_b[L], b_exp], writes=[pf_b[L]])
                    else:
                        for j in range(jg, je):
                            sl = j - jg
                            kk = 8 - min(i - j, 8)
                            P.op("dve", lambda en: en.tensor_tensor(out=pf[L][:, sl * 128:(sl + 1) * 128], in0=pf[L][:, sl * 128:(sl + 1) * 128],
                                                                    in1=expA[:, h, kk, :], op=ALU.mult),
                                 reads=[pf_b[L], b_exp], writes=[pf_b[L]])
                P.op("pool", lambda en: en.tensor_tensor(out=pbf[L][:, 0:nj * 128].rearrange("p (a b) -> p a b", b=128),
                                                         in0=pf[L][:, 0:nj * 128].rearrange("p (a b) -> p a b", b=128),
                                                         in1=maskT[:, jg:je, :], op=ALU.mult),
                     reads=[pf_b[L], maskT_b], writes=[pbf_b[L]])

            def emit_pv(k):
                h, jg = items[k]
                je = min(i + 1, jg + 4)
                L = k % NBD
                for j in range(jg, je):
                    sl = j - jg
                    for (bk, lh) in ((2, kvtok[:, j, 0:128]), (3, kvtok[:, j, 128:256]), (6, ones_bf[:, :])):
                        P.op("pe", lambda en: en.matmul(pb[bk][:, 0:128], lhsT=lh, rhs=pbf[L][:, sl * 128:(sl + 1) * 128],
                                                        start=(j == 0), stop=(j == i)),
                             reads=[b_k, pbf_b[L], b_const], writes=[pb_b[bk]])

            def emit_fin_dve(h):
                P.op("dve", lambda en: en.reciprocal(out=rc[:], in_=pb[6][:, 0:128]), reads=[], writes=[pb_b[6], rc_b])
                for c in range(2):
                    P.op("dve", lambda en: en.tensor_tensor(out=on[:, c, :], in0=pb[2 + c][:, 0:128], in1=rc[:], op=ALU.mult),
                         reads=[rc_b], writes=[pb_b[2 + c], on_b])

            def emit_fin_pe(h):
                pbs = (h % 2) * 64
                col = ((h // 2) % 4) * 128
                for c in range(2):
                    P.op("pe", lambda en: en.matmul(pb[5][pbs:pbs + 64, col:col + 128], lhsT=wuv[:, h, c, :], rhs=on[:, c, :],
                                                    start=(c == 0), stop=(c == 1)),
                         reads=[b_k, on_b], writes=[pb_b[5]])
                if h % 2 == 1:
                    P.op("act", lambda en: en.activation(out=ya_st[yk][:, h // 2, :], in_=pb[5][:, col:col + 128], func=AF.Copy),
                         reads=[], writes=[pb_b[5], ya_b[yk]])

            for k0 in range(min(NBD - 1, len(items))):
                emit_logits(k0)
            for k in range(len(items)):
                h, jg = items[k]
                if k + NBD - 1 < len(items):
                    emit_logits(k + NBD - 1)
                emit_post(k)
                emit_pv(k)
                if jg + 4 > i:
                    emit_fin_dve(h)
                    emit_fin_pe(h)
            P.dma("sp", yT.ap()[0:1024, qs].rearrange("(c p) t -> p c t", p=128), ya_st[yk][:], reads=[ya_b[yk]], writes=[db("yT", 0)])

    with Scope(P) as sw:
        kd = sw.sb("kd", [128, 2, T], BF16)
        vtk = sw.sb("vtk", [128, 16, 128], BF16)
        expB = sw.sb("expB", [128, 16, 2, 128], BF16)
        esk = sw.sb("esk", [128, 16], F32)
        b_k = Buf("kside")
        b_exp = Buf("expB")
        P.dma("sp", kd[:], s_kdupT.ap().rearrange("(g p) t -> p g t", p=128), reads=[db("kdupT")], writes=[b_k])
        P.dma("sp", vtk[:], s_vbtok.ap().rearrange("(tt p) c -> p tt c", p=128), reads=[db("vbtok")], writes=[b_k])
        P.op("act", lambda en: en.activation(out=esk[:], in_=spk[:, SP_SINK + e * 16:SP_SINK + (e + 1) * 16], func=AF.Exp),
             reads=[b_spk], writes=[b_exp])
        s_expB = E["s_expB"]
        if not E["state"].get("expB"):
            E["state"]["expB"] = True
            hk = [sw.sb(f"hk{i}", [128, 128], F32) for i in range(4)]
            hk_b = [Buf() for _ in range(4)]
            n = 0
            for hb in range(16):
                for kx in range(2):
                    dj = 1 - kx
                    k = n % 4
                    bk5 = [4, 2][n % 2]
                    n += 1
                    P.dma("sp", hk[k][:], bass.AP(s_vrow, hb * XT + XA + dj * 128, [[1, 128], [1, 128]]), reads=[db("vrow")], writes=[hk_b[k]])
                    P.op("pe", lambda en: en.matmul(pb[bk5][:, 0:128], lhsT=jflip, rhs=hk[k][:], start=True, stop=True),
                         reads=[hk_b[k], b_cst], writes=[pb_b[bk5]])
                    P.op("act", lambda en: en.activation(out=expB[:, hb, kx, :], in_=pb[bk5][:, 0:128], func=AF.Exp),
                         reads=[], writes=[pb_b[bk5], b_exp])
            P.dma("sp", s_expB.ap(), expB[:].rearrange("p h k q -> p (h k q)"), reads=[b_exp], writes=[db("expB")])
        else:
            P.dma("sp", expB[:].rearrange("p h k q -> p (h k q)"), s_expB.ap(), reads=[db("expB")], writes=[b_exp])
        qb = [sw.sb(f"qb{i}", [128, 8, 128], BF16) for i in range(2)]
        qb_b = [Buf(), Buf()]
        pf = [sw.sb(f"pf{i}", [128, 512], F32) for i in range(2)]
        pf_b = [Buf(), Buf()]
        pbf = [sw.sb(f"pbf{i}", [128, 512], BF16) for i in range(2)]
        pbf_b = [Buf(), Buf()]
        dn = sw.sb("dn", [128, 128], F32)
        dn_b = Buf()
        yb_st = [sw.sb(f"yb_st{i}", [128, 8, 128], BF16) for i in range(2)]
        yb_b = [Buf(), Buf()]
        cnt = 0
        for a_ in range(E["cfg"].get("swa_blocks", 8)):
            nb = 8 + a_
            qk = nb % 2
            qs = slice(a_ * 128, (a_ + 1) * 128)
            P.dma("sp", qb[qk][:], s_qbT.ap()[:, qs].rearrange("(c p) t -> p c t", p=128), reads=[db("qbT")], writes=[qb_b[qk]])
            for m in range(8):
                Lb = [(0, 1), (5, 6)][cnt % 2]
                Ls = cnt % 2
                cnt += 1
                units = []
                for hh in range(2):
                    for kx in range(2):
                        dj = 1 - kx
                        if nb - dj >= 0:
                            units.append((hh, kx, nb - dj, hh * 2 + kx))
                for (hh, kx, j, sl) in units:
                    hb = 2 * m + hh
                    g = hb // 8
                    pbs = hh * 64
                    bkx = Lb[hh]
                    P.op("pe", lambda en: en.matmul(pb[bkx][:, kx * 128:(kx + 1) * 128], lhsT=kd[pbs:pbs + 64, g, j * 128:(j + 1) * 128],
                                                    rhs=qb[qk][pbs:pbs + 64, m, :], start=True, stop=True),
                         reads=[b_k, qb_b[qk]], writes=[pb_b[bkx]])
                stg_ = E["cfg"].get("swa_stage", 4)
                if stg_ < 2:
                    continue
                L = Ls
                for hh in range(2):
                    bkx = Lb[hh]
                    a, b = (0, 2)
                    P.op("act", lambda en: en.activation(out=pf[L][:, hh * 256 + a * 128:hh * 256 + b * 128], in_=pb[bkx][:, a * 128:b * 128], func=AF.Exp, scale=0.125),
                         reads=[], writes=[pb_b[bkx], pf_b[L]])
                    P.op("dve", lambda en: en.tensor_tensor(out=pbf[L][:, hh * 256 + a * 128:hh * 256 + b * 128], in0=pf[L][:, hh * 256 + a * 128:hh * 256 + b * 128],
                                                            in1=expB[:, 2 * m + hh, a:b, :].rearrange("p k q -> p (k q)"), op=ALU.mult),
                         reads=[pf_b[L], b_exp], writes=[pbf_b[L]])
                    if a_ == 0:
                        P.op("dve", lambda en: en.tensor_scalar(out=pbf[L][:, hh * 256:hh * 256 + 128], in0=pbf[L][:, hh * 256:hh * 256 + 128],
                                                                scalar1=flg[:, 0:1], scalar2=None, op0=ALU.mult),
                             reads=[pbf_b[L], b_flg], writes=[pbf_b[L]])
                if stg_ < 3:
                    continue
                for hh in range(2):
                    us = [u for u in units if u[0] == hh]
                    hb = 2 * m + hh
                    g = hb // 8
                    pbs = hh * 64
                    for ui, (_, kx, j, sl) in enumerate(us):
                        P.op("pe", lambda en: en.matmul(pb[2][pbs:pbs + 64, 0:128], lhsT=vtk[:, j, g * 64:(g + 1) * 64], rhs=pbf[L][:, sl * 128:(sl + 1) * 128],
                                                        start=(ui == 0), stop=(ui == len(us) - 1)),
                             reads=[b_k, pbf_b[L]], writes=[pb_b[2]])
                        P.op("pe", lambda en: en.matmul(pb[3][pbs:pbs + 64, 0:128], lhsT=ones_bf[:, 0:64], rhs=pbf[L][:, sl * 128:(sl + 1) * 128],
                                                        start=(ui == 0), stop=(ui == len(us) - 1)),
                             reads=[b_const, pbf_b[L]], writes=[pb_b[3]])
                if stg_ < 4:
                    continue
                for hh in range(2):
                    hb = 2 * m + hh
                    pbs = hh * 64
                    P.op("dve", lambda en: en.tensor_scalar(out=dn[pbs:pbs + 64, :], in0=pb[3][pbs:pbs + 64, 0:128], scalar1=esk[pbs:pbs + 64, hb:hb + 1],
                                                            scalar2=None, op0=ALU.add),
                         reads=[b_exp], writes=[pb_b[3], dn_b])
                P.op("dve", lambda en: en.reciprocal(out=dn[:], in_=dn[:]), reads=[dn_b], writes=[dn_b])
                P.op("dve", lambda en: en.tensor_tensor(out=yb_st[qk][:, m, :], in0=pb[2][:, 0:128], in1=dn[:], op=ALU.mult),
                     reads=[dn_b], writes=[pb_b[2], yb_b[qk]])
            P.dma("sp", yT.ap()[1024:2048, qs].rearrange("(c p) t -> p c t", p=128), yb_st[qk][:], reads=[yb_b[qk]], writes=[db("yT", 0)])


def odd_attention(E, l):
    o = l // 2
    P, nc, Wd, db, gemm, simple_blocks, norm_x = (E[k] for k in ("P", "nc", "Wd", "db", "gemm", "simple_blocks", "norm_x"))
    spk, b_spk, pb, pb_b, pbh, pbh_b, ident_f, ident_bf, b_cst, b_const, ones_bf, ones_f, bd_bf, triu_f, triu_bf, eps_t = (E[k] for k in (
        "spk", "b_spk", "pb", "pb_b", "pbh", "pbh_b", "ident_f", "ident_bf", "b_cst", "b_const", "ones_bf", "ones_f", "bd_bf", "triu_f", "triu_bf", "eps_t"))
    xT, yT = E["xT"], E["yT"]
    s_qT, s_kT, s_vtok, s_lf = E["s_qT"], E["s_kT"], E["s_vtok"], E["s_lf"]
    flg, b_flg, cc_gather, kpack_o, gk_o, lfp, glf = (E[k] for k in ("flg", "b_flg", "cc_gather", "kpack_o", "gk_o", "lfp", "glf"))

    def rms_grp(S, srcs, src_b, lhsT_ones, gcol, inv_n, dsts, dst_bufs, ncols=TC):
        C = len(srcs)
        sq, sq_b, rstd, rstd_b = S["sq"], S["sq_b"], S["rstd"], S["rstd_b"]
        for c in range(C):
            P.op("act", lambda en: en.activation(out=sq[:, c, 0:ncols], in_=srcs[c], func=AF.Square),
                 reads=[src_b], writes=[sq_b])
        for c in range(C):
            P.op("pe", lambda en: en.matmul(pb[4][:, 0:ncols], lhsT=lhsT_ones[:, :], rhs=sq[:, c, 0:ncols],
                                            start=(c == 0), stop=(c == C - 1)),
                 reads=[sq_b, b_const], writes=[pb_b[4]])
        P.op("act", lambda en: en.activation(out=rstd[:, 0:ncols], in_=pb[4][:, 0:ncols], func=AF.Sqrt,
                                             scale=inv_n, bias=eps_t[:, 0:1]),
             reads=[b_const], writes=[pb_b[4], rstd_b])
        P.op("dve", lambda en: en.reciprocal(out=rstd[:, 0:ncols], in_=rstd[:, 0:ncols]), reads=[rstd_b], writes=[rstd_b])
        for c in range(C):
            P.op("dve", lambda en: en.scalar_tensor_tensor(out=dsts[c], in0=srcs[c], scalar=spk[:, gcol + c:gcol + c + 1],
                                                           in1=rstd[:, 0:ncols], op0=ALU.mult, op1=ALU.mult),
                 reads=[src_b, rstd_b, b_spk], writes=dst_bufs)

    for ps in range(0 if E["cfg"].get("skip_proj") else 1):
        t0 = 0
        with Scope(P) as so:
            hT = so.sb("hT", [128, 16, TT], BF16)
            hT_b = Buf("hT")
            with Scope(P) as sn:
                S0 = dict(xs=sn.sb("xs", [128, 16, TC], F32), xs_b=Buf(), sq=sn.sb("sq", [128, 16, TC], BF16),
                          sq_b=Buf(), rstd=sn.sb("rstd", [128, TC], F32), rstd_b=Buf())
                norm_x(S0, hT, hT_b, SP_ATTN + l * 16, t0)
            S = dict(sq=so.sb("sq", [128, 1, TC], BF16), sq_b=Buf(), rstd=so.sb("rstd", [128, TC], F32), rstd_b=Buf())
            stg1 = so.sb("stg1", [128, TT], F32)
            stg1_b = Buf("stg1")
            ob = so.sb("ob", [128, TT], BF16)
            ob_b = Buf("ob")
            vb_st = so.sb("vb_st", [128, TT], BF16)
            vb_b = Buf()
            vtok_st = so.sb("vtok_st", [128, 8, 128], BF16)
            vtok_b = Buf()
            negfb = so.sb("negfb", [32, 1], F32)
            negfb_b = Buf()
            fst = so.sb("fst", [32, TT], F32)
            fst_b = Buf()
            lf_tok = so.sb("lf_tok", [128, 8, 32], F32)
            lf_tok_b = Buf()
            P.op("dve", lambda en: en.tensor_scalar(out=negfb[:], in0=spk[0:32, SP_FB + o:SP_FB + o + 1], scalar1=-1.0, scalar2=None, op0=ALU.mult),
                 reads=[b_spk], writes=[negfb_b])

            def tsl(tci):
                return slice(tci * TC, (tci + 1) * TC)

            def in_epi(bi, m, tag, tci):
                kind = tag[0]
                last = (tci == TT // TC - 1)
                if kind in ("q", "k"):
                    c = tag[1]
                    P.op("act", lambda en: en.activation(out=stg1[:, tsl(tci)], in_=pb[bi][:], func=AF.Copy),
                         reads=[], writes=[pb_b[bi], stg1_b])
                    if last:
                        gcol = (SP_CQN if kind == "q" else SP_CKN) + o
                        dst = s_qT if kind == "q" else s_kT
                        for t2 in range(TT // TC):
                            rms_grp(S, [stg1[:, tsl(t2)]], stg1_b, bd_bf, gcol, 1.0 / 64, [ob[:, tsl(t2)]], [ob_b])
                        if kind == "q":
                            P.dma("sp", s_qT.ap()[c * 128:(c + 1) * 128, 0:TT], ob[:], reads=[ob_b], writes=[db("qT")])
                        else:
                            P.dma("sp", s_kT.ap()[c * 128:(c + 1) * 128, TO:TO + TT], ob[:], reads=[ob_b], writes=[db("kT")])
                            P.dma("sp", kpack_o[c // 8].ap()[(c % 8) * 128:(c % 8 + 1) * 128, :], ob[:], reads=[ob_b], writes=[db("kpack_o", c // 8)])
                elif kind == "v":
                    c = tag[1]
                    P.op("act", lambda en: en.activation(out=vb_st[:, tsl(tci)], in_=pb[bi][:], func=AF.Copy),
                         reads=[], writes=[pb_b[bi], vb_b])
                    if last:
                        for tt in range(TT // 128):
                            P.op("pe", lambda en: en.transpose(pbh[:, (tt % 4) * 128:(tt % 4 + 1) * 128], vb_st[:, tt * 128:(tt + 1) * 128], ident_bf[:]),
                                 reads=[vb_b, b_const], writes=[pbh_b])
                            if tt % 4 == 3:
                                P.op("act", lambda en: en.activation(out=vtok_st[:, tt - 3:tt + 1, :], in_=pbh[:, 0:512].rearrange("p (a b) -> p a b", b=128), func=AF.Copy),
                                     reads=[], writes=[pbh_b, vtok_b])
                        P.dma("sp", s_vtok.ap()[TO:TO + TT, c * 128:(c + 1) * 128].rearrange("(tt p) c -> p tt c", p=128), vtok_st[:],
                              reads=[vtok_b], writes=[db("vtok")])
                        for hv in range(2):
                            P.dma("sp", kpack_o[2 + hv].ap().rearrange("(t a) c -> t (a c)", a=2)[:, c * 128:(c + 1) * 128].rearrange("(tt p) c -> p tt c", p=128),
                                  vtok_st[:, hv * 4:(hv + 1) * 4, :], reads=[vtok_b], writes=[db("kpack_o", 2 + hv)])
                elif kind == "f":
                    P.op("act", lambda en: en.activation(out=fst[:, tsl(tci)], in_=pb[bi][0:32, :], func=AF.Exp, scale=-1.0, bias=negfb[:, 0:1]),
                         reads=[negfb_b], writes=[pb_b[bi], fst_b])
                    if last:
                        P.op("act", lambda en: en.activation(out=fst[:], in_=fst[:], func=AF.Ln, bias=ones_f[0:32, 0:1]),
                             reads=[fst_b, b_const], writes=[fst_b])
                        for tt in range(TT // 128):
                            P.op("pe", lambda en: en.transpose(pb[5][:, tt * 32:(tt + 1) * 32], fst[0:32, tt * 128:(tt + 1) * 128], ident_f[0:32, 0:32]),
                                 reads=[fst_b, b_cst], writes=[pb_b[5]])
                        P.op("act", lambda en: en.activation(out=lf_tok[:], in_=pb[5][:, 0:256].rearrange("p (a b) -> p a b", b=32), func=AF.Copy),
                             reads=[], writes=[pb_b[5], lf_tok_b])
                        P.dma("sp", s_lf.ap()[TO:TO + TT, :].rearrange("(tt p) c -> p tt c", p=128), lf_tok[:],
                              reads=[lf_tok_b], writes=[db("lf")])
                        P.dma("sp", lfp.ap().rearrange("(tt p) c -> p tt c", p=128), lf_tok[:],
                              reads=[lf_tok_b], writes=[db("lfp")])
            blocks = []
            for kind, base in (("q", 0), ("k", 2048), ("v", 4096)):
                for b4 in range(4):
                    blocks.append(([(base + b4 * 512, 512)], [(c * 128, 128, (kind, b4 * 4 + c), 0) for c in range(4)]))
            blocks.append(([(6144, 32)], [(0, 32, ("f",), 0)]))
            gemm(hT, hT_b, 16, lambda c0, n: Wd["w_in_odd"].ap()[o, :, c0:c0 + n], blocks, in_epi)

    if not E["cfg"].get("skip_proj"):
        for i4 in range(4):
            cc_gather(kpack_o[i4], gk_o[i4], [db("kpack_o", i4)], [db("gk_o", i4)])
        cc_gather(lfp, glf, [db("lfp")], [db("glf")])
        for i4 in range(2):
            P.dma("sp", s_kT.ap()[i4 * 1024:(i4 + 1) * 1024, 0:TO], gk_o[i4].ap()[0:1024, :], reads=[db("gk_o", i4)], writes=[db("kT")])
            P.dma("sp", s_vtok.ap()[i4 * 512:(i4 + 1) * 512, :], gk_o[2 + i4].ap()[0:1024, :].rearrange("(t a) c -> t (a c)", a=2),
                  reads=[db("gk_o", 2 + i4)], writes=[db("vtok")])
        P.dma("sp", s_lf.ap()[0:TO, :], glf.ap()[0:1024, :], reads=[db("glf")], writes=[db("lf")])
    if E["cfg"].get("stop_after_proj"):
        return
    with Scope(P) as sa:
        lft = sa.sb("lft", [128, 16, 32], F32)
        lft_b = Buf()
        ncum = sa.sb("ncum", [128, 16, 32], F32)
        Cb = sa.sb("Cb", [128, 16, 32], F32)
        cum_b = Buf("cum")
        P.dma("sp", lft[:], s_lf.ap().rearrange("(tt p) c -> p tt c", p=128), reads=[db("lf")], writes=[lft_b])
        P.op("dve", lambda en: en.tensor_scalar(out=lft[:, 0:8, :], in0=lft[:, 0:8, :], scalar1=flg[:, 0:1], scalar2=None, op0=ALU.mult),
             reads=[lft_b, b_flg], writes=[lft_b])
        for j in range(16):
            for j2 in range(j + 1):
                P.op("pe", lambda en: en.matmul(pb[5][:, j * 32:(j + 1) * 32], lhsT=(triu_f if j2 == j else ones_f[:, :]), rhs=lft[:, j2, :],
                                                start=(j2 == 0), stop=(j2 == j)),
                     reads=[lft_b, b_cst, b_const], writes=[pb_b[5]])
            for j2 in range(j + 1):
                P.op("pe", lambda en: en.matmul(pb[6][:, j * 32:(j + 1) * 32], lhsT=ones_f[:, :], rhs=lft[:, j2, :],
                                                start=(j2 == 0), stop=(j2 == j)),
                     reads=[lft_b, b_const], writes=[pb_b[6]])
        P.op("act", lambda en: en.activation(out=ncum[:], in_=pb[5][:].rearrange("p (a b) -> p a b", b=32), func=AF.Copy),
             reads=[], writes=[pb_b[5], cum_b])
        P.op("act", lambda en: en.activation(out=Cb[:], in_=pb[6][:].rearrange("p (a b) -> p a b", b=32), func=AF.Copy),
             reads=[], writes=[pb_b[6], cum_b])
        s_nq = E["s_nq"]
        dm = sa.sb("dm", [128, 16, 32], F32)
        dhi = sa.sb("dhi", [128, 16, 32], BF16)
        dhf = sa.sb("dhf", [128, 16, 32], F32)
        dlo = sa.sb("dlo", [128, 16, 32], BF16)
        dl2 = sa.sb("dl2", [128, 16, 32], BF16)
        nqT = sa.sb("nqT", [32, 3, TO], BF16)
        dm_b = Buf("dm")
        nqT_b = Buf("nqT")
        P.op("dve", lambda en: en.memset(dm[:, 0:8, :], 0.0), writes=[dm_b])
        for t_ in range(8, 16):
            ge = 4 * (t_ // 4) + 3
            P.op("dve", lambda en: en.tensor_tensor(out=dm[:, t_, :], in0=Cb[:, ge, :], in1=ncum[:, t_, :], op=ALU.subtract),
                 reads=[cum_b], writes=[dm_b])
        P.op("dve", lambda en: en.tensor_scalar(out=dm[:], in0=dm[:], scalar1=8.0, scalar2=None, op0=ALU.mult), reads=[dm_b], writes=[dm_b])
        P.op("dve", lambda en: en.tensor_copy(out=dhi[:], in_=dm[:]), reads=[dm_b], writes=[dm_b])
        P.op("dve", lambda en: en.tensor_copy(out=dhf[:], in_=dhi[:]), reads=[dm_b], writes=[dm_b])
        P.op("dve", lambda en: en.tensor_tensor(out=dhf[:], in0=dm[:], in1=dhf[:], op=ALU.subtract), reads=[dm_b], writes=[dm_b])
        P.op("dve", lambda en: en.tensor_copy(out=dlo[:], in_=dhf[:]), reads=[dm_b], writes=[dm_b])
        P.op("dve", lambda en: en.tensor_copy(out=dm[:], in_=dlo[:]), reads=[dm_b], writes=[dm_b])
        P.op("dve", lambda en: en.tensor_tensor(out=dhf[:], in0=dhf[:], in1=dm[:], op=ALU.subtract), reads=[dm_b], writes=[dm_b])
        P.op("dve", lambda en: en.tensor_copy(out=dl2[:], in_=dhf[:]), reads=[dm_b], writes=[dm_b])
        for w, src in enumerate((dhi, dlo, dl2)):
            for tt in range(8):
                P.op("pe", lambda en: en.transpose(pbh[0:32, tt * 128:(tt + 1) * 128], src[:, 8 + tt, :], ident_bf[:]),
                     reads=[dm_b, b_const], writes=[pbh_b])
            P.op("act", lambda en: en.activation(out=nqT[:, w, :], in_=pbh[0:32, :], func=AF.Copy),
                 reads=[], writes=[pbh_b, nqT_b])
        P.dma("sp", s_nq.ap().rearrange("w h t -> h w t"), nqT[:], reads=[nqT_b], writes=[db("nq")])
        P.op("dve", lambda en: en.tensor_scalar(out=ncum[:, 0:8, :], in0=ncum[:, 0:8, :], scalar1=flg[:, 2:3], scalar2=None, op0=ALU.add),
             reads=[cum_b, dm_b, b_flg], writes=[cum_b])
        mneg = sa.sb("mneg", [128, 128], BF16)
        mneg_b = Buf("mneg")
        P.op("dve", lambda en: en.tensor_scalar(out=mneg[:], in0=triu_f, scalar1=30000.0, scalar2=-30000.0, op0=ALU.mult, op1=ALU.add),
             reads=[b_cst], writes=[mneg_b])
        kaug = [[sa.sb(f"kaug{i}{hh}", [128, T], BF16) for hh in range(2)] for i in range(2)]
        qaug = [[sa.sb(f"qaug{i}{hh}", [128, TO], BF16) for hh in range(2)] for i in range(2)]
        vm = [sa.sb(f"vm{i}", [128, 16, 128], BF16) for i in range(2)]
        m_b = [Buf(), Buf()]
        NBF = 4
        LBF = [0, 1, 4, 5]
        Bm = [sa.sb(f"Bm{i}", [128, 4], F32) for i in range(NBF)]
        Bm_b = [Buf() for _ in range(NBF)]
        pbf = [sa.sb(f"pbf{i}", [128, 512], BF16) for i in range(NBF)]
        pbf_b = [Buf() for _ in range(NBF)]
        rcp = sa.sb("rcp", [128, 512], F32)
        rcp_b = Buf()
        yst = [sa.sb(f"yst{i}", [128, 512], BF16) for i in range(2)]
        yst_b = [Buf(), Buf()]
        cnt = 0
        yc = 0
        for m in range(E["cfg"].get("fox_pairs", 16)):
            mk = m % 2
            for hh in range(2):
                h = 2 * m + hh
                own = slice(hh * 64, (hh + 1) * 64)
                oth = slice((1 - hh) * 64, (2 - hh) * 64)
                o0 = (1 - hh) * 64
                P.op("dve", lambda en: en.memset(kaug[mk][hh][oth, :], 0.0), writes=[m_b[mk]])
                P.op("dve", lambda en: en.memset(kaug[mk][hh][o0:o0 + 3, :], 1.0), writes=[m_b[mk]])
                P.op("dve", lambda en: en.memset(qaug[mk][hh][oth, :], 0.0), writes=[m_b[mk]])
                P.dma("sp", kaug[mk][hh][own, :], s_kT.ap()[h * 64:(h + 1) * 64, :], reads=[db("kT")], writes=[m_b[mk]])
                P.dma("sp", qaug[mk][hh][own, :], s_qT.ap()[h * 64:(h + 1) * 64, :], reads=[db("qT")], writes=[m_b[mk]])
                P.dma("sp", qaug[mk][hh][o0:o0 + 3, :], s_nq.ap()[:, h, :], reads=[db("nq")], writes=[m_b[mk]])
            P.dma("sp", vm[mk][:], s_vtok.ap()[:, m * 128:(m + 1) * 128].rearrange("(tt p) c -> p tt c", p=128), reads=[db("vtok")], writes=[m_b[mk]])
            for Gl in range(2):
                G = 2 + Gl
                jmax = 4 * G + 3
                items = [(hh, j) for hh in range(2) for j in range(jmax + 1)]

                def f_logits(k):
                    hh, j = items[k]
                    L = LBF[k % NBF]
                    i_lo = max(4 * G, j)
                    col0 = (i_lo - 4 * G) * 128
                    P.op("pe", lambda en: en.matmul(pb[L][:, col0:512], lhsT=kaug[mk][hh][:, j * 128:(j + 1) * 128],
                                                    rhs=qaug[mk][hh][:, 4 * Gl * 128 + col0:(4 * Gl + 4) * 128], start=True, stop=(j < 4 * G)),
                         reads=[m_b[mk]], writes=[pb_b[L]])
                    if j >= 4 * G:
                        P.op("pe", lambda en: en.matmul(pb[L][:, col0:col0 + 128], lhsT=ident_bf[:], rhs=mneg[:], start=False, stop=True),
                             reads=[b_const, mneg_b], writes=[pb_b[L]])

                def f_post(k):
                    hh, j = items[k]
                    h = 2 * m + hh
                    L = k % NBF
                    BK = LBF[L]
                    i_lo = max(4 * G, j)
                    P.op("dve", lambda en: en.tensor_scalar(out=Bm[L][:, 0:1], in0=Cb[:, 4 * G + 3, h:h + 1], scalar1=-1.0, scalar2=ncum[:, j, h:h + 1],
                                                            op0=ALU.mult, op1=ALU.add),
                         reads=[cum_b], writes=[Bm_b[L]])
                    cs = slice((i_lo - 4 * G) * 128, 512)
                    P.op("act", lambda en: en.activation(out=pbf[L][:, cs], in_=pb[BK][:, cs], func=AF.Exp, scale=0.125,
                                                         bias=Bm[L][:, 0:1]),
                         reads=[Bm_b[L]], writes=[pb_b[BK], pbf_b[L]])

                def f_pv(k):
                    hh, j = items[k]
                    pbs = hh * 64
                    L = k % NBF
                    i_lo = max(4 * G, j)
                    col0 = (i_lo - 4 * G) * 128
                    P.op("pe", lambda en: en.matmul(pb[2][pbs:pbs + 64, col0:512], lhsT=vm[mk][:, j, hh * 64:(hh + 1) * 64], rhs=pbf[L][:, col0:512],
                                                    start=(j == 0), stop=(j == jmax)),
                         reads=[m_b[mk], pbf_b[L]], writes=[pb_b[2]])
                    P.op("pe", lambda en: en.matmul(pb[3][pbs:pbs + 64, col0:512], lhsT=ones_bf[:, 0:64], rhs=pbf[L][:, col0:512],
                                                    start=(j == 0), stop=(j == jmax)),
                         reads=[b_const, pbf_b[L]], writes=[pb_b[3]])

                for k0 in range(NBF - 1):
                    f_logits(k0)
                for k in range(len(items)):
                    if k + NBF - 1 < len(items):
                        f_logits(k + NBF - 1)
                    f_post(k)
                    f_pv(k)
                yk = yc % 2
                yc += 1
                P.op("dve", lambda en: en.reciprocal(out=rcp[:], in_=pb[3][:]), reads=[], writes=[pb_b[3], rcp_b])
                P.op("dve", lambda en: en.tensor_tensor(out=yst[yk][:], in0=pb[2][:], in1=rcp[:], op=ALU.mult),
                     reads=[rcp_b], writes=[pb_b[2], yst_b[yk]])
                P.dma("sp", yT.ap()[m * 128:(m + 1) * 128, Gl * 512:(Gl + 1) * 512], yst[yk][:], reads=[yst_b[yk]], writes=[db("yT", 0)])


def build(cfg=None):
    cfg = cfg or {}
    layers = cfg.get("layers", list(range(DEPTH)))
    nc = bass.Bass("TRN2", target_bir_lowering=False)

    def din(name, shape):
        return nc.dram_tensor(name, list(shape), F32, kind="ExternalInput")
    x_in = din("x", [TO, D])
    p_in = din("p", [DEPTH, TO, 256])
    flg_in = din("flg", [128, 4])
    Wd = {n: din(n, s) for n, s in WEIGHTS}
    sp_in = din("sp", [128, NSP])
    cst_in = din("cst", [128, 512])
    oh_in = din("oh", [33, XA + XB])
    out_d = nc.dram_tensor("out", [TO, D], F32, kind="ExternalOutput")
    dbg = {}
    for name, shape in cfg.get("dumps", []):
        dbg[name] = nc.dram_tensor("dbg_" + name, list(shape), F32, kind="ExternalOutput")

    def scr(name, shape, dt):
        if name in cfg.get("expose", ()):
            return nc.dram_tensor(name, list(shape), dt, kind="ExternalOutput")
        return nc.dram_tensor(name, list(shape), dt)
    xT = scr("xT", [D, TO], F32)
    yT = scr("yT", [D, TO], BF16)
    s_kvT = scr("s_kvT", [256, T], BF16)
    s_kvtok = scr("s_kvtok", [T, 256], BF16)
    s_kidxT = scr("s_kidxT", [64, T], BF16)
    s_widx = scr("s_widx", [TO, 16], F32)
    s_qiT = scr("s_qiT", [1024, TO], BF16)
    s_qaT = scr("s_qaT", [4096, TO], BF16)
    s_qbT = scr("s_qbT", [1024, TO], BF16)
    s_kdupT = scr("s_kdupT", [256, T], BF16)
    s_vbtok = scr("s_vbtok", [T, 128], BF16)
    s_qT = scr("s_qT", [2048, TO], BF16)
    s_kT = scr("s_kT", [2048, T], BF16)
    s_vtok = scr("s_vtok", [T, 2048], BF16)
    s_lf = scr("s_lf", [T, 32], F32)
    s_vrow = scr("s_vrow", [16, XA + XB], F32)
    s_nq = scr("s_nq", [3, 32, TO], BF16)
    s_expA = scr("s_expA", [128, 16 * 9 * 128], BF16)
    s_expB = scr("s_expB", [128, 16 * 2 * 128], BF16)
    kpack_e = scr("kpack_e", [960, 1024], BF16)
    gk_e = scr("gk_e", [1920, 1024], BF16)
    kpack_o = [scr(f"kpack_o{i}", [1024, 1024], BF16) for i in range(4)]
    gk_o = [scr(f"gk_o{i}", [2048, 1024], BF16) for i in range(4)]
    lfp = scr("lfp", [1024, 32], F32)
    glf = scr("glf", [2048, 32], F32)
    hx_in = scr("hx_in", [128, 32], F32)
    hxg = scr("hxg", [256, 32], F32)
    dbufs = {}

    def db(*key):
        if key not in dbufs:
            dbufs[key] = Buf(str(key))
        return dbufs[key]

    with ExitStack() as st:
        P = Prog(nc, st)

        def gsb(name, shape, dt):
            return st.enter_context(nc.sbuf_tensor(name, list(shape), dt))

        spk = gsb("spk", [128, NSP], F32)
        cst = gsb("cst_sb", [128, 512], F32)
        ident_bf = gsb("ident_bf", [128, 128], BF16)
        ones_bf = gsb("ones_bf", [128, 128], BF16)
        bd_bf = gsb("bd_bf", [128, 128], BF16)
        ones_f = gsb("ones_f", [128, 128], F32)
        eps_t = gsb("eps_t", [128, 1], F32)
        triu_bf = gsb("triu_bf", [128, 128], BF16)
        halo = gsb("halo", [128, 2], F32)
        flg = gsb("flg_sb", [128, 4], F32)
        b_flg = Buf("flg")
        ccs = P._newsem("ccs")
        cc_n = [0]
        WSLOT = 11008
        NWS = 2
        wbuf = [gsb(f"wbuf{i}", [128, WSLOT], BF16) for i in range(NWS)]
        wb_b = [Buf(f"wb{i}") for i in range(NWS)]
        wptr = [0]
        b_spk, b_cst, b_const, b_halo = Buf(), Buf(), Buf(), Buf()
        ident_f = cst[:, 0:128]
        jflip = cst[:, 128:256]
        triu_f = cst[:, 256:384]
        cneg = cst[:, 384:512]
        pb = [st.enter_context(nc.psum_tensor(f"pb{i}", [128, 512], F32)) for i in range(7)]
        pbh = st.enter_context(nc.psum_tensor("pbh", [128, 1024], BF16))
        pb_b = [Buf(f"pb{i}") for i in range(7)]
        pbh_b = Buf("pbh")

        P.dma("sp", spk[:], sp_in.ap(), writes=[b_spk])
        P.dma("sp", cst[:], cst_in.ap(), writes=[b_cst])
        P.dma("sp", flg[:], flg_in.ap(), writes=[b_flg])
        P.op("dve", lambda e: e.memset(ones_bf[:], 1.0), writes=[b_const])
        P.op("dve", lambda e: e.memset(ones_f[:], 1.0), writes=[b_const])
        P.op("dve", lambda e: e.memset(eps_t[:], EPS), writes=[b_const])
        P.op("dve", lambda e: e.memset(bd_bf[:], 0.0), writes=[b_const])
        P.op("dve", lambda e: e.memset(bd_bf[0:64, 0:64], 1.0), writes=[b_const])
        P.op("dve", lambda e: e.memset(bd_bf[64:128, 64:128], 1.0), writes=[b_const])
        P.op("dve", lambda e: e.tensor_copy(out=ident_bf[:], in_=ident_f), reads=[b_cst], writes=[b_const])
        P.op("dve", lambda e: e.tensor_copy(out=triu_bf[:], in_=triu_f), reads=[b_cst], writes=[b_const])
        P.op("dve", lambda e: e.memset(spk[32:33, SP_RELB:SP_RELB + 32], NEG), reads=[], writes=[b_spk])
        P.barrier()

        gemm_bank = [0]

        def cc_gather(src_t, dst_t, in_bufs, out_bufs):
            P._deps("pool", list(in_bufs), list(out_bufs))
            cc_n[0] += 1
            nc.gpsimd.collective_compute("AllGather", ALU.bypass, replica_groups=[[0, 4], [1, 5], [2, 6], [3, 7]],
                                         ins=[src_t.ap()], outs=[dst_t.ap()]).then_inc(ccs, 1)
            nc.gpsimd.wait_ge(ccs, cc_n[0])
            P.op("pool", lambda e: e.memset(halo[0:1, 0:1], 0.0), reads=list(in_bufs), writes=list(out_bufs))
        state = {}

        def wview(si, KC, ntot):
            return wbuf[si][:, 0:KC * ntot].rearrange("p (kc n) -> p kc n", n=ntot)

        def gemm(src, src_b, KC, wsrc, blocks, epi, ntc=2, tc_off=0, pre=None):
            for segs, chunks in blocks:
                si = wptr[0]
                wptr[0] = (wptr[0] + 1) % NWS
                ntot = sum(n for _, n in segs)
                wv = wview(si, KC, ntot)
                off = 0
                for (c0, ncols) in segs:
                    P.dma("pool", wv[:, :, off:off + ncols],
                          wsrc(c0, ncols).rearrange("(kc p) n -> p kc n", p=128), writes=[wb_b[si]])
                    off += ncols
                for (coff, m, tag, pbase) in chunks:
                    if pre is not None:
                        pre(wv, wb_b[si], coff, m, tag)
                    for tci in range(ntc):
                        bi = gemm_bank[0]
                        gemm_bank[0] = (gemm_bank[0] + 1) % 4
                        for kc in range(KC):
                            P.op("pe", lambda e: e.matmul(pb[bi][pbase:pbase + m, :], lhsT=wv[:, kc, coff:coff + m],
                                                          rhs=src[:, kc, tc_off + tci * TC:tc_off + (tci + 1) * TC],
                                                          start=(kc == 0), stop=(kc == KC - 1)),
                                 reads=[wb_b[si], src_b], writes=[pb_b[bi]])
                        epi(bi, m, tag, tci)

        def simple_blocks(col0, ncols_total, wcols, tagfn=None, m=128):
            blocks = []
            c = 0
            ci = 0
            while c < ncols_total:
                n = min(wcols, ncols_total - c)
                chunks = []
                o = 0
                while o < n:
                    mm = min(m, n - o)
                    chunks.append((o, mm, ci if tagfn is None else tagfn(ci), 0))
                    o += mm
                    ci += 1
                blocks.append(([(col0 + c, n)], chunks))
                c += n
            return blocks

        def rms_finish(S, src, src_b, C, lhsT_ones, gcol, inv_n, dst_fn, dst_bufs, ncols=TC, nparts=128):
            sq, sq_b, rstd, rstd_b = S["sq"], S["sq_b"], S["rstd"], S["rstd_b"]
            for c in range(C):
                P.op("act", lambda e: e.activation(out=sq[0:nparts, c, 0:ncols], in_=src(c), func=AF.Square),
                     reads=[src_b], writes=[sq_b])
            for c in range(C):
                P.op("pe", lambda e: e.matmul(pb[4][0:nparts, 0:ncols], lhsT=lhsT_ones[0:nparts, 0:nparts],
                                              rhs=sq[0:nparts, c, 0:ncols], start=(c == 0), stop=(c == C - 1)),
                     reads=[sq_b, b_const], writes=[pb_b[4]])
            P.op("act", lambda e: e.activation(out=rstd[0:nparts, 0:ncols], in_=pb[4][0:nparts, 0:ncols], func=AF.Sqrt,
                                               scale=inv_n, bias=eps_t[0:nparts, 0:1]),
                 reads=[b_const], writes=[pb_b[4], rstd_b])
            P.op("dve", lambda e: e.reciprocal(out=rstd[0:nparts, 0:ncols], in_=rstd[0:nparts, 0:ncols]),
                 reads=[rstd_b], writes=[rstd_b])
            for c in range(C):
                P.op("dve", lambda e: e.scalar_tensor_tensor(out=dst_fn(c), in0=src(c),
                                                             scalar=spk[0:nparts, gcol + c:gcol + c + 1],
                                                             in1=rstd[0:nparts, 0:ncols], op0=ALU.mult, op1=ALU.mult),
                     reads=[src_b, rstd_b, b_spk], writes=dst_bufs)

        def norm_x(S, hT, hT_b, gcol, t0):
            xs, xs_b = S["xs"], S["xs_b"]
            for tci in range(TT // TC):
                ta = t0 + tci * TC
                P.dma("sp", xs[:], xT.ap()[:, ta:ta + TC].rearrange("(kc p) t -> p kc t", p=128),
                      reads=[db("xT", ta // TC)], writes=[xs_b])
                rms_finish(S, lambda c: xs[:, c, :], xs_b, 16, ones_bf, gcol, 1.0 / D,
                           lambda c: hT[:, c, tci * TC:(tci + 1) * TC], [hT_b])

        def resid_epi(S, t0):
            def epi(bi, m, tag, tci):
                ta = t0 + tci * TC
                k = S["xr_i"][0]
                S["xr_i"][0] = (k + 1) % 2
                xr, xr_b = S["xr"][k], S["xr_b"][k]
                P.dma("sp", xr[:], xT.ap()[tag * 128:(tag + 1) * 128, ta:ta + TC],
                      reads=[db("xT", ta // TC)], writes=[xr_b])
                P.op("dve", lambda e: e.tensor_tensor(out=xr[:], in0=pb[bi][:], in1=xr[:], op=ALU.add),
                     reads=[xr_b], writes=[pb_b[bi], xr_b])
                P.dma("sp", xT.ap()[tag * 128:(tag + 1) * 128, ta:ta + TC], xr[:],
                      reads=[xr_b], writes=[db("xT", ta // TC)])
            return epi

        def dump(name, src_ap_dram):
            pass

        with Scope(P) as sc:
            xin = [sc.sb(f"xin{i}", [128, D], F32) for i in range(2)]
            xin_b = [Buf(), Buf()]
            stg = [sc.sb(f"xstg{i}", [128, 16, 128], F32) for i in range(2)]
            stg_b = [Buf(), Buf()]
            for tt in range(TO // 128):
                k = tt % 2
                P.dma("sp", xin[k][:], x_in.ap()[tt * 128:(tt + 1) * 128, :], writes=[xin_b[k]])
                for g in range(4):
                    bi = 5 + (g % 2)
                    for j in range(4):
                        fc = g * 4 + j
                        P.op("pe", lambda e: e.transpose(pb[bi][:, j * 128:(j + 1) * 128], xin[k][:, fc * 128:(fc + 1) * 128], ident_f),
                             reads=[xin_b[k], b_cst], writes=[pb_b[bi]])
                    P.op("act", lambda e: e.activation(out=stg[k][:, g * 4:(g + 1) * 4, :], in_=pb[bi][:].rearrange("p (a b) -> p a b", b=128), func=AF.Copy),
                         reads=[], writes=[pb_b[bi], stg_b[k]])
                P.dma("sp", xT.ap()[:, tt * 128:(tt + 1) * 128].rearrange("(fc p) t -> p fc t", p=128), stg[k][:],
                      reads=[stg_b[k]], writes=[db("xT", tt // 4)])

        for l in layers:
            E = dict(locals())
            E['state'] = state
            if "attn" in cfg.get("parts", ("attn", "out", "ffn", "ple")):
                if l % 2 == 0:
                    even_attention(E, l)
                else:
                    odd_attention(E, l)
            token_local(E, l, cfg.get("parts", ("attn", "out", "ffn", "ple")))

        with Scope(P) as sc:
            xo = [sc.sb(f"xo{i}", [128, 16, 128], F32) for i in range(2)]
            xo_b = [Buf(), Buf()]
            ostg = [sc.sb(f"ostg{i}", [128, D], F32) for i in range(2)]
            ostg_b = [Buf(), Buf()]
            for tt in range(TO // 128):
                k = tt % 2
                P.dma("sp", xo[k][:], xT.ap()[:, tt * 128:(tt + 1) * 128].rearrange("(fc p) t -> p fc t", p=128),
                      reads=[db("xT", tt // 4)], writes=[xo_b[k]])
                for g in range(4):
                    bi = 5 + (g % 2)
                    for j in range(4):
                        fc = g * 4 + j
                        P.op("pe", lambda e: e.transpose(pb[bi][:, j * 128:(j + 1) * 128], xo[k][:, fc, :], ident_f),
                             reads=[xo_b[k], b_cst], writes=[pb_b[bi]])
                    P.op("act", lambda e: e.activation(out=ostg[k][:, g * 512:(g + 1) * 512], in_=pb[bi][:], func=AF.Copy),
                         reads=[], writes=[pb_b[bi], ostg_b[k]])
                P.dma("sp", out_d.ap()[tt * 128:(tt + 1) * 128, :], ostg[k][:], reads=[ostg_b[k]], writes=[db("out")])
        P.barrier()
    return nc


_NC_CACHE = {}


def make_in_maps(inp, batches):
    cst, oh = host_consts()
    sp = pack_small(inp)
    wmap = {n: np.ascontiguousarray(inp[n], dtype=np.float32) for n, _ in WEIGHTS}
    in_maps = []
    for c in range(8):
        b = batches[c]
        half = c // 4
        flg = np.zeros((128, 4), np.float32)
        flg[:, 0] = float(half)
        flg[:, 1] = 0.0 if half else -1e30
        flg[:, 2] = 0.0 if half else NEG
        m = dict(x=np.ascontiguousarray(inp["x"][b, half * TO:(half + 1) * TO], dtype=np.float32),
                 p=np.ascontiguousarray(inp["p"][:, b, half * TO:(half + 1) * TO], dtype=np.float32),
                 flg=flg, sp=sp, cst=cst, oh=oh)
        m.update(wmap)
        in_maps.append(m)
    return in_maps


def kernel(**inputs):
    inp = {k: np.asarray(v) for k, v in inputs.items()}
    if "nc" not in _NC_CACHE:
        _NC_CACHE["nc"] = build()
    nc = _NC_CACHE["nc"]
    in_maps = make_in_maps(inp, [0, 1, 2, 3, 0, 1, 2, 3])
    res = run_bass_kernel_spmd(nc, in_maps, core_ids=list(range(8)))
    out = np.stack([np.concatenate([res.results[b]["out"], res.results[b + 4]["out"]], axis=0) for b in range(4)], axis=0)
    return out.astype(np.float32)
```

```python
import math
from contextlib import ExitStack
import numpy as np
import concourse.bass as bass
import concourse.mybir as mybir
from concourse.bass_utils import run_bass_kernel_spmd

F32 = mybir.dt.float32
BF16 = mybir.dt.bfloat16
ALU = mybir.AluOpType
AF = mybir.ActivationFunctionType

EPOCH = 30000
NDSEM = 40

D = 2048
T = 2048
DEPTH = 4
TT = 1024
TO = 1024
TC = 512
DFF = 5504
NFC = 43
EPS = 1e-6
XA = 1280
XB = 384
NEG = -30000.0

SP_ATTN = 0
SP_FFN = 64
SP_PLE = 128
SP_CONV = 192
SP_CQ = 1224
SP_CKV = 1232
SP_AQ = 1236
SP_BQ = 1240
SP_BK = 1242
SP_CQN = 1244
SP_CKN = 1246
SP_FB = 1248
SP_SINK = 1250
SP_RELB = 1282
SP_B31 = 1320
NSP = 1340

WEIGHTS = [
    ("w_in_even", (2, 2048, 2128)), ("a_w_uq", (2, 512, 4096)), ("a_w_qidx", (2, 512, 1024)),
    ("a_w_uv", (2, 16, 256, 64)), ("w_out_even", (2, 2048, 2048)), ("w_in_odd", (2, 2048, 6176)),
    ("w_out_odd", (2, 2048, 2048)), ("w_up", (4, 2048, 11008)), ("w_down", (4, 5504, 2048)),
    ("w_ple_gate", (4, 2048, 2048)), ("w_ple_proj", (4, 256, 2048)),
]


class Buf:
    __slots__ = ("name", "w", "r", "rd")

    def __init__(self, name=""):
        self.name = name
        self.w = None
        self.r = {}
        self.rd = []


class Prog:
    ENGS = ("pe", "act", "dve", "pool", "sp")

    def __init__(self, nc, stack):
        self.nc = nc
        self.stack = stack
        self.eng = {"pe": nc.tensor, "act": nc.scalar, "dve": nc.vector,
                    "pool": nc.gpsimd, "sp": nc.sync}
        self.cnt = {e: 0 for e in self.ENGS}
        self.esems = {e: [] for e in self.ENGS}
        self.seen_e = {e: {p: 0 for p in self.ENGS} for e in self.ENGS}
        self.seen_d = {e: {} for e in self.ENGS}
        self.dsem = {}
        for q in ("sp", "pool"):
            self.dsem[q] = [[self._newsem(f"d{q}{i}"), 0] for i in range(NDSEM)]
        self.dptr = {"sp": 0, "pool": 0}
        self.bar_sem = self._newsem("bar")
        self.bar_cnt = 0
        self.n_inst = 0

    def _newsem(self, name):
        return self.stack.enter_context(self.nc.semaphore(name))

    def _esem(self, e, idx):
        ep = (idx - 1) // EPOCH
        while len(self.esems[e]) <= ep:
            self.esems[e].append(self._newsem(f"e{e}{len(self.esems[e])}"))
        return self.esems[e][ep], (idx - 1) % EPOCH + 1

    def _wait(self, e, ev):
        if ev is None:
            return
        if ev[0] == "e":
            _, p, idx = ev
            if p == e and e == "pe":
                return
            if self.seen_e[e][p] >= idx:
                return
            self.seen_e[e][p] = idx
            s, v = self._esem(p, idx)
            self.eng[e].wait_ge(s, v)
        else:
            _, s, v, key = ev
            if self.seen_d[e].get(key, 0) >= v:
                return
            self.seen_d[e][key] = v
            self.eng[e].wait_ge(s, v)
        self.n_inst += 1

    def _deps(self, e, reads, writes):
        for b in reads:
            self._wait(e, b.w)
        for b in writes:
            self._wait(e, b.w)
            for p, idx in b.r.items():
                if p != e:
                    self._wait(e, ("e", p, idx))
            for ev in b.rd:
                self._wait(e, ev)

    def _mark(self, ev, reads, writes):
        for b in reads:
            if ev[0] == "e":
                b.r[ev[1]] = ev[2]
            else:
                b.rd.append(ev)
        for b in writes:
            b.w = ev
            b.r = {}
            b.rd = []

    def op(self, e, fn, reads=(), writes=()):
        self._deps(e, reads, writes)
        inst = fn(self.eng[e])
        self.cnt[e] += 1
        idx = self.cnt[e]
        s, _ = self._esem(e, idx)
        inst.then_inc(s, 1)
        self.n_inst += 1
        self._mark(("e", e, idx), reads, writes)

    def dma(self, q, out, in_, reads=(), writes=(), **kw):
        self._deps(q, reads, writes)
        slot = self.dsem[q][self.dptr[q]]
        key = (q, self.dptr[q])
        self.dptr[q] = (self.dptr[q] + 1) % NDSEM
        if slot[1] > 0:
            self._wait(q, ("d", slot[0], slot[1], key))
        inst = self.eng[q].dma_start(out=out, in_=in_, **kw)
        slot[1] += 16
        inst.then_inc(slot[0], 16)
        self.n_inst += 1
        self._mark(("d", slot[0], slot[1], key), reads, writes)

    def barrier(self):
        for p in self.ENGS:
            if p != "sp" and self.cnt[p] > 0:
                self._wait("sp", ("e", p, self.cnt[p]))
        for q in ("sp", "pool"):
            for i, slot in enumerate(self.dsem[q]):
                if slot[1] > 0:
                    self._wait("sp", ("d", slot[0], slot[1], (q, i)))
        self.bar_cnt += 1
        self.eng["sp"].sem_inc(self.bar_sem, 1)
        for e in self.ENGS:
            if e != "sp":
                self.eng[e].wait_ge(self.bar_sem, self.bar_cnt)
                for p in self.ENGS:
                    self.seen_e[e][p] = self.cnt[p]
                for q in ("sp", "pool"):
                    for i, slot in enumerate(self.dsem[q]):
                        self.seen_d[e][(q, i)] = slot[1]
        self.n_inst += 6


class Scope:
    def __init__(self, P):
        self.P = P
        self.st = ExitStack()

    def __enter__(self):
        self.st.__enter__()
        return self

    _uid = [0]

    def sb(self, name, shape, dt):
        Scope._uid[0] += 1
        return self.st.enter_context(self.P.nc.sbuf_tensor(f"{name}_{Scope._uid[0]}", list(shape), dt))

    def __exit__(self, *a):
        self.P.barrier()
        return self.st.__exit__(*a)


def rel_bucket_np(n):
    n = np.maximum(n, 0)
    exact = 16
    nf = np.maximum(n, 1).astype(np.float32)
    large = exact + (np.log(nf / np.float32(exact)) / np.float32(math.log(1024 / exact))
                     * np.float32(32 - exact)).astype(np.int32)
    large = np.minimum(large, 31)
    return np.where(n < exact, n, large)


def host_consts():
    cst = np.zeros((128, 512), np.float32)
    cst[:, 0:128] = np.eye(128)
    cst[:, 128:256] = np.eye(128)[::-1]
    i = np.arange(128)
    cst[:, 256:384] = (i[None, :] >= i[:, None]).astype(np.float32)
    cst[:, 384:512] = np.where(i[None, :] <= i[:, None], 0.0, -1e30)
    oh = np.zeros((33, XA + XB), np.float32)
    y = np.arange(XA)
    xx = y - 127
    b = np.where(xx < 0, 32, rel_bucket_np(xx))
    oh[b, y] = 1.0
    y = np.arange(XB)
    xx = y - 127
    b = np.where((xx < 0) | (xx >= 128), 32, rel_bucket_np(xx))
    oh[b, XA + y] = 1.0
    return cst, oh


def pack_small(inp):
    sp = np.zeros((128, NSP), np.float32)

    def fm(v):
        return np.ascontiguousarray(v.reshape(-1, 128).T)
    for l in range(4):
        sp[:, SP_ATTN + l * 16:SP_ATTN + (l + 1) * 16] = fm(inp["attn_norm"][l])
        sp[:, SP_FFN + l * 16:SP_FFN + (l + 1) * 16] = fm(inp["ffn_norm"][l])
        sp[:, SP_PLE + l * 16:SP_PLE + (l + 1) * 16] = fm(inp["ple_norm"][l])
        for k in range(3):
            c0 = SP_CONV + (l * 3 + k) * 86
            sp[:, c0:c0 + 86] = fm(inp["ffn_conv"][l, k])
    for e in range(2):
        sp[:, SP_CQ + e * 4:SP_CQ + e * 4 + 4] = fm(inp["a_cq_norm"][e])
        sp[:, SP_CKV + e * 2:SP_CKV + e * 2 + 2] = fm(inp["a_ckv_norm"][e])
        sp[:, SP_AQ + e * 2:SP_AQ + e * 2 + 2] = fm(inp["a_q_norm"][e])
        sp[:, SP_BQ + e] = np.tile(inp["b_q_norm"][e], 2)
        sp[:, SP_BK + e] = np.tile(inp["b_k_norm"][e], 2)
        sp[:, SP_CQN + e] = np.tile(inp["c_q_norm"][e], 2)
        sp[:, SP_CKN + e] = np.tile(inp["c_k_norm"][e], 2)
        sp[0:32, SP_FB + e] = inp["c_forget_bias"][e]
        sp[:, SP_SINK + e * 16:SP_SINK + (e + 1) * 16] = inp["b_sinks"][e][None, :]
    sp[0:32, SP_RELB:SP_RELB + 32] = inp["rel_bias"]
    sp[:, SP_B31:SP_B31 + 16] = inp["rel_bias"][31, 0:16][None, :]
    return sp


def token_local(E, l, parts):
    P, nc, Wd, db, gemm, simple_blocks, rms_finish, norm_x, resid_epi = (E[k] for k in (
        "P", "nc", "Wd", "db", "gemm", "simple_blocks", "rms_finish", "norm_x", "resid_epi"))
    xT, yT, spk, b_spk, pb, pb_b, halo, b_halo, p_in, ident_f, b_cst, wbuf, wb_b, wptr, wview = (E[k] for k in (
        "xT", "yT", "spk", "b_spk", "pb", "pb_b", "halo", "b_halo", "p_in", "ident_f", "b_cst", "wbuf", "wb_b", "wptr", "wview"))
    NWS = len(wbuf)
    flg, b_flg, cc_gather, hx_in, hxg, ones_bf = (E[k] for k in ("flg", "b_flg", "cc_gather", "hx_in", "hxg", "ones_bf"))
    for ps in range(1):
        t0 = 0
        with Scope(P) as so:
            hT = so.sb("hT", [128, 16, TT], BF16)
            hT_b = Buf("hT")
            hh = so.sb("hh", [128, 16, 2], BF16)
            hh_b = Buf("hh")

            def norm_scope(gcol, with_halo=False):
                with Scope(P) as sn:
                    S = dict(xs=sn.sb("xs", [128, 16, TC], F32), xs_b=Buf(), sq=sn.sb("sq", [128, 16, TC], BF16),
                             sq_b=Buf(), rstd=sn.sb("rstd", [128, TC], F32), rstd_b=Buf())
                    norm_x(S, hT, hT_b, gcol, t0)
                    if with_halo:
                        hxo = sn.sb("hxo", [128, 16, 2], F32)
                        hxo_b = Buf("hxo")
                        P.dma("sp", hxo[:], hxg.ap()[0:128, :].rearrange("p (kc t) -> p kc t", t=2), reads=[db("hxg")], writes=[hxo_b])
                        rms_finish(S, lambda c: hxo[:, c, :], hxo_b, 16, ones_bf, gcol, 1.0 / D,
                                   lambda c: hh[:, c, :], [hh_b], ncols=2)

            def mk_xr(sc):
                return dict(xr=[sc.sb(f"xr{i}", [128, TC], F32) for i in range(2)], xr_b=[Buf(), Buf()], xr_i=[0])

            if "out" in parts:
                with Scope(P) as s1:
                    S = mk_xr(s1)
                    P.dma("sp", hT[:], yT.ap()[:, t0:t0 + TT].rearrange("(kc p) t -> p kc t", p=128),
                          reads=[db("yT", 0)], writes=[hT_b])
                    wn = "w_out_even" if l % 2 == 0 else "w_out_odd"
                    gemm(hT, hT_b, 16, lambda c0, n: Wd[wn].ap()[l // 2, :, c0:c0 + n],
                         simple_blocks(0, D, 512), resid_epi(S, t0))
            if "ffn" in parts:
                with Scope(P) as sh:
                    hxs = sh.sb("hxs", [128, 16, 2], F32)
                    hxs_b = Buf("hxs")
                    P.dma("sp", hxs[:], xT.ap()[:, TO - 2:TO].rearrange("(kc p) t -> p kc t", p=128), reads=[db("xT", 1)], writes=[hxs_b])
                    P.dma("sp", hx_in.ap().rearrange("p (kc t) -> p kc t", t=2), hxs[:], reads=[hxs_b], writes=[db("hx_in")])
                cc_gather(hx_in, hxg, [db("hx_in")], [db("hxg")])
                norm_scope(SP_FFN + l * 16, with_halo=True)
                with Scope(P) as s2:
                    S = mk_xr(s2)
                    act = s2.sb("act", [128, NFC, TT], BF16)
                    act_b = Buf("act")
                    stg = {"g": s2.sb("sg", [128, TT + 2], F32), "u": s2.sb("su", [128, TT + 2], F32)}
                    stg_b = {"g": Buf("sg"), "u": Buf("su")}
                    cv = {"g": s2.sb("ga", [128, TT], F32), "u": s2.sb("ua", [128, TT], F32)}
                    cv_b = {"g": Buf("ga"), "u": Buf("ua")}

                    def up_pre(wv, wvb, coff, m, tag):
                        kind = tag[0]
                        for kc in range(16):
                            P.op("pe", lambda e: e.matmul(pb[6][:, 0:2], lhsT=wv[:, kc, coff:coff + m], rhs=hh[:, kc, :],
                                                          start=(kc == 0), stop=(kc == 15)),
                                 reads=[wvb, hh_b], writes=[pb_b[6]])
                        P.op("act", lambda e: e.activation(out=stg[kind][:, 0:2], in_=pb[6][:, 0:2], func=AF.Copy, scale=flg[:, 0:1]),
                             reads=[b_flg], writes=[pb_b[6], stg_b[kind]])

                    def conv_finish(kind, i):
                        c = i if kind == "g" else NFC + i
                        s_, sb_, a_, ab_ = stg[kind], stg_b[kind], cv[kind], cv_b[kind]
                        wc = [SP_CONV + (l * 3 + k) * 86 + c for k in range(3)]
                        P.op("act", lambda e: e.activation(out=a_[:], in_=s_[:, 2:TT + 2], func=AF.Copy,
                                                           scale=spk[:, wc[2]:wc[2] + 1]),
                             reads=[sb_, b_spk], writes=[ab_])
                        P.op("dve", lambda e: e.scalar_tensor_tensor(out=a_[:], in0=s_[:, 1:TT + 1], scalar=spk[:, wc[1]:wc[1] + 1],
                                                                     in1=a_[:], op0=ALU.mult, op1=ALU.add),
                             reads=[sb_, ab_, b_spk], writes=[ab_])
                        P.op("dve", lambda e: e.scalar_tensor_tensor(out=a_[:], in0=s_[:, 0:TT], scalar=spk[:, wc[0]:wc[0] + 1],
                                                                     in1=a_[:], op0=ALU.mult, op1=ALU.add),
                             reads=[sb_, ab_, b_spk], writes=[ab_])

                    def up_epi(bi, m, tag, tci):
                        kind, i = tag
                        c = i if kind == "g" else NFC + i
                        P.op("act", lambda e: e.activation(out=stg[kind][:, 2 + tci * TC:2 + (tci + 1) * TC], in_=pb[bi][:], func=AF.Copy),
                             reads=[], writes=[pb_b[bi], stg_b[kind]])
                        if tci == TT // TC - 1:
                            conv_finish(kind, i)
                            if kind == "u":
                                P.op("act", lambda e: e.activation(out=cv["g"][:], in_=cv["g"][:], func=AF.Silu),
                                     reads=[cv_b["g"]], writes=[cv_b["g"]])
                                P.op("dve", lambda e: e.tensor_tensor(out=act[:, i, :], in0=cv["g"][:], in1=cv["u"][:], op=ALU.mult),
                                     reads=[cv_b["g"], cv_b["u"]], writes=[act_b])
                    blocks = []
                    for i0 in range(0, NFC, 2):
                        npair = min(2, NFC - i0)
                        w = npair * 128
                        segs = [(i0 * 128, w), (DFF + i0 * 128, w)]
                        chunks = []
                        for j in range(npair):
                            chunks.append((j * 128, 128, ("g", i0 + j), 0))
                            chunks.append((w + j * 128, 128, ("u", i0 + j), 0))
                        blocks.append((segs, chunks))
                    gemm(hT, hT_b, 16, lambda c0, n: Wd["w_up"].ap()[l, :, c0:c0 + n], blocks, up_epi, pre=up_pre)
                    gemm(act, act_b, NFC, lambda c0, n: Wd["w_down"].ap()[l, :, c0:c0 + n],
                         simple_blocks(0, D, 256), resid_epi(S, t0))
            if "ple" in parts:
                norm_scope(SP_PLE + l * 16)
                with Scope(P) as s3:
                    S = mk_xr(s3)
                    pT = s3.sb("pT", [128, 2, TT], BF16)
                    pT_b = Buf("pT")
                    pl = [s3.sb(f"pl{i}", [128, 256], F32) for i in range(2)]
                    pl_b = [Buf(), Buf()]
                    sg = [s3.sb(f"sgt{i}", [128, TC], F32) for i in range(2)]
                    sg_b = [Buf(), Buf()]
                    for tt in range(TT // 128):
                        k = tt % 2
                        P.dma("sp", pl[k][:], p_in.ap()[l, t0 + tt * 128:t0 + (tt + 1) * 128, :], writes=[pl_b[k]])
                        for cc in range(2):
                            P.op("pe", lambda e: e.transpose(pb[5][:, cc * 128:(cc + 1) * 128], pl[k][:, cc * 128:(cc + 1) * 128], ident_f),
                                 reads=[pl_b[k], b_cst], writes=[pb_b[5]])
                        P.op("act", lambda e: e.activation(out=pT[:, :, tt * 128:(tt + 1) * 128],
                                                           in_=pb[5][:, 0:256].rearrange("p (a b) -> p a b", b=128), func=AF.Copy),
                             reads=[], writes=[pb_b[5], pT_b])
                    cnt = 0
                    for nb in range(D // 512):
                        sa = wptr[0]
                        sbb = (wptr[0] + 1) % NWS
                        wa = wview(sa, 16, 512)
                        wp = wview(sbb, 2, 512)
                        P.dma("pool", wa, Wd["w_ple_gate"].ap()[l, :, nb * 512:(nb + 1) * 512].rearrange("(kc p) n -> p kc n", p=128),
                              writes=[wb_b[sa]])
                        P.dma("pool", wp, Wd["w_ple_proj"].ap()[l, :, nb * 512:(nb + 1) * 512].rearrange("(kc p) n -> p kc n", p=128),
                              writes=[wb_b[sbb]])
                        for ci in range(4):
                            nchunk = nb * 4 + ci
                            for tci in range(TT // TC):
                                ba = cnt % 2
                                bb = 2 + cnt % 2
                                kx = cnt % 2
                                cnt += 1
                                ta = t0 + tci * TC
                                for kc in range(16):
                                    P.op("pe", lambda e: e.matmul(pb[ba][:], lhsT=wa[:, kc, ci * 128:(ci + 1) * 128],
                                                                  rhs=hT[:, kc, tci * TC:(tci + 1) * TC], start=(kc == 0), stop=(kc == 15)),
                                         reads=[wb_b[sa], hT_b], writes=[pb_b[ba]])
                                for kc in range(2):
                                    P.op("pe", lambda e: e.matmul(pb[bb][:], lhsT=wp[:, kc, ci * 128:(ci + 1) * 128],
                                                                  rhs=pT[:, kc, tci * TC:(tci + 1) * TC], start=(kc == 0), stop=(kc == 1)),
                                         reads=[wb_b[sbb], pT_b], writes=[pb_b[bb]])
                                P.op("act", lambda e: e.activation(out=sg[kx][:], in_=pb[ba][:], func=AF.Sigmoid),
                                     reads=[], writes=[pb_b[ba], sg_b[kx]])
                                P.op("dve", lambda e: e.tensor_tensor(out=sg[kx][:], in0=sg[kx][:], in1=pb[bb][:], op=ALU.mult),
                                     reads=[sg_b[kx]], writes=[pb_b[bb], sg_b[kx]])
                                xr, xr_b = S["xr"][kx], S["xr_b"][kx]
                                P.dma("sp", xr[:], xT.ap()[nchunk * 128:(nchunk + 1) * 128, ta:ta + TC],
                                      reads=[db("xT", ta // TC)], writes=[xr_b])
                                P.op("dve", lambda e: e.tensor_tensor(out=xr[:], in0=sg[kx][:], in1=xr[:], op=ALU.add),
                                     reads=[sg_b[kx], xr_b], writes=[xr_b])
                                P.dma("sp", xT.ap()[nchunk * 128:(nchunk + 1) * 128, ta:ta + TC], xr[:],
                                      reads=[xr_b], writes=[db("xT", ta // TC)])
                        wptr[0] = (wptr[0] + 2) % NWS


def even_attention(E, l):
    e = l // 2
    P, nc, Wd, db, gemm, simple_blocks, norm_x = (E[k] for k in ("P", "nc", "Wd", "db", "gemm", "simple_blocks", "norm_x"))
    spk, b_spk, pb, pb_b, pbh, pbh_b, ident_f, ident_bf, b_cst, b_const, ones_bf, bd_bf, jflip, cneg, eps_t = (E[k] for k in (
        "spk", "b_spk", "pb", "pb_b", "pbh", "pbh_b", "ident_f", "ident_bf", "b_cst", "b_const", "ones_bf", "bd_bf", "jflip", "cneg", "eps_t"))
    xT, yT, oh_in = E["xT"], E["yT"], E["oh_in"]
    s_kvT, s_kvtok, s_kidxT, s_widx, s_qiT, s_qaT, s_qbT, s_kdupT, s_vbtok, s_vrow = (E[k] for k in (
        "s_kvT", "s_kvtok", "s_kidxT", "s_widx", "s_qiT", "s_qaT", "s_qbT", "s_kdupT", "s_vbtok", "s_vrow"))
    XT = XA + XB
    flg, b_flg, cc_gather, kpack_e, gk_e = (E[k] for k in ("flg", "b_flg", "cc_gather", "kpack_e", "gk_e"))

    if not E["state"].get("vrow"):
        E["state"]["vrow"] = True
        with Scope(P) as sv:
            ohs = sv.sb("ohs", [33, XT], F32)
            ohs_b = Buf()
            vr = sv.sb("vr", [16, XT], F32)
            vr_b = Buf()
            P.dma("sp", ohs[:], oh_in.ap(), writes=[ohs_b])
            for (hc, x0, x1) in [(0, 0, 512), (0, 512, 1024), (0, 1024, XA), (16, XA, XT)]:
                P.op("pe", lambda en: en.matmul(pb[5][0:16, 0:x1 - x0], lhsT=spk[0:33, SP_RELB + hc:SP_RELB + hc + 16],
                                                rhs=ohs[:, x0:x1], start=True, stop=True),
                     reads=[ohs_b, b_spk], writes=[pb_b[5]])
                P.op("act", lambda en: en.activation(out=vr[:, x0:x1], in_=pb[5][0:16, 0:x1 - x0], func=AF.Copy),
                     reads=[], writes=[pb_b[5], vr_b])
            P.dma("sp", s_vrow.ap(), vr[:], reads=[vr_b], writes=[db("vrow")])

    def rms_grp(S, srcs, src_b, lhsT_ones, gcol, inv_n, dsts, dst_bufs, ncols=TC):
        C = len(srcs)
        sq, sq_b, rstd, rstd_b = S["sq"], S["sq_b"], S["rstd"], S["rstd_b"]
        for c in range(C):
            P.op("act", lambda en: en.activation(out=sq[:, c, 0:ncols], in_=srcs[c], func=AF.Square),
                 reads=[src_b], writes=[sq_b])
        for c in range(C):
            P.op("pe", lambda en: en.matmul(pb[4][:, 0:ncols], lhsT=lhsT_ones[:, :], rhs=sq[:, c, 0:ncols],
                                            start=(c == 0), stop=(c == C - 1)),
                 reads=[sq_b, b_const], writes=[pb_b[4]])
        P.op("act", lambda en: en.activation(out=rstd[:, 0:ncols], in_=pb[4][:, 0:ncols], func=AF.Sqrt,
                                             scale=inv_n, bias=eps_t[:, 0:1]),
             reads=[b_const], writes=[pb_b[4], rstd_b])
        P.op("dve", lambda en: en.reciprocal(out=rstd[:, 0:ncols], in_=rstd[:, 0:ncols]), reads=[rstd_b], writes=[rstd_b])
        for c in range(C):
            P.op("dve", lambda en: en.scalar_tensor_tensor(out=dsts[c], in0=srcs[c], scalar=spk[:, gcol + c:gcol + c + 1],
                                                           in1=rstd[:, 0:ncols], op0=ALU.mult, op1=ALU.mult),
                 reads=[src_b, rstd_b, b_spk], writes=dst_bufs)
    E["rms_grp"] = rms_grp

    for ps in range(0 if E["cfg"].get("skip_proj") else 1):
        t0 = 0
        with Scope(P) as so:
            hT = so.sb("hT", [128, 16, TT], BF16)
            hT_b = Buf("hT")
            with Scope(P) as sn:
                S0 = dict(xs=sn.sb("xs", [128, 16, TC], F32), xs_b=Buf(), sq=sn.sb("sq", [128, 16, TC], BF16),
                          sq_b=Buf(), rstd=sn.sb("rstd", [128, TC], F32), rstd_b=Buf())
                norm_x(S0, hT, hT_b, SP_ATTN + l * 16, t0)
            S = dict(sq=so.sb("sq", [128, 4, TC], BF16), sq_b=Buf(), rstd=so.sb("rstd", [128, TC], F32), rstd_b=Buf())
            stg4 = so.sb("stg4", [128, 4, TT], F32)
            stg4_b = Buf("stg4")
            stg1 = so.sb("stg1", [128, TT], F32)
            stg1_b = Buf("stg1")
            stg2 = so.sb("stg2", [128, 2, TT], F32)
            stg2_b = Buf("stg2")
            cqT = so.sb("cqT", [128, 4, TT], BF16)
            cqT_b = Buf("cqT")
            kvn = so.sb("kvn", [128, 2, TT], BF16)
            kvn_b = Buf("kvn")
            kvtok_st = so.sb("kvtok_st", [128, 8, 256], BF16)
            kvtok_b = Buf()
            kidx_st = so.sb("kidx_st", [64, TT], BF16)
            kidx_b = Buf()
            widx_st = so.sb("widx_st", [16, TT], F32)
            widx_b = Buf()
            widx_tok = so.sb("widx_tok", [128, 8, 16], F32)
            widx_tok_b = Buf()
            ob = so.sb("ob", [128, TT], BF16)
            ob_b = Buf("ob")
            ob2 = so.sb("ob2", [128, 2, TT], BF16)
            ob2_b = Buf("ob2")
            oq = [so.sb(f"oq{i}", [128, TC], BF16) for i in range(2)]
            oq_b = [Buf(), Buf()]
            oq_i = [0]
            vb_st = so.sb("vb_st", [128, TT], BF16)
            vb_b = Buf()
            vtok_st = so.sb("vtok_st", [128, 8, 128], BF16)
            vtok_b = Buf()

            def tsl(tci):
                return slice(tci * TC, (tci + 1) * TC)

            def in_epi(bi, m, tag, tci):
                kind = tag[0]
                last = (tci == TT // TC - 1)
                if kind == "cq":
                    c = tag[1]
                    P.op("act", lambda en: en.activation(out=stg4[:, c, tsl(tci)], in_=pb[bi][:], func=AF.Copy),
                         reads=[], writes=[pb_b[bi], stg4_b])
                    if c == 3 and last:
                        for t2 in range(TT // TC):
                            rms_grp(S, [stg4[:, cc, tsl(t2)] for cc in range(4)], stg4_b, ones_bf, SP_CQ + e * 4, 1.0 / 512,
                                    [cqT[:, cc, tsl(t2)] for cc in range(4)], [cqT_b])
                elif kind == "ckv":
                    c = tag[1]
                    P.op("act", lambda en: en.activation(out=stg2[:, c, tsl(tci)], in_=pb[bi][:], func=AF.Copy),
                         reads=[], writes=[pb_b[bi], stg2_b])
                    if c == 1 and last:
                        for t2 in range(TT // TC):
                            rms_grp(S, [stg2[:, cc, tsl(t2)] for cc in range(2)], stg2_b, ones_bf, SP_CKV + e * 2, 1.0 / 256,
                                    [kvn[:, cc, tsl(t2)] for cc in range(2)], [kvn_b])
                        P.dma("sp", s_kvT.ap()[:, TO:TO + TT].rearrange("(c p) t -> p c t", p=128), kvn[:],
                              reads=[kvn_b], writes=[db("kvT")])
                        P.dma("sp", kpack_e.ap()[0:256, :].rearrange("(c p) t -> p c t", p=128), kvn[:],
                              reads=[kvn_b], writes=[db("kpack_e")])
                        for tt in range(TT // 128):
                            for cc in range(2):
                                P.op("pe", lambda en: en.transpose(pbh[:, cc * 128:(cc + 1) * 128], kvn[:, cc, tt * 128:(tt + 1) * 128], ident_bf[:]),
                                     reads=[kvn_b, b_const], writes=[pbh_b])
                            P.op("act", lambda en: en.activation(out=kvtok_st[:, tt, :], in_=pbh[:, 0:256], func=AF.Copy),
                                 reads=[], writes=[pbh_b, kvtok_b])
                        P.dma("sp", s_kvtok.ap()[TO:TO + TT, :].rearrange("(tt p) c -> p tt c", p=128), kvtok_st[:],
                              reads=[kvtok_b], writes=[db("kvtok")])
                        P.dma("sp", kpack_e.ap()[576:832, :].rearrange("r (a c) -> (r a) c", c=256).rearrange("(tt p) c -> p tt c", p=128), kvtok_st[:],
                              reads=[kvtok_b], writes=[db("kpack_e")])
                elif kind == "kidx":
                    P.op("act", lambda en: en.activation(out=kidx_st[:, tsl(tci)], in_=pb[bi][0:64, :], func=AF.Copy),
                         reads=[], writes=[pb_b[bi], kidx_b])
                    if last:
                        P.dma("sp", s_kidxT.ap()[:, TO:TO + TT], kidx_st[:], reads=[kidx_b], writes=[db("kidxT")])
                        P.dma("sp", kpack_e.ap()[256:320, :], kidx_st[:], reads=[kidx_b], writes=[db("kpack_e")])
                elif kind == "widx":
                    P.op("act", lambda en: en.activation(out=widx_st[:, tsl(tci)], in_=pb[bi][0:16, :], func=AF.Copy),
                         reads=[], writes=[pb_b[bi], widx_b])
                    if last:
                        for tt in range(TT // 128):
                            P.op("pe", lambda en: en.transpose(pb[5][:, tt * 16:(tt + 1) * 16], widx_st[0:16, tt * 128:(tt + 1) * 128], ident_f[0:16, 0:16]),
                                 reads=[widx_b, b_cst], writes=[pb_b[5]])
                        P.op("act", lambda en: en.activation(out=widx_tok[:], in_=pb[5][:, 0:128].rearrange("p (a b) -> p a b", b=16), func=AF.Copy),
                             reads=[], writes=[pb_b[5], widx_tok_b])
                        P.dma("sp", s_widx.ap()[t0:t0 + TT, :].rearrange("(tt p) c -> p tt c", p=128), widx_tok[:],
                              reads=[widx_tok_b], writes=[db("widx")])
                elif kind == "qb":
                    c = tag[1]
                    P.op("act", lambda en: en.activation(out=stg1[:, tsl(tci)], in_=pb[bi][:], func=AF.Copy),
                         reads=[], writes=[pb_b[bi], stg1_b])
                    if last:
                        for t2 in range(TT // TC):
                            rms_grp(S, [stg1[:, tsl(t2)]], stg1_b, bd_bf, SP_BQ + e, 1.0 / 64, [ob[:, tsl(t2)]], [ob_b])
                        P.dma("sp", s_qbT.ap()[c * 128:(c + 1) * 128, t0:t0 + TT], ob[:], reads=[ob_b], writes=[db("qbT")])
                elif kind == "kb":
                    g, half = tag[1], tag[2]
                    pbs = half * 64
                    P.op("act", lambda en: en.activation(out=stg1[pbs:pbs + 64, tsl(tci)], in_=pb[bi][pbs:pbs + 64, :], func=AF.Copy),
                         reads=[], writes=[pb_b[bi], stg1_b])
                    if half == 1 and last:
                        for t2 in range(TT // TC):
                            rms_grp(S, [stg1[:, tsl(t2)]], stg1_b, bd_bf, SP_BK + e, 1.0 / 64, [ob[:, tsl(t2)]], [ob_b])
                        P.dma("sp", s_kdupT.ap()[g * 128:(g + 1) * 128, TO:TO + TT], ob[:], reads=[ob_b], writes=[db("kdupT")])
                        P.dma("sp", kpack_e.ap()[320 + g * 128:320 + (g + 1) * 128, :], ob[:], reads=[ob_b], writes=[db("kpack_e")])
                elif kind == "vb":
                    P.op("act", lambda en: en.activation(out=vb_st[:, tsl(tci)], in_=pb[bi][:], func=AF.Copy),
                         reads=[], writes=[pb_b[bi], vb_b])
                    if last:
                        for tt in range(TT // 128):
                            P.op("pe", lambda en: en.transpose(pbh[:, (tt % 4) * 128:(tt % 4 + 1) * 128], vb_st[:, tt * 128:(tt + 1) * 128], ident_bf[:]),
                                 reads=[vb_b, b_const], writes=[pbh_b])
                            if tt % 4 == 3:
                                P.op("act", lambda en: en.activation(out=vtok_st[:, tt - 3:tt + 1, :], in_=pbh[:, 0:512].rearrange("p (a b) -> p a b", b=128), func=AF.Copy),
                                     reads=[], writes=[pbh_b, vtok_b])
                        P.dma("sp", s_vbtok.ap()[TO:TO + TT, :].rearrange("(tt p) c -> p tt c", p=128), vtok_st[:],
                              reads=[vtok_b], writes=[db("vbtok")])
                        P.dma("sp", kpack_e.ap()[832:960, :].rearrange("r (a c) -> (r a) c", c=128).rearrange("(tt p) c -> p tt c", p=128), vtok_st[:],
                              reads=[vtok_b], writes=[db("kpack_e")])
                elif kind == "qa":
                    h, cc = tag[1], tag[2]
                    P.op("act", lambda en: en.activation(out=stg2[:, cc, tsl(tci)], in_=pb[bi][:], func=AF.Copy),
                         reads=[], writes=[pb_b[bi], stg2_b])
                    if cc == 1 and last:
                        for t2 in range(TT // TC):
                            rms_grp(S, [stg2[:, c2, tsl(t2)] for c2 in range(2)], stg2_b, ones_bf, SP_AQ + e * 2, 1.0 / 256,
                                    [ob2[:, c2, tsl(t2)] for c2 in range(2)], [ob2_b])
                        P.dma("sp", s_qaT.ap()[h * 256:(h + 1) * 256, t0:t0 + TT].rearrange("(c p) t -> p c t", p=128), ob2[:],
                              reads=[ob2_b], writes=[db("qaT")])
                elif kind == "qi":
                    c = tag[1]
                    k = oq_i[0]
                    oq_i[0] = (k + 1) % 2
                    P.op("act", lambda en: en.activation(out=oq[k][:], in_=pb[bi][:], func=AF.Copy),
                         reads=[], writes=[pb_b[bi], oq_b[k]])
                    P.dma("sp", s_qiT.ap()[c * 128:(c + 1) * 128, t0 + tci * TC:t0 + (tci + 1) * TC], oq[k][:],
                          reads=[oq_b[k]], writes=[db("qiT")])

            blocks = [
                ([(0, 512)], [(c * 128, 128, ("cq", c), 0) for c in range(4)]),
                ([(512, 336)], [(0, 128, ("ckv", 0), 0), (128, 128, ("ckv", 1), 0), (256, 64, ("kidx",), 0), (320, 16, ("widx",), 0)]),
                ([(848, 512)], [(c * 128, 128, ("qb", c), 0) for c in range(4)]),
                ([(1360, 512)], [(c * 128, 128, ("qb", 4 + c), 0) for c in range(4)]),
                ([(1872, 256)], [(0, 64, ("kb", 0, 0), 0), (0, 64, ("kb", 0, 1), 64), (64, 64, ("kb", 1, 0), 0), (64, 64, ("kb", 1, 1), 64),
                                 (128, 128, ("vb",), 0)]),
            ]
            gemm(hT, hT_b, 16, lambda c0, n: Wd["w_in_even"].ap()[e, :, c0:c0 + n], blocks, in_epi)
            blocks = []
            for b4 in range(2):
                blocks.append(([(b4 * 2048, 2048)], [((hh * 2 + cc) * 128, 128, ("qa", b4 * 8 + hh, cc), 0) for hh in range(8) for cc in range(2)]))
            gemm(cqT, cqT_b, 4, lambda c0, n: Wd["a_w_uq"].ap()[e, :, c0:c0 + n], blocks, in_epi)
            gemm(cqT, cqT_b, 4, lambda c0, n: Wd["a_w_qidx"].ap()[e, :, c0:c0 + n],
                 [([(0, 1024)], [(c * 128, 128, ("qi", c), 0) for c in range(8)])], in_epi)

    if not E["cfg"].get("skip_proj"):
        cc_gather(kpack_e, gk_e, [db("kpack_e")], [db("gk_e")])
        P.dma("sp", s_kvT.ap()[:, 0:TO], gk_e.ap()[0:256, :], reads=[db("gk_e")], writes=[db("kvT")])
        P.dma("sp", s_kidxT.ap()[:, 0:TO], gk_e.ap()[256:320, :], reads=[db("gk_e")], writes=[db("kidxT")])
        P.dma("sp", s_kdupT.ap()[:, 0:TO], gk_e.ap()[320:576, :], reads=[db("gk_e")], writes=[db("kdupT")])
        P.dma("sp", s_kvtok.ap()[0:TO, :], gk_e.ap()[576:832, :].rearrange("r (a c) -> (r a) c", c=256), reads=[db("gk_e")], writes=[db("kvtok")])
        P.dma("sp", s_vbtok.ap()[0:TO, :], gk_e.ap()[832:960, :].rearrange("r (a c) -> (r a) c", c=128), reads=[db("gk_e")], writes=[db("vbtok")])
    if E["cfg"].get("stop_after_proj"):
        return
    att_scale = 1.0 / 16.0
    with Scope(P) as sa:
        kvT = sa.sb("kvT", [128, 2, T], BF16)
        kvtok = sa.sb("kvtok", [128, 16, 256], BF16)
        kidx2 = sa.sb("kidx2", [128, T], BF16)
        wuv = sa.sb("wuv", [128, 16, 2, 64], BF16)
        expA = sa.sb("expA", [128, 16, 9, 128], BF16)
        b_k = Buf("kside")
        b_exp = Buf("expA")
        P.dma("sp", kvT[:], s_kvT.ap().rearrange("(c p) t -> p c t", p=128), reads=[db("kvT")], writes=[b_k])
        P.dma("sp", kvtok[:], s_kvtok.ap().rearrange("(tt p) c -> p tt c", p=128), reads=[db("kvtok")], writes=[b_k])
        P.dma("sp", kidx2[0:64, :], s_kidxT.ap(), reads=[db("kidxT")], writes=[b_k])
        P.dma("sp", kidx2[64:128, :], s_kidxT.ap(), reads=[db("kidxT")], writes=[b_k])
        P.dma("pool", wuv[:], Wd["a_w_uv"].ap()[e].rearrange("h (cc p) d -> p h cc d", p=128), writes=[b_k])
        s_expA = E["s_expA"]
        if not E["state"].get("expA"):
            E["state"]["expA"] = True
            hk = [sa.sb(f"hk{i}", [128, 128], F32) for i in range(4)]
            hk_b = [Buf() for _ in range(4)]
            n = 0
            for h in range(16):
                for dj in range(9):
                    k = n % 4
                    bk5 = 5 + (n % 2)
                    n += 1
                    P.dma("sp", hk[k][:], bass.AP(s_vrow, h * XT + dj * 128, [[1, 128], [1, 128]]), reads=[db("vrow")], writes=[hk_b[k]])
                    P.op("pe", lambda en: en.matmul(pb[bk5][:, 0:128], lhsT=jflip, rhs=hk[k][:], start=True, stop=True),
                         reads=[hk_b[k], b_cst], writes=[pb_b[bk5]])
                    P.op("act", lambda en: en.activation(out=expA[:, h, 8 - dj, :], in_=pb[bk5][:, 0:128], func=AF.Exp),
                         reads=[], writes=[pb_b[bk5], b_exp])
            P.dma("sp", s_expA.ap(), expA[:].rearrange("p h k q -> p (h k q)"), reads=[b_exp], writes=[db("expA")])
        else:
            P.dma("sp", expA[:].rearrange("p h k q -> p (h k q)"), s_expA.ap(), reads=[db("expA")], writes=[b_exp])
        qi = [sa.sb(f"qi{i}", [128, 8, 128], BF16) for i in range(2)]
        qa = [sa.sb(f"qa{i}", [128, 32, 128], BF16) for i in range(2)]
        wq = [sa.sb(f"wq{i}", [128, 16], F32) for i in range(2)]
        q_b = [Buf(), Buf()]
        score = sa.sb("score", [128, T], F32)
        score_b = Buf("score")
        work = sa.sb("work", [128, T], F32)
        work_b = Buf("work")
        m8 = sa.sb("m8", [128, 8], F32)
        m8_b = Buf("m8")
        thr = sa.sb("thr", [128, 1], F32)
        cand = sa.sb("cand", [128, 1], F32)
        cntt = sa.sb("cntt", [128, 1], F32)
        cnd = sa.sb("cnd", [128, 1], F32)
        thr_b, cand_b, cnt_b, cnd_b = Buf(), Buf(), Buf(), Buf()
        mask01 = sa.sb("mask01", [128, T], BF16)
        mask01_b = Buf()
        maskT = sa.sb("maskT", [128, 16, 128], BF16)
        maskT_b = Buf()
        rl = [sa.sb(f"rl{i}", [128, 512], F32) for i in range(2)]
        rl_b = [Buf(), Buf()]
        NBD = 3
        LBD = [0, 1, 4]
        pf = [sa.sb(f"pf{i}", [128, 512], F32) for i in range(NBD)]
        pf_b = [Buf() for _ in range(NBD)]
        pbf = [sa.sb(f"pbf{i}", [128, 512], BF16) for i in range(NBD)]
        pbf_b = [Buf() for _ in range(NBD)]
        rc = sa.sb("rc", [128, 128], F32)
        acc32 = sa.sb("acc32", [128, 3, 128], F32)
        acc_b = Buf("acc32")
        rc_b = Buf()
        on = sa.sb("on", [128, 2, 128], BF16)
        on_b = Buf()
        ya_st = [sa.sb(f"ya_st{i}", [128, 8, 128], BF16) for i in range(2)]
        ya_b = [Buf(), Buf()]
        cnt = [0, 0]
        for a_ in range(E["cfg"].get("dsa_tiles", 8)):
            i = 8 + a_
            qk = i % 2
            N = (i + 1) * 128
            qs = slice(a_ * 128, (a_ + 1) * 128)
            ks = slice(i * 128, (i + 1) * 128)
            P.dma("sp", qi[qk][:], s_qiT.ap()[:, qs].rearrange("(c p) t -> p c t", p=128), reads=[db("qiT")], writes=[q_b[qk]])
            P.dma("sp", qa[qk][:], s_qaT.ap()[:, qs].rearrange("(c p) t -> p c t", p=128), reads=[db("qaT")], writes=[q_b[qk]])
            P.dma("sp", wq[qk][:], s_widx.ap()[qs, :], reads=[db("widx")], writes=[q_b[qk]])
            for h in range(16):
                pbs = (h % 2) * 64
                for n0 in range(0, N, 512):
                    n1 = min(N, n0 + 512)
                    bk = cnt[0] % 2
                    cnt[0] += 1
                    P.op("pe", lambda en: en.matmul(pb[bk][:, 0:n1 - n0], lhsT=qi[qk][pbs:pbs + 64, h // 2, :], rhs=kidx2[pbs:pbs + 64, n0:n1],
                                                    start=True, stop=True),
                         reads=[q_b[qk], b_k], writes=[pb_b[bk]])
                    P.op("act", lambda en: en.activation(out=rl[bk][:, 0:n1 - n0], in_=pb[bk][:, 0:n1 - n0], func=AF.Relu),
                         reads=[], writes=[pb_b[bk], rl_b[bk]])
                    if h == 0:
                        P.op("dve", lambda en: en.tensor_scalar(out=score[:, n0:n1], in0=rl[bk][:, 0:n1 - n0], scalar1=wq[qk][:, 0:1], scalar2=None, op0=ALU.mult),
                             reads=[rl_b[bk], q_b[qk]], writes=[score_b])
                    else:
                        P.op("dve", lambda en: en.scalar_tensor_tensor(out=score[:, n0:n1], in0=rl[bk][:, 0:n1 - n0], scalar=wq[qk][:, h:h + 1],
                                                                       in1=score[:, n0:n1], op0=ALU.mult, op1=ALU.add),
                             reads=[rl_b[bk], q_b[qk], score_b], writes=[score_b])
            P.op("dve", lambda en: en.tensor_tensor(out=score[:, ks], in0=score[:, ks], in1=cneg, op=ALU.add),
                 reads=[score_b, b_cst], writes=[score_b])
            P.op("dve", lambda en: en.tensor_scalar(out=score[:, 0:TO], in0=score[:, 0:TO], scalar1=flg[:, 1:2], scalar2=None, op0=ALU.add),
                 reads=[score_b, b_flg], writes=[score_b])
            P.op("dve", lambda en: en.memset(thr[:], -4096.0), writes=[thr_b])
            for kk in range(23):
                step = 8192.0 / (2 ** (kk + 1))
                P.op("dve", lambda en: en.tensor_scalar(out=cand[:], in0=thr[:], scalar1=step, scalar2=None, op0=ALU.add),
                     reads=[thr_b], writes=[cand_b])
                P.op("dve", lambda en: en.tensor_scalar(out=work[:, 0:N], in0=score[:, 0:N], scalar1=cand[:, 0:1], scalar2=0.0,
                                                        op0=ALU.is_ge, op1=ALU.add, accum_out=cntt[:, 0:1]),
                     reads=[score_b, cand_b], writes=[work_b, cnt_b])
                P.op("dve", lambda en: en.tensor_scalar(out=cnd[:], in0=cntt[:], scalar1=255.5, scalar2=None, op0=ALU.is_ge),
                     reads=[cnt_b], writes=[cnd_b])
                P.op("dve", lambda en: en.scalar_tensor_tensor(out=thr[:], in0=cnd[:], scalar=step, in1=thr[:], op0=ALU.mult, op1=ALU.add),
                     reads=[cnd_b, thr_b], writes=[thr_b])
            P.op("dve", lambda en: en.tensor_scalar(out=mask01[:, 0:N], in0=score[:, 0:N], scalar1=thr[:, 0:1], scalar2=None, op0=ALU.is_ge),
                 reads=[score_b, thr_b], writes=[mask01_b])
            for j0 in range(0, i + 1, 8):
                j1 = min(i + 1, j0 + 8)
                for j in range(j0, j1):
                    P.op("pe", lambda en: en.transpose(pbh[:, (j - j0) * 128:(j - j0 + 1) * 128], mask01[:, j * 128:(j + 1) * 128], ident_bf[:]),
                         reads=[mask01_b, b_const], writes=[pbh_b])
                P.op("act", lambda en: en.activation(out=maskT[:, j0:j1, :], in_=pbh[:, 0:(j1 - j0) * 128].rearrange("p (a b) -> p a b", b=128), func=AF.Copy),
                     reads=[], writes=[pbh_b, maskT_b])
            yk = i % 2
            items = [(h, jg) for h in range(16) for jg in range(0, i + 1, 4)]

            def emit_logits(k):
                h, jg = items[k]
                je = min(i + 1, jg + 4)
                L = k % NBD
                BK = LBD[L]
                for j in range(jg, je):
                    sl = j - jg
                    for c in range(2):
                        P.op("pe", lambda en: en.matmul(pb[BK][:, sl * 128:(sl + 1) * 128], lhsT=kvT[:, c, j * 128:(j + 1) * 128],
                                                        rhs=qa[qk][:, 2 * h + c, :], start=(c == 0), stop=(c == 1)),
                             reads=[b_k, q_b[qk]], writes=[pb_b[BK]])

            def emit_post(k):
                h, jg = items[k]
                je = min(i + 1, jg + 4)
                nj = je - jg
                L = k % NBD
                BK = LBD[L]
                far = (i - (je - 1)) >= 8
                if far:
                    P.op("act", lambda en: en.activation(out=pf[L][:, 0:nj * 128], in_=pb[BK][:, 0:nj * 128], func=AF.Exp, scale=att_scale,
                                                         bias=spk[:, SP_B31 + h:SP_B31 + h + 1]),
                         reads=[b_spk], writes=[pb_b[BK], pf_b[L]])
                else:
                    P.op("act", lambda en: en.activation(out=pf[L][:, 0:nj * 128], in_=pb[BK][:, 0:nj * 128], func=AF.Exp, scale=att_scale),
                         reads=[], writes=[pb_b[BK], pf_b[L]])
                    if i - jg <= 8:
                        k0 = 8 - (i - jg)
                        P.op("dve", lambda en: en.tensor_tensor(out=pf[L][:, 0:nj * 128].rearrange("p (a b) -> p a b", b=128),
                                                                in0=pf[L][:, 0:nj * 128].rearrange("p (a b) -> p a b", b=128),
                                                                in1=expA[:, h, k0:k0 + nj, :], op=ALU.mult),
                             reads=[pf_b[L], b_exp], writes=[pf_b[L]])
                    else:
                        for j in range(jg, je):
                            sl = j - jg
                            kk = 8 - min(i - j, 8)
                            P.op("dve", lambda en: en.tensor_tensor(out=pf[L][:, sl * 128:(sl + 1) * 128], in0=pf[L][:, sl * 128:(sl + 1) * 128],
                                                                    in1=expA[:, h, kk, :], op=ALU.mult),
                                 reads=[pf_b[L], b_exp], writes=[pf_b[L]])
                P.op("pool", lambda en: en.tensor_tensor(out=pbf[L][:, 0:nj * 128].rearrange("p (a b) -> p a b", b=128),
                                                         in0=pf[L][:, 0:nj * 128].rearrange("p (a b) -> p a b", b=128),
                                                         in1=maskT[:, jg:je, :], op=ALU.mult),
                     reads=[pf_b[L], maskT_b], writes=[pbf_b[L]])

            def emit_pv(k):
                h, jg = items[k]
                je = min(i + 1, jg + 4)
                L = k % NBD
                for j in range(jg, je):
                    sl = j - jg
                    for (bk, lh) in ((2, kvtok[:, j, 0:128]), (3, kvtok[:, j, 128:256]), (6, ones_bf[:, :])):
                        P.op("pe", lambda en: en.matmul(pb[bk][:, 0:128], lhsT=lh, rhs=pbf[L][:, sl * 128:(sl + 1) * 128],
                                                        start=(j == 0), stop=(j == i)),
                             reads=[b_k, pbf_b[L], b_const], writes=[pb_b[bk]])

            def emit_fin_dve(h):
                for c3, bk in enumerate((2, 3, 6)):
                    P.op("act", lambda en: en.activation(out=acc32[:, c3, :], in_=pb[bk][:, 0:128], func=AF.Copy),
                         reads=[], writes=[pb_b[bk], acc_b])
                P.op("dve", lambda en: en.reciprocal(out=rc[:], in_=acc32[:, 2, :]), reads=[acc_b], writes=[rc_b])
                for c in range(2):
                    P.op("dve", lambda en: en.tensor_tensor(out=on[:, c, :], in0=acc32[:, c, :], in1=rc[:], op=ALU.mult),
                         reads=[rc_b, acc_b], writes=[on_b])

            def emit_fin_pe(h):
                pbs = (h % 2) * 64
                col = ((h // 2) % 4) * 128
                for c in range(2):
                    P.op("pe", lambda en: en.matmul(pb[5][pbs:pbs + 64, col:col + 128], lhsT=wuv[:, h, c, :], rhs=on[:, c, :],
                                                    start=(c == 0), stop=(c == 1)),
                         reads=[b_k, on_b], writes=[pb_b[5]])
                if h % 2 == 1:
                    P.op("act", lambda en: en.activation(out=ya_st[yk][:, h // 2, :], in_=pb[5][:, col:col + 128], func=AF.Copy),
                         reads=[], writes=[pb_b[5], ya_b[yk]])

            for k0 in range(min(NBD - 1, len(items))):
                emit_logits(k0)
            for k in range(len(items)):
                h, jg = items[k]
                if k + NBD - 1 < len(items):
                    emit_logits(k + NBD - 1)
                emit_post(k)
                emit_pv(k)
                if jg + 4 > i:
                    emit_fin_dve(h)
                    emit_fin_pe(h)
            P.dma("sp", yT.ap()[0:1024, qs].rearrange("(c p) t -> p c t", p=128), ya_st[yk][:], reads=[ya_b[yk]], writes=[db("yT", 0)])

    with Scope(P) as sw:
        kd = sw.sb("kd", [128, 2, T], BF16)
        vtk = sw.sb("vtk", [128, 16, 128], BF16)
        expB = sw.sb("expB", [128, 16, 2, 128], BF16)
        esk = sw.sb("esk", [128, 16], F32)
        b_k = Buf("kside")
        b_exp = Buf("expB")
        P.dma("sp", kd[:], s_kdupT.ap().rearrange("(g p) t -> p g t", p=128), reads=[db("kdupT")], writes=[b_k])
        P.dma("sp", vtk[:], s_vbtok.ap().rearrange("(tt p) c -> p tt c", p=128), reads=[db("vbtok")], writes=[b_k])
        P.op("act", lambda en: en.activation(out=esk[:], in_=spk[:, SP_SINK + e * 16:SP_SINK + (e + 1) * 16], func=AF.Exp),
             reads=[b_spk], writes=[b_exp])
        s_expB = E["s_expB"]
        if not E["state"].get("expB"):
            E["state"]["expB"] = True
            hk = [sw.sb(f"hk{i}", [128, 128], F32) for i in range(4)]
            hk_b = [Buf() for _ in range(4)]
            n = 0
            for hb in range(16):
                for kx in range(2):
                    dj = 1 - kx
                    k = n % 4
                    bk5 = [4, 2][n % 2]
                    n += 1
                    P.dma("sp", hk[k][:], bass.AP(s_vrow, hb * XT + XA + dj * 128, [[1, 128], [1, 128]]), reads=[db("vrow")], writes=[hk_b[k]])
                    P.op("pe", lambda en: en.matmul(pb[bk5][:, 0:128], lhsT=jflip, rhs=hk[k][:], start=True, stop=True),
                         reads=[hk_b[k], b_cst], writes=[pb_b[bk5]])
                    P.op("act", lambda en: en.activation(out=expB[:, hb, kx, :], in_=pb[bk5][:, 0:128], func=AF.Exp),
                         reads=[], writes=[pb_b[bk5], b_exp])
            P.dma("sp", s_expB.ap(), expB[:].rearrange("p h k q -> p (h k q)"), reads=[b_exp], writes=[db("expB")])
        else:
            P.dma("sp", expB[:].rearrange("p h k q -> p (h k q)"), s_expB.ap(), reads=[db("expB")], writes=[b_exp])
        qb = [sw.sb(f"qb{i}", [128, 8, 128], BF16) for i in range(2)]
        qb_b = [Buf(), Buf()]
        pf = [sw.sb(f"pf{i}", [128, 512], F32) for i in range(2)]
        pf_b = [Buf(), Buf()]
        pbf = [sw.sb(f"pbf{i}", [128, 512], BF16) for i in range(2)]
        pbf_b = [Buf(), Buf()]
        dn = sw.sb("dn", [128, 128], F32)
        dn_b = Buf()
        yb_st = [sw.sb(f"yb_st{i}", [128, 8, 128], BF16) for i in range(2)]
        yb_b = [Buf(), Buf()]
        cnt = 0
        for a_ in range(E["cfg"].get("swa_blocks", 8)):
            nb = 8 + a_
            qk = nb % 2
            qs = slice(a_ * 128, (a_ + 1) * 128)
            P.dma("sp", qb[qk][:], s_qbT.ap()[:, qs].rearrange("(c p) t -> p c t", p=128), reads=[db("qbT")], writes=[qb_b[qk]])
            for m in range(8):
                Lb = [(0, 1), (5, 6)][cnt % 2]
                Ls = cnt % 2
                cnt += 1
                units = []
                for hh in range(2):
                    for kx in range(2):
                        dj = 1 - kx
                        if nb - dj >= 0:
                            units.append((hh, kx, nb - dj, hh * 2 + kx))
                for (hh, kx, j, sl) in units:
                    hb = 2 * m + hh
                    g = hb // 8
                    pbs = hh * 64
                    bkx = Lb[hh]
                    P.op("pe", lambda en: en.matmul(pb[bkx][:, kx * 128:(kx + 1) * 128], lhsT=kd[pbs:pbs + 64, g, j * 128:(j + 1) * 128],
                                                    rhs=qb[qk][pbs:pbs + 64, m, :], start=True, stop=True),
                         reads=[b_k, qb_b[qk]], writes=[pb_b[bkx]])
                stg_ = E["cfg"].get("swa_stage", 4)
                if stg_ < 2:
                    continue
                L = Ls
                for hh in range(2):
                    bkx = Lb[hh]
                    a, b = (0, 2)
                    P.op("act", lambda en: en.activation(out=pf[L][:, hh * 256 + a * 128:hh * 256 + b * 128], in_=pb[bkx][:, a * 128:b * 128], func=AF.Exp, scale=0.125),
                         reads=[], writes=[pb_b[bkx], pf_b[L]])
                    P.op("dve", lambda en: en.tensor_tensor(out=pbf[L][:, hh * 256 + a * 128:hh * 256 + b * 128], in0=pf[L][:, hh * 256 + a * 128:hh * 256 + b * 128],
                                                            in1=expB[:, 2 * m + hh, a:b, :].rearrange("p k q -> p (k q)"), op=ALU.mult),
                         reads=[pf_b[L], b_exp], writes=[pbf_b[L]])
                    if a_ == 0:
                        P.op("dve", lambda en: en.tensor_scalar(out=pbf[L][:, hh * 256:hh * 256 + 128], in0=pbf[L][:, hh * 256:hh * 256 + 128],
                                                                scalar1=flg[:, 0:1], scalar2=None, op0=ALU.mult),
                             reads=[pbf_b[L], b_flg], writes=[pbf_b[L]])
                if stg_ < 3:
                    continue
                for hh in range(2):
                    us = [u for u in units if u[0] == hh]
                    hb = 2 * m + hh
                    g = hb // 8
                    pbs = hh * 64
                    for ui, (_, kx, j, sl) in enumerate(us):
                        P.op("pe", lambda en: en.matmul(pb[2][pbs:pbs + 64, 0:128], lhsT=vtk[:, j, g * 64:(g + 1) * 64], rhs=pbf[L][:, sl * 128:(sl + 1) * 128],
                                                        start=(ui == 0), stop=(ui == len(us) - 1)),
                             reads=[b_k, pbf_b[L]], writes=[pb_b[2]])
                        P.op("pe", lambda en: en.matmul(pb[3][pbs:pbs + 64, 0:128], lhsT=ones_bf[:, 0:64], rhs=pbf[L][:, sl * 128:(sl + 1) * 128],
                                                        start=(ui == 0), stop=(ui == len(us) - 1)),
                             reads=[b_const, pbf_b[L]], writes=[pb_b[3]])
                if stg_ < 4:
                    continue
                for hh in range(2):
                    hb = 2 * m + hh
                    pbs = hh * 64
                    P.op("dve", lambda en: en.tensor_scalar(out=dn[pbs:pbs + 64, :], in0=pb[3][pbs:pbs + 64, 0:128], scalar1=esk[pbs:pbs + 64, hb:hb + 1],
                                                            scalar2=None, op0=ALU.add),
                         reads=[b_exp], writes=[pb_b[3], dn_b])
                P.op("dve", lambda en: en.reciprocal(out=dn[:], in_=dn[:]), reads=[dn_b], writes=[dn_b])
                P.op("dve", lambda en: en.tensor_tensor(out=yb_st[qk][:, m, :], in0=pb[2][:, 0:128], in1=dn[:], op=ALU.mult),
                     reads=[dn_b], writes=[pb_b[2], yb_b[qk]])
            P.dma("sp", yT.ap()[1024:2048, qs].rearrange("(c p) t -> p c t", p=128), yb_st[qk][:], reads=[yb_b[qk]], writes=[db("yT", 0)])


def odd_attention(E, l):
    o = l // 2
    P, nc, Wd, db, gemm, simple_blocks, norm_x = (E[k] for k in ("P", "nc", "Wd", "db", "gemm", "simple_blocks", "norm_x"))
    spk, b_spk, pb, pb_b, pbh, pbh_b, ident_f, ident_bf, b_cst, b_const, ones_bf, ones_f, bd_bf, triu_f, triu_bf, eps_t = (E[k] for k in (
        "spk", "b_spk", "pb", "pb_b", "pbh", "pbh_b", "ident_f", "ident_bf", "b_cst", "b_const", "ones_bf", "ones_f", "bd_bf", "triu_f", "triu_bf", "eps_t"))
    xT, yT = E["xT"], E["yT"]
    s_qT, s_kT, s_vtok, s_lf = E["s_qT"], E["s_kT"], E["s_vtok"], E["s_lf"]
    flg, b_flg, cc_gather, kpack_o, gk_o, lfp, glf = (E[k] for k in ("flg", "b_flg", "cc_gather", "kpack_o", "gk_o", "lfp", "glf"))

    def rms_grp(S, srcs, src_b, lhsT_ones, gcol, inv_n, dsts, dst_bufs, ncols=TC):
        C = len(srcs)
        sq, sq_b, rstd, rstd_b = S["sq"], S["sq_b"], S["rstd"], S["rstd_b"]
        for c in range(C):
            P.op("act", lambda en: en.activation(out=sq[:, c, 0:ncols], in_=srcs[c], func=AF.Square),
                 reads=[src_b], writes=[sq_b])
        for c in range(C):
            P.op("pe", lambda en: en.matmul(pb[4][:, 0:ncols], lhsT=lhsT_ones[:, :], rhs=sq[:, c, 0:ncols],
                                            start=(c == 0), stop=(c == C - 1)),
                 reads=[sq_b, b_const], writes=[pb_b[4]])
        P.op("act", lambda en: en.activation(out=rstd[:, 0:ncols], in_=pb[4][:, 0:ncols], func=AF.Sqrt,
                                             scale=inv_n, bias=eps_t[:, 0:1]),
             reads=[b_const], writes=[pb_b[4], rstd_b])
        P.op("dve", lambda en: en.reciprocal(out=rstd[:, 0:ncols], in_=rstd[:, 0:ncols]), reads=[rstd_b], writes=[rstd_b])
        for c in range(C):
            P.op("dve", lambda en: en.scalar_tensor_tensor(out=dsts[c], in0=srcs[c], scalar=spk[:, gcol + c:gcol + c + 1],
                                                           in1=rstd[:, 0:ncols], op0=ALU.mult, op1=ALU.mult),
                 reads=[src_b, rstd_b, b_spk], writes=dst_bufs)

    for ps in range(0 if E["cfg"].get("skip_proj") else 1):
        t0 = 0
        with Scope(P) as so:
            hT = so.sb("hT", [128, 16, TT], BF16)
            hT_b = Buf("hT")
            with Scope(P) as sn:
                S0 = dict(xs=sn.sb("xs", [128, 16, TC], F32), xs_b=Buf(), sq=sn.sb("sq", [128, 16, TC], BF16),
                          sq_b=Buf(), rstd=sn.sb("rstd", [128, TC], F32), rstd_b=Buf())
                norm_x(S0, hT, hT_b, SP_ATTN + l * 16, t0)
            S = dict(sq=so.sb("sq", [128, 1, TC], BF16), sq_b=Buf(), rstd=so.sb("rstd", [128, TC], F32), rstd_b=Buf())
            stg1 = so.sb("stg1", [128, TT], F32)
            stg1_b = Buf("stg1")
            ob = so.sb("ob", [128, TT], BF16)
            ob_b = Buf("ob")
            vb_st = so.sb("vb_st", [128, TT], BF16)
            vb_b = Buf()
            vtok_st = so.sb("vtok_st", [128, 8, 128], BF16)
            vtok_b = Buf()
            negfb = so.sb("negfb", [32, 1], F32)
            negfb_b = Buf()
            fst = so.sb("fst", [32, TT], F32)
            fst_b = Buf()
            lf_tok = so.sb("lf_tok", [128, 8, 32], F32)
            lf_tok_b = Buf()
            P.op("dve", lambda en: en.tensor_scalar(out=negfb[:], in0=spk[0:32, SP_FB + o:SP_FB + o + 1], scalar1=-1.0, scalar2=None, op0=ALU.mult),
                 reads=[b_spk], writes=[negfb_b])

            def tsl(tci):
                return slice(tci * TC, (tci + 1) * TC)

            def in_epi(bi, m, tag, tci):
                kind = tag[0]
                last = (tci == TT // TC - 1)
                if kind in ("q", "k"):
                    c = tag[1]
                    P.op("act", lambda en: en.activation(out=stg1[:, tsl(tci)], in_=pb[bi][:], func=AF.Copy),
                         reads=[], writes=[pb_b[bi], stg1_b])
                    if last:
                        gcol = (SP_CQN if kind == "q" else SP_CKN) + o
                        dst = s_qT if kind == "q" else s_kT
                        for t2 in range(TT // TC):
                            rms_grp(S, [stg1[:, tsl(t2)]], stg1_b, bd_bf, gcol, 1.0 / 64, [ob[:, tsl(t2)]], [ob_b])
                        if kind == "q":
                            P.dma("sp", s_qT.ap()[c * 128:(c + 1) * 128, 0:TT], ob[:], reads=[ob_b], writes=[db("qT")])
                        else:
                            P.dma("sp", s_kT.ap()[c * 128:(c + 1) * 128, TO:TO + TT], ob[:], reads=[ob_b], writes=[db("kT")])
                            P.dma("sp", kpack_o[c // 8].ap()[(c % 8) * 128:(c % 8 + 1) * 128, :], ob[:], reads=[ob_b], writes=[db("kpack_o", c // 8)])
                elif kind == "v":
                    c = tag[1]
                    P.op("act", lambda en: en.activation(out=vb_st[:, tsl(tci)], in_=pb[bi][:], func=AF.Copy),
                         reads=[], writes=[pb_b[bi], vb_b])
                    if last:
                        for tt in range(TT // 128):
                            P.op("pe", lambda en: en.transpose(pbh[:, (tt % 4) * 128:(tt % 4 + 1) * 128], vb_st[:, tt * 128:(tt + 1) * 128], ident_bf[:]),
                                 reads=[vb_b, b_const], writes=[pbh_b])
                            if tt % 4 == 3:
                                P.op("act", lambda en: en.activation(out=vtok_st[:, tt - 3:tt + 1, :], in_=pbh[:, 0:512].rearrange("p (a b) -> p a b", b=128), func=AF.Copy),
                                     reads=[], writes=[pbh_b, vtok_b])
                        P.dma("sp", s_vtok.ap()[TO:TO + TT, c * 128:(c + 1) * 128].rearrange("(tt p) c -> p tt c", p=128), vtok_st[:],
                              reads=[vtok_b], writes=[db("vtok")])
                        for hv in range(2):
                            P.dma("sp", kpack_o[2 + hv].ap().rearrange("(t a) c -> t (a c)", a=2)[:, c * 128:(c + 1) * 128].rearrange("(tt p) c -> p tt c", p=128),
                                  vtok_st[:, hv * 4:(hv + 1) * 4, :], reads=[vtok_b], writes=[db("kpack_o", 2 + hv)])
                elif kind == "f":
                    P.op("act", lambda en: en.activation(out=fst[:, tsl(tci)], in_=pb[bi][0:32, :], func=AF.Exp, scale=-1.0, bias=negfb[:, 0:1]),
                         reads=[negfb_b], writes=[pb_b[bi], fst_b])
                    if last:
                        P.op("act", lambda en: en.activation(out=fst[:], in_=fst[:], func=AF.Ln, bias=ones_f[0:32, 0:1]),
                             reads=[fst_b, b_const], writes=[fst_b])
                        for tt in range(TT // 128):
                            P.op("pe", lambda en: en.transpose(pb[5][:, tt * 32:(tt + 1) * 32], fst[0:32, tt * 128:(tt + 1) * 128], ident_f[0:32, 0:32]),
                                 reads=[fst_b, b_cst], writes=[pb_b[5]])
                        P.op("act", lambda en: en.activation(out=lf_tok[:], in_=pb[5][:, 0:256].rearrange("p (a b) -> p a b", b=32), func=AF.Copy),
                             reads=[], writes=[pb_b[5], lf_tok_b])
                        P.dma("sp", s_lf.ap()[TO:TO + TT, :].rearrange("(tt p) c -> p tt c", p=128), lf_tok[:],
                              reads=[lf_tok_b], writes=[db("lf")])
                        P.dma("sp", lfp.ap().rearrange("(tt p) c -> p tt c", p=128), lf_tok[:],
                              reads=[lf_tok_b], writes=[db("lfp")])
            blocks = []
            for kind, base in (("q", 0), ("k", 2048), ("v", 4096)):
                for b4 in range(4):
                    blocks.append(([(base + b4 * 512, 512)], [(c * 128, 128, (kind, b4 * 4 + c), 0) for c in range(4)]))
            blocks.append(([(6144, 32)], [(0, 32, ("f",), 0)]))
            gemm(hT, hT_b, 16, lambda c0, n: Wd["w_in_odd"].ap()[o, :, c0:c0 + n], blocks, in_epi)

    if not E["cfg"].get("skip_proj"):
        for i4 in range(4):
            cc_gather(kpack_o[i4], gk_o[i4], [db("kpack_o", i4)], [db("gk_o", i4)])
        cc_gather(lfp, glf, [db("lfp")], [db("glf")])
        for i4 in range(2):
            P.dma("sp", s_kT.ap()[i4 * 1024:(i4 + 1) * 1024, 0:TO], gk_o[i4].ap()[0:1024, :], reads=[db("gk_o", i4)], writes=[db("kT")])
            P.dma("sp", s_vtok.ap()[i4 * 512:(i4 + 1) * 512, :], gk_o[2 + i4].ap()[0:1024, :].rearrange("(t a) c -> t (a c)", a=2),
                  reads=[db("gk_o", 2 + i4)], writes=[db("vtok")])
        P.dma("sp", s_lf.ap()[0:TO, :], glf.ap()[0:1024, :], reads=[db("glf")], writes=[db("lf")])
    if E["cfg"].get("stop_after_proj"):
        return
    with Scope(P) as sa:
        lft = sa.sb("lft", [128, 16, 32], F32)
        lft_b = Buf()
        ncum = sa.sb("ncum", [128, 16, 32], F32)
        Cb = sa.sb("Cb", [128, 16, 32], F32)
        cum_b = Buf("cum")
        P.dma("sp", lft[:], s_lf.ap().rearrange("(tt p) c -> p tt c", p=128), reads=[db("lf")], writes=[lft_b])
        P.op("dve", lambda en: en.tensor_scalar(out=lft[:, 0:8, :], in0=lft[:, 0:8, :], scalar1=flg[:, 0:1], scalar2=None, op0=ALU.mult),
             reads=[lft_b, b_flg], writes=[lft_b])
        for j in range(16):
            for j2 in range(j + 1):
                P.op("pe", lambda en: en.matmul(pb[5][:, j * 32:(j + 1) * 32], lhsT=(triu_f if j2 == j else ones_f[:, :]), rhs=lft[:, j2, :],
                                                start=(j2 == 0), stop=(j2 == j)),
                     reads=[lft_b, b_cst, b_const], writes=[pb_b[5]])
            for j2 in range(j + 1):
                P.op("pe", lambda en: en.matmul(pb[6][:, j * 32:(j + 1) * 32], lhsT=ones_f[:, :], rhs=lft[:, j2, :],
                                                start=(j2 == 0), stop=(j2 == j)),
                     reads=[lft_b, b_const], writes=[pb_b[6]])
        P.op("act", lambda en: en.activation(out=ncum[:], in_=pb[5][:].rearrange("p (a b) -> p a b", b=32), func=AF.Copy),
             reads=[], writes=[pb_b[5], cum_b])
        P.op("act", lambda en: en.activation(out=Cb[:], in_=pb[6][:].rearrange("p (a b) -> p a b", b=32), func=AF.Copy),
             reads=[], writes=[pb_b[6], cum_b])
        s_nq = E["s_nq"]
        dm = sa.sb("dm", [128, 16, 32], F32)
        dhi = sa.sb("dhi", [128, 16, 32], BF16)
        dhf = sa.sb("dhf", [128, 16, 32], F32)
        dlo = sa.sb("dlo", [128, 16, 32], BF16)
        dl2 = sa.sb("dl2", [128, 16, 32], BF16)
        nqT = sa.sb("nqT", [32, 3, TO], BF16)
        dm_b = Buf("dm")
        nqT_b = Buf("nqT")
        P.op("dve", lambda en: en.memset(dm[:, 0:8, :], 0.0), writes=[dm_b])
        for t_ in range(8, 16):
            ge = 4 * (t_ // 4) + 3
            P.op("dve", lambda en: en.tensor_tensor(out=dm[:, t_, :], in0=Cb[:, ge, :], in1=ncum[:, t_, :], op=ALU.subtract),
                 reads=[cum_b], writes=[dm_b])
        P.op("dve", lambda en: en.tensor_scalar(out=dm[:], in0=dm[:], scalar1=8.0, scalar2=None, op0=ALU.mult), reads=[dm_b], writes=[dm_b])
        P.op("dve", lambda en: en.tensor_copy(out=dhi[:], in_=dm[:]), reads=[dm_b], writes=[dm_b])
        P.op("dve", lambda en: en.tensor_copy(out=dhf[:], in_=dhi[:]), reads=[dm_b], writes=[dm_b])
        P.op("dve", lambda en: en.tensor_tensor(out=dhf[:], in0=dm[:], in1=dhf[:], op=ALU.subtract), reads=[dm_b], writes=[dm_b])
        P.op("dve", lambda en: en.tensor_copy(out=dlo[:], in_=dhf[:]), reads=[dm_b], writes=[dm_b])
        P.op("dve", lambda en: en.tensor_copy(out=dm[:], in_=dlo[:]), reads=[dm_b], writes=[dm_b])
        P.op("dve", lambda en: en.tensor_tensor(out=dhf[:], in0=dhf[:], in1=dm[:], op=ALU.subtract), reads=[dm_b], writes=[dm_b])
        P.op("dve", lambda en: en.tensor_copy(out=dl2[:], in_=dhf[:]), reads=[dm_b], writes=[dm_b])
        for w, src in enumerate((dhi, dlo, dl2)):
            for tt in range(8):
                P.op("pe", lambda en: en.transpose(pbh[0:32, tt * 128:(tt + 1) * 128], src[:, 8 + tt, :], ident_bf[:]),
                     reads=[dm_b, b_const], writes=[pbh_b])
            P.op("act", lambda en: en.activation(out=nqT[:, w, :], in_=pbh[0:32, :], func=AF.Copy),
                 reads=[], writes=[pbh_b, nqT_b])
        P.dma("sp", s_nq.ap().rearrange("w h t -> h w t"), nqT[:], reads=[nqT_b], writes=[db("nq")])
        P.op("dve", lambda en: en.tensor_scalar(out=ncum[:, 0:8, :], in0=ncum[:, 0:8, :], scalar1=flg[:, 2:3], scalar2=None, op0=ALU.add),
             reads=[cum_b, dm_b, b_flg], writes=[cum_b])
        mneg = sa.sb("mneg", [128, 128], BF16)
        mneg_b = Buf("mneg")
        P.op("dve", lambda en: en.tensor_scalar(out=mneg[:], in0=triu_f, scalar1=30000.0, scalar2=-30000.0, op0=ALU.mult, op1=ALU.add),
             reads=[b_cst], writes=[mneg_b])
        kaug = [[sa.sb(f"kaug{i}{hh}", [128, T], BF16) for hh in range(2)] for i in range(2)]
        qaug = [[sa.sb(f"qaug{i}{hh}", [128, TO], BF16) for hh in range(2)] for i in range(2)]
        vm = [sa.sb(f"vm{i}", [128, 16, 128], BF16) for i in range(2)]
        m_b = [Buf(), Buf()]
        NBF = 4
        LBF = [0, 1, 4, 5]
        Bm = [sa.sb(f"Bm{i}", [128, 4], F32) for i in range(NBF)]
        Bm_b = [Buf() for _ in range(NBF)]
        pbf = [sa.sb(f"pbf{i}", [128, 512], BF16) for i in range(NBF)]
        pbf_b = [Buf() for _ in range(NBF)]
        rcp = sa.sb("rcp", [128, 512], F32)
        rcp_b = Buf()
        yst = [sa.sb(f"yst{i}", [128, 512], BF16) for i in range(2)]
        yst_b = [Buf(), Buf()]
        cnt = 0
        yc = 0
        for m in range(E["cfg"].get("fox_pairs", 16)):
            mk = m % 2
            for hh in range(2):
                h = 2 * m + hh
                own = slice(hh * 64, (hh + 1) * 64)
                oth = slice((1 - hh) * 64, (2 - hh) * 64)
                o0 = (1 - hh) * 64
                P.op("dve", lambda en: en.memset(kaug[mk][hh][oth, :], 0.0), writes=[m_b[mk]])
                P.op("dve", lambda en: en.memset(kaug[mk][hh][o0:o0 + 3, :], 1.0), writes=[m_b[mk]])
                P.op("dve", lambda en: en.memset(qaug[mk][hh][oth, :], 0.0), writes=[m_b[mk]])
                P.dma("sp", kaug[mk][hh][own, :], s_kT.ap()[h * 64:(h + 1) * 64, :], reads=[db("kT")], writes=[m_b[mk]])
                P.dma("sp", qaug[mk][hh][own, :], s_qT.ap()[h * 64:(h + 1) * 64, :], reads=[db("qT")], writes=[m_b[mk]])
                P.dma("sp", qaug[mk][hh][o0:o0 + 3, :], s_nq.ap()[:, h, :], reads=[db("nq")], writes=[m_b[mk]])
            P.dma("sp", vm[mk][:], s_vtok.ap()[:, m * 128:(m + 1) * 128].rearrange("(tt p) c -> p tt c", p=128), reads=[db("vtok")], writes=[m_b[mk]])
            for Gl in range(2):
                G = 2 + Gl
                jmax = 4 * G + 3
                items = [(hh, j) for hh in range(2) for j in range(jmax + 1)]

                def f_logits(k):
                    hh, j = items[k]
                    L = LBF[k % NBF]
                    i_lo = max(4 * G, j)
                    col0 = (i_lo - 4 * G) * 128
                    P.op("pe", lambda en: en.matmul(pb[L][:, col0:512], lhsT=kaug[mk][hh][:, j * 128:(j + 1) * 128],
                                                    rhs=qaug[mk][hh][:, 4 * Gl * 128 + col0:(4 * Gl + 4) * 128], start=True, stop=(j < 4 * G)),
                         reads=[m_b[mk]], writes=[pb_b[L]])
                    if j >= 4 * G:
                        P.op("pe", lambda en: en.matmul(pb[L][:, col0:col0 + 128], lhsT=ident_bf[:], rhs=mneg[:], start=False, stop=True),
                             reads=[b_const, mneg_b], writes=[pb_b[L]])

                def f_post(k):
                    hh, j = items[k]
                    h = 2 * m + hh
                    L = k % NBF
                    BK = LBF[L]
                    i_lo = max(4 * G, j)
                    P.op("dve", lambda en: en.tensor_scalar(out=Bm[L][:, 0:1], in0=Cb[:, 4 * G + 3, h:h + 1], scalar1=-1.0, scalar2=ncum[:, j, h:h + 1],
                                                            op0=ALU.mult, op1=ALU.add),
                         reads=[cum_b], writes=[Bm_b[L]])
                    cs = slice((i_lo - 4 * G) * 128, 512)
                    P.op("act", lambda en: en.activation(out=pbf[L][:, cs], in_=pb[BK][:, cs], func=AF.Exp, scale=0.125,
                                                         bias=Bm[L][:, 0:1]),
                         reads=[Bm_b[L]], writes=[pb_b[BK], pbf_b[L]])

                def f_pv(k):
                    hh, j = items[k]
                    pbs = hh * 64
                    L = k % NBF
                    i_lo = max(4 * G, j)
                    col0 = (i_lo - 4 * G) * 128
                    P.op("pe", lambda en: en.matmul(pb[2][pbs:pbs + 64, col0:512], lhsT=vm[mk][:, j, hh * 64:(hh + 1) * 64], rhs=pbf[L][:, col0:512],
                                                    start=(j == 0), stop=(j == jmax)),
                         reads=[m_b[mk], pbf_b[L]], writes=[pb_b[2]])
                    P.op("pe", lambda en: en.matmul(pb[3][pbs:pbs + 64, col0:512], lhsT=ones_bf[:, 0:64], rhs=pbf[L][:, col0:512],
                                                    start=(j == 0), stop=(j == jmax)),
                         reads=[b_const, pbf_b[L]], writes=[pb_b[3]])

                for k0 in range(NBF - 1):
                    f_logits(k0)
                for k in range(len(items)):
                    if k + NBF - 1 < len(items):
                        f_logits(k + NBF - 1)
                    f_post(k)
                    f_pv(k)
                yk = yc % 2
                yc += 1
                P.op("dve", lambda en: en.reciprocal(out=rcp[:], in_=pb[3][:]), reads=[], writes=[pb_b[3], rcp_b])
                P.op("dve", lambda en: en.tensor_tensor(out=yst[yk][:], in0=pb[2][:], in1=rcp[:], op=ALU.mult),
                     reads=[rcp_b], writes=[pb_b[2], yst_b[yk]])
                P.dma("sp", yT.ap()[m * 128:(m + 1) * 128, Gl * 512:(Gl + 1) * 512], yst[yk][:], reads=[yst_b[yk]], writes=[db("yT", 0)])


def build(cfg=None):
    cfg = cfg or {}
    layers = cfg.get("layers", list(range(DEPTH)))
    nc = bass.Bass("TRN2", target_bir_lowering=False)

    def din(name, shape):
        return nc.dram_tensor(name, list(shape), F32, kind="ExternalInput")
    x_in = din("x", [TO, D])
    p_in = din("p", [DEPTH, TO, 256])
    flg_in = din("flg", [128, 4])
    Wd = {n: din(n, s) for n, s in WEIGHTS}
    sp_in = din("sp", [128, NSP])
    cst_in = din("cst", [128, 512])
    oh_in = din("oh", [33, XA + XB])
    out_d = nc.dram_tensor("out", [TO, D], F32, kind="ExternalOutput")
    dbg = {}
    for name, shape in cfg.get("dumps", []):
        dbg[name] = nc.dram_tensor("dbg_" + name, list(shape), F32, kind="ExternalOutput")

    def scr(name, shape, dt):
        if name in cfg.get("expose", ()):
            return nc.dram_tensor(name, list(shape), dt, kind="ExternalOutput")
        return nc.dram_tensor(name, list(shape), dt)
    xT = scr("xT", [D, TO], F32)
    yT = scr("yT", [D, TO], BF16)
    s_kvT = scr("s_kvT", [256, T], BF16)
    s_kvtok = scr("s_kvtok", [T, 256], BF16)
    s_kidxT = scr("s_kidxT", [64, T], BF16)
    s_widx = scr("s_widx", [TO, 16], F32)
    s_qiT = scr("s_qiT", [1024, TO], BF16)
    s_qaT = scr("s_qaT", [4096, TO], BF16)
    s_qbT = scr("s_qbT", [1024, TO], BF16)
    s_kdupT = scr("s_kdupT", [256, T], BF16)
    s_vbtok = scr("s_vbtok", [T, 128], BF16)
    s_qT = scr("s_qT", [2048, TO], BF16)
    s_kT = scr("s_kT", [2048, T], BF16)
    s_vtok = scr("s_vtok", [T, 2048], BF16)
    s_lf = scr("s_lf", [T, 32], F32)
    s_vrow = scr("s_vrow", [16, XA + XB], F32)
    s_nq = scr("s_nq", [3, 32, TO], BF16)
    s_expA = scr("s_expA", [128, 16 * 9 * 128], BF16)
    s_expB = scr("s_expB", [128, 16 * 2 * 128], BF16)
    kpack_e = scr("kpack_e", [960, 1024], BF16)
    gk_e = scr("gk_e", [1920, 1024], BF16)
    kpack_o = [scr(f"kpack_o{i}", [1024, 1024], BF16) for i in range(4)]
    gk_o = [scr(f"gk_o{i}", [2048, 1024], BF16) for i in range(4)]
    lfp = scr("lfp", [1024, 32], F32)
    glf = scr("glf", [2048, 32], F32)
    hx_in = scr("hx_in", [128, 32], F32)
    hxg = scr("hxg", [256, 32], F32)
    dbufs = {}

    def db(*key):
        if key not in dbufs:
            dbufs[key] = Buf(str(key))
        return dbufs[key]

    with ExitStack() as st:
        P = Prog(nc, st)

        def gsb(name, shape, dt):
            return st.enter_context(nc.sbuf_tensor(name, list(shape), dt))

        spk = gsb("spk", [128, NSP], F32)
        cst = gsb("cst_sb", [128, 512], F32)
        ident_bf = gsb("ident_bf", [128, 128], BF16)
        ones_bf = gsb("ones_bf", [128, 128], BF16)
        bd_bf = gsb("bd_bf", [128, 128], BF16)
        ones_f = gsb("ones_f", [128, 128], F32)
        eps_t = gsb("eps_t", [128, 1], F32)
        triu_bf = gsb("triu_bf", [128, 128], BF16)
        halo = gsb("halo", [128, 2], F32)
        flg = gsb("flg_sb", [128, 4], F32)
        b_flg = Buf("flg")
        ccs = P._newsem("ccs")
        cc_n = [0]
        WSLOT = 11008
        NWS = 2
        wbuf = [gsb(f"wbuf{i}", [128, WSLOT], BF16) for i in range(NWS)]
        wb_b = [Buf(f"wb{i}") for i in range(NWS)]
        wptr = [0]
        b_spk, b_cst, b_const, b_halo = Buf(), Buf(), Buf(), Buf()
        ident_f = cst[:, 0:128]
        jflip = cst[:, 128:256]
        triu_f = cst[:, 256:384]
        cneg = cst[:, 384:512]
        pb = [st.enter_context(nc.psum_tensor(f"pb{i}", [128, 512], F32)) for i in range(7)]
        pbh = st.enter_context(nc.psum_tensor("pbh", [128, 1024], BF16))
        pb_b = [Buf(f"pb{i}") for i in range(7)]
        pbh_b = Buf("pbh")

        P.dma("sp", spk[:], sp_in.ap(), writes=[b_spk])
        P.dma("sp", cst[:], cst_in.ap(), writes=[b_cst])
        P.dma("sp", flg[:], flg_in.ap(), writes=[b_flg])
        P.op("dve", lambda e: e.memset(ones_bf[:], 1.0), writes=[b_const])
        P.op("dve", lambda e: e.memset(ones_f[:], 1.0), writes=[b_const])
        P.op("dve", lambda e: e.memset(eps_t[:], EPS), writes=[b_const])
        P.op("dve", lambda e: e.memset(bd_bf[:], 0.0), writes=[b_const])
        P.op("dve", lambda e: e.memset(bd_bf[0:64, 0:64], 1.0), writes=[b_const])
        P.op("dve", lambda e: e.memset(bd_bf[64:128, 64:128], 1.0), writes=[b_const])
        P.op("dve", lambda e: e.tensor_copy(out=ident_bf[:], in_=ident_f), reads=[b_cst], writes=[b_const])
        P.op("dve", lambda e: e.tensor_copy(out=triu_bf[:], in_=triu_f), reads=[b_cst], writes=[b_const])
        P.op("dve", lambda e: e.memset(spk[32:33, SP_RELB:SP_RELB + 32], NEG), reads=[], writes=[b_spk])
        P.barrier()

        gemm_bank = [0]

        def cc_gather(src_t, dst_t, in_bufs, out_bufs):
            P._deps("pool", list(in_bufs), list(out_bufs))
            cc_n[0] += 1
            nc.gpsimd.collective_compute("AllGather", ALU.bypass, replica_groups=[[0, 4], [1, 5], [2, 6], [3, 7]],
                                         ins=[src_t.ap()], outs=[dst_t.ap()]).then_inc(ccs, 1)
            nc.gpsimd.wait_ge(ccs, cc_n[0])
            P.op("pool", lambda e: e.memset(halo[0:1, 0:1], 0.0), reads=list(in_bufs), writes=list(out_bufs))
        state = {}

        def wview(si, KC, ntot):
            return wbuf[si][:, 0:KC * ntot].rearrange("p (kc n) -> p kc n", n=ntot)

        def gemm(src, src_b, KC, wsrc, blocks, epi, ntc=2, tc_off=0, pre=None):
            for segs, chunks in blocks:
                si = wptr[0]
                wptr[0] = (wptr[0] + 1) % NWS
                ntot = sum(n for _, n in segs)
                wv = wview(si, KC, ntot)
                off = 0
                for (c0, ncols) in segs:
                    P.dma("pool", wv[:, :, off:off + ncols],
                          wsrc(c0, ncols).rearrange("(kc p) n -> p kc n", p=128), writes=[wb_b[si]])
                    off += ncols
                for (coff, m, tag, pbase) in chunks:
                    if pre is not None:
                        pre(wv, wb_b[si], coff, m, tag)
                    for tci in range(ntc):
                        bi = gemm_bank[0]
                        gemm_bank[0] = (gemm_bank[0] + 1) % 4
                        for kc in range(KC):
                            P.op("pe", lambda e: e.matmul(pb[bi][pbase:pbase + m, :], lhsT=wv[:, kc, coff:coff + m],
                                                          rhs=src[:, kc, tc_off + tci * TC:tc_off + (tci + 1) * TC],
                                                          start=(kc == 0), stop=(kc == KC - 1)),
                                 reads=[wb_b[si], src_b], writes=[pb_b[bi]])
                        epi(bi, m, tag, tci)

        def simple_blocks(col0, ncols_total, wcols, tagfn=None, m=128):
            blocks = []
            c = 0
            ci = 0
            while c < ncols_total:
                n = min(wcols, ncols_total - c)
                chunks = []
                o = 0
                while o < n:
                    mm = min(m, n - o)
                    chunks.append((o, mm, ci if tagfn is None else tagfn(ci), 0))
                    o += mm
                    ci += 1
                blocks.append(([(col0 + c, n)], chunks))
                c += n
            return blocks

        def rms_finish(S, src, src_b, C, lhsT_ones, gcol, inv_n, dst_fn, dst_bufs, ncols=TC, nparts=128):
            sq, sq_b, rstd, rstd_b = S["sq"], S["sq_b"], S["rstd"], S["rstd_b"]
            for c in range(C):
                P.op("act", lambda e: e.activation(out=sq[0:nparts, c, 0:ncols], in_=src(c), func=AF.Square),
                     reads=[src_b], writes=[sq_b])
            for c in range(C):
                P.op("pe", lambda e: e.matmul(pb[4][0:nparts, 0:ncols], lhsT=lhsT_ones[0:nparts, 0:nparts],
                                              rhs=sq[0:nparts, c, 0:ncols], start=(c == 0), stop=(c == C - 1)),
                     reads=[sq_b, b_const], writes=[pb_b[4]])
            P.op("act", lambda e: e.activation(out=rstd[0:nparts, 0:ncols], in_=pb[4][0:nparts, 0:ncols], func=AF.Sqrt,
                                               scale=inv_n, bias=eps_t[0:nparts, 0:1]),
                 reads=[b_const], writes=[pb_b[4], rstd_b])
            P.op("dve", lambda e: e.reciprocal(out=rstd[0:nparts, 0:ncols], in_=rstd[0:nparts, 0:ncols]),
                 reads=[rstd_b], writes=[rstd_b])
            for c in range(C):
                P.op("dve", lambda e: e.scalar_tensor_tensor(out=dst_fn(c), in0=src(c),
                                                             scalar=spk[0:nparts, gcol + c:gcol + c + 1],
                                                             in1=rstd[0:nparts, 0:ncols], op0=ALU.mult, op1=ALU.mult),
                     reads=[src_b, rstd_b, b_spk], writes=dst_bufs)

        def norm_x(S, hT, hT_b, gcol, t0):
            xs, xs_b = S["xs"], S["xs_b"]
            for tci in range(TT // TC):
                ta = t0 + tci * TC
                P.dma("sp", xs[:], xT.ap()[:, ta:ta + TC].rearrange("(kc p) t -> p kc t", p=128),
                      reads=[db("xT", ta // TC)], writes=[xs_b])
                rms_finish(S, lambda c: xs[:, c, :], xs_b, 16, ones_bf, gcol, 1.0 / D,
                           lambda c: hT[:, c, tci * TC:(tci + 1) * TC], [hT_b])

        def resid_epi(S, t0):
            def epi(bi, m, tag, tci):
                ta = t0 + tci * TC
                k = S["xr_i"][0]
                S["xr_i"][0] = (k + 1) % 2
                xr, xr_b = S["xr"][k], S["xr_b"][k]
                P.dma("sp", xr[:], xT.ap()[tag * 128:(tag + 1) * 128, ta:ta + TC],
                      reads=[db("xT", ta // TC)], writes=[xr_b])
                P.op("dve", lambda e: e.tensor_tensor(out=xr[:], in0=pb[bi][:], in1=xr[:], op=ALU.add),
                     reads=[xr_b], writes=[pb_b[bi], xr_b])
                P.dma("sp", xT.ap()[tag * 128:(tag + 1) * 128, ta:ta + TC], xr[:],
                      reads=[xr_b], writes=[db("xT", ta // TC)])
            return epi

        def dump(name, src_ap_dram):
            pass

        with Scope(P) as sc:
            xin = [sc.sb(f"xin{i}", [128, D], F32) for i in range(2)]
            xin_b = [Buf(), Buf()]
            stg = [sc.sb(f"xstg{i}", [128, 16, 128], F32) for i in range(2)]
            stg_b = [Buf(), Buf()]
            for tt in range(TO // 128):
                k = tt % 2
                P.dma("sp", xin[k][:], x_in.ap()[tt * 128:(tt + 1) * 128, :], writes=[xin_b[k]])
                for g in range(4):
                    bi = 5 + (g % 2)
                    for j in range(4):
                        fc = g * 4 + j
                        P.op("pe", lambda e: e.transpose(pb[bi][:, j * 128:(j + 1) * 128], xin[k][:, fc * 128:(fc + 1) * 128], ident_f),
                             reads=[xin_b[k], b_cst], writes=[pb_b[bi]])
                    P.op("act", lambda e: e.activation(out=stg[k][:, g * 4:(g + 1) * 4, :], in_=pb[bi][:].rearrange("p (a b) -> p a b", b=128), func=AF.Copy),
                         reads=[], writes=[pb_b[bi], stg_b[k]])
                P.dma("sp", xT.ap()[:, tt * 128:(tt + 1) * 128].rearrange("(fc p) t -> p fc t", p=128), stg[k][:],
                      reads=[stg_b[k]], writes=[db("xT", tt // 4)])

        for l in layers:
            E = dict(locals())
            E['state'] = state
            if "attn" in cfg.get("parts", ("attn", "out", "ffn", "ple")):
                if l % 2 == 0:
                    even_attention(E, l)
                else:
                    odd_attention(E, l)
            token_local(E, l, cfg.get("parts", ("attn", "out", "ffn", "ple")))

        with Scope(P) as sc:
            xo = [sc.sb(f"xo{i}", [128, 16, 128], F32) for i in range(2)]
            xo_b = [Buf(), Buf()]
            ostg = [sc.sb(f"ostg{i}", [128, D], F32) for i in range(2)]
            ostg_b = [Buf(), Buf()]
            for tt in range(TO // 128):
                k = tt % 2
                P.dma("sp", xo[k][:], xT.ap()[:, tt * 128:(tt + 1) * 128].rearrange("(fc p) t -> p fc t", p=128),
                      reads=[db("xT", tt // 4)], writes=[xo_b[k]])
                for g in range(4):
                    bi = 5 + (g % 2)
                    for j in range(4):
                        fc = g * 4 + j
                        P.op("pe", lambda e: e.transpose(pb[bi][:, j * 128:(j + 1) * 128], xo[k][:, fc, :], ident_f),
                             reads=[xo_b[k], b_cst], writes=[pb_b[bi]])
                    P.op("act", lambda e: e.activation(out=ostg[k][:, g * 512:(g + 1) * 512], in_=pb[bi][:], func=AF.Copy),
                         reads=[], writes=[pb_b[bi], ostg_b[k]])
                P.dma("sp", out_d.ap()[tt * 128:(tt + 1) * 128, :], ostg[k][:], reads=[ostg_b[k]], writes=[db("out")])
        P.barrier()
    return nc


_NC_CACHE = {}


def make_in_maps(inp, batches):
    cst, oh = host_consts()
    sp = pack_small(inp)
    wmap = {n: np.ascontiguousarray(inp[n], dtype=np.float32) for n, _ in WEIGHTS}
    in_maps = []
    for c in range(8):
        b = batches[c]
        half = c // 4
        flg = np.zeros((128, 4), np.float32)
        flg[:, 0] = float(half)
        flg[:, 1] = 0.0 if half else -1e30
        flg[:, 2] = 0.0 if half else NEG
        m = dict(x=np.ascontiguousarray(inp["x"][b, half * TO:(half + 1) * TO], dtype=np.float32),
                 p=np.ascontiguousarray(inp["p"][:, b, half * TO:(half + 1) * TO], dtype=np.float32),
                 flg=flg, sp=sp, cst=cst, oh=oh)
        m.update(wmap)
        in_maps.append(m)
    return in_maps


def kernel(**inputs):
    inp = {k: np.asarray(v) for k, v in inputs.items()}
    if "nc" not in _NC_CACHE:
        _NC_CACHE["nc"] = build()
    nc = _NC_CACHE["nc"]
    in_maps = make_in_maps(inp, [0, 1, 2, 3, 0, 1, 2, 3])
    res = run_bass_kernel_spmd(nc, in_maps, core_ids=list(range(8)))
    out = np.stack([np.concatenate([res.results[b]["out"], res.results[b + 4]["out"]], axis=0) for b in range(4)], axis=0)
    return out.astype(np.float32)
```

```python
import math
from contextlib import ExitStack
import numpy as np
import concourse.bass as bass
import concourse.mybir as mybir
from concourse.bass_utils import run_bass_kernel_spmd

F32 = mybir.dt.float32
BF16 = mybir.dt.bfloat16
ALU = mybir.AluOpType
AF = mybir.ActivationFunctionType

EPOCH = 30000
NDSEM = 40

D = 2048
T = 2048
DEPTH = 4
TT = 1024
TO = 1024
TC = 512
DFF = 5504
NFC = 43
EPS = 1e-6
XA = 1280
XB = 384
NEG = -30000.0

SP_ATTN = 0
SP_FFN = 64
SP_PLE = 128
SP_CONV = 192
SP_CQ = 1224
SP_CKV = 1232
SP_AQ = 1236
SP_BQ = 1240
SP_BK = 1242
SP_CQN = 1244
SP_CKN = 1246
SP_FB = 1248
SP_SINK = 1250
SP_RELB = 1282
SP_B31 = 1320
NSP = 1340

WEIGHTS = [
    ("w_in_even", (2, 2048, 2128)), ("a_w_uq", (2, 512, 4096)), ("a_w_qidx", (2, 512, 1024)),
    ("a_w_uv", (2, 16, 256, 64)), ("w_out_even", (2, 2048, 2048)), ("w_in_odd", (2, 2048, 6176)),
    ("w_out_odd", (2, 2048, 2048)), ("w_up", (4, 2048, 11008)), ("w_down", (4, 5504, 2048)),
    ("w_ple_gate", (4, 2048, 2048)), ("w_ple_proj", (4, 256, 2048)),
]


class Buf:
    __slots__ = ("name", "w", "r", "rd")

    def __init__(self, name=""):
        self.name = name
        self.w = None
        self.r = {}
        self.rd = []


class Prog:
    ENGS = ("pe", "act", "dve", "pool", "sp")

    def __init__(self, nc, stack):
        self.nc = nc
        self.stack = stack
        self.eng = {"pe": nc.tensor, "act": nc.scalar, "dve": nc.vector,
                    "pool": nc.gpsimd, "sp": nc.sync}
        self.cnt = {e: 0 for e in self.ENGS}
        self.esems = {e: [] for e in self.ENGS}
        self.seen_e = {e: {p: 0 for p in self.ENGS} for e in self.ENGS}
        self.seen_d = {e: {} for e in self.ENGS}
        self.dsem = {}
        for q in ("sp", "pool"):
            self.dsem[q] = [[self._newsem(f"d{q}{i}"), 0] for i in range(NDSEM)]
        self.dptr = {"sp": 0, "pool": 0}
        self.bar_sem = self._newsem("bar")
        self.bar_cnt = 0
        self.n_inst = 0

    def _newsem(self, name):
        return self.stack.enter_context(self.nc.semaphore(name))

    def _esem(self, e, idx):
        ep = (idx - 1) // EPOCH
        while len(self.esems[e]) <= ep:
            self.esems[e].append(self._newsem(f"e{e}{len(self.esems[e])}"))
        return self.esems[e][ep], (idx - 1) % EPOCH + 1

    def _wait(self, e, ev):
        if ev is None:
            return
        if ev[0] == "e":
            _, p, idx = ev
            if p == e and e == "pe":
                return
            if self.seen_e[e][p] >= idx:
                return
            self.seen_e[e][p] = idx
            s, v = self._esem(p, idx)
            self.eng[e].wait_ge(s, v)
        else:
            _, s, v, key = ev
            if self.seen_d[e].get(key, 0) >= v:
                return
            self.seen_d[e][key] = v
            self.eng[e].wait_ge(s, v)
        self.n_inst += 1

    def _deps(self, e, reads, writes):
        for b in reads:
            self._wait(e, b.w)
        for b in writes:
            self._wait(e, b.w)
            for p, idx in b.r.items():
                if p != e:
                    self._wait(e, ("e", p, idx))
            for ev in b.rd:
                self._wait(e, ev)

    def _mark(self, ev, reads, writes):
        for b in reads:
            if ev[0] == "e":
                b.r[ev[1]] = ev[2]
            else:
                b.rd.append(ev)
        for b in writes:
            b.w = ev
            b.r = {}
            b.rd = []

    def op(self, e, fn, reads=(), writes=()):
        self._deps(e, reads, writes)
        inst = fn(self.eng[e])
        self.cnt[e] += 1
        idx = self.cnt[e]
        s, _ = self._esem(e, idx)
        inst.then_inc(s, 1)
        self.n_inst += 1
        self._mark(("e", e, idx), reads, writes)

    def dma(self, q, out, in_, reads=(), writes=(), **kw):
        self._deps(q, reads, writes)
        slot = self.dsem[q][self.dptr[q]]
        key = (q, self.dptr[q])
        self.dptr[q] = (self.dptr[q] + 1) % NDSEM
        if slot[1] > 0:
            self._wait(q, ("d", slot[0], slot[1], key))
        inst = self.eng[q].dma_start(out=out, in_=in_, **kw)
        slot[1] += 16
        inst.then_inc(slot[0], 16)
        self.n_inst += 1
        self._mark(("d", slot[0], slot[1], key), reads, writes)

    def barrier(self):
        for p in self.ENGS:
            if p != "sp" and self.cnt[p] > 0:
                self._wait("sp", ("e", p, self.cnt[p]))
        for q in ("sp", "pool"):
            for i, slot in enumerate(self.dsem[q]):
                if slot[1] > 0:
                    self._wait("sp", ("d", slot[0], slot[1], (q, i)))
        self.bar_cnt += 1
        self.eng["sp"].sem_inc(self.bar_sem, 1)
        for e in self.ENGS:
            if e != "sp":
                self.eng[e].wait_ge(self.bar_sem, self.bar_cnt)
                for p in self.ENGS:
                    self.seen_e[e][p] = self.cnt[p]
                for q in ("sp", "pool"):
                    for i, slot in enumerate(self.dsem[q]):
                        self.seen_d[e][(q, i)] = slot[1]
        self.n_inst += 6


class Scope:
    def __init__(self, P):
        self.P = P
        self.st = ExitStack()

    def __enter__(self):
        self.st.__enter__()
        return self

    _uid = [0]

    def sb(self, name, shape, dt):
        Scope._uid[0] += 1
        return self.st.enter_context(self.P.nc.sbuf_tensor(f"{name}_{Scope._uid[0]}", list(shape), dt))

    def __exit__(self, *a):
        self.P.barrier()
        return self.st.__exit__(*a)


def rel_bucket_np(n):
    n = np.maximum(n, 0)
    exact = 16
    nf = np.maximum(n, 1).astype(np.float32)
    large = exact + (np.log(nf / np.float32(exact)) / np.float32(math.log(1024 / exact))
                     * np.float32(32 - exact)).astype(np.int32)
    large = np.minimum(large, 31)
    return np.where(n < exact, n, large)


def host_consts():
    cst = np.zeros((128, 512), np.float32)
    cst[:, 0:128] = np.eye(128)
    cst[:, 128:256] = np.eye(128)[::-1]
    i = np.arange(128)
    cst[:, 256:384] = (i[None, :] >= i[:, None]).astype(np.float32)
    cst[:, 384:512] = np.where(i[None, :] <= i[:, None], 0.0, -1e30)
    oh = np.zeros((33, XA + XB), np.float32)
    y = np.arange(XA)
    xx = y - 127
    b = np.where(xx < 0, 32, rel_bucket_np(xx))
    oh[b, y] = 1.0
    y = np.arange(XB)
    xx = y - 127
    b = np.where((xx < 0) | (xx >= 128), 32, rel_bucket_np(xx))
    oh[b, XA + y] = 1.0
    return cst, oh


def pack_small(inp):
    sp = np.zeros((128, NSP), np.float32)

    def fm(v):
        return np.ascontiguousarray(v.reshape(-1, 128).T)
    for l in range(4):
        sp[:, SP_ATTN + l * 16:SP_ATTN + (l + 1) * 16] = fm(inp["attn_norm"][l])
        sp[:, SP_FFN + l * 16:SP_FFN + (l + 1) * 16] = fm(inp["ffn_norm"][l])
        sp[:, SP_PLE + l * 16:SP_PLE + (l + 1) * 16] = fm(inp["ple_norm"][l])
        for k in range(3):
            c0 = SP_CONV + (l * 3 + k) * 86
            sp[:, c0:c0 + 86] = fm(inp["ffn_conv"][l, k])
    for e in range(2):
        sp[:, SP_CQ + e * 4:SP_CQ + e * 4 + 4] = fm(inp["a_cq_norm"][e])
        sp[:, SP_CKV + e * 2:SP_CKV + e * 2 + 2] = fm(inp["a_ckv_norm"][e])
        sp[:, SP_AQ + e * 2:SP_AQ + e * 2 + 2] = fm(inp["a_q_norm"][e])
        sp[:, SP_BQ + e] = np.tile(inp["b_q_norm"][e], 2)
        sp[:, SP_BK + e] = np.tile(inp["b_k_norm"][e], 2)
        sp[:, SP_CQN + e] = np.tile(inp["c_q_norm"][e], 2)
        sp[:, SP_CKN + e] = np.tile(inp["c_k_norm"][e], 2)
        sp[0:32, SP_FB + e] = inp["c_forget_bias"][e]
        sp[:, SP_SINK + e * 16:SP_SINK + (e + 1) * 16] = inp["b_sinks"][e][None, :]
    sp[0:32, SP_RELB:SP_RELB + 32] = inp["rel_bias"]
    sp[:, SP_B31:SP_B31 + 16] = inp["rel_bias"][31, 0:16][None, :]
    return sp


def token_local(E, l, parts):
    P, nc, Wd, db, gemm, simple_blocks, rms_finish, norm_x, resid_epi = (E[k] for k in (
        "P", "nc", "Wd", "db", "gemm", "simple_blocks", "rms_finish", "norm_x", "resid_epi"))
    xT, yT, spk, b_spk, pb, pb_b, halo, b_halo, p_in, ident_f, b_cst, wbuf, wb_b, wptr, wview = (E[k] for k in (
        "xT", "yT", "spk", "b_spk", "pb", "pb_b", "halo", "b_halo", "p_in", "ident_f", "b_cst", "wbuf", "wb_b", "wptr", "wview"))
    NWS = len(wbuf)
    flg, b_flg, cc_gather, hx_in, hxg, ones_bf = (E[k] for k in ("flg", "b_flg", "cc_gather", "hx_in", "hxg", "ones_bf"))
    for ps in range(1):
        t0 = 0
        with Scope(P) as so:
            hT = so.sb("hT", [128, 16, TT], BF16)
            hT_b = Buf("hT")
            hh = so.sb("hh", [128, 16, 2], BF16)
            hh_b = Buf("hh")

            def norm_scope(gcol, with_halo=False):
                with Scope(P) as sn:
                    S = dict(xs=sn.sb("xs", [128, 16, TC], F32), xs_b=Buf(), sq=sn.sb("sq", [128, 16, TC], BF16),
                             sq_b=Buf(), rstd=sn.sb("rstd", [128, TC], F32), rstd_b=Buf())
                    norm_x(S, hT, hT_b, gcol, t0)
                    if with_halo:
                        hxo = sn.sb("hxo", [128, 16, 2], F32)
                        hxo_b = Buf("hxo")
                        P.dma("sp", hxo[:], hxg.ap()[0:128, :].rearrange("p (kc t) -> p kc t", t=2), reads=[db("hxg")], writes=[hxo_b])
                        rms_finish(S, lambda c: hxo[:, c, :], hxo_b, 16, ones_bf, gcol, 1.0 / D,
                                   lambda c: hh[:, c, :], [hh_b], ncols=2)

            def mk_xr(sc):
                return dict(xr=[sc.sb(f"xr{i}", [128, TC], F32) for i in range(2)], xr_b=[Buf(), Buf()], xr_i=[0])

            if "out" in parts:
                with Scope(P) as s1:
                    S = mk_xr(s1)
                    P.dma("sp", hT[:], yT.ap()[:, t0:t0 + TT].rearrange("(kc p) t -> p kc t", p=128),
                          reads=[db("yT", 0)], writes=[hT_b])
                    wn = "w_out_even" if l % 2 == 0 else "w_out_odd"
                    gemm(hT, hT_b, 16, lambda c0, n: Wd[wn].ap()[l // 2, :, c0:c0 + n],
                         simple_blocks(0, D, 512), resid_epi(S, t0))
            if "ffn" in parts:
                with Scope(P) as sh:
                    hxs = sh.sb("hxs", [128, 16, 2], F32)
                    hxs_b = Buf("hxs")
                    P.dma("sp", hxs[:], xT.ap()[:, TO - 2:TO].rearrange("(kc p) t -> p kc t", p=128), reads=[db("xT", 1)], writes=[hxs_b])
                    P.dma("sp", hx_in.ap().rearrange("p (kc t) -> p kc t", t=2), hxs[:], reads=[hxs_b], writes=[db("hx_in")])
                cc_gather(hx_in, hxg, [db("hx_in")], [db("hxg")])
                norm_scope(SP_FFN + l * 16, with_halo=True)
                with Scope(P) as s2:
                    S = mk_xr(s2)
                    act = s2.sb("act", [128, NFC, TT], BF16)
                    act_b = Buf("act")
                    stg = {"g": s2.sb("sg", [128, TT + 2], F32), "u": s2.sb("su", [128, TT + 2], F32)}
                    stg_b = {"g": Buf("sg"), "u": Buf("su")}
                    cv = {"g": s2.sb("ga", [128, TT], F32), "u": s2.sb("ua", [128, TT], F32)}
                    cv_b = {"g": Buf("ga"), "u": Buf("ua")}

                    def up_pre(wv, wvb, coff, m, tag):
                        kind = tag[0]
                        for kc in range(16):
                            P.op("pe", lambda e: e.matmul(pb[6][:, 0:2], lhsT=wv[:, kc, coff:coff + m], rhs=hh[:, kc, :],
                                                          start=(kc == 0), stop=(kc == 15)),
                                 reads=[wvb, hh_b], writes=[pb_b[6]])
                        P.op("act", lambda e: e.activation(out=stg[kind][:, 0:2], in_=pb[6][:, 0:2], func=AF.Copy, scale=flg[:, 0:1]),
                             reads=[b_flg], writes=[pb_b[6], stg_b[kind]])

                    def conv_finish(kind, i):
                        c = i if kind == "g" else NFC + i
                        s_, sb_, a_, ab_ = stg[kind], stg_b[kind], cv[kind], cv_b[kind]
                        wc = [SP_CONV + (l * 3 + k) * 86 + c for k in range(3)]
                        P.op("act", lambda e: e.activation(out=a_[:], in_=s_[:, 2:TT + 2], func=AF.Copy,
                                                           scale=spk[:, wc[2]:wc[2] + 1]),
                             reads=[sb_, b_spk], writes=[ab_])
                        P.op("dve", lambda e: e.scalar_tensor_tensor(out=a_[:], in0=s_[:, 1:TT + 1], scalar=spk[:, wc[1]:wc[1] + 1],
                                                                     in1=a_[:], op0=ALU.mult, op1=ALU.add),
                             reads=[sb_, ab_, b_spk], writes=[ab_])
                        P.op("dve", lambda e: e.scalar_tensor_tensor(out=a_[:], in0=s_[:, 0:TT], scalar=spk[:, wc[0]:wc[0] + 1],
                                                                     in1=a_[:], op0=ALU.mult, op1=ALU.add),
                             reads=[sb_, ab_, b_spk], writes=[ab_])

                    def up_epi(bi, m, tag, tci):
                        kind, i = tag
                        c = i if kind == "g" else NFC + i
                        P.op("act", lambda e: e.activation(out=stg[kind][:, 2 + tci * TC:2 + (tci + 1) * TC], in_=pb[bi][:], func=AF.Copy),
                             reads=[], writes=[pb_b[bi], stg_b[kind]])
                        if tci == TT // TC - 1:
                            conv_finish(kind, i)
                            if kind == "u":
                                P.op("act", lambda e: e.activation(out=cv["g"][:], in_=cv["g"][:], func=AF.Silu),
                                     reads=[cv_b["g"]], writes=[cv_b["g"]])
                                P.op("dve", lambda e: e.tensor_tensor(out=act[:, i, :], in0=cv["g"][:], in1=cv["u"][:], op=ALU.mult),
                                     reads=[cv_b["g"], cv_b["u"]], writes=[act_b])
                    blocks = []
                    for i0 in range(0, NFC, 2):
                        npair = min(2, NFC - i0)
                        w = npair * 128
                        segs = [(i0 * 128, w), (DFF + i0 * 128, w)]
                        chunks = []
                        for j in range(npair):
                            chunks.append((j * 128, 128, ("g", i0 + j), 0))
                            chunks.append((w + j * 128, 128, ("u", i0 + j), 0))
                        blocks.append((segs, chunks))
                    gemm(hT, hT_b, 16, lambda c0, n: Wd["w_up"].ap()[l, :, c0:c0 + n], blocks, up_epi, pre=up_pre)
                    gemm(act, act_b, NFC, lambda c0, n: Wd["w_down"].ap()[l, :, c0:c0 + n],
                         simple_blocks(0, D, 256), resid_epi(S, t0))
            if "ple" in parts:
                norm_scope(SP_PLE + l * 16)
                with Scope(P) as s3:
                    S = mk_xr(s3)
                    pT = s3.sb("pT", [128, 2, TT], BF16)
                    pT_b = Buf("pT")
                    pl = [s3.sb(f"pl{i}", [128, 256], F32) for i in range(2)]
                    pl_b = [Buf(), Buf()]
                    sg = [s3.sb(f"sgt{i}", [128, TC], F32) for i in range(2)]
                    sg_b = [Buf(), Buf()]
                    for tt in range(TT // 128):
                        k = tt % 2
                        P.dma("sp", pl[k][:], p_in.ap()[l, t0 + tt * 128:t0 + (tt + 1) * 128, :], writes=[pl_b[k]])
                        for cc in range(2):
                            P.op("pe", lambda e: e.transpose(pb[5][:, cc * 128:(cc + 1) * 128], pl[k][:, cc * 128:(cc + 1) * 128], ident_f),
                                 reads=[pl_b[k], b_cst], writes=[pb_b[5]])
                        P.op("act", lambda e: e.activation(out=pT[:, :, tt * 128:(tt + 1) * 128],
                                                           in_=pb[5][:, 0:256].rearrange("p (a b) -> p a b", b=128), func=AF.Copy),
                             reads=[], writes=[pb_b[5], pT_b])
                    cnt = 0
                    for nb in range(D // 512):
                        sa = wptr[0]
                        sbb = (wptr[0] + 1) % NWS
                        wa = wview(sa, 16, 512)
                        wp = wview(sbb, 2, 512)
                        P.dma("pool", wa, Wd["w_ple_gate"].ap()[l, :, nb * 512:(nb + 1) * 512].rearrange("(kc p) n -> p kc n", p=128),
                              writes=[wb_b[sa]])
                        P.dma("pool", wp, Wd["w_ple_proj"].ap()[l, :, nb * 512:(nb + 1) * 512].rearrange("(kc p) n -> p kc n", p=128),
                              writes=[wb_b[sbb]])
                        for ci in range(4):
                            nchunk = nb * 4 + ci
                            for tci in range(TT // TC):
                                ba = cnt % 2
                                bb = 2 + cnt % 2
                                kx = cnt % 2
                                cnt += 1
                                ta = t0 + tci * TC
                                for kc in range(16):
                                    P.op("pe", lambda e: e.matmul(pb[ba][:], lhsT=wa[:, kc, ci * 128:(ci + 1) * 128],
                                                                  rhs=hT[:, kc, tci * TC:(tci + 1) * TC], start=(kc == 0), stop=(kc == 15)),
                                         reads=[wb_b[sa], hT_b], writes=[pb_b[ba]])
                                for kc in range(2):
                                    P.op("pe", lambda e: e.matmul(pb[bb][:], lhsT=wp[:, kc, ci * 128:(ci + 1) * 128],
                                                                  rhs=pT[:, kc, tci * TC:(tci + 1) * TC], start=(kc == 0), stop=(kc == 1)),
                                         reads=[wb_b[sbb], pT_b], writes=[pb_b[bb]])
                                P.op("act", lambda e: e.activation(out=sg[kx][:], in_=pb[ba][:], func=AF.Sigmoid),
                                     reads=[], writes=[pb_b[ba], sg_b[kx]])
                                P.op("dve", lambda e: e.tensor_tensor(out=sg[kx][:], in0=sg[kx][:], in1=pb[bb][:], op=ALU.mult),
                                     reads=[sg_b[kx]], writes=[pb_b[bb], sg_b[kx]])
                                xr, xr_b = S["xr"][kx], S["xr_b"][kx]
                                P.dma("sp", xr[:], xT.ap()[nchunk * 128:(nchunk + 1) * 128, ta:ta + TC],
                                      reads=[db("xT", ta // TC)], writes=[xr_b])
                                P.op("dve", lambda e: e.tensor_tensor(out=xr[:], in0=sg[kx][:], in1=xr[:], op=ALU.add),
                                     reads=[sg_b[kx], xr_b], writes=[xr_b])
                                P.dma("sp", xT.ap()[nchunk * 128:(nchunk + 1) * 128, ta:ta + TC], xr[:],
                                      reads=[xr_b], writes=[db("xT", ta // TC)])
                        wptr[0] = (wptr[0] + 2) % NWS


def even_attention(E, l):
    e = l // 2
    P, nc, Wd, db, gemm, simple_blocks, norm_x = (E[k] for k in ("P", "nc", "Wd", "db", "gemm", "simple_blocks", "norm_x"))
    spk, b_spk, pb, pb_b, pbh, pbh_b, ident_f, ident_bf, b_cst, b_const, ones_bf, bd_bf, jflip, cneg, eps_t = (E[k] for k in (
        "spk", "b_spk", "pb", "pb_b", "pbh", "pbh_b", "ident_f", "ident_bf", "b_cst", "b_const", "ones_bf", "bd_bf", "jflip", "cneg", "eps_t"))
    xT, yT, oh_in = E["xT"], E["yT"], E["oh_in"]
    s_kvT, s_kvtok, s_kidxT, s_widx, s_qiT, s_qaT, s_qbT, s_kdupT, s_vbtok, s_vrow = (E[k] for k in (
        "s_kvT", "s_kvtok", "s_kidxT", "s_widx", "s_qiT", "s_qaT", "s_qbT", "s_kdupT", "s_vbtok", "s_vrow"))
    XT = XA + XB
    flg, b_flg, cc_gather, kpack_e, gk_e = (E[k] for k in ("flg", "b_flg", "cc_gather", "kpack_e", "gk_e"))

    if not E["state"].get("vrow"):
        E["state"]["vrow"] = True
        with Scope(P) as sv:
            ohs = sv.sb("ohs", [33, XT], F32)
            ohs_b = Buf()
            vr = sv.sb("vr", [16, XT], F32)
            vr_b = Buf()
            P.dma("sp", ohs[:], oh_in.ap(), writes=[ohs_b])
            for (hc, x0, x1) in [(0, 0, 512), (0, 512, 1024), (0, 1024, XA), (16, XA, XT)]:
                P.op("pe", lambda en: en.matmul(pb[5][0:16, 0:x1 - x0], lhsT=spk[0:33, SP_RELB + hc:SP_RELB + hc + 16],
                                                rhs=ohs[:, x0:x1], start=True, stop=True),
                     reads=[ohs_b, b_spk], writes=[pb_b[5]])
                P.op("act", lambda en: en.activation(out=vr[:, x0:x1], in_=pb[5][0:16, 0:x1 - x0], func=AF.Copy),
                     reads=[], writes=[pb_b[5], vr_b])
            P.dma("sp", s_vrow.ap(), vr[:], reads=[vr_b], writes=[db("vrow")])

    def rms_grp(S, srcs, src_b, lhsT_ones, gcol, inv_n, dsts, dst_bufs, ncols=TC):
        C = len(srcs)
        sq, sq_b, rstd, rstd_b = S["sq"], S["sq_b"], S["rstd"], S["rstd_b"]
        for c in range(C):
            P.op("act", lambda en: en.activation(out=sq[:, c, 0:ncols], in_=srcs[c], func=AF.Square),
                 reads=[src_b], writes=[sq_b])
        for c in range(C):
            P.op("pe", lambda en: en.matmul(pb[4][:, 0:ncols], lhsT=lhsT_ones[:, :], rhs=sq[:, c, 0:ncols],
                                            start=(c == 0), stop=(c == C - 1)),
                 reads=[sq_b, b_const], writes=[pb_b[4]])
        P.op("act", lambda en: en.activation(out=rstd[:, 0:ncols], in_=pb[4][:, 0:ncols], func=AF.Sqrt,
                                             scale=inv_n, bias=eps_t[:, 0:1]),
             reads=[b_const], writes=[pb_b[4], rstd_b])
        P.op("dve", lambda en: en.reciprocal(out=rstd[:, 0:ncols], in_=rstd[:, 0:ncols]), reads=[rstd_b], writes=[rstd_b])
        for c in range(C):
            P.op("dve", lambda en: en.scalar_tensor_tensor(out=dsts[c], in0=srcs[c], scalar=spk[:, gcol + c:gcol + c + 1],
                                                           in1=rstd[:, 0:ncols], op0=ALU.mult, op1=ALU.mult),
                 reads=[src_b, rstd_b, b_spk], writes=dst_bufs)
    E["rms_grp"] = rms_grp

    for ps in range(0 if E["cfg"].get("skip_proj") else 1):
        t0 = 0
        with Scope(P) as so:
            hT = so.sb("hT", [128, 16, TT], BF16)
            hT_b = Buf("hT")
            with Scope(P) as sn:
                S0 = dict(xs=sn.sb("xs", [128, 16, TC], F32), xs_b=Buf(), sq=sn.sb("sq", [128, 16, TC], BF16),
                          sq_b=Buf(), rstd=sn.sb("rstd", [128, TC], F32), rstd_b=Buf())
                norm_x(S0, hT, hT_b, SP_ATTN + l * 16, t0)
            S = dict(sq=so.sb("sq", [128, 4, TC], BF16), sq_b=Buf(), rstd=so.sb("rstd", [128, TC], F32), rstd_b=Buf())
            stg4 = so.sb("stg4", [128, 4, TT], F32)
            stg4_b = Buf("stg4")
            stg1 = so.sb("stg1", [128, TT], F32)
            stg1_b = Buf("stg1")
            stg2 = so.sb("stg2", [128, 2, TT], F32)
            stg2_b = Buf("stg2")
            cqT = so.sb("cqT", [128, 4, TT], BF16)
            cqT_b = Buf("cqT")
            kvn = so.sb("kvn", [128, 2, TT], BF16)
            kvn_b = Buf("kvn")
            kvtok_st = so.sb("kvtok_st", [128, 8, 256], BF16)
            kvtok_b = Buf()
            kidx_st = so.sb("kidx_st", [64, TT], BF16)
            kidx_b = Buf()
            widx_st = so.sb("widx_st", [16, TT], F32)
            widx_b = Buf()
            widx_tok = so.sb("widx_tok", [128, 8, 16], F32)
            widx_tok_b = Buf()
            ob = so.sb("ob", [128, TT], BF16)
            ob_b = Buf("ob")
            ob2 = so.sb("ob2", [128, 2, TT], BF16)
            ob2_b = Buf("ob2")
            oq = [so.sb(f"oq{i}", [128, TC], BF16) for i in range(2)]
            oq_b = [Buf(), Buf()]
            oq_i = [0]
            vb_st = so.sb("vb_st", [128, TT], BF16)
            vb_b = Buf()
            vtok_st = so.sb("vtok_st", [128, 8, 128], BF16)
            vtok_b = Buf()

            def tsl(tci):
                return slice(tci * TC, (tci + 1) * TC)

            def in_epi(bi, m, tag, tci):
                kind = tag[0]
                last = (tci == TT // TC - 1)
                if kind == "cq":
                    c = tag[1]
                    P.op("act", lambda en: en.activation(out=stg4[:, c, tsl(tci)], in_=pb[bi][:], func=AF.Copy),
                         reads=[], writes=[pb_b[bi], stg4_b])
                    if c == 3 and last:
                        for t2 in range(TT // TC):
                            rms_grp(S, [stg4[:, cc, tsl(t2)] for cc in range(4)], stg4_b, ones_bf, SP_CQ + e * 4, 1.0 / 512,
                                    [cqT[:, cc, tsl(t2)] for cc in range(4)], [cqT_b])
                elif kind == "ckv":
                    c = tag[1]
                    P.op("act", lambda en: en.activation(out=stg2[:, c, tsl(tci)], in_=pb[bi][:], func=AF.Copy),
                         reads=[], writes=[pb_b[bi], stg2_b])
                    if c == 1 and last:
                        for t2 in range(TT // TC):
                            rms_grp(S, [stg2[:, cc, tsl(t2)] for cc in range(2)], stg2_b, ones_bf, SP_CKV + e * 2, 1.0 / 256,
                                    [kvn[:, cc, tsl(t2)] for cc in range(2)], [kvn_b])
                        P.dma("sp", s_kvT.ap()[:, TO:TO + TT].rearrange("(c p) t -> p c t", p=128), kvn[:],
                              reads=[kvn_b], writes=[db("kvT")])
                        P.dma("sp", kpack_e.ap()[0:256, :].rearrange("(c p) t -> p c t", p=128), kvn[:],
                              reads=[kvn_b], writes=[db("kpack_e")])
                        for tt in range(TT // 128):
                            for cc in range(2):
                                P.op("pe", lambda en: en.transpose(pbh[:, cc * 128:(cc + 1) * 128], kvn[:, cc, tt * 128:(tt + 1) * 128], ident_bf[:]),
                                     reads=[kvn_b, b_const], writes=[pbh_b])
                            P.op("act", lambda en: en.activation(out=kvtok_st[:, tt, :], in_=pbh[:, 0:256], func=AF.Copy),
                                 reads=[], writes=[pbh_b, kvtok_b])
                        P.dma("sp", s_kvtok.ap()[TO:TO + TT, :].rearrange("(tt p) c -> p tt c", p=128), kvtok_st[:],
                              reads=[kvtok_b], writes=[db("kvtok")])
                        P.dma("sp", kpack_e.ap()[576:832, :].rearrange("r (a c) -> (r a) c", c=256).rearrange("(tt p) c -> p tt c", p=128), kvtok_st[:],
                              reads=[kvtok_b], writes=[db("kpack_e")])
                elif kind == "kidx":
                    P.op("act", lambda en: en.activation(out=kidx_st[:, tsl(tci)], in_=pb[bi][0:64, :], func=AF.Copy),
                         reads=[], writes=[pb_b[bi], kidx_b])
                    if last:
                        P.dma("sp", s_kidxT.ap()[:, TO:TO + TT], kidx_st[:], reads=[kidx_b], writes=[db("kidxT")])
                        P.dma("sp", kpack_e.ap()[256:320, :], kidx_st[:], reads=[kidx_b], writes=[db("kpack_e")])
                elif kind == "widx":
                    P.op("act", lambda en: en.activation(out=widx_st[:, tsl(tci)], in_=pb[bi][0:16, :], func=AF.Copy),
                         reads=[], writes=[pb_b[bi], widx_b])
                    if last:
                        for tt in range(TT // 128):
                            P.op("pe", lambda en: en.transpose(pb[5][:, tt * 16:(tt + 1) * 16], widx_st[0:16, tt * 128:(tt + 1) * 128], ident_f[0:16, 0:16]),
                                 reads=[widx_b, b_cst], writes=[pb_b[5]])
                        P.op("act", lambda en: en.activation(out=widx_tok[:], in_=pb[5][:, 0:128].rearrange("p (a b) -> p a b", b=16), func=AF.Copy),
                             reads=[], writes=[pb_b[5], widx_tok_b])
                        P.dma("sp", s_widx.ap()[t0:t0 + TT, :].rearrange("(tt p) c -> p tt c", p=128), widx_tok[:],
                              reads=[widx_tok_b], writes=[db("widx")])
                elif kind == "qb":
                    c = tag[1]
                    P.op("act", lambda en: en.activation(out=stg1[:, tsl(tci)], in_=pb[bi][:], func=AF.Copy),
                         reads=[], writes=[pb_b[bi], stg1_b])
                    if last:
                        for t2 in range(TT // TC):
                            rms_grp(S, [stg1[:, tsl(t2)]], stg1_b, bd_bf, SP_BQ + e, 1.0 / 64, [ob[:, tsl(t2)]], [ob_b])
                        P.dma("sp", s_qbT.ap()[c * 128:(c + 1) * 128, t0:t0 + TT], ob[:], reads=[ob_b], writes=[db("qbT")])
                elif kind == "kb":
                    g, half = tag[1], tag[2]
                    pbs = half * 64
                    P.op("act", lambda en: en.activation(out=stg1[pbs:pbs + 64, tsl(tci)], in_=pb[bi][pbs:pbs + 64, :], func=AF.Copy),
                         reads=[], writes=[pb_b[bi], stg1_b])
                    if half == 1 and last:
                        for t2 in range(TT // TC):
                            rms_grp(S, [stg1[:, tsl(t2)]], stg1_b, bd_bf, SP_BK + e, 1.0 / 64, [ob[:, tsl(t2)]], [ob_b])
                        P.dma("sp", s_kdupT.ap()[g * 128:(g + 1) * 128, TO:TO + TT], ob[:], reads=[ob_b], writes=[db("kdupT")])
                        P.dma("sp", kpack_e.ap()[320 + g * 128:320 + (g + 1) * 128, :], ob[:], reads=[ob_b], writes=[db("kpack_e")])
                elif kind == "vb":
                    P.op("act", lambda en: en.activation(out=vb_st[:, tsl(tci)], in_=pb[bi][:], func=AF.Copy),
                         reads=[], writes=[pb_b[bi], vb_b])
                    if last:
                        for tt in range(TT // 128):
                            P.op("pe", lambda en: en.transpose(pbh[:, (tt % 4) * 128:(tt % 4 + 1) * 128], vb_st[:, tt * 128:(tt + 1) * 128], ident_bf[:]),
                                 reads=[vb_b, b_const], writes=[pbh_b])
                            if tt % 4 == 3:
                                P.op("act", lambda en: en.activation(out=vtok_st[:, tt - 3:tt + 1, :], in_=pbh[:, 0:512].rearrange("p (a b) -> p a b", b=128), func=AF.Copy),
                                     reads=[], writes=[pbh_b, vtok_b])
                        P.dma("sp", s_vbtok.ap()[TO:TO + TT, :].rearrange("(tt p) c -> p tt c", p=128), vtok_st[:],
                              reads=[vtok_b], writes=[db("vbtok")])
                        P.dma("sp", kpack_e.ap()[832:960, :].rearrange("r (a c) -> (r a) c", c=128).rearrange("(tt p) c -> p tt c", p=128), vtok_st[:],
                              reads=[vtok_b], writes=[db("kpack_e")])
                elif kind == "qa":
                    h, cc = tag[1], tag[2]
                    P.op("act", lambda en: en.activation(out=stg2[:, cc, tsl(tci)], in_=pb[bi][:], func=AF.Copy),
                         reads=[], writes=[pb_b[bi], stg2_b])
                    if cc == 1 and last:
                        for t2 in range(TT // TC):
                            rms_grp(S, [stg2[:, c2, tsl(t2)] for c2 in range(2)], stg2_b, ones_bf, SP_AQ + e * 2, 1.0 / 256,
                                    [ob2[:, c2, tsl(t2)] for c2 in range(2)], [ob2_b])
                        P.dma("sp", s_qaT.ap()[h * 256:(h + 1) * 256, t0:t0 + TT].rearrange("(c p) t -> p c t", p=128), ob2[:],
                              reads=[ob2_b], writes=[db("qaT")])
                elif kind == "qi":
                    c = tag[1]
                    k = oq_i[0]
                    oq_i[0] = (k + 1) % 2
                    P.op("act", lambda en: en.activation(out=oq[k][:], in_=pb[bi][:], func=AF.Copy),
                         reads=[], writes=[pb_b[bi], oq_b[k]])
                    P.dma("sp", s_qiT.ap()[c * 128:(c + 1) * 128, t0 + tci * TC:t0 + (tci + 1) * TC], oq[k][:],
                          reads=[oq_b[k]], writes=[db("qiT")])

            blocks = [
                ([(0, 512)], [(c * 128, 128, ("cq", c), 0) for c in range(4)]),
                ([(512, 336)], [(0, 128, ("ckv", 0), 0), (128, 128, ("ckv", 1), 0), (256, 64, ("kidx",), 0), (320, 16, ("widx",), 0)]),
                ([(848, 512)], [(c * 128, 128, ("qb", c), 0) for c in range(4)]),
                ([(1360, 512)], [(c * 128, 128, ("qb", 4 + c), 0) for c in range(4)]),
                ([(1872, 256)], [(0, 64, ("kb", 0, 0), 0), (0, 64, ("kb", 0, 1), 64), (64, 64, ("kb", 1, 0), 0), (64, 64, ("kb", 1, 1), 64),
                                 (128, 128, ("vb",), 0)]),
            ]
            gemm(hT, hT_b, 16, lambda c0, n: Wd["w_in_even"].ap()[e, :, c0:c0 + n], blocks, in_epi)
            blocks = []
            for b4 in range(2):
                blocks.append(([(b4 * 2048, 2048)], [((hh * 2 + cc) * 128, 128, ("qa", b4 * 8 + hh, cc), 0) for hh in range(8) for cc in range(2)]))
            gemm(cqT, cqT_b, 4, lambda c0, n: Wd["a_w_uq"].ap()[e, :, c0:c0 + n], blocks, in_epi)
            gemm(cqT, cqT_b, 4, lambda c0, n: Wd["a_w_qidx"].ap()[e, :, c0:c0 + n],
                 [([(0, 1024)], [(c * 128, 128, ("qi", c), 0) for c in range(8)])], in_epi)

    if not E["cfg"].get("skip_proj"):
        cc_gather(kpack_e, gk_e, [db("kpack_e")], [db("gk_e")])
        P.dma("sp", s_kvT.ap()[:, 0:TO], gk_e.ap()[0:256, :], reads=[db("gk_e")], writes=[db("kvT")])
        P.dma("sp", s_kidxT.ap()[:, 0:TO], gk_e.ap()[256:320, :], reads=[db("gk_e")], writes=[db("kidxT")])
        P.dma("sp", s_kdupT.ap()[:, 0:TO], gk_e.ap()[320:576, :], reads=[db("gk_e")], writes=[db("kdupT")])
        P.dma("sp", s_kvtok.ap()[0:TO, :], gk_e.ap()[576:832, :].rearrange("r (a c) -> (r a) c", c=256), reads=[db("gk_e")], writes=[db("kvtok")])
        P.dma("sp", s_vbtok.ap()[0:TO, :], gk_e.ap()[832:960, :].rearrange("r (a c) -> (r a) c", c=128), reads=[db("gk_e")], writes=[db("vbtok")])
    if E["cfg"].get("stop_after_proj"):
        return
    att_scale = 1.0 / 16.0
    with Scope(P) as sa:
        kvT = sa.sb("kvT", [128, 2, T], BF16)
        kvtok = sa.sb("kvtok", [128, 16, 256], BF16)
        kidx2 = sa.sb("kidx2", [128, T], BF16)
        wuv = sa.sb("wuv", [128, 16, 2, 64], BF16)
        expA = sa.sb("expA", [128, 16, 9, 128], BF16)
        b_k = Buf("kside")
        b_exp = Buf("expA")
        P.dma("sp", kvT[:], s_kvT.ap().rearrange("(c p) t -> p c t", p=128), reads=[db("kvT")], writes=[b_k])
        P.dma("sp", kvtok[:], s_kvtok.ap().rearrange("(tt p) c -> p tt c", p=128), reads=[db("kvtok")], writes=[b_k])
        P.dma("sp", kidx2[0:64, :], s_kidxT.ap(), reads=[db("kidxT")], writes=[b_k])
        P.dma("sp", kidx2[64:128, :], s_kidxT.ap(), reads=[db("kidxT")], writes=[b_k])
        P.dma("pool", wuv[:], Wd["a_w_uv"].ap()[e].rearrange("h (cc p) d -> p h cc d", p=128), writes=[b_k])
        s_expA = E["s_expA"]
        if not E["state"].get("expA"):
            E["state"]["expA"] = True
            hk = [sa.sb(f"hk{i}", [128, 128], F32) for i in range(4)]
            hk_b = [Buf() for _ in range(4)]
            n = 0
            for h in range(16):
                for dj in range(9):
                    k = n % 4
                    bk5 = 5 + (n % 2)
                    n += 1
                    P.dma("sp", hk[k][:], bass.AP(s_vrow, h * XT + dj * 128, [[1, 128], [1, 128]]), reads=[db("vrow")], writes=[hk_b[k]])
                    P.op("pe", lambda en: en.matmul(pb[bk5][:, 0:128], lhsT=jflip, rhs=hk[k][:], start=True, stop=True),
                         reads=[hk_b[k], b_cst], writes=[pb_b[bk5]])
                    P.op("act", lambda en: en.activation(out=expA[:, h, 8 - dj, :], in_=pb[bk5][:, 0:128], func=AF.Exp),
                         reads=[], writes=[pb_b[bk5], b_exp])
            P.dma("sp", s_expA.ap(), expA[:].rearrange("p h k q -> p (h k q)"), reads=[b_exp], writes=[db("expA")])
        else:
            P.dma("sp", expA[:].rearrange("p h k q -> p (h k q)"), s_expA.ap(), reads=[db("expA")], writes=[b_exp])
        qi = [sa.sb(f"qi{i}", [128, 8, 128], BF16) for i in range(2)]
        qa = [sa.sb(f"qa{i}", [128, 32, 128], BF16) for i in range(2)]
        wq = [sa.sb(f"wq{i}", [128, 16], F32) for i in range(2)]
        q_b = [Buf(), Buf()]
        score = sa.sb("score", [128, T], F32)
        score_b = Buf("score")
        work = sa.sb("work", [128, T], F32)
        work_b = Buf("work")
        m8 = sa.sb("m8", [128, 8], F32)
        m8_b = Buf("m8")
        thr = sa.sb("thr", [128, 1], F32)
        cand = sa.sb("cand", [128, 1], F32)
        cntt = sa.sb("cntt", [128, 1], F32)
        cnd = sa.sb("cnd", [128, 1], F32)
        thr_b, cand_b, cnt_b, cnd_b = Buf(), Buf(), Buf(), Buf()
        mask01 = sa.sb("mask01", [128, T], BF16)
        mask01_b = Buf()
        maskT = sa.sb("maskT", [128, 16, 128], BF16)
        maskT_b = Buf()
        rl = [sa.sb(f"rl{i}", [128, 512], F32) for i in range(2)]
        rl_b = [Buf(), Buf()]
        NBD = 3
        LBD = [0, 1, 4]
        pf = [sa.sb(f"pf{i}", [128, 512], F32) for i in range(NBD)]
        pf_b = [Buf() for _ in range(NBD)]
        pbf = [sa.sb(f"pbf{i}", [128, 512], BF16) for i in range(NBD)]
        pbf_b = [Buf() for _ in range(NBD)]
        rc = sa.sb("rc", [128, 128], F32)
        rc_b = Buf()
        on = sa.sb("on", [128, 2, 128], BF16)
        on_b = Buf()
        ya_st = [sa.sb(f"ya_st{i}", [128, 8, 128], BF16) for i in range(2)]
        ya_b = [Buf(), Buf()]
        cnt = [0, 0]
        for a_ in range(E["cfg"].get("dsa_tiles", 8)):
            i = 8 + a_
            qk = i % 2
            N = (i + 1) * 128
            qs = slice(a_ * 128, (a_ + 1) * 128)
            ks = slice(i * 128, (i + 1) * 128)
            P.dma("sp", qi[qk][:], s_qiT.ap()[:, qs].rearrange("(c p) t -> p c t", p=128), reads=[db("qiT")], writes=[q_b[qk]])
            P.dma("sp", qa[qk][:], s_qaT.ap()[:, qs].rearrange("(c p) t -> p c t", p=128), reads=[db("qaT")], writes=[q_b[qk]])
            P.dma("sp", wq[qk][:], s_widx.ap()[qs, :], reads=[db("widx")], writes=[q_b[qk]])
            for h in range(16):
                pbs = (h % 2) * 64
                for n0 in range(0, N, 512):
                    n1 = min(N, n0 + 512)
                    bk = cnt[0] % 2
                    cnt[0] += 1
                    P.op("pe", lambda en: en.matmul(pb[bk][:, 0:n1 - n0], lhsT=qi[qk][pbs:pbs + 64, h // 2, :], rhs=kidx2[pbs:pbs + 64, n0:n1],
                                                    start=True, stop=True),
                         reads=[q_b[qk], b_k], writes=[pb_b[bk]])
                    P.op("act", lambda en: en.activation(out=rl[bk][:, 0:n1 - n0], in_=pb[bk][:, 0:n1 - n0], func=AF.Relu),
                         reads=[], writes=[pb_b[bk], rl_b[bk]])
                    if h == 0:
                        P.op("dve", lambda en: en.tensor_scalar(out=score[:, n0:n1], in0=rl[bk][:, 0:n1 - n0], scalar1=wq[qk][:, 0:1], scalar2=None, op0=ALU.mult),
                             reads=[rl_b[bk], q_b[qk]], writes=[score_b])
                    else:
                        P.op("dve", lambda en: en.scalar_tensor_tensor(out=score[:, n0:n1], in0=rl[bk][:, 0:n1 - n0], scalar=wq[qk][:, h:h + 1],
                                                                       in1=score[:, n0:n1], op0=ALU.mult, op1=ALU.add),
                             reads=[rl_b[bk], q_b[qk], score_b], writes=[score_b])
            P.op("dve", lambda en: en.tensor_tensor(out=score[:, ks], in0=score[:, ks], in1=cneg, op=ALU.add),
                 reads=[score_b, b_cst], writes=[score_b])
            P.op("dve", lambda en: en.tensor_scalar(out=score[:, 0:TO], in0=score[:, 0:TO], scalar1=flg[:, 1:2], scalar2=None, op0=ALU.add),
                 reads=[score_b, b_flg], writes=[score_b])
            P.op("dve", lambda en: en.memset(thr[:], -1024.0), writes=[thr_b])
            for kk in range(21):
                step = 2048.0 / (2 ** (kk + 1))
                P.op("dve", lambda en: en.tensor_scalar(out=cand[:], in0=thr[:], scalar1=step, scalar2=None, op0=ALU.add),
                     reads=[thr_b], writes=[cand_b])
                P.op("dve", lambda en: en.tensor_scalar(out=work[:, 0:N], in0=score[:, 0:N], scalar1=cand[:, 0:1], scalar2=0.0,
                                                        op0=ALU.is_ge, op1=ALU.add, accum_out=cntt[:, 0:1]),
                     reads=[score_b, cand_b], writes=[work_b, cnt_b])
                P.op("dve", lambda en: en.tensor_scalar(out=cnd[:], in0=cntt[:], scalar1=255.5, scalar2=None, op0=ALU.is_ge),
                     reads=[cnt_b], writes=[cnd_b])
                P.op("dve", lambda en: en.scalar_tensor_tensor(out=thr[:], in0=cnd[:], scalar=step, in1=thr[:], op0=ALU.mult, op1=ALU.add),
                     reads=[cnd_b, thr_b], writes=[thr_b])
            P.op("dve", lambda en: en.tensor_scalar(out=mask01[:, 0:N], in0=score[:, 0:N], scalar1=thr[:, 0:1], scalar2=None, op0=ALU.is_ge),
                 reads=[score_b, thr_b], writes=[mask01_b])
            for j0 in range(0, i + 1, 8):
                j1 = min(i + 1, j0 + 8)
                for j in range(j0, j1):
                    P.op("pe", lambda en: en.transpose(pbh[:, (j - j0) * 128:(j - j0 + 1) * 128], mask01[:, j * 128:(j + 1) * 128], ident_bf[:]),
                         reads=[mask01_b, b_const], writes=[pbh_b])
                P.op("act", lambda en: en.activation(out=maskT[:, j0:j1, :], in_=pbh[:, 0:(j1 - j0) * 128].rearrange("p (a b) -> p a b", b=128), func=AF.Copy),
                     reads=[], writes=[pbh_b, maskT_b])
            yk = i % 2
            items = [(h, jg) for h in range(16) for jg in range(0, i + 1, 4)]

            def emit_logits(k):
                h, jg = items[k]
                je = min(i + 1, jg + 4)
                L = k % NBD
                BK = LBD[L]
                for j in range(jg, je):
                    sl = j - jg
                    for c in range(2):
                        P.op("pe", lambda en: en.matmul(pb[BK][:, sl * 128:(sl + 1) * 128], lhsT=kvT[:, c, j * 128:(j + 1) * 128],
                                                        rhs=qa[qk][:, 2 * h + c, :], start=(c == 0), stop=(c == 1)),
                             reads=[b_k, q_b[qk]], writes=[pb_b[BK]])

            def emit_post(k):
                h, jg = items[k]
                je = min(i + 1, jg + 4)
                nj = je - jg
                L = k % NBD
                BK = LBD[L]
                far = (i - (je - 1)) >= 8
                if far:
                    P.op("act", lambda en: en.activation(out=pf[L][:, 0:nj * 128], in_=pb[BK][:, 0:nj * 128], func=AF.Exp, scale=att_scale,
                                                         bias=spk[:, SP_B31 + h:SP_B31 + h + 1]),
                         reads=[b_spk], writes=[pb_b[BK], pf_b[L]])
                else:
                    P.op("act", lambda en: en.activation(out=pf[L][:, 0:nj * 128], in_=pb[BK][:, 0:nj * 128], func=AF.Exp, scale=att_scale),
                         reads=[], writes=[pb_b[BK], pf_b[L]])
                    if i - jg <= 8:
                        k0 = 8 - (i - jg)
                        P.op("dve", lambda en: en.tensor_tensor(out=pf[L][:, 0:nj * 128].rearrange("p (a b) -> p a b", b=128),
                                                                in0=pf[L][:, 0:nj * 128].rearrange("p (a b) -> p a b", b=128),
                                                                in1=expA[:, h, k0:k0 + nj, :], op=ALU.mult),
                             reads=[pf_b[L], b_exp], writes=[pf_b[L]])
                    else:
                        for j in range(jg, je):
                            sl = j - jg
                            kk = 8 - min(i - j, 8)
                            P.op("dve", lambda en: en.tensor_tensor(out=pf[L][:, sl * 128:(sl + 1) * 128], in0=pf[L][:, sl * 128:(sl + 1) * 128],
                                                                    in1=expA[:, h, kk, :], op=ALU.mult),
                                 reads=[pf_b[L], b_exp], writes=[pf_b[L]])
                P.op("pool", lambda en: en.tensor_tensor(out=pbf[L][:, 0:nj * 128].rearrange("p (a b) -> p a b", b=128),
                                                         in0=pf[L][:, 0:nj * 128].rearrange("p (a b) -> p a b", b=128),
                                                         in1=maskT[:, jg:je, :], op=ALU.mult),
                     reads=[pf_b[L], maskT_b], writes=[pbf_b[L]])

            def emit_pv(k):
                h, jg = items[k]
                je = min(i + 1, jg + 4)
                L = k % NBD
                for j in range(jg, je):
                    sl = j - jg
                    for (bk, lh) in ((2, kvtok[:, j, 0:128]), (3, kvtok[:, j, 128:256]), (6, ones_bf[:, :])):
                        P.op("pe", lambda en: en.matmul(pb[bk][:, 0:128], lhsT=lh, rhs=pbf[L][:, sl * 128:(sl + 1) * 128],
                                                        start=(j == 0), stop=(j == i)),
                             reads=[b_k, pbf_b[L], b_const], writes=[pb_b[bk]])

            def emit_fin_dve(h):
                P.op("dve", lambda en: en.reciprocal(out=rc[:], in_=pb[6][:, 0:128]), reads=[], writes=[pb_b[6], rc_b])
                for c in range(2):
                    P.op("dve", lambda en: en.tensor_tensor(out=on[:, c, :], in0=pb[2 + c][:, 0:128], in1=rc[:], op=ALU.mult),
                         reads=[rc_b], writes=[pb_b[2 + c], on_b])

            def emit_fin_pe(h):
                pbs = (h % 2) * 64
                col = ((h // 2) % 4) * 128
                for c in range(2):
                    P.op("pe", lambda en: en.matmul(pb[5][pbs:pbs + 64, col:col + 128], lhsT=wuv[:, h, c, :], rhs=on[:, c, :],
                                                    start=(c == 0), stop=(c == 1)),
                         reads=[b_k, on_b], writes=[pb_b[5]])
                if h % 2 == 1:
                    P.op("act", lambda en: en.activation(out=ya_st[yk][:, h // 2, :], in_=pb[5][:, col:col + 128], func=AF.Copy),
                         reads=[], writes=[pb_b[5], ya_b[yk]])

            for k0 in range(min(NBD - 1, len(items))):
                emit_logits(k0)
            for k in range(len(items)):
                h, jg = items[k]
                if k + NBD - 1 < len(items):
                    emit_logits(k + NBD - 1)
                emit_post(k)
                emit_pv(k)
                if jg + 4 > i:
                    emit_fin_dve(h)
                    emit_fin_pe(h)
            P.dma("sp", yT.ap()[0:1024, qs].rearrange("(c p) t -> p c t", p=128), ya_st[yk][:], reads=[ya_b[yk]], writes=[db("yT", 0)])

    with Scope(P) as sw:
        kd = sw.sb("kd", [128, 2, T], BF16)
        vtk = sw.sb("vtk", [128, 16, 128], BF16)
        expB = sw.sb("expB", [128, 16, 2, 128], BF16)
        esk = sw.sb("esk", [128, 16], F32)
        b_k = Buf("kside")
        b_exp = Buf("expB")
        P.dma("sp", kd[:], s_kdupT.ap().rearrange("(g p) t -> p g t", p=128), reads=[db("kdupT")], writes=[b_k])
        P.dma("sp", vtk[:], s_vbtok.ap().rearrange("(tt p) c -> p tt c", p=128), reads=[db("vbtok")], writes=[b_k])
        P.op("act", lambda en: en.activation(out=esk[:], in_=spk[:, SP_SINK + e * 16:SP_SINK + (e + 1) * 16], func=AF.Exp),
             reads=[b_spk], writes=[b_exp])
        s_expB = E["s_expB"]
        if not E["state"].get("expB"):
            E["state"]["expB"] = True
            hk = [sw.sb(f"hk{i}", [128, 128], F32) for i in range(4)]
            hk_b = [Buf() for _ in range(4)]
            n = 0
            for hb in range(16):
                for kx in range(2):
                    dj = 1 - kx
                    k = n % 4
                    bk5 = [4, 2][n % 2]
                    n += 1
                    P.dma("sp", hk[k][:], bass.AP(s_vrow, hb * XT + XA + dj * 128, [[1, 128], [1, 128]]), reads=[db("vrow")], writes=[hk_b[k]])
                    P.op("pe", lambda en: en.matmul(pb[bk5][:, 0:128], lhsT=jflip, rhs=hk[k][:], start=True, stop=True),
                         reads=[hk_b[k], b_cst], writes=[pb_b[bk5]])
                    P.op("act", lambda en: en.activation(out=expB[:, hb, kx, :], in_=pb[bk5][:, 0:128], func=AF.Exp),
                         reads=[], writes=[pb_b[bk5], b_exp])
            P.dma("sp", s_expB.ap(), expB[:].rearrange("p h k q -> p (h k q)"), reads=[b_exp], writes=[db("expB")])
        else:
            P.dma("sp", expB[:].rearrange("p h k q -> p (h k q)"), s_expB.ap(), reads=[db("expB")], writes=[b_exp])
        qb = [sw.sb(f"qb{i}", [128, 8, 128], BF16) for i in range(2)]
        qb_b = [Buf(), Buf()]
        pf = [sw.sb(f"pf{i}", [128, 512], F32) for i in range(2)]
        pf_b = [Buf(), Buf()]
        pbf = [sw.sb(f"pbf{i}", [128, 512], BF16) for i in range(2)]
        pbf_b = [Buf(), Buf()]
        dn = sw.sb("dn", [128, 128], F32)
        dn_b = Buf()
        yb_st = [sw.sb(f"yb_st{i}", [128, 8, 128], BF16) for i in range(2)]
        yb_b = [Buf(), Buf()]
        cnt = 0
        for a_ in range(E["cfg"].get("swa_blocks", 8)):
            nb = 8 + a_
            qk = nb % 2
            qs = slice(a_ * 128, (a_ + 1) * 128)
            P.dma("sp", qb[qk][:], s_qbT.ap()[:, qs].rearrange("(c p) t -> p c t", p=128), reads=[db("qbT")], writes=[qb_b[qk]])
            for m in range(8):
                Lb = [(0, 1), (5, 6)][cnt % 2]
                Ls = cnt % 2
                cnt += 1
                units = []
                for hh in range(2):
                    for kx in range(2):
                        dj = 1 - kx
                        if nb - dj >= 0:
                            units.append((hh, kx, nb - dj, hh * 2 + kx))
                for (hh, kx, j, sl) in units:
                    hb = 2 * m + hh
                    g = hb // 8
                    pbs = hh * 64
                    bkx = Lb[hh]
                    P.op("pe", lambda en: en.matmul(pb[bkx][:, kx * 128:(kx + 1) * 128], lhsT=kd[pbs:pbs + 64, g, j * 128:(j + 1) * 128],
                                                    rhs=qb[qk][pbs:pbs + 64, m, :], start=True, stop=True),
                         reads=[b_k, qb_b[qk]], writes=[pb_b[bkx]])
                stg_ = E["cfg"].get("swa_stage", 4)
                if stg_ < 2:
                    continue
                L = Ls
                for hh in range(2):
                    bkx = Lb[hh]
                    a, b = (0, 2)
                    P.op("act", lambda en: en.activation(out=pf[L][:, hh * 256 + a * 128:hh * 256 + b * 128], in_=pb[bkx][:, a * 128:b * 128], func=AF.Exp, scale=0.125),
                         reads=[], writes=[pb_b[bkx], pf_b[L]])
                    P.op("dve", lambda en: en.tensor_tensor(out=pbf[L][:, hh * 256 + a * 128:hh * 256 + b * 128], in0=pf[L][:, hh * 256 + a * 128:hh * 256 + b * 128],
                                                            in1=expB[:, 2 * m + hh, a:b, :].rearrange("p k q -> p (k q)"), op=ALU.mult),
                         reads=[pf_b[L], b_exp], writes=[pbf_b[L]])
                    if a_ == 0:
                        P.op("dve", lambda en: en.tensor_scalar(out=pbf[L][:, hh * 256:hh * 256 + 128], in0=pbf[L][:, hh * 256:hh * 256 + 128],
                                                                scalar1=flg[:, 0:1], scalar2=None, op0=ALU.mult),
                             reads=[pbf_b[L], b_flg], writes=[pbf_b[L]])
                if stg_ < 3:
                    continue
                for hh in range(2):
                    us = [u for u in units if u[0] == hh]
                    hb = 2 * m + hh
                    g = hb // 8
                    pbs = hh * 64
                    for ui, (_, kx, j, sl) in enumerate(us):
                        P.op("pe", lambda en: en.matmul(pb[2][pbs:pbs + 64, 0:128], lhsT=vtk[:, j, g * 64:(g + 1) * 64], rhs=pbf[L][:, sl * 128:(sl + 1) * 128],
                                                        start=(ui == 0), stop=(ui == len(us) - 1)),
                             reads=[b_k, pbf_b[L]], writes=[pb_b[2]])
                        P.op("pe", lambda en: en.matmul(pb[3][pbs:pbs + 64, 0:128], lhsT=ones_bf[:, 0:64], rhs=pbf[L][:, sl * 128:(sl + 1) * 128],
                                                        start=(ui == 0), stop=(ui == len(us) - 1)),
                             reads=[b_const, pbf_b[L]], writes=[pb_b[3]])
                if stg_ < 4:
                    continue
                for hh in range(2):
                    hb = 2 * m + hh
                    pbs = hh * 64
                    P.op("dve", lambda en: en.tensor_scalar(out=dn[pbs:pbs + 64, :], in0=pb[3][pbs:pbs + 64, 0:128], scalar1=esk[pbs:pbs + 64, hb:hb + 1],
                                                            scalar2=None, op0=ALU.add),
                         reads=[b_exp], writes=[pb_b[3], dn_b])
                P.op("dve", lambda en: en.reciprocal(out=dn[:], in_=dn[:]), reads=[dn_b], writes=[dn_b])
                P.op("dve", lambda en: en.tensor_tensor(out=yb_st[qk][:, m, :], in0=pb[2][:, 0:128], in1=dn[:], op=ALU.mult),
                     reads=[dn_b], writes=[pb_b[2], yb_b[qk]])
            P.dma("sp", yT.ap()[1024:2048, qs].rearrange("(c p) t -> p c t", p=128), yb_st[qk][:], reads=[yb_b[qk]], writes=[db("yT", 0)])


def odd_attention(E, l):
    o = l // 2
    P, nc, Wd, db, gemm, simple_blocks, norm_x = (E[k] for k in ("P", "nc", "Wd", "db", "gemm", "simple_blocks", "norm_x"))
    spk, b_spk, pb, pb_b, pbh, pbh_b, ident_f, ident_bf, b_cst, b_const, ones_bf, ones_f, bd_bf, triu_f, triu_bf, eps_t = (E[k] for k in (
        "spk", "b_spk", "pb", "pb_b", "pbh", "pbh_b", "ident_f", "ident_bf", "b_cst", "b_const", "ones_bf", "ones_f", "bd_bf", "triu_f", "triu_bf", "eps_t"))
    xT, yT = E["xT"], E["yT"]
    s_qT, s_kT, s_vtok, s_lf = E["s_qT"], E["s_kT"], E["s_vtok"], E["s_lf"]
    flg, b_flg, cc_gather, kpack_o, gk_o, lfp, glf = (E[k] for k in ("flg", "b_flg", "cc_gather", "kpack_o", "gk_o", "lfp", "glf"))

    def rms_grp(S, srcs, src_b, lhsT_ones, gcol, inv_n, dsts, dst_bufs, ncols=TC):
        C = len(srcs)
        sq, sq_b, rstd, rstd_b = S["sq"], S["sq_b"], S["rstd"], S["rstd_b"]
        for c in range(C):
            P.op("act", lambda en: en.activation(out=sq[:, c, 0:ncols], in_=srcs[c], func=AF.Square),
                 reads=[src_b], writes=[sq_b])
        for c in range(C):
            P.op("pe", lambda en: en.matmul(pb[4][:, 0:ncols], lhsT=lhsT_ones[:, :], rhs=sq[:, c, 0:ncols],
                                            start=(c == 0), stop=(c == C - 1)),
                 reads=[sq_b, b_const], writes=[pb_b[4]])
        P.op("act", lambda en: en.activation(out=rstd[:, 0:ncols], in_=pb[4][:, 0:ncols], func=AF.Sqrt,
                                             scale=inv_n, bias=eps_t[:, 0:1]),
             reads=[b_const], writes=[pb_b[4], rstd_b])
        P.op("dve", lambda en: en.reciprocal(out=rstd[:, 0:ncols], in_=rstd[:, 0:ncols]), reads=[rstd_b], writes=[rstd_b])
        for c in range(C):
            P.op("dve", lambda en: en.scalar_tensor_tensor(out=dsts[c], in0=srcs[c], scalar=spk[:, gcol + c:gcol + c + 1],
                                                           in1=rstd[:, 0:ncols], op0=ALU.mult, op1=ALU.mult),
                 reads=[src_b, rstd_b, b_spk], writes=dst_bufs)

    for ps in range(0 if E["cfg"].get("skip_proj") else 1):
        t0 = 0
        with Scope(P) as so:
            hT = so.sb("hT", [128, 16, TT], BF16)
            hT_b = Buf("hT")
            with Scope(P) as sn:
                S0 = dict(xs=sn.sb("xs", [128, 16, TC], F32), xs_b=Buf(), sq=sn.sb("sq", [128, 16, TC], BF16),
                          sq_b=Buf(), rstd=sn.sb("rstd", [128, TC], F32), rstd_b=Buf())
                norm_x(S0, hT, hT_b, SP_ATTN + l * 16, t0)
            S = dict(sq=so.sb("sq", [128, 1, TC], BF16), sq_b=Buf(), rstd=so.sb("rstd", [128, TC], F32), rstd_b=Buf())
            stg1 = so.sb("stg1", [128, TT], F32)
            stg1_b = Buf("stg1")
            ob = so.sb("ob", [128, TT], BF16)
            ob_b = Buf("ob")
            vb_st = so.sb("vb_st", [128, TT], BF16)
            vb_b = Buf()
            vtok_st = so.sb("vtok_st", [128, 8, 128], BF16)
            vtok_b = Buf()
            negfb = so.sb("negfb", [32, 1], F32)
            negfb_b = Buf()
            fst = so.sb("fst", [32, TT], F32)
            fst_b = Buf()
            lf_tok = so.sb("lf_tok", [128, 8, 32], F32)
            lf_tok_b = Buf()
            P.op("dve", lambda en: en.tensor_scalar(out=negfb[:], in0=spk[0:32, SP_FB + o:SP_FB + o + 1], scalar1=-1.0, scalar2=None, op0=ALU.mult),
                 reads=[b_spk], writes=[negfb_b])

            def tsl(tci):
                return slice(tci * TC, (tci + 1) * TC)

            def in_epi(bi, m, tag, tci):
                kind = tag[0]
                last = (tci == TT // TC - 1)
                if kind in ("q", "k"):
                    c = tag[1]
                    P.op("act", lambda en: en.activation(out=stg1[:, tsl(tci)], in_=pb[bi][:], func=AF.Copy),
                         reads=[], writes=[pb_b[bi], stg1_b])
                    if last:
                        gcol = (SP_CQN if kind == "q" else SP_CKN) + o
                        dst = s_qT if kind == "q" else s_kT
                        for t2 in range(TT // TC):
                            rms_grp(S, [stg1[:, tsl(t2)]], stg1_b, bd_bf, gcol, 1.0 / 64, [ob[:, tsl(t2)]], [ob_b])
                        if kind == "q":
                            P.dma("sp", s_qT.ap()[c * 128:(c + 1) * 128, 0:TT], ob[:], reads=[ob_b], writes=[db("qT")])
                        else:
                            P.dma("sp", s_kT.ap()[c * 128:(c + 1) * 128, TO:TO + TT], ob[:], reads=[ob_b], writes=[db("kT")])
                            P.dma("sp", kpack_o[c // 8].ap()[(c % 8) * 128:(c % 8 + 1) * 128, :], ob[:], reads=[ob_b], writes=[db("kpack_o", c // 8)])
                elif kind == "v":
                    c = tag[1]
                    P.op("act", lambda en: en.activation(out=vb_st[:, tsl(tci)], in_=pb[bi][:], func=AF.Copy),
                         reads=[], writes=[pb_b[bi], vb_b])
                    if last:
                        for tt in range(TT // 128):
                            P.op("pe", lambda en: en.transpose(pbh[:, (tt % 4) * 128:(tt % 4 + 1) * 128], vb_st[:, tt * 128:(tt + 1) * 128], ident_bf[:]),
                                 reads=[vb_b, b_const], writes=[pbh_b])
                            if tt % 4 == 3:
                                P.op("act", lambda en: en.activation(out=vtok_st[:, tt - 3:tt + 1, :], in_=pbh[:, 0:512].rearrange("p (a b) -> p a b", b=128), func=AF.Copy),
                                     reads=[], writes=[pbh_b, vtok_b])
                        P.dma("sp", s_vtok.ap()[TO:TO + TT, c * 128:(c + 1) * 128].rearrange("(tt p) c -> p tt c", p=128), vtok_st[:],
                              reads=[vtok_b], writes=[db("vtok")])
                        for hv in range(2):
                            P.dma("sp", kpack_o[2 + hv].ap().rearrange("(t a) c -> t (a c)", a=2)[:, c * 128:(c + 1) * 128].rearrange("(tt p) c -> p tt c", p=128),
                                  vtok_st[:, hv * 4:(hv + 1) * 4, :], reads=[vtok_b], writes=[db("kpack_o", 2 + hv)])
                elif kind == "f":
                    P.op("act", lambda en: en.activation(out=fst[:, tsl(tci)], in_=pb[bi][0:32, :], func=AF.Exp, scale=-1.0, bias=negfb[:, 0:1]),
                         reads=[negfb_b], writes=[pb_b[bi], fst_b])
                    if last:
                        P.op("act", lambda en: en.activation(out=fst[:], in_=fst[:], func=AF.Ln, bias=ones_f[0:32, 0:1]),
                             reads=[fst_b, b_const], writes=[fst_b])
                        for tt in range(TT // 128):
                            P.op("pe", lambda en: en.transpose(pb[5][:, tt * 32:(tt + 1) * 32], fst[0:32, tt * 128:(tt + 1) * 128], ident_f[0:32, 0:32]),
                                 reads=[fst_b, b_cst], writes=[pb_b[5]])
                        P.op("act", lambda en: en.activation(out=lf_tok[:], in_=pb[5][:, 0:256].rearrange("p (a b) -> p a b", b=32), func=AF.Copy),
                             reads=[], writes=[pb_b[5], lf_tok_b])
                        P.dma("sp", s_lf.ap()[TO:TO + TT, :].rearrange("(tt p) c -> p tt c", p=128), lf_tok[:],
                              reads=[lf_tok_b], writes=[db("lf")])
                        P.dma("sp", lfp.ap().rearrange("(tt p) c -> p tt c", p=128), lf_tok[:],
                              reads=[lf_tok_b], writes=[db("lfp")])
            blocks = []
            for kind, base in (("q", 0), ("k", 2048), ("v", 4096)):
                for b4 in range(4):
                    blocks.append(([(base + b4 * 512, 512)], [(c * 128, 128, (kind, b4 * 4 + c), 0) for c in range(4)]))
            blocks.append(([(6144, 32)], [(0, 32, ("f",), 0)]))
            gemm(hT, hT_b, 16, lambda c0, n: Wd["w_in_odd"].ap()[o, :, c0:c0 + n], blocks, in_epi)

    if not E["cfg"].get("skip_proj"):
        for i4 in range(4):
            cc_gather(kpack_o[i4], gk_o[i4], [db("kpack_o", i4)], [db("gk_o", i4)])
        cc_gather(lfp, glf, [db("lfp")], [db("glf")])
        for i4 in range(2):
            P.dma("sp", s_kT.ap()[i4 * 1024:(i4 + 1) * 1024, 0:TO], gk_o[i4].ap()[0:1024, :], reads=[db("gk_o", i4)], writes=[db("kT")])
            P.dma("sp", s_vtok.ap()[i4 * 512:(i4 + 1) * 512, :], gk_o[2 + i4].ap()[0:1024, :].rearrange("(t a) c -> t (a c)", a=2),
                  reads=[db("gk_o", 2 + i4)], writes=[db("vtok")])
        P.dma("sp", s_lf.ap()[0:TO, :], glf.ap()[0:1024, :], reads=[db("glf")], writes=[db("lf")])
    if E["cfg"].get("stop_after_proj"):
        return
    with Scope(P) as sa:
        lft = sa.sb("lft", [128, 16, 32], F32)
        lft_b = Buf()
        ncum = sa.sb("ncum", [128, 16, 32], F32)
        Cb = sa.sb("Cb", [128, 16, 32], F32)
        cum_b = Buf("cum")
        P.dma("sp", lft[:], s_lf.ap().rearrange("(tt p) c -> p tt c", p=128), reads=[db("lf")], writes=[lft_b])
        P.op("dve", lambda en: en.tensor_scalar(out=lft[:, 0:8, :], in0=lft[:, 0:8, :], scalar1=flg[:, 0:1], scalar2=None, op0=ALU.mult),
             reads=[lft_b, b_flg], writes=[lft_b])
        for j in range(16):
            for j2 in range(j + 1):
                P.op("pe", lambda en: en.matmul(pb[5][:, j * 32:(j + 1) * 32], lhsT=(triu_f if j2 == j else ones_f[:, :]), rhs=lft[:, j2, :],
                                                start=(j2 == 0), stop=(j2 == j)),
                     reads=[lft_b, b_cst, b_const], writes=[pb_b[5]])
            for j2 in range(j + 1):
                P.op("pe", lambda en: en.matmul(pb[6][:, j * 32:(j + 1) * 32], lhsT=ones_f[:, :], rhs=lft[:, j2, :],
                                                start=(j2 == 0), stop=(j2 == j)),
                     reads=[lft_b, b_const], writes=[pb_b[6]])
        P.op("act", lambda en: en.activation(out=ncum[:], in_=pb[5][:].rearrange("p (a b) -> p a b", b=32), func=AF.Copy),
             reads=[], writes=[pb_b[5], cum_b])
        P.op("act", lambda en: en.activation(out=Cb[:], in_=pb[6][:].rearrange("p (a b) -> p a b", b=32), func=AF.Copy),
             reads=[], writes=[pb_b[6], cum_b])
        s_nq = E["s_nq"]
        dm = sa.sb("dm", [128, 16, 32], F32)
        dhi = sa.sb("dhi", [128, 16, 32], BF16)
        dhf = sa.sb("dhf", [128, 16, 32], F32)
        dlo = sa.sb("dlo", [128, 16, 32], BF16)
        dl2 = sa.sb("dl2", [128, 16, 32], BF16)
        nqT = sa.sb("nqT", [32, 3, TO], BF16)
        dm_b = Buf("dm")
        nqT_b = Buf("nqT")
        P.op("dve", lambda en: en.memset(dm[:, 0:8, :], 0.0), writes=[dm_b])
        for t_ in range(8, 16):
            ge = 4 * (t_ // 4) + 3
            P.op("dve", lambda en: en.tensor_tensor(out=dm[:, t_, :], in0=Cb[:, ge, :], in1=ncum[:, t_, :], op=ALU.subtract),
                 reads=[cum_b], writes=[dm_b])
        P.op("dve", lambda en: en.tensor_scalar(out=dm[:], in0=dm[:], scalar1=8.0, scalar2=None, op0=ALU.mult), reads=[dm_b], writes=[dm_b])
        P.op("dve", lambda en: en.tensor_copy(out=dhi[:], in_=dm[:]), reads=[dm_b], writes=[dm_b])
        P.op("dve", lambda en: en.tensor_copy(out=dhf[:], in_=dhi[:]), reads=[dm_b], writes=[dm_b])
        P.op("dve", lambda en: en.tensor_tensor(out=dhf[:], in0=dm[:], in1=dhf[:], op=ALU.subtract), reads=[dm_b], writes=[dm_b])
        P.op("dve", lambda en: en.tensor_copy(out=dlo[:], in_=dhf[:]), reads=[dm_b], writes=[dm_b])
        P.op("dve", lambda en: en.tensor_copy(out=dm[:], in_=dlo[:]), reads=[dm_b], writes=[dm_b])
        P.op("dve", lambda en: en.tensor_tensor(out=dhf[:], in0=dhf[:], in1=dm[:], op=ALU.subtract), reads=[dm_b], writes=[dm_b])
        P.op("dve", lambda en: en.tensor_copy(out=dl2[:], in_=dhf[:]), reads=[dm_b], writes=[dm_b])
        for w, src in enumerate((dhi, dlo, dl2)):
            for tt in range(8):
                P.op("pe", lambda en: en.transpose(pbh[0:32, tt * 128:(tt + 1) * 128], src[:, 8 + tt, :], ident_bf[:]),
                     reads=[dm_b, b_const], writes=[pbh_b])
            P.op("act", lambda en: en.activation(out=nqT[:, w, :], in_=pbh[0:32, :], func=AF.Copy),
                 reads=[], writes=[pbh_b, nqT_b])
        P.dma("sp", s_nq.ap().rearrange("w h t -> h w t"), nqT[:], reads=[nqT_b], writes=[db("nq")])
        P.op("dve", lambda en: en.tensor_scalar(out=ncum[:, 0:8, :], in0=ncum[:, 0:8, :], scalar1=flg[:, 2:3], scalar2=None, op0=ALU.add),
             reads=[cum_b, dm_b, b_flg], writes=[cum_b])
        mneg = sa.sb("mneg", [128, 128], BF16)
        mneg_b = Buf("mneg")
        P.op("dve", lambda en: en.tensor_scalar(out=mneg[:], in0=triu_f, scalar1=30000.0, scalar2=-30000.0, op0=ALU.mult, op1=ALU.add),
             reads=[b_cst], writes=[mneg_b])
        kaug = [[sa.sb(f"kaug{i}{hh}", [128, T], BF16) for hh in range(2)] for i in range(2)]
        qaug = [[sa.sb(f"qaug{i}{hh}", [128, TO], BF16) for hh in range(2)] for i in range(2)]
        vm = [sa.sb(f"vm{i}", [128, 16, 128], BF16) for i in range(2)]
        m_b = [Buf(), Buf()]
        NBF = 4
        LBF = [0, 1, 4, 5]
        Bm = [sa.sb(f"Bm{i}", [128, 4], F32) for i in range(NBF)]
        Bm_b = [Buf() for _ in range(NBF)]
        pbf = [sa.sb(f"pbf{i}", [128, 512], BF16) for i in range(NBF)]
        pbf_b = [Buf() for _ in range(NBF)]
        rcp = sa.sb("rcp", [128, 512], F32)
        rcp_b = Buf()
        yst = [sa.sb(f"yst{i}", [128, 512], BF16) for i in range(2)]
        yst_b = [Buf(), Buf()]
        cnt = 0
        yc = 0
        for m in range(E["cfg"].get("fox_pairs", 16)):
            mk = m % 2
            for hh in range(2):
                h = 2 * m + hh
                own = slice(hh * 64, (hh + 1) * 64)
                oth = slice((1 - hh) * 64, (2 - hh) * 64)
                o0 = (1 - hh) * 64
                P.op("dve", lambda en: en.memset(kaug[mk][hh][oth, :], 0.0), writes=[m_b[mk]])
                P.op("dve", lambda en: en.memset(kaug[mk][hh][o0:o0 + 3, :], 1.0), writes=[m_b[mk]])
                P.op("dve", lambda en: en.memset(qaug[mk][hh][oth, :], 0.0), writes=[m_b[mk]])
                P.dma("sp", kaug[mk][hh][own, :], s_kT.ap()[h * 64:(h + 1) * 64, :], reads=[db("kT")], writes=[m_b[mk]])
                P.dma("sp", qaug[mk][hh][own, :], s_qT.ap()[h * 64:(h + 1) * 64, :], reads=[db("qT")], writes=[m_b[mk]])
                P.dma("sp", qaug[mk][hh][o0:o0 + 3, :], s_nq.ap()[:, h, :], reads=[db("nq")], writes=[m_b[mk]])
            P.dma("sp", vm[mk][:], s_vtok.ap()[:, m * 128:(m + 1) * 128].rearrange("(tt p) c -> p tt c", p=128), reads=[db("vtok")], writes=[m_b[mk]])
            for Gl in range(2):
                G = 2 + Gl
                jmax = 4 * G + 3
                items = [(hh, j) for hh in range(2) for j in range(jmax + 1)]

                def f_logits(k):
                    hh, j = items[k]
                    L = LBF[k % NBF]
                    i_lo = max(4 * G, j)
                    col0 = (i_lo - 4 * G) * 128
                    P.op("pe", lambda en: en.matmul(pb[L][:, col0:512], lhsT=kaug[mk][hh][:, j * 128:(j + 1) * 128],
                                                    rhs=qaug[mk][hh][:, 4 * Gl * 128 + col0:(4 * Gl + 4) * 128], start=True, stop=(j < 4 * G)),
                         reads=[m_b[mk]], writes=[pb_b[L]])
                    if j >= 4 * G:
                        P.op("pe", lambda en: en.matmul(pb[L][:, col0:col0 + 128], lhsT=ident_bf[:], rhs=mneg[:], start=False, stop=True),
                             reads=[b_const, mneg_b], writes=[pb_b[L]])

                def f_post(k):
                    hh, j = items[k]
                    h = 2 * m + hh
                    L = k % NBF
                    BK = LBF[L]
                    i_lo = max(4 * G, j)
                    P.op("dve", lambda en: en.tensor_scalar(out=Bm[L][:, 0:1], in0=Cb[:, 4 * G + 3, h:h + 1], scalar1=-1.0, scalar2=ncum[:, j, h:h + 1],
                                                            op0=ALU.mult, op1=ALU.add),
                         reads=[cum_b], writes=[Bm_b[L]])
                    cs = slice((i_lo - 4 * G) * 128, 512)
                    P.op("act", lambda en: en.activation(out=pbf[L][:, cs], in_=pb[BK][:, cs], func=AF.Exp, scale=0.125,
                                                         bias=Bm[L][:, 0:1]),
                         reads=[Bm_b[L]], writes=[pb_b[BK], pbf_b[L]])

                def f_pv(k):
                    hh, j = items[k]
                    pbs = hh * 64
                    L = k % NBF
                    i_lo = max(4 * G, j)
                    col0 = (i_lo - 4 * G) * 128
                    P.op("pe", lambda en: en.matmul(pb[2][pbs:pbs + 64, col0:512], lhsT=vm[mk][:, j, hh * 64:(hh + 1) * 64], rhs=pbf[L][:, col0:512],
                                                    start=(j == 0), stop=(j == jmax)),
                         reads=[m_b[mk], pbf_b[L]], writes=[pb_b[2]])
                    P.op("pe", lambda en: en.matmul(pb[3][pbs:pbs + 64, col0:512], lhsT=ones_bf[:, 0:64], rhs=pbf[L][:, col0:512],
                                                    start=(j == 0), stop=(j == jmax)),
                         reads=[b_const, pbf_b[L]], writes=[pb_b[3]])

                for k0 in range(NBF - 1):
                    f_logits(k0)
                for k in range(len(items)):
                    if k + NBF - 1 < len(items):
                        f_logits(k + NBF - 1)
                    f_post(k)
                    f_pv(k)
                yk = yc % 2
                yc += 1
                P.op("dve", lambda en: en.reciprocal(out=rcp[:], in_=pb[3][:]), reads=[], writes=[pb_b[3], rcp_b])
                P.op("dve", lambda en: en.tensor_tensor(out=yst[yk][:], in0=pb[2][:], in1=rcp[:], op=ALU.mult),
                     reads=[rcp_b], writes=[pb_b[2], yst_b[yk]])
                P.dma("sp", yT.ap()[m * 128:(m + 1) * 128, Gl * 512:(Gl + 1) * 512], yst[yk][:], reads=[yst_b[yk]], writes=[db("yT", 0)])


def build(cfg=None):
    cfg = cfg or {}
    layers = cfg.get("layers", list(range(DEPTH)))
    nc = bass.Bass("TRN2", target_bir_lowering=False)

    def din(name, shape):
        return nc.dram_tensor(name, list(shape), F32, kind="ExternalInput")
    x_in = din("x", [TO, D])
    p_in = din("p", [DEPTH, TO, 256])
    flg_in = din("flg", [128, 4])
    Wd = {n: din(n, s) for n, s in WEIGHTS}
    sp_in = din("sp", [128, NSP])
    cst_in = din("cst", [128, 512])
    oh_in = din("oh", [33, XA + XB])
    out_d = nc.dram_tensor("out", [TO, D], F32, kind="ExternalOutput")
    dbg = {}
    for name, shape in cfg.get("dumps", []):
        dbg[name] = nc.dram_tensor("dbg_" + name, list(shape), F32, kind="ExternalOutput")

    def scr(name, shape, dt):
        if name in cfg.get("expose", ()):
            return nc.dram_tensor(name, list(shape), dt, kind="ExternalOutput")
        return nc.dram_tensor(name, list(shape), dt)
    xT = scr("xT", [D, TO], F32)
    yT = scr("yT", [D, TO], BF16)
    s_kvT = scr("s_kvT", [256, T], BF16)
    s_kvtok = scr("s_kvtok", [T, 256], BF16)
    s_kidxT = scr("s_kidxT", [64, T], BF16)
    s_widx = scr("s_widx", [TO, 16], F32)
    s_qiT = scr("s_qiT", [1024, TO], BF16)
    s_qaT = scr("s_qaT", [4096, TO], BF16)
    s_qbT = scr("s_qbT", [1024, TO], BF16)
    s_kdupT = scr("s_kdupT", [256, T], BF16)
    s_vbtok = scr("s_vbtok", [T, 128], BF16)
    s_qT = scr("s_qT", [2048, TO], BF16)
    s_kT = scr("s_kT", [2048, T], BF16)
    s_vtok = scr("s_vtok", [T, 2048], BF16)
    s_lf = scr("s_lf", [T, 32], F32)
    s_vrow = scr("s_vrow", [16, XA + XB], F32)
    s_nq = scr("s_nq", [3, 32, TO], BF16)
    s_expA = scr("s_expA", [128, 16 * 9 * 128], BF16)
    s_expB = scr("s_expB", [128, 16 * 2 * 128], BF16)
    kpack_e = scr("kpack_e", [960, 1024], BF16)
    gk_e = scr("gk_e", [1920, 1024], BF16)
    kpack_o = [scr(f"kpack_o{i}", [1024, 1024], BF16) for i in range(4)]
    gk_o = [scr(f"gk_o{i}", [2048, 1024], BF16) for i in range(4)]
    lfp = scr("lfp", [1024, 32], F32)
    glf = scr("glf", [2048, 32], F32)
    hx_in = scr("hx_in", [128, 32], F32)
    hxg = scr("hxg", [256, 32], F32)
    dbufs = {}

    def db(*key):
        if key not in dbufs:
            dbufs[key] = Buf(str(key))
        return dbufs[key]

    with ExitStack() as st:
        P = Prog(nc, st)

        def gsb(name, shape, dt):
            return st.enter_context(nc.sbuf_tensor(name, list(shape), dt))

        spk = gsb("spk", [128, NSP], F32)
        cst = gsb("cst_sb", [128, 512], F32)
        ident_bf = gsb("ident_bf", [128, 128], BF16)
        ones_bf = gsb("ones_bf", [128, 128], BF16)
        bd_bf = gsb("bd_bf", [128, 128], BF16)
        ones_f = gsb("ones_f", [128, 128], F32)
        eps_t = gsb("eps_t", [128, 1], F32)
        triu_bf = gsb("triu_bf", [128, 128], BF16)
        halo = gsb("halo", [128, 2], F32)
        flg = gsb("flg_sb", [128, 4], F32)
        b_flg = Buf("flg")
        ccs = P._newsem("ccs")
        cc_n = [0]
        WSLOT = 11008
        NWS = 2
        wbuf = [gsb(f"wbuf{i}", [128, WSLOT], BF16) for i in range(NWS)]
        wb_b = [Buf(f"wb{i}") for i in range(NWS)]
        wptr = [0]
        b_spk, b_cst, b_const, b_halo = Buf(), Buf(), Buf(), Buf()
        ident_f = cst[:, 0:128]
        jflip = cst[:, 128:256]
        triu_f = cst[:, 256:384]
        cneg = cst[:, 384:512]
        pb = [st.enter_context(nc.psum_tensor(f"pb{i}", [128, 512], F32)) for i in range(7)]
        pbh = st.enter_context(nc.psum_tensor("pbh", [128, 1024], BF16))
        pb_b = [Buf(f"pb{i}") for i in range(7)]
        pbh_b = Buf("pbh")

        P.dma("sp", spk[:], sp_in.ap(), writes=[b_spk])
        P.dma("sp", cst[:], cst_in.ap(), writes=[b_cst])
        P.dma("sp", flg[:], flg_in.ap(), writes=[b_flg])
        P.op("dve", lambda e: e.memset(ones_bf[:], 1.0), writes=[b_const])
        P.op("dve", lambda e: e.memset(ones_f[:], 1.0), writes=[b_const])
        P.op("dve", lambda e: e.memset(eps_t[:], EPS), writes=[b_const])
        P.op("dve", lambda e: e.memset(bd_bf[:], 0.0), writes=[b_const])
        P.op("dve", lambda e: e.memset(bd_bf[0:64, 0:64], 1.0), writes=[b_const])
        P.op("dve", lambda e: e.memset(bd_bf[64:128, 64:128], 1.0), writes=[b_const])
        P.op("dve", lambda e: e.tensor_copy(out=ident_bf[:], in_=ident_f), reads=[b_cst], writes=[b_const])
        P.op("dve", lambda e: e.tensor_copy(out=triu_bf[:], in_=triu_f), reads=[b_cst], writes=[b_const])
        P.op("dve", lambda e: e.memset(spk[32:33, SP_RELB:SP_RELB + 32], NEG), reads=[], writes=[b_spk])
        P.barrier()

        gemm_bank = [0]

        def cc_gather(src_t, dst_t, in_bufs, out_bufs):
            P._deps("pool", list(in_bufs), list(out_bufs))
            cc_n[0] += 1
            nc.gpsimd.collective_compute("AllGather", ALU.bypass, replica_groups=[[0, 4], [1, 5], [2, 6], [3, 7]],
                                         ins=[src_t.ap()], outs=[dst_t.ap()]).then_inc(ccs, 1)
            nc.gpsimd.wait_ge(ccs, cc_n[0])
            P.op("pool", lambda e: e.memset(halo[0:1, 0:1], 0.0), reads=list(in_bufs), writes=list(out_bufs))
        state = {}

        def wview(si, KC, ntot):
            return wbuf[si][:, 0:KC * ntot].rearrange("p (kc n) -> p kc n", n=ntot)

        def gemm(src, src_b, KC, wsrc, blocks, epi, ntc=2, tc_off=0, pre=None):
            for segs, chunks in blocks:
                si = wptr[0]
                wptr[0] = (wptr[0] + 1) % NWS
                ntot = sum(n for _, n in segs)
                wv = wview(si, KC, ntot)
                off = 0
                for (c0, ncols) in segs:
                    P.dma("pool", wv[:, :, off:off + ncols],
                          wsrc(c0, ncols).rearrange("(kc p) n -> p kc n", p=128), writes=[wb_b[si]])
                    off += ncols
                for (coff, m, tag, pbase) in chunks:
                    if pre is not None:
                        pre(wv, wb_b[si], coff, m, tag)
                    for tci in range(ntc):
                        bi = gemm_bank[0]
                        gemm_bank[0] = (gemm_bank[0] + 1) % 4
                        for kc in range(KC):
                            P.op("pe", lambda e: e.matmul(pb[bi][pbase:pbase + m, :], lhsT=wv[:, kc, coff:coff + m],
                                                          rhs=src[:, kc, tc_off + tci * TC:tc_off + (tci + 1) * TC],
                                                          start=(kc == 0), stop=(kc == KC - 1)),
                                 reads=[wb_b[si], src_b], writes=[pb_b[bi]])
                        epi(bi, m, tag, tci)

        def simple_blocks(col0, ncols_total, wcols, tagfn=None, m=128):
            blocks = []
            c = 0
            ci = 0
            while c < ncols_total:
                n = min(wcols, ncols_total - c)
                chunks = []
                o = 0
                while o < n:
                    mm = min(m, n - o)
                    chunks.append((o, mm, ci if tagfn is None else tagfn(ci), 0))
                    o += mm
                    ci += 1
                blocks.append(([(col0 + c, n)], chunks))
                c += n
            return blocks

        def rms_finish(S, src, src_b, C, lhsT_ones, gcol, inv_n, dst_fn, dst_bufs, ncols=TC, nparts=128):
            sq, sq_b, rstd, rstd_b = S["sq"], S["sq_b"], S["rstd"], S["rstd_b"]
            for c in range(C):
                P.op("act", lambda e: e.activation(out=sq[0:nparts, c, 0:ncols], in_=src(c), func=AF.Square),
                     reads=[src_b], writes=[sq_b])
            for c in range(C):
                P.op("pe", lambda e: e.matmul(pb[4][0:nparts, 0:ncols], lhsT=lhsT_ones[0:nparts, 0:nparts],
                                              rhs=sq[0:nparts, c, 0:ncols], start=(c == 0), stop=(c == C - 1)),
                     reads=[sq_b, b_const], writes=[pb_b[4]])
            P.op("act", lambda e: e.activation(out=rstd[0:nparts, 0:ncols], in_=pb[4][0:nparts, 0:ncols], func=AF.Sqrt,
                                               scale=inv_n, bias=eps_t[0:nparts, 0:1]),
                 reads=[b_const], writes=[pb_b[4], rstd_b])
            P.op("dve", lambda e: e.reciprocal(out=rstd[0:nparts, 0:ncols], in_=rstd[0:nparts, 0:ncols]),
                 reads=[rstd_b], writes=[rstd_b])
            for c in range(C):
                P.op("dve", lambda e: e.scalar_tensor_tensor(out=dst_fn(c), in0=src(c),
                                                             scalar=spk[0:nparts, gcol + c:gcol + c + 1],
                                                             in1=rstd[0:nparts, 0:ncols], op0=ALU.mult, op1=ALU.mult),
                     reads=[src_b, rstd_b, b_spk], writes=dst_bufs)

        def norm_x(S, hT, hT_b, gcol, t0):
            xs, xs_b = S["xs"], S["xs_b"]
            for tci in range(TT // TC):
                ta = t0 + tci * TC
                P.dma("sp", xs[:], xT.ap()[:, ta:ta + TC].rearrange("(kc p) t -> p kc t", p=128),
                      reads=[db("xT", ta // TC)], writes=[xs_b])
                rms_finish(S, lambda c: xs[:, c, :], xs_b, 16, ones_bf, gcol, 1.0 / D,
                           lambda c: hT[:, c, tci * TC:(tci + 1) * TC], [hT_b])

        def resid_epi(S, t0):
            def epi(bi, m, tag, tci):
                ta = t0 + tci * TC
                k = S["xr_i"][0]
                S["xr_i"][0] = (k + 1) % 2
                xr, xr_b = S["xr"][k], S["xr_b"][k]
                P.dma("sp", xr[:], xT.ap()[tag * 128:(tag + 1) * 128, ta:ta + TC],
                      reads=[db("xT", ta // TC)], writes=[xr_b])
                P.op("dve", lambda e: e.tensor_tensor(out=xr[:], in0=pb[bi][:], in1=xr[:], op=ALU.add),
                     reads=[xr_b], writes=[pb_b[bi], xr_b])
                P.dma("sp", xT.ap()[tag * 128:(tag + 1) * 128, ta:ta + TC], xr[:],
                      reads=[xr_b], writes=[db("xT", ta // TC)])
            return epi

        def dump(name, src_ap_dram):
            pass

        with Scope(P) as sc:
            xin = [sc.sb(f"xin{i}", [128, D], F32) for i in range(2)]
            xin_b = [Buf(), Buf()]
            stg = [sc.sb(f"xstg{i}", [128, 16, 128], F32) for i in range(2)]
            stg_b = [Buf(), Buf()]
            for tt in range(TO // 128):
                k = tt % 2
                P.dma("sp", xin[k][:], x_in.ap()[tt * 128:(tt + 1) * 128, :], writes=[xin_b[k]])
                for g in range(4):
                    bi = 5 + (g % 2)
                    for j in range(4):
                        fc = g * 4 + j
                        P.op("pe", lambda e: e.transpose(pb[bi][:, j * 128:(j + 1) * 128], xin[k][:, fc * 128:(fc + 1) * 128], ident_f),
                             reads=[xin_b[k], b_cst], writes=[pb_b[bi]])
                    P.op("act", lambda e: e.activation(out=stg[k][:, g * 4:(g + 1) * 4, :], in_=pb[bi][:].rearrange("p (a b) -> p a b", b=128), func=AF.Copy),
                         reads=[], writes=[pb_b[bi], stg_b[k]])
                P.dma("sp", xT.ap()[:, tt * 128:(tt + 1) * 128].rearrange("(fc p) t -> p fc t", p=128), stg[k][:],
                      reads=[stg_b[k]], writes=[db("xT", tt // 4)])

        for l in layers:
            E = dict(locals())
            E['state'] = state
            if "attn" in cfg.get("parts", ("attn", "out", "ffn", "ple")):
                if l % 2 == 0:
                    even_attention(E, l)
                else:
                    odd_attention(E, l)
            token_local(E, l, cfg.get("parts", ("attn", "out", "ffn", "ple")))

        with Scope(P) as sc:
            xo = [sc.sb(f"xo{i}", [128, 16, 128], F32) for i in range(2)]
            xo_b = [Buf(), Buf()]
            ostg = [sc.sb(f"ostg{i}", [128, D], F32) for i in range(2)]
            ostg_b = [Buf(), Buf()]
            for tt in range(TO // 128):
                k = tt % 2
                P.dma("sp", xo[k][:], xT.ap()[:, tt * 128:(tt + 1) * 128].rearrange("(fc p) t -> p fc t", p=128),
                      reads=[db("xT", tt // 4)], writes=[xo_b[k]])
                for g in range(4):
                    bi = 5 + (g % 2)
                    for j in range(4):
                        fc = g * 4 + j
                        P.op("pe", lambda e: e.transpose(pb[bi][:, j * 128:(j + 1) * 128], xo[k][:, fc, :], ident_f),
                             reads=[xo_b[k], b_cst], writes=[pb_b[bi]])
                    P.op("act", lambda e: e.activation(out=ostg[k][:, g * 512:(g + 1) * 512], in_=pb[bi][:], func=AF.Copy),
                         reads=[], writes=[pb_b[bi], ostg_b[k]])
                P.dma("sp", out_d.ap()[tt * 128:(tt + 1) * 128, :], ostg[k][:], reads=[ostg_b[k]], writes=[db("out")])
        P.barrier()
    return nc


_NC_CACHE = {}


def make_in_maps(inp, batches):
    cst, oh = host_consts()
    sp = pack_small(inp)
    wmap = {n: np.ascontiguousarray(inp[n], dtype=np.float32) for n, _ in WEIGHTS}
    in_maps = []
    for c in range(8):
        b = batches[c]
        half = c // 4
        flg = np.zeros((128, 4), np.float32)
        flg[:, 0] = float(half)
        flg[:, 1] = 0.0 if half else -1e30
        flg[:, 2] = 0.0 if half else NEG
        m = dict(x=np.ascontiguousarray(inp["x"][b, half * TO:(half + 1) * TO], dtype=np.float32),
                 p=np.ascontiguousarray(inp["p"][:, b, half * TO:(half + 1) * TO], dtype=np.float32),
                 flg=flg, sp=sp, cst=cst, oh=oh)
        m.update(wmap)
        in_maps.append(m)
    return in_maps


def kernel(**inputs):
    inp = {k: np.asarray(v) for k, v in inputs.items()}
    if "nc" not in _NC_CACHE:
        _NC_CACHE["nc"] = build()
    nc = _NC_CACHE["nc"]
    in_maps = make_in_maps(inp, [0, 1, 2, 3, 0, 1, 2, 3])
    res = run_bass_kernel_spmd(nc, in_maps, core_ids=list(range(8)))
    out = np.stack([np.concatenate([res.results[b]["out"], res.results[b + 4]["out"]], axis=0) for b in range(4)], axis=0)
    return out.astype(np.float32)
```
